# Optimizing a Trainium2 kernel written in Bass

```python
import math
import jax
import jax.numpy as jnp
from jax import lax
import numpy as np

D_MODEL = 1024
BATCH = 2
SEQ = 8192
DEPTH = 4
DEC_BATCH = 128
DEC_SEQ = 8
PAST_LEN = 8192
PAGE_SIZE = 128

N_META = 16
N_EVEN = (DEPTH + 1) // 2
N_ODD = DEPTH // 2
D_SSM = D_MODEL // 2
SSM_GROUP = 16
N_SSM_GROUPS = D_SSM // SSM_GROUP
SSM_STATE = 64
HEAD_DIM = 64
N_Q_HEADS = (D_MODEL // 2) // HEAD_DIM
N_KV_HEADS = 2
GQ = N_Q_HEADS // N_KV_HEADS
D_Q = N_Q_HEADS * HEAD_DIM
D_KV = N_KV_HEADS * HEAD_DIM
D_IN_MIX = D_SSM + D_Q + 2 * D_KV
D_OUT_MIX = D_SSM + D_Q
WINDOW = 128
BLOCK = 128
N_BUCKETS = 32
MAX_DISTANCE = 128
CONV_DIM = D_MODEL
CONV_WIDTH = 31
D_FF = 2816
FFN_CONV_WIDTH = 3
EPS = 1e-6
NEG_INF = -1e30

kernel_name = 'hybrid_s5_swa_conformer_decode_step'


def _window_rows():
    return min(WINDOW, PAST_LEN)


def rmsnorm(x, g):
    xf = x.astype(jnp.float32)
    y = xf * lax.rsqrt(jnp.mean(xf * xf, axis=-1, keepdims=True) + EPS) * g.astype(jnp.float32)
    return y.astype(x.dtype)


def causal_dwconv(x_ext, w):
    return lax.conv_general_dilated(x_ext, w.astype(x_ext.dtype)[:, None, :], (1,), 'VALID',
                                    dimension_numbers=('NWC', 'WIO', 'NWC'),
                                    feature_group_count=x_ext.shape[-1])


def t5_bucket(dist):
    n = jnp.maximum(dist, 0)
    max_exact = N_BUCKETS // 2
    nf = jnp.maximum(n, max_exact).astype(jnp.float32)
    large = max_exact + (jnp.log(nf / max_exact) / math.log(MAX_DISTANCE / max_exact)
                         * (N_BUCKETS - max_exact)).astype(jnp.int32)
    large = jnp.minimum(large, N_BUCKETS - 1)
    return jnp.where(n < max_exact, n, large)


def rel_bias_for(dist, rel_bias):
    b = rel_bias.astype(jnp.float32)[t5_bucket(dist)]
    return jnp.moveaxis(b, -1, 0).reshape(N_KV_HEADS, GQ, dist.shape[0], dist.shape[1])


def sink_softmax(s, sink):
    m = jnp.maximum(jnp.max(s, axis=-1, keepdims=True), sink)
    p = jnp.exp(s - m)
    return p / (jnp.sum(p, axis=-1, keepdims=True) + jnp.exp(sink - m))


def _complex_affine_combine(e1, e2):
    a1r, a1i, b1r, b1i = e1
    a2r, a2i, b2r, b2i = e2
    return (a2r * a1r - a2i * a1i, a2r * a1i + a2i * a1r,
            a2r * b1r - a2i * b1i + b2r, a2r * b1i + a2i * b1r + b2i)


def s5_mixer(u, h0_re, h0_im, lam_re, lam_im, log_step, b_re, b_im, c_re, c_im, d_skip, w_glu, b_glu):
    f32 = jnp.float32
    n, t, _ = u.shape
    uf = u.astype(f32)
    ug = uf.reshape(n, t, N_SSM_GROUPS, SSM_GROUP)
    lr, li = lam_re.astype(f32), lam_im.astype(f32)
    dt = jnp.exp(log_step.astype(f32))[:, None]
    decay = jnp.exp(lr * dt)
    a_re = decay * jnp.cos(li * dt)
    a_im = decay * jnp.sin(li * dt)
    den = lr * lr + li * li
    num_re = a_re - 1.0
    coef_re = (num_re * lr + a_im * li) / den
    coef_im = (a_im * lr - num_re * li) / den
    br, bi = b_re.astype(f32), b_im.astype(f32)
    bbar_re = coef_re[..., None] * br - coef_im[..., None] * bi
    bbar_im = coef_re[..., None] * bi + coef_im[..., None] * br
    bu_re = jnp.einsum('ntgc,gpc->ntgp', ug, bbar_re)
    bu_im = jnp.einsum('ntgc,gpc->ntgp', ug, bbar_im)
    h0r, h0i = h0_re.astype(f32), h0_im.astype(f32)
    bu_re = bu_re.at[:, 0].add(a_re * h0r - a_im * h0i)
    bu_im = bu_im.at[:, 0].add(a_re * h0i + a_im * h0r)
    a_full_re = jnp.broadcast_to(a_re, bu_re.shape)
    a_full_im = jnp.broadcast_to(a_im, bu_im.shape)
    _, _, hr, hi = lax.associative_scan(_complex_affine_combine, (a_full_re, a_full_im, bu_re, bu_im), axis=1)
    y = (jnp.einsum('ntgp,gcp->ntgc', hr, c_re.astype(f32))
         - jnp.einsum('ntgp,gcp->ntgc', hi, c_im.astype(f32)))
    y = y.reshape(n, t, D_SSM) + d_skip.astype(f32) * uf
    g = jax.nn.gelu(y)
    out = g * jax.nn.sigmoid(g @ w_glu.astype(f32) + b_glu.astype(f32))
    return out.astype(u.dtype), hr[:, -1], hi[:, -1]


def swa_prompt(q, k, v, rel_bias, sink, w_rows):
    n, t, _ = q.shape
    pad = BLOCK - N_META
    length = t + pad
    nb = length // BLOCK
    padt = lambda z: jnp.pad(z, ((0, 0), (pad, 0), (0, 0)))
    qb = padt(q).reshape(n, nb, BLOCK, N_KV_HEADS, GQ, HEAD_DIM)
    kb = padt(k).reshape(n, nb, BLOCK, N_KV_HEADS, HEAD_DIM)
    vb = padt(v).reshape(n, nb, BLOCK, N_KV_HEADS, HEAD_DIM)
    band = lambda z: jnp.concatenate(
        [jnp.concatenate([jnp.zeros_like(z[:, :1]), z[:, :-1]], axis=1), z], axis=2)
    kband, vband = band(kb), band(vb)
    i = jnp.arange(BLOCK)
    j = jnp.arange(2 * BLOCK)
    dist = BLOCK + i[:, None] - j[None, :]
    kpos = (jnp.arange(nb)[:, None] - 1) * BLOCK + j[None, :] - pad
    mask = (kpos >= 0)[:, None, :] & ((dist >= 0) & (dist < WINDOW))[None]
    s = jnp.einsum('bnqkgd,bnskd->bnkgqs', qb, kband, preferred_element_type=jnp.float32) * (HEAD_DIM ** -0.5)
    s = s + rel_bias_for(dist, rel_bias)
    s = jnp.where(mask[None, :, None, None], s, NEG_INF)
    p = sink_softmax(s, sink.astype(jnp.float32).reshape(N_KV_HEADS, GQ, 1, 1))
    o = jnp.einsum('bnkgqs,bnskd->bnqkgd', p.astype(vband.dtype), vband)
    o = o.reshape(n, length, D_Q)[:, pad:]
    k4 = k.reshape(n, t, N_KV_HEADS, HEAD_DIM)
    v4 = v.reshape(n, t, N_KV_HEADS, HEAD_DIM)
    return o, k4[:, -w_rows:], v4[:, -w_rows:]


def swa_sample(q, k, v, k_buf, v_buf, rel_bias, sink):
    n, t, _ = q.shape
    w = k_buf.shape[1]
    kc = jnp.concatenate([k_buf.astype(k.dtype), k.reshape(n, t, N_KV_HEADS, HEAD_DIM)], axis=1)
    vc = jnp.concatenate([v_buf.astype(v.dtype), v.reshape(n, t, N_KV_HEADS, HEAD_DIM)], axis=1)
    dist = (jnp.arange(t)[:, None] + w) - jnp.arange(w + t)[None, :]
    mask = (dist >= 0) & (dist < WINDOW)
    qh = q.reshape(n, t, N_KV_HEADS, GQ, HEAD_DIM)
    s = jnp.einsum('bqkgd,bskd->bkgqs', qh, kc, preferred_element_type=jnp.float32) * (HEAD_DIM ** -0.5)
    s = s + rel_bias_for(dist, rel_bias)
    s = jnp.where(mask, s, NEG_INF)
    p = sink_softmax(s, sink.astype(jnp.float32).reshape(N_KV_HEADS, GQ, 1, 1))
    o = jnp.einsum('bkgqs,bskd->bqkgd', p.astype(vc.dtype), vc).reshape(n, t, D_Q)
    return o, kc[:, -w:], vc[:, -w:]


def conformer_conv(h, past, w_pw1, w_dw, b_dw, ln_g, ln_b, w_pw2):
    z = h @ w_pw1
    a, g = jnp.split(z, 2, axis=-1)
    gl = a * jax.nn.sigmoid(g)
    ext = jnp.concatenate([past.astype(gl.dtype), gl], axis=1)
    y = causal_dwconv(ext, w_dw).astype(jnp.float32) + b_dw.astype(jnp.float32)
    mu = jnp.mean(y, axis=-1, keepdims=True)
    yc = y - mu
    var = jnp.mean(yc * yc, axis=-1, keepdims=True)
    y = yc * lax.rsqrt(var + EPS) * ln_g.astype(jnp.float32) + ln_b.astype(jnp.float32)
    y = jax.nn.silu(y).astype(h.dtype)
    return y @ w_pw2, ext[:, -(CONV_WIDTH - 1):]


def conv_ffn(h, past, w_up, w_conv, b_conv, w_down):
    z = h @ w_up
    g, u = jnp.split(z, 2, axis=-1)
    ext = jnp.concatenate([past.astype(g.dtype), g], axis=1)
    gc = causal_dwconv(ext, w_conv) + b_conv.astype(g.dtype)
    y = jax.nn.gelu(gc) * u
    return y @ w_down, ext[:, -(FFN_CONV_WIDTH - 1):]


def trunk(x, ssm_re, ssm_im, swa_k, swa_v, conv_st, ffn_st, p):
    n = x.shape[0]
    prompt = swa_k is None
    w_rows = _window_rows()
    new_sr, new_si, new_k, new_v, new_c, new_f = [], [], [], [], [], []
    for layer in range(DEPTH):
        idx = layer // 2
        h = rmsnorm(x, p['g_mix'][layer])
        if layer % 2 == 0:
            z = h @ p['w_in_mix'][idx]
            u = z[..., :D_SSM]
            q = z[..., D_SSM:D_SSM + D_Q]
            k = z[..., D_SSM + D_Q:D_SSM + D_Q + D_KV]
            v = z[..., D_SSM + D_Q + D_KV:]
            if prompt:
                h0r = jnp.zeros((n, N_SSM_GROUPS, SSM_STATE), jnp.float32)
                h0i = jnp.zeros((n, N_SSM_GROUPS, SSM_STATE), jnp.float32)
            else:
                h0r, h0i = ssm_re[idx], ssm_im[idx]
            ya, sr, si = s5_mixer(u, h0r, h0i, p['ssm_lambda_re'][idx], p['ssm_lambda_im'][idx],
                                  p['ssm_log_step'][idx], p['ssm_b_re'][idx], p['ssm_b_im'][idx],
                                  p['ssm_c_re'][idx], p['ssm_c_im'][idx], p['ssm_d'][idx],
                                  p['ssm_w_glu'][idx], p['ssm_b_glu'][idx])
            if prompt:
                yb, nk, nv = swa_prompt(q, k, v, p['rel_bias'], p['attn_sinks'][idx], w_rows)
            else:
                yb, nk, nv = swa_sample(q, k, v, swa_k[idx], swa_v[idx], p['rel_bias'], p['attn_sinks'][idx])
            x = x + jnp.concatenate([ya, yb.astype(ya.dtype)], axis=-1) @ p['w_out_mix'][idx]
            new_sr.append(sr)
            new_si.append(si)
            new_k.append(nk)
            new_v.append(nv)
        else:
            past = jnp.zeros((n, CONV_WIDTH - 1, CONV_DIM), h.dtype) if prompt else conv_st[idx]
            yc, nc = conformer_conv(h, past, p['conv_w_pw1'][idx], p['conv_w_dw'][idx], p['conv_b_dw'][idx],
                                    p['conv_ln_g'][idx], p['conv_ln_b'][idx], p['conv_w_pw2'][idx])
            x = x + yc
            new_c.append(nc)
        h = rmsnorm(x, p['g_ffn'][layer])
        past = jnp.zeros((n, FFN_CONV_WIDTH - 1, D_FF), h.dtype) if prompt else ffn_st[layer]
        yf, nf = conv_ffn(h, past, p['ffn_w_up'][layer], p['ffn_w_conv'][layer], p['ffn_b_conv'][layer],
                          p['ffn_w_down'][layer])
        x = x + yf
        new_f.append(nf)
    states = (jnp.stack(new_sr), jnp.stack(new_si), jnp.stack(new_k), jnp.stack(new_v),
              jnp.stack(new_c), jnp.stack(new_f))
    return rmsnorm(x, p['g_final']), states


def setup_inputs(seed: int = 0) -> dict:
    key = jax.random.key(seed)
    ks = jax.random.split(key, 40)
    f32 = jnp.float32
    nrm = lambda k, shape, scale: jax.random.normal(k, shape, f32) * scale
    w_rows = _window_rows()
    inp = {}
    inp['x_prompt'] = nrm(ks[0], (BATCH, SEQ, D_MODEL), 1.0)
    inp['x_sample'] = nrm(ks[1], (DEC_BATCH, DEC_SEQ, D_MODEL), 1.0)
    inp['state_ssm_re'] = nrm(ks[2], (N_EVEN, DEC_BATCH, N_SSM_GROUPS, SSM_STATE), 0.1)
    inp['state_ssm_im'] = nrm(ks[3], (N_EVEN, DEC_BATCH, N_SSM_GROUPS, SSM_STATE), 0.1)
    inp['cache_swa_k'] = nrm(ks[4], (N_EVEN, DEC_BATCH, w_rows, N_KV_HEADS, HEAD_DIM), 1.0)
    inp['cache_swa_v'] = nrm(ks[5], (N_EVEN, DEC_BATCH, w_rows, N_KV_HEADS, HEAD_DIM), 1.0)
    inp['state_conv'] = nrm(ks[6], (N_ODD, DEC_BATCH, CONV_WIDTH - 1, CONV_DIM), 0.5)
    inp['state_ffn'] = nrm(ks[7], (DEPTH, DEC_BATCH, FFN_CONV_WIDTH - 1, D_FF), 1.0)
    inp['meta_tokens'] = nrm(ks[8], (N_META, D_MODEL), 1.0)
    inp['g_mix'] = 1.0 + nrm(ks[9], (DEPTH, D_MODEL), 0.01)
    inp['g_ffn'] = 1.0 + nrm(ks[10], (DEPTH, D_MODEL), 0.01)
    inp['g_final'] = 1.0 + nrm(ks[11], (D_MODEL,), 0.01)
    inp['w_in_mix'] = nrm(ks[12], (N_EVEN, D_MODEL, D_IN_MIX), D_MODEL ** -0.5)
    inp['ssm_lambda_re'] = -0.5 + nrm(ks[13], (N_EVEN, N_SSM_GROUPS, SSM_STATE), 0.01)
    inp['ssm_lambda_im'] = (jnp.pi * jnp.arange(SSM_STATE, dtype=f32)
                            + nrm(ks[14], (N_EVEN, N_SSM_GROUPS, SSM_STATE), 0.01))
    inp['ssm_log_step'] = jax.random.uniform(ks[15], (N_EVEN, N_SSM_GROUPS), f32,
                                             math.log(1e-3), math.log(1e-1))
    inp['ssm_b_re'] = nrm(ks[16], (N_EVEN, N_SSM_GROUPS, SSM_STATE, SSM_GROUP), (2 * SSM_GROUP) ** -0.5)
    inp['ssm_b_im'] = nrm(ks[17], (N_EVEN, N_SSM_GROUPS, SSM_STATE, SSM_GROUP), (2 * SSM_GROUP) ** -0.5)
    inp['ssm_c_re'] = nrm(ks[18], (N_EVEN, N_SSM_GROUPS, SSM_GROUP, SSM_STATE), 0.5)
    inp['ssm_c_im'] = nrm(ks[19], (N_EVEN, N_SSM_GROUPS, SSM_GROUP, SSM_STATE), 0.5)
    inp['ssm_d'] = nrm(ks[20], (N_EVEN, D_SSM), 1.0)
    inp['ssm_w_glu'] = nrm(ks[21], (N_EVEN, D_SSM, D_SSM), D_SSM ** -0.5)
    inp['ssm_b_glu'] = nrm(ks[22], (N_EVEN, D_SSM), 0.01)
    inp['rel_bias'] = nrm(ks[23], (N_BUCKETS, N_Q_HEADS), 0.5)
    inp['attn_sinks'] = nrm(ks[24], (N_EVEN, N_Q_HEADS), 0.5)
    inp['w_out_mix'] = nrm(ks[25], (N_EVEN, D_OUT_MIX, D_MODEL), D_OUT_MIX ** -0.5)
    inp['conv_w_pw1'] = nrm(ks[26], (N_ODD, D_MODEL, 2 * CONV_DIM), D_MODEL ** -0.5)
    inp['conv_w_dw'] = nrm(ks[27], (N_ODD, CONV_WIDTH, CONV_DIM), CONV_WIDTH ** -0.5)
    inp['conv_b_dw'] = nrm(ks[28], (N_ODD, CONV_DIM), 0.01)
    inp['conv_ln_g'] = 1.0 + nrm(ks[29], (N_ODD, CONV_DIM), 0.01)
    inp['conv_ln_b'] = nrm(ks[30], (N_ODD, CONV_DIM), 0.01)
    inp['conv_w_pw2'] = nrm(ks[31], (N_ODD, CONV_DIM, D_MODEL), CONV_DIM ** -0.5)
    inp['ffn_w_up'] = nrm(ks[32], (DEPTH, D_MODEL, 2 * D_FF), D_MODEL ** -0.5)
    inp['ffn_w_conv'] = nrm(ks[33], (DEPTH, FFN_CONV_WIDTH, D_FF), FFN_CONV_WIDTH ** -0.5)
    inp['ffn_b_conv'] = nrm(ks[34], (DEPTH, D_FF), 0.01)
    inp['ffn_w_down'] = nrm(ks[35], (DEPTH, D_FF, D_MODEL), D_FF ** -0.5)
    return inp


def reference(x_prompt, x_sample, state_ssm_re, state_ssm_im, cache_swa_k, cache_swa_v, state_conv, state_ffn,
              meta_tokens, g_mix, g_ffn, g_final, w_in_mix, ssm_lambda_re, ssm_lambda_im, ssm_log_step,
              ssm_b_re, ssm_b_im, ssm_c_re, ssm_c_im, ssm_d, ssm_w_glu, ssm_b_glu, rel_bias, attn_sinks,
              w_out_mix, conv_w_pw1, conv_w_dw, conv_b_dw, conv_ln_g, conv_ln_b, conv_w_pw2,
              ffn_w_up, ffn_w_conv, ffn_b_conv, ffn_w_down):
    p = {'g_mix': g_mix, 'g_ffn': g_ffn, 'g_final': g_final, 'w_in_mix': w_in_mix,
         'ssm_lambda_re': ssm_lambda_re, 'ssm_lambda_im': ssm_lambda_im, 'ssm_log_step': ssm_log_step,
         'ssm_b_re': ssm_b_re, 'ssm_b_im': ssm_b_im, 'ssm_c_re': ssm_c_re, 'ssm_c_im': ssm_c_im,
         'ssm_d': ssm_d, 'ssm_w_glu': ssm_w_glu, 'ssm_b_glu': ssm_b_glu, 'rel_bias': rel_bias,
         'attn_sinks': attn_sinks, 'w_out_mix': w_out_mix, 'conv_w_pw1': conv_w_pw1, 'conv_w_dw': conv_w_dw,
         'conv_b_dw': conv_b_dw, 'conv_ln_g': conv_ln_g, 'conv_ln_b': conv_ln_b, 'conv_w_pw2': conv_w_pw2,
         'ffn_w_up': ffn_w_up, 'ffn_w_conv': ffn_w_conv, 'ffn_b_conv': ffn_b_conv, 'ffn_w_down': ffn_w_down}
    n_b = x_prompt.shape[0]
    meta = jnp.broadcast_to(meta_tokens.astype(x_prompt.dtype)[None], (n_b, N_META, D_MODEL))
    xp = jnp.concatenate([meta, x_prompt], axis=1)
    yp, (sr_p, si_p, k_p, v_p, c_p, f_p) = trunk(xp, None, None, None, None, None, None, p)
    ys, (sr_s, si_s, k_s, v_s, c_s, f_s) = trunk(x_sample, state_ssm_re, state_ssm_im, cache_swa_k, cache_swa_v,
                                                 state_conv, state_ffn, p)
    return (yp[:, N_META:], ys, sr_p, si_p, k_p, v_p, c_p, f_p, sr_s, si_s, k_s, v_s, c_s, f_s)
```

```python
import contextlib
import math
import numpy as np
import concourse.bass as bass
import concourse.mybir as mybir
from concourse.bass_utils import run_bass_kernel_spmd

F32 = mybir.dt.float32
BF16 = mybir.dt.bfloat16
AF = mybir.ActivationFunctionType
ALU = mybir.AluOpType

D = 1024
NC8 = 8
DFF = 2816
NJ = 22
NPAD = 112
TPAD = 8320
NBLK = 65
NSEQ = 16
EPS = 1e-6
NEG = -30000.0


def t5_bucket_np(dist):
    n = np.maximum(dist, 0)
    max_exact = 16
    nf = np.maximum(n, max_exact).astype(np.float32)
    large = max_exact + (np.log(nf / max_exact) / math.log(128 / max_exact) * (32 - max_exact)).astype(np.int32)
    large = np.minimum(large, 31)
    return np.where(n < max_exact, n, large)


class Buf:
    __slots__ = ("t", "name", "last_w", "readers", "dsem", "dcnt")

    def __init__(self, t, name):
        self.t = t
        self.name = name
        self.last_w = None
        self.readers = {}
        self.dsem = None
        self.dcnt = 0

    def __getitem__(self, idx):
        return self.t[idx]


class _Stop(Exception):
    pass


_LAST = {}


class KB:
    def __init__(self):
        self.nc = bass.Bass("TRN2", target_bir_lowering=False)
        self.es = contextlib.ExitStack()
        nc = self.nc
        self.eng = {"pe": nc.tensor, "act": nc.scalar, "dve": nc.vector, "pool": nc.gpsimd, "sp": nc.sync}
        self.sem = {}
        self.cnt = {}
        self.seen = {e: {} for e in self.eng}
        self.uid = 0
        self.psum_rr = 0
        self.out_events = []
        self.dead = False

    def start(self):
        import os
        for i in range(int(os.environ.get("KDUMMYSEM", "0"))):
            self.es.enter_context(self.nc.semaphore("dummy%d" % i))
        for e in self.eng:
            self.sem[e] = self.es.enter_context(self.nc.semaphore("prog_" + e))
            self.cnt[e] = 0
        self.banks = []
        for i in range(8):
            t = self.es.enter_context(self.nc.psum_tensor("bank%d" % i, [128, 512], F32))
            self.banks.append(Buf(t, "bank%d" % i))

    def sb(self, name, shape, dtype):
        self.uid += 1
        t = self.es.enter_context(self.nc.sbuf_tensor("%s_%d" % (name, self.uid), list(shape), dtype))
        return Buf(t, name)

    def dram(self, name, shape, dtype, kind):
        return self.nc.dram_tensor(name, list(shape), dtype, kind=kind)

    def bank(self):
        b = self.banks[self.psum_rr % 8]
        self.psum_rr += 1
        return b

    def _waits(self, e, reads, writes):
        deps = []
        for b in reads:
            if b.last_w is not None:
                deps.append((b.last_w, True))
        for b in writes:
            if b.last_w is not None:
                deps.append((b.last_w, False))
            for ev in b.readers.values():
                deps.append((ev, False))
        own = self.sem.get(e)
        for (sem, val), raw in deps:
            if sem is own:
                if e in ("pe", "sp"):
                    continue
            key = id(sem)
            if self.seen[e].get(key, 0) < val:
                self.eng[e].wait_ge(sem, val)
                self.seen[e][key] = val

    def chk(self, tag):
        import os
        if os.environ.get("KSTOP") == tag:
            for i in range(int(os.environ.get("KEXTRA", "0"))):
                tgt = self._xtra if os.environ.get("KXT") else self._misc
                w = 128 if os.environ.get("KXT") else 1
                if os.environ.get("KXT") == "2":
                    self.op("dve", lambda: self.nc.vector.tensor_copy(self._xtra[:, 0:128], self._xtra2[:, 0:128]), reads=[self._xtra2], writes=[self._xtra])
                else:
                    self.op("dve", lambda: self.nc.vector.memset(tgt[:, 0:w], 0.0), writes=[tgt])
            self.dead = True
            if os.environ.get("KRAISE"):
                self.dead = False
                self.finish()
                raise _Stop()

    def op(self, e, fn, reads=(), writes=(), inc=True):
        if self.dead:
            return None
        self._waits(e, reads, writes)
        inst = fn()
        if inc:
            self.cnt[e] += 1
            inst.then_inc(self.sem[e], 1)
            ev = (self.sem[e], self.cnt[e])
        else:
            ev = (self.sem[e], self.cnt[e] + 1)
        for b in writes:
            b.last_w = ev
            b.readers = {}
        for b in reads:
            b.readers[e] = ev
        return inst

    def dma(self, q, out_ap, in_ap, reads=(), writes=(), slow=False, is_out=False):
        if self.dead:
            return None
        self._waits(q, reads, writes)
        kw = {}
        if slow:
            kw["allow_slow_non_contiguous"] = True
        inst = self.eng[q].dma_start(out=out_ap, in_=in_ap, **kw)
        tgt = writes[0] if writes else (reads[0] if reads else None)
        if tgt is None:
            tgt = self._misc
        if tgt.dsem is None:
            self.uid += 1
            tgt.dsem = self.es.enter_context(self.nc.semaphore("d_%s_%d" % (tgt.name, self.uid)))
        tgt.dcnt += 16
        inst.then_inc(tgt.dsem, 16)
        ev = (tgt.dsem, tgt.dcnt)
        for b in writes:
            b.last_w = ev
            b.readers = {}
        for b in reads:
            b.readers[("dma", id(tgt))] = ev
        self.out_events.append(ev)
        return inst

    def finish(self):
        last = {}
        for sem, val in self.out_events:
            k = id(sem)
            if k not in last or last[k][1] < val:
                last[k] = (sem, val)
        for sem, val in last.values():
            self.eng["sp"].wait_ge(sem, val)
        for e in self.eng:
            for e2 in ("pe", "act", "dve", "pool"):
                if e2 != e and self.cnt[e2] > 0:
                    self.eng[e].wait_ge(self.sem[e2], self.cnt[e2])


def V(buf, p0, np_, off, dims):
    t = buf.t
    shape = t.shape
    fsz = 1
    for s in shape[1:]:
        fsz *= s
    return bass.AP(t, p0 * fsz + off, [[fsz, np_]] + [[s, c] for (s, c) in dims])


def build(nchunk_prompt=17, do_sample=True, dbg=False, dbg_ci=1):
    try:
        return _build(nchunk_prompt, do_sample, dbg, dbg_ci)
    except _Stop:
        return _LAST["kb"].nc


def _build(nchunk_prompt=17, do_sample=True, dbg=False, dbg_ci=1):
    kb = KB()
    _LAST["kb"] = kb
    nc = kb.nc
    es = kb.es
    with es:
        kb.start()
        import os as _os
        kb._misc = kb.sb("misc", [128, 1], F32)
        PE, ACT, DVE, POOL, SP = "pe", "act", "dve", "pool", "sp"
        din = {}

        def DI(name, shape):
            din[name] = kb.dram(name, shape, F32, "ExternalInput")
            return din[name]

        dout = {}

        def DO(name, shape):
            dout[name] = kb.dram(name, shape, F32, "ExternalOutput")
            return dout[name]

        xT_p = DI("xT_p", [D, TPAD])
        xT_s = DI("xT_s", [D, 128])
        st_ssm = DI("st_ssm", [2, 2, 128, 16, NSEQ])
        st_kT = DI("st_kT", [2, NSEQ, 128, 128])
        st_v = DI("st_v", [2, NSEQ, 128, 128])
        st_conv = DI("st_conv", [2, D, NSEQ, 30])
        st_ffn = DI("st_ffn", [4, DFF, NSEQ, 2])
        gvec = DI("gvec", [128, 9, 8])
        w_in = DI("w_in", [2, D, 1408])
        lam = DI("lam", [2, 3, 128, 16])
        ssm_b = DI("ssm_b", [2, 2, 32, 64, 16])
        ssm_c = DI("ssm_c", [2, 2, 32, 16, 64])
        ssm_d = DI("ssm_d", [2, 128, 4])
        w_glu = DI("w_glu", [2, 512, 512])
        b_glu = DI("b_glu", [2, 128, 4])
        relb = DI("relb", [32, 8])
        sinks = DI("sinks", [2, 128, 4])
        w_out = DI("w_out", [2, D, D])
        w_pw1 = DI("w_pw1", [2, D, 8, 256])
        w_dw = DI("w_dw", [2, 128, 8, 31])
        cvec = DI("cvec", [2, 128, 3, 8])
        w_pw2 = DI("w_pw2", [2, D, D])
        w_up = DI("w_up", [4, D, NJ, 256])
        f_cw = DI("f_cw", [4, 128, NJ, 3])
        f_cb = DI("f_cb", [4, 128, NJ])
        w_dn = DI("w_dn", [4, DFF, D])
        oh_bucket = DI("oh_bucket", [32, 384])
        msk_ext = DI("msk_ext", [8, 384])
        antiI = DI("antiI", [128, 128])

        yT_p = DO("yT_p", [D, TPAD])
        yT_s = DO("yT_s", [D, 128])
        o_ssm_p = DO("o_ssm_p", [2, 2, 128, 16])
        o_ssm_s = DO("o_ssm_s", [2, 2, 128, 16, NSEQ])
        o_kT_p = DO("o_kT_p", [2, 128, 128])
        o_v_p = DO("o_v_p", [2, 128, 128])
        o_kT_s = DO("o_kT_s", [2, NSEQ, 128, 128])
        o_v_s = DO("o_v_s", [2, NSEQ, 128, 128])
        o_conv_p = DO("o_conv_p", [2, D, 30])
        o_conv_s = DO("o_conv_s", [2, D, NSEQ, 30])
        o_ffn_p = DO("o_ffn_p", [4, DFF, 2])
        o_ffn_s = DO("o_ffn_s", [4, DFF, NSEQ, 2])
        scr = kb.dram("scr_bias", [8, 384], F32, "Internal")
        if dbg:
            dbg_o = DO("dbg", [16, D, 512])
        dbgc = [0]

        def dump16(Xl, W):
            if not dbg:
                return
            for c in range(8):
                kb.dma(POOL, dbg_o.ap()[dbgc[0], c * 128:(c + 1) * 128, 0:W], Xl[c][:, 0:W], reads=[Xl[c]], is_out=True)
            dbgc[0] += 1

        def dump(Xl, W):
            if not dbg:
                return
            for c in range(8):
                kb.dma(SP, dbg_o.ap()[dbgc[0], c * 128:(c + 1) * 128, 0:W], Xl[c][:, 0:W], reads=[Xl[c]], is_out=True)
            dbgc[0] += 1

        ident = kb.sb("ident", [128, 128], F32)
        kb.op(POOL, lambda: nc.gpsimd.memset(ident[:], 1.0), writes=[ident])
        kb.op(POOL, lambda: nc.gpsimd.affine_select(ident[:], ident[:], pattern=[[-1, 128]], compare_op=ALU.is_equal,
                                                     fill=0.0, base=0, channel_multiplier=1), reads=[ident], writes=[ident])
        ones_bf = kb.sb("ones_bf", [128, 128], BF16)
        kb.op(DVE, lambda: nc.vector.memset(ones_bf[:], 1.0), writes=[ones_bf])
        Oz = kb.sb("Oz", [128, 192], BF16)
        Oz0 = kb.sb("Oz0", [128, 192], BF16)
        for o in (Oz, Oz0):
            kb.op(DVE, lambda o=o: nc.vector.memset(o[:], 0.0), writes=[o])
            kb.op(DVE, lambda o=o: nc.vector.memset(o[:, 64:128], 1.0), writes=[o])
        kb.op(DVE, lambda: nc.vector.memset(Oz0[0:NPAD, :], 0.0), writes=[Oz0])
        gv = kb.sb("gv", [128, 9, 8], F32)
        kb.dma(SP, gv[:], gvec.ap(), writes=[gv])
        kb.chk("c0")

        WSL = [kb.sb("wslab%d" % i, [128, 8, 256], BF16) for i in range(2)]
        WDN = [kb.sb("wdn%d" % i, [128, 11, 128], BF16) for i in range(2)]
        wctr = {"a": 0, "b": 0}
        UT = [kb.sb("UT%d" % m, [128, 512], BF16) for m in range(4)]
        U32 = [kb.sb("U32_%d" % m, [128, 512], F32) for m in range(4)]
        QT = [[kb.sb("QT%d_%d" % (e, c), [128, 512], BF16) for c in range(4)] for e in range(2)]
        for e in range(2):
            for c in range(4):
                kb.op(POOL, lambda e=e, c=c: nc.gpsimd.memset(QT[e][c][:], 0.0), writes=[QT[e][c]])

        def load_w(dram_ap, kc, ncols):
            b = WSL[wctr["a"] % len(WSL)]
            wctr["a"] += 1
            import os
            if os.environ.get("KLOADW") == "hw" and ncols == 128:
                src = dram_ap.rearrange("(k p) n -> p k n", p=128)
                for hh in range(0, kc, 4):
                    stg = P32[8 + (hh // 4)]
                    kb.dma(SP, V(stg, 0, 128, 0, [(128, 4), (1, 128)]), src[:, hh:hh + 4, :], writes=[stg])
                    kb.op(DVE, lambda: nc.vector.tensor_copy(b[:, hh:hh + 4, 0:128], V(stg, 0, 128, 0, [(128, 4), (1, 128)])), reads=[stg], writes=[b])
                return b
            kb.dma(POOL, b[:, 0:kc, 0:ncols], dram_ap.rearrange("(k p) n -> p k n", p=128), writes=[b])
            return b

        def load_wdn(dram_ap):
            b = WDN[wctr["b"] % len(WDN)]
            wctr["b"] += 1
            kb.dma(POOL, b[:, :, :], dram_ap.rearrange("(j p) n -> p j n", p=128), writes=[b])
            return b

        def mm_group(out_ap, pairs, bankbuf, rbufs):
            n = len(pairs)
            if _os.environ.get("KHOIST"):
                allr = []
                for (_l, _r, bs_) in pairs:
                    allr += list(bs_)
                kb._waits(PE, allr + list(rbufs), [bankbuf])
            for i, (l, r, bs) in enumerate(pairs):
                kb.op(PE, lambda l=l, r=r, i=i: nc.tensor.matmul(out_ap, l, r, start=(i == 0), stop=(i == n - 1)),
                      reads=list(bs) + list(rbufs), writes=[bankbuf], inc=(i == n - 1))

        P32 = [kb.sb("P32_%d" % i, [128, 608], F32) for i in range(14)]
        P16 = [kb.sb("P16_%d" % i, [128, 512], BF16) for i in range(22)]
        kb._xtra = P32[5]
        kb._xtra2 = P32[6]
        sq = P16[12:20]
        rs = P32[12]
        rinv = P32[13]
        eps_t = kb.sb("eps_t", [128, 1], F32)
        kb.op(DVE, lambda: nc.vector.memset(eps_t[:], EPS), writes=[eps_t])

        def rmsnorm(X, gi, Hout, W, out_f32=None):
            lvl = int(_os.environ.get("KRMS", "9"))
            if lvl < 1:
                return
            for c in range(8):
                kb.op(DVE, lambda c=c: nc.vector.tensor_tensor(sq[c][:, 0:W], X[c][:, 0:W], X[c][:, 0:W], ALU.mult), reads=[X[c]], writes=[sq[c]])
            if lvl < 2:
                return
            bk = kb.bank()
            mm_group(bk[:, 0:W], [(ones_bf[:], sq[c][:, 0:W], [sq[c]]) for c in range(8)], bk, [ones_bf])
            if lvl < 3:
                return
            kb.op(DVE, lambda: nc.vector.tensor_scalar(rs[:, 0:W], bk[:, 0:W], 1.0 / D, EPS, ALU.mult, ALU.add), reads=[bk], writes=[rs])
            kb.op(ACT, lambda: nc.scalar.activation(rs[:, 0:W], rs[:, 0:W], AF.Sqrt), reads=[rs], writes=[rs])
            if lvl < 4:
                return
            kb.op(DVE, lambda: nc.vector.reciprocal(rinv[:, 0:W], rs[:, 0:W]), reads=[rs], writes=[rinv])
            if lvl < 5:
                return
            for c in range(8):
                o = Hout[c] if out_f32 is None else out_f32[c]
                kb.op(DVE, lambda c=c, o=o: nc.vector.scalar_tensor_tensor(o[:, 0:W], X[c][:, 0:W], gv[:, gi, c:c + 1], rinv[:, 0:W],
                                                                          ALU.mult, ALU.mult),
                      reads=[X[c], gv, rinv], writes=[o])

        X = [kb.sb("X%d" % c, [128, 512], F32) for c in range(8)]
        H = [kb.sb("H%d" % c, [128, 512], BF16) for c in range(8)]

        S5 = []
        import os as _os
        for li in range(0 if _os.environ.get("KSKIP_S5") else 2):
            T = {}
            lm = kb.sb("lam", [128, 3, 16], F32)
            kb.dma(SP, lm[:], lam.ap()[li].rearrange("k p j -> p k j"), writes=[lm])
            dt = kb.sb("dt", [128, 16], F32)
            kb.op(ACT, lambda: nc.scalar.activation(dt[:], lm[:, 2, :], AF.Exp), reads=[lm], writes=[dt])
            lrdt = kb.sb("lrdt", [128, 16], F32)
            th = kb.sb("th", [128, 16], F32)
            kb.op(DVE, lambda: nc.vector.tensor_tensor(lrdt[:], lm[:, 0, :], dt[:], ALU.mult), reads=[lm, dt], writes=[lrdt])
            kb.op(DVE, lambda: nc.vector.tensor_tensor(th[:], lm[:, 1, :], dt[:], ALU.mult), reads=[lm, dt], writes=[th])
            rho = kb.sb("rho", [128, 16], F32)
            kb.op(ACT, lambda: nc.scalar.activation(rho[:], lrdt[:], AF.Exp), reads=[lrdt], writes=[rho])
            T["rho"] = rho
            kk = P32[0]
            kb.op(POOL, lambda: nc.gpsimd.iota(kk[:, 0:64], pattern=[[1, 64]], base=1, channel_multiplier=0,
                                               allow_small_or_imprecise_dtypes=True), writes=[kk])
            ctab = kb.sb("ctab", [128, 16, 64], F32)
            stab = kb.sb("stab", [128, 16, 64], F32)
            negpi = kb.sb("negpi", [128, 1], F32)
            ki32 = kb.sb("ki32", [128, 64], mybir.dt.int32)
            kb.op(DVE, lambda: nc.vector.memset(negpi[:], -math.pi), writes=[negpi])
            for j in range(16):
                ang = P32[1 + (j % 2)]
                kb.op(DVE, lambda j=j, ang=ang: nc.vector.tensor_scalar(ang[:, 0:64], kk[:, 0:64], th[:, j:j + 1], None, ALU.mult),
                      reads=[kk, th], writes=[ang])
                for (dst, sh, ti) in ((stab, 0.5, 3), (ctab, 0.75, 5)):
                    tmp = P32[ti + (j % 2)]
                    kb.op(DVE, lambda sh=sh, tmp=tmp, ang=ang: nc.vector.tensor_scalar(tmp[:, 0:64], ang[:, 0:64], 1.0 / (2 * math.pi), sh, ALU.mult, ALU.add),
                          reads=[ang], writes=[tmp])
                    kb.op(DVE, lambda tmp=tmp: nc.vector.tensor_copy(ki32[:, 0:64], tmp[:, 0:64]), reads=[tmp], writes=[ki32])
                    kb.op(DVE, lambda tmp=tmp: nc.vector.tensor_copy(tmp[:, 64:128], ki32[:, 0:64]), reads=[ki32], writes=[tmp])
                    kb.op(DVE, lambda tmp=tmp: nc.vector.tensor_tensor(tmp[:, 0:64], tmp[:, 0:64], tmp[:, 64:128], ALU.subtract), reads=[tmp], writes=[tmp])
                    kb.op(DVE, lambda tmp=tmp: nc.vector.tensor_scalar(tmp[:, 64:128], tmp[:, 0:64], 0.0, None, ALU.is_lt), reads=[tmp], writes=[tmp])
                    kb.op(DVE, lambda tmp=tmp: nc.vector.tensor_tensor(tmp[:, 0:64], tmp[:, 0:64], tmp[:, 64:128], ALU.add), reads=[tmp], writes=[tmp])
                    kb.op(ACT, lambda dst=dst, tmp=tmp, j=j: nc.scalar.activation(dst[:, j, :], tmp[:, 0:64], AF.Sin, bias=negpi[:], scale=2 * math.pi),
                          reads=[tmp, negpi], writes=[dst])
            T["ctab"], T["stab"] = ctab, stab
            kb.chk("s5ang")
            are = kb.sb("are", [128, 16], F32)
            aim = kb.sb("aim", [128, 16], F32)
            kb.op(DVE, lambda: nc.vector.tensor_tensor(are[:], rho[:], ctab[:, :, 0], ALU.mult), reads=[rho, ctab], writes=[are])
            kb.op(DVE, lambda: nc.vector.tensor_tensor(aim[:], rho[:], stab[:, :, 0], ALU.mult), reads=[rho, stab], writes=[aim])
            den = kb.sb("den", [128, 16], F32)
            t1 = kb.sb("t1", [128, 16], F32)
            t2 = kb.sb("t2", [128, 16], F32)
            kb.op(DVE, lambda: nc.vector.tensor_tensor(den[:], lm[:, 0, :], lm[:, 0, :], ALU.mult), reads=[lm], writes=[den])
            kb.op(DVE, lambda: nc.vector.tensor_tensor(t1[:], lm[:, 1, :], lm[:, 1, :], ALU.mult), reads=[lm], writes=[t1])
            kb.op(DVE, lambda: nc.vector.tensor_tensor(den[:], den[:], t1[:], ALU.add), reads=[den, t1], writes=[den])
            rden = kb.sb("rden", [128, 16], F32)
            kb.op(DVE, lambda: nc.vector.reciprocal(rden[:], den[:]), reads=[den], writes=[rden])
            nre = kb.sb("nre", [128, 16], F32)
            kb.op(DVE, lambda: nc.vector.tensor_scalar_add(nre[:], are[:], -1.0), reads=[are], writes=[nre])
            cre = kb.sb("cre", [128, 16], F32)
            cim = kb.sb("cim", [128, 16], F32)
            ncim = kb.sb("ncim", [128, 16], F32)
            kb.op(DVE, lambda: nc.vector.tensor_tensor(t1[:], nre[:], lm[:, 0, :], ALU.mult), reads=[nre, lm], writes=[t1])
            kb.op(DVE, lambda: nc.vector.tensor_tensor(t2[:], aim[:], lm[:, 1, :], ALU.mult), reads=[aim, lm], writes=[t2])
            kb.op(DVE, lambda: nc.vector.tensor_tensor(t1[:], t1[:], t2[:], ALU.add), reads=[t1, t2], writes=[t1])
            kb.op(DVE, lambda: nc.vector.tensor_tensor(cre[:], t1[:], rden[:], ALU.mult), reads=[t1, rden], writes=[cre])
            kb.op(DVE, lambda: nc.vector.tensor_tensor(t1[:], aim[:], lm[:, 0, :], ALU.mult), reads=[aim, lm], writes=[t1])
            kb.op(DVE, lambda: nc.vector.tensor_tensor(t2[:], nre[:], lm[:, 1, :], ALU.mult), reads=[nre, lm], writes=[t2])
            kb.op(DVE, lambda: nc.vector.tensor_tensor(t1[:], t1[:], t2[:], ALU.subtract), reads=[t1, t2], writes=[t1])
            kb.op(DVE, lambda: nc.vector.tensor_tensor(cim[:], t1[:], rden[:], ALU.mult), reads=[t1, rden], writes=[cim])
            kb.op(DVE, lambda: nc.vector.tensor_scalar_mul(ncim[:], cim[:], -1.0), reads=[cim], writes=[ncim])
            kb.chk("s5coef")
            Bl = [kb.sb("Bl_re", [128, 16, 128], BF16), kb.sb("Bl_im", [128, 16, 128], BF16)]
            Cl = [kb.sb("Cl_re", [128, 16, 128], BF16), kb.sb("Cl_im", [128, 16, 128], BF16)]
            for j in range(16):
                o = 128 * (j % 2)
                zb = [P32[7], P32[8]]
                zc = [P32[9], P32[10]]
                zbb = [P32[11], P32[12]]
                for z in zb + zc:
                    kb.op(POOL, lambda z=z, o=o: nc.gpsimd.memset(z[:, o:o + 128], 0.0), writes=[z])
                for ri in range(2):
                    for e in range(2):
                        g = 2 * j + e
                        c0 = 32 * (j % 4) + 16 * e
                        kb.dma(SP, zb[ri][64 * e:64 * e + 64, o + c0:o + c0 + 16], ssm_b.ap()[li, ri, g], writes=[zb[ri]])
                        kb.dma(SP, zc[ri][c0:c0 + 16, o + 64 * e:o + 64 * e + 64], ssm_c.ap()[li, ri, g], writes=[zc[ri]])
                kb.op(DVE, lambda j=j, o=o: nc.vector.tensor_scalar(zbb[0][:, o:o + 128], zb[0][:, o:o + 128], cre[:, j:j + 1], None, ALU.mult),
                      reads=[zb[0], cre], writes=[zbb[0]])
                kb.op(DVE, lambda j=j, o=o: nc.vector.scalar_tensor_tensor(zbb[0][:, o:o + 128], zb[1][:, o:o + 128], ncim[:, j:j + 1], zbb[0][:, o:o + 128],
                                                                      ALU.mult, ALU.add), reads=[zb[1], ncim, zbb[0]], writes=[zbb[0]])
                kb.op(DVE, lambda j=j, o=o: nc.vector.tensor_scalar(zbb[1][:, o:o + 128], zb[1][:, o:o + 128], cre[:, j:j + 1], None, ALU.mult),
                      reads=[zb[1], cre], writes=[zbb[1]])
                kb.op(DVE, lambda j=j, o=o: nc.vector.scalar_tensor_tensor(zbb[1][:, o:o + 128], zb[0][:, o:o + 128], cim[:, j:j + 1], zbb[1][:, o:o + 128],
                                                                      ALU.mult, ALU.add), reads=[zb[0], cim, zbb[1]], writes=[zbb[1]])
                for ri in range(2):
                    bk = kb.bank()
                    kb.op(PE, lambda bk=bk, ri=ri, o=o: nc.tensor.transpose(bk[:, 0:128], zbb[ri][:, o:o + 128], ident[:]),
                          reads=[zbb[ri], ident], writes=[bk])
                    kb.op(PE, lambda bk=bk, ri=ri, o=o: nc.tensor.transpose(bk[:, 128:256], zc[ri][:, o:o + 128], ident[:]),
                          reads=[zc[ri], ident], writes=[bk])
                    kb.op(DVE, lambda bk=bk, ri=ri, j=j: nc.vector.tensor_copy(Bl[ri][:, j, :], bk[:, 0:128]), reads=[bk], writes=[Bl[ri]])
                    if ri == 0:
                        kb.op(DVE, lambda bk=bk, ri=ri, j=j: nc.vector.tensor_copy(Cl[ri][:, j, :], bk[:, 128:256]), reads=[bk], writes=[Cl[ri]])
                    else:
                        kb.op(DVE, lambda bk=bk, ri=ri, j=j: nc.vector.tensor_scalar(Cl[ri][:, j, :], bk[:, 128:256], -1.0, None, ALU.mult), reads=[bk], writes=[Cl[ri]])
            T["Bl"], T["Cl"] = Bl, Cl
            kb.chk("s5bc")
            dsk = kb.sb("dsk", [128, 4], F32)
            bgl = kb.sb("bgl", [128, 4], F32)
            kb.dma(SP, dsk[:], ssm_d.ap()[li], writes=[dsk])
            kb.dma(SP, bgl[:], b_glu.ap()[li], writes=[bgl])
            T["dsk"], T["bgl"] = dsk, bgl
            rho9 = kb.sb("rho9", [128, 16, 9], F32)
            kb.op(DVE, lambda: nc.vector.memset(rho9[:], 0.0), writes=[rho9])
            kb.op(DVE, lambda: nc.vector.tensor_scalar(rho9[:, :, 1:9], V(rho, 0, 128, 0, [(1, 16), (0, 8)]), 1.0, None, ALU.mult),
                  reads=[rho], writes=[rho9])
            T["rho9"] = rho9
            T["car"] = [kb.sb("car_re", [128, 16], F32), kb.sb("car_im", [128, 16], F32)]
            for cbuf in T["car"]:
                kb.op(DVE, lambda cbuf=cbuf: nc.vector.memset(cbuf[:], 0.0), writes=[cbuf])
            S5.append(T)
        kb.chk("s5")

        kb.dead = bool(_os.environ.get("KSKIP_BIAS"))
        rb = kb.sb("rb", [32, 8], F32)
        oh = P32[3]
        kb.dma(SP, rb[:], relb.ap(), writes=[rb])
        kb.dma(SP, oh[0:32, 0:384], oh_bucket.ap(), writes=[oh])
        mk8 = P32[4]
        kb.dma(SP, mk8[0:8, 0:384], msk_ext.ap(), writes=[mk8])
        bk = kb.bank()
        kb.op(PE, lambda: nc.tensor.matmul(bk[0:8, 0:384], rb[:], oh[0:32, 0:384], start=True, stop=True), reads=[rb, oh], writes=[bk])
        bv = P32[5]
        kb.op(DVE, lambda: nc.vector.tensor_tensor(bv[0:8, 0:384], bk[0:8, 0:384], mk8[0:8, 0:384], ALU.add), reads=[bk, mk8], writes=[bv])
        kb.dma(SP, scr.ap(), bv[0:8, 0:384], reads=[bv], writes=[kb._misc])
        aI = kb.sb("aI", [128, 128], F32)
        kb.dma(SP, aI[:], antiI.ap(), writes=[aI])
        Etab = []
        for c in range(4):
            E = kb.sb("Etab%d" % c, [128, 512], F32)
            hank = P32[c % 2]
            kb.dma(SP, hank[:, 0:512], bass.AP(scr, 2 * c * 384, [[1, 128], [384, 2], [1, 256]]), reads=[kb._misc], writes=[hank])
            bk = kb.bank()
            for e in range(2):
                kb.op(PE, lambda e=e, bk=bk, hank=hank: nc.tensor.matmul(bk[:, e * 128:(e + 1) * 128], aI[:], hank[:, e * 256:e * 256 + 128], start=True, stop=True),
                      reads=[aI, hank], writes=[bk])
                kb.op(PE, lambda e=e, bk=bk, hank=hank: nc.tensor.matmul(bk[:, 256 + e * 128:256 + (e + 1) * 128], aI[:], hank[:, e * 256 + 128:e * 256 + 256],
                                                                   start=True, stop=True), reads=[aI, hank], writes=[bk])
            kb.op(DVE, lambda E=E, bk=bk: nc.vector.tensor_copy(E[:], bk[:]), reads=[bk], writes=[E])
            kb.op(ACT, lambda E=E: nc.scalar.activation(E[:], E[:], AF.Exp), reads=[E], writes=[E])
            Etab.append(E)
        EAt = kb.sb("EAt", [128, 64], F32)
        EBt = kb.sb("EBt", [8, 64], F32)
        hk = P32[2]
        kb.dma(SP, hk[:, 0:64], bass.AP(scr, 120, [[1, 128], [384, 8], [1, 8]]), reads=[kb._misc], writes=[hk], slow=True)
        kb.dma(SP, hk[0:8, 64:128], bass.AP(scr, 248, [[1, 8], [384, 8], [1, 8]]), reads=[kb._misc], writes=[hk], slow=True)
        bk = kb.bank()
        kb.op(PE, lambda: nc.tensor.matmul(bk[:, 0:64], aI[:], hk[:, 0:64], start=True, stop=True), reads=[aI, hk], writes=[bk])
        kb.op(PE, lambda: nc.tensor.matmul(bk[0:8, 64:128], aI[0:8, 120:128], hk[0:8, 64:128], start=True, stop=True), reads=[aI, hk], writes=[bk])
        kb.op(DVE, lambda: nc.vector.tensor_copy(EAt[:], bk[:, 0:64]), reads=[bk], writes=[EAt])
        kb.op(ACT, lambda: nc.scalar.activation(EAt[:], EAt[:], AF.Exp), reads=[EAt], writes=[EAt])
        kb.op(DVE, lambda: nc.vector.tensor_copy(EBt[:], bk[0:8, 64:128]), reads=[bk], writes=[EBt])
        kb.op(ACT, lambda: nc.scalar.activation(EBt[:], EBt[:], AF.Exp), reads=[EBt], writes=[EBt])
        EsT = []
        for li in range(2):
            sk = kb.sb("sk", [128, 4], F32)
            kb.dma(SP, sk[:], sinks.ap()[li], writes=[sk])
            kb.op(ACT, lambda sk=sk: nc.scalar.activation(sk[:], sk[:], AF.Exp), reads=[sk], writes=[sk])
            est = sk
            EsT.append(est)

        kb.dead = False
        kb.chk("bias")
        kb.dead = bool(_os.environ.get("KSKIP_STATE"))
        NVB = 5
        ATT = []
        for li in range(2):
            A = {}
            A["KK"] = [kb.sb("KK%d" % k, [128, 128 + 512], BF16) for k in range(2)]
            A["Vz"] = [kb.sb("Vz%d" % i, [128, 320], BF16) for i in range(NVB)]
            for vz in A["Vz"]:
                kb.op(POOL, lambda vz=vz: nc.gpsimd.memset(vz[:], 0.0), writes=[vz])
            A["kT32"] = kb.sb("kT32", [128, 128], F32)
            A["v32"] = kb.sb("v32", [128, 128], F32)
            ATT.append(A)
        CONV = []
        for li in range(2):
            C = {}
            C["halo"] = kb.sb("chalo", [128, 8, 30], F32)
            kb.op(POOL, lambda b=C["halo"]: nc.gpsimd.memset(b[:], 0.0), writes=[C["halo"]])
            C["wdw"] = kb.sb("wdw", [128, 8, 31], F32)
            kb.dma(SP, C["wdw"][:], w_dw.ap()[li], writes=[C["wdw"]])
            C["cv"] = kb.sb("cv", [128, 3, 8], F32)
            kb.dma(SP, C["cv"][:], cvec.ap()[li], writes=[C["cv"]])
            CONV.append(C)
        FFN = []
        for l in range(4):
            Fd = {}
            Fd["halo"] = kb.sb("fhalo", [128, NJ, 2], F32)
            kb.op(POOL, lambda b=Fd["halo"]: nc.gpsimd.memset(b[:], 0.0), writes=[Fd["halo"]])
            Fd["cw"] = kb.sb("fcw", [128, NJ, 3], F32)
            Fd["cb"] = kb.sb("fcb", [128, NJ], F32)
            kb.dma(SP, Fd["cw"][:], f_cw.ap()[l], writes=[Fd["cw"]])
            kb.dma(SP, Fd["cb"][:], f_cb.ap()[l], writes=[Fd["cb"]])
            FFN.append(Fd)

        kb.dead = False
        MIX = P16[16:22] + [kb.sb("MIX%d" % c, [128, 512], BF16) for c in range(2)]
        XR = [[P32[0], P32[1]], [P32[2], P32[3]]]
        GG = [[P32[4], P32[5]], [P32[6], P32[7]]]
        tA = [P32[8], P32[9]]
        tB = [P32[10], P32[11]]
        YS = [P32[12], P32[13]]
        HB = [[P16[0], P16[1], P16[2], P16[3]], [P16[4], P16[5], P16[6], P16[7]]]
        GEL = P16[8:12]
        cfx = [kb.sb("cfx%d" % i, [128, 2], F32) for i in range(4)]
        sso = [kb.sb("sso%d" % i, [128, NSEQ], F32) for i in range(2)]
        m9 = kb.sb("m9", [128, NSEQ * 9], F32)
        PT = P16[12:16]
        EX = [P32[0], P32[1]]
        dn_t = P32[2]
        YF = P16
        GR = [P32[0], P32[1], P32[2]]
        ACC = [P32[3], P32[4], P32[5]]
        SG = [P32[12], P32[13]]
        LNY = P32[0:8]
        GLW = [P32[8], P32[9]]
        SQF = [P32[10], P32[11]]
        mu_t = P32[12]
        LNS = P16[0:8]
        rr = {"t": 0, "pt": 0, "ex": 0, "gr": 0, "acc": 0, "sg": 0, "ys": 0}

        def nxt(lst, key):
            b = lst[rr[key] % len(lst)]
            rr[key] += 1
            return b

        def s5_core(T, W, nrep, L, sample_h0=None, ssm_out=None):
            def v3(b):
                return V(b, 0, 128, 0, [(L, nrep), (1, L)])
            for m in range(4):
                bky = kb.bank()
                cpairs = []
                for half in range(2):
                    for q in range(2):
                        jj = 2 * half + q
                        j = 4 * m + jj
                        bre, bim = kb.bank(), kb.bank()
                        for ri, bkk in ((0, bre), (1, bim)):
                            mm_group(bkk[:, 0:W], [(T["Bl"][ri][:, j, :], UT[m][:, 0:W], [T["Bl"][ri], UT[m]])], bkk, [])
                        cv = V(T["ctab"], 0, 128, j * 64, [(0, nrep), (1, L)])
                        sv = V(T["stab"], 0, 128, j * 64, [(0, nrep), (1, L)])
                        a1, a2 = tA[0], tB[0]
                        kb.op(DVE, lambda: nc.vector.tensor_tensor(v3(a1), v3(bre), cv, ALU.mult), reads=[bre, T["ctab"]], writes=[a1])
                        kb.op(DVE, lambda: nc.vector.tensor_tensor(v3(a2), v3(bim), sv, ALU.mult), reads=[bim, T["stab"]], writes=[a2])
                        kb.op(DVE, lambda: nc.vector.tensor_tensor(XR[0][q][:, 0:W], a1[:, 0:W], a2[:, 0:W], ALU.add),
                              reads=[a1, a2], writes=[XR[0][q]])
                        a1, a2 = tA[1], tB[1]
                        kb.op(DVE, lambda: nc.vector.tensor_tensor(v3(a1), v3(bim), cv, ALU.mult), reads=[bim, T["ctab"]], writes=[a1])
                        kb.op(DVE, lambda: nc.vector.tensor_tensor(v3(a2), v3(bre), sv, ALU.mult), reads=[bre, T["stab"]], writes=[a2])
                        kb.op(DVE, lambda: nc.vector.tensor_tensor(XR[1][q][:, 0:W], a1[:, 0:W], a2[:, 0:W], ALU.subtract),
                              reads=[a1, a2], writes=[XR[1][q]])
                    j0 = 4 * m + 2 * half
                    if sample_h0 is None:
                        nseg = W // 64
                        for sgi in range(nseg):
                            for q in range(2):
                                j = j0 + q
                                for ri in range(2):
                                    kb.op(DVE, lambda q=q, j=j, ri=ri, sgi=sgi: nc.vector.tensor_tensor_scan(
                                        GG[ri][q][:, sgi * 64:(sgi + 1) * 64], V(T["rho"], 0, 128, j, [(0, 64)]),
                                        XR[ri][q][:, sgi * 64:(sgi + 1) * 64], T["car"][ri][:, j:j + 1], ALU.mult, ALU.add),
                                        reads=[T["rho"], XR[ri][q], T["car"][ri]], writes=[GG[ri][q]])
                            col = sgi * 64 + 63
                            for ri in range(2):
                                for q in range(2):
                                    kb.op(DVE, lambda ri=ri, q=q: nc.vector.tensor_copy(cfx[ri][:, q:q + 1], GG[ri][q][:, col:col + 1]),
                                          reads=[GG[ri][q]], writes=[cfx[ri]])
                            cc = V(T["ctab"], 0, 128, j0 * 64 + 63, [(64, 2)])
                            ss = V(T["stab"], 0, 128, j0 * 64 + 63, [(64, 2)])
                            kb.op(DVE, lambda: nc.vector.tensor_tensor(cfx[2][:], cfx[0][:], cc, ALU.mult), reads=[cfx[0], T["ctab"]], writes=[cfx[2]])
                            kb.op(DVE, lambda: nc.vector.tensor_tensor(cfx[3][:], cfx[1][:], ss, ALU.mult), reads=[cfx[1], T["stab"]], writes=[cfx[3]])
                            kb.op(DVE, lambda: nc.vector.tensor_tensor(T["car"][0][:, j0:j0 + 2], cfx[2][:], cfx[3][:], ALU.subtract),
                                  reads=[cfx[2], cfx[3]], writes=[T["car"][0]])
                            kb.op(DVE, lambda: nc.vector.tensor_tensor(cfx[2][:], cfx[0][:], ss, ALU.mult), reads=[cfx[0], T["stab"]], writes=[cfx[2]])
                            kb.op(DVE, lambda: nc.vector.tensor_tensor(cfx[3][:], cfx[1][:], cc, ALU.mult), reads=[cfx[1], T["ctab"]], writes=[cfx[3]])
                            kb.op(DVE, lambda: nc.vector.tensor_tensor(T["car"][1][:, j0:j0 + 2], cfx[2][:], cfx[3][:], ALU.add),
                                  reads=[cfx[2], cfx[3]], writes=[T["car"][1]])
                    else:
                        for q in range(2):
                            j = j0 + q
                            for ri in range(2):
                                x9, g9 = tA[ri], tB[ri]
                                kb.dma(SP, V(x9, 0, 128, 0, [(9, NSEQ)]), st_ssm.ap()[sample_h0, ri][:, j, :], writes=[x9], slow=True)
                                kb.op(DVE, lambda: nc.vector.tensor_copy(V(x9, 0, 128, 1, [(9, NSEQ), (1, 8)]),
                                                                         V(XR[ri][q], 0, 128, 0, [(8, NSEQ), (1, 8)])),
                                      reads=[XR[ri][q]], writes=[x9])
                                if ri == 0:
                                    kb.op(DVE, lambda: nc.vector.tensor_copy(V(m9, 0, 128, 0, [(9, NSEQ), (1, 9)]), V(T["rho9"], 0, 128, j * 9, [(0, NSEQ), (1, 9)])),
                                          reads=[T["rho9"]], writes=[m9])
                                kb.op(DVE, lambda: nc.vector.tensor_tensor_scan(g9[:, 0:NSEQ * 9], m9[:, 0:NSEQ * 9],
                                                                                x9[:, 0:NSEQ * 9], 0.0, ALU.mult, ALU.add),
                                      reads=[m9, x9], writes=[g9])
                                kb.op(DVE, lambda: nc.vector.tensor_copy(V(GG[ri][q], 0, 128, 0, [(8, NSEQ), (1, 8)]),
                                                                         V(g9, 0, 128, 1, [(9, NSEQ), (1, 8)])),
                                      reads=[g9], writes=[GG[ri][q]])
                    for q in range(2):
                        jj = 2 * half + q
                        j = j0 + q
                        cv = V(T["ctab"], 0, 128, j * 64, [(0, nrep), (1, L)])
                        sv = V(T["stab"], 0, 128, j * 64, [(0, nrep), (1, L)])
                        a1, a2 = tA[0], tB[0]
                        kb.op(POOL, lambda: nc.gpsimd.tensor_tensor(v3(a1), v3(GG[0][q]), cv, ALU.mult), reads=[GG[0][q], T["ctab"]], writes=[a1])
                        kb.op(POOL, lambda: nc.gpsimd.tensor_tensor(v3(a2), v3(GG[1][q]), sv, ALU.mult), reads=[GG[1][q], T["stab"]], writes=[a2])
                        kb.op(POOL, lambda: nc.gpsimd.tensor_tensor(HB[0][jj][:, 0:W], a1[:, 0:W], a2[:, 0:W], ALU.subtract),
                              reads=[a1, a2], writes=[HB[0][jj]])
                        if sample_h0 is not None:
                            kb.op(POOL, lambda: nc.gpsimd.tensor_tensor(sso[0][:, :], V(a1, 0, 128, 7, [(8, NSEQ)]), V(a2, 0, 128, 7, [(8, NSEQ)]), ALU.subtract),
                                  reads=[a1, a2], writes=[sso[0]])
                            kb.dma(SP, o_ssm_s.ap()[sample_h0, 0][:, j, :], sso[0][:, :], reads=[sso[0]], is_out=True)
                        a1, a2 = tA[1], tB[1]
                        kb.op(POOL, lambda: nc.gpsimd.tensor_tensor(v3(a1), v3(GG[0][q]), sv, ALU.mult), reads=[GG[0][q], T["stab"]], writes=[a1])
                        kb.op(POOL, lambda: nc.gpsimd.tensor_tensor(v3(a2), v3(GG[1][q]), cv, ALU.mult), reads=[GG[1][q], T["ctab"]], writes=[a2])
                        kb.op(POOL, lambda: nc.gpsimd.tensor_tensor(HB[1][jj][:, 0:W], a1[:, 0:W], a2[:, 0:W], ALU.add),
                              reads=[a1, a2], writes=[HB[1][jj]])
                        if sample_h0 is not None:
                            kb.op(POOL, lambda: nc.gpsimd.tensor_tensor(sso[1][:, :], V(a1, 0, 128, 7, [(8, NSEQ)]), V(a2, 0, 128, 7, [(8, NSEQ)]), ALU.add),
                                  reads=[a1, a2], writes=[sso[1]])
                            kb.dma(SP, o_ssm_s.ap()[sample_h0, 1][:, j, :], sso[1][:, :], reads=[sso[1]], is_out=True)
                        for ri in range(2):
                            cpairs.append((T["Cl"][ri][:, j, :], HB[ri][jj][:, 0:W], [T["Cl"][ri], HB[ri][jj]]))
                mm_group(bky[:, 0:W], cpairs, bky, [])
                ys = U32[m]
                kb.op(DVE, lambda: nc.vector.scalar_tensor_tensor(ys[:, 0:W], U32[m][:, 0:W], T["dsk"][:, m:m + 1], bky[:, 0:W], ALU.mult, ALU.add),
                      reads=[U32[m], T["dsk"], bky], writes=[ys])
                kb.op(ACT, lambda: nc.scalar.activation(U32[m][:, 0:W], ys[:, 0:W], AF.Gelu_apprx_tanh), reads=[ys], writes=[U32[m]])
                kb.op(ACT, lambda: nc.scalar.copy(GEL[m][:, 0:W], U32[m][:, 0:W]), reads=[U32[m]], writes=[GEL[m]])
            for n in range(4):
                wb = load_w(w_glu.ap()[li_cur[0]][:, n * 128:(n + 1) * 128], 4, 128)
                bk = kb.bank()
                mm_group(bk[:, 0:W], [(wb[:, k, 0:128], GEL[k][:, 0:W], [wb, GEL[k]]) for k in range(4)], bk, [])
                sg = SG[n % 2]
                kb.op(DVE, lambda: nc.vector.tensor_scalar(sg[:, 0:W], bk[:, 0:W], T["bgl"][:, n:n + 1], None, ALU.add), reads=[bk, T["bgl"]], writes=[sg])
                kb.op(ACT, lambda: nc.scalar.activation(sg[:, 0:W], sg[:, 0:W], AF.Sigmoid), reads=[sg], writes=[sg])
                kb.op(DVE, lambda: nc.vector.tensor_tensor(MIX[n][:, 0:W], U32[n][:, 0:W], sg[:, 0:W], ALU.mult),
                      reads=[U32[n], sg], writes=[MIX[n]])

        li_cur = [0]

        def in_proj(li, W, want_v_tok_blocks):
            for n in range(4):
                wb = load_w(w_in.ap()[li][:, n * 128:(n + 1) * 128], 8, 128)
                if n == 0:
                    kb.chk("ip_w")
                bk = kb.bank()
                _mv = _os.environ.get("KMM", "")
                if _mv == "k":
                    mm_group(bk[:, 0:W], [(ones_bf[:], H[k][:, 0:W], [ones_bf, H[k]]) for k in range(8)], bk, [])
                elif _mv == "m":
                    for k in range(8):
                        kb.op(DVE, lambda k=k: nc.vector.tensor_copy(H[k][:, 0:W], X[k][:, 0:W]), reads=[X[k]], writes=[H[k]])
                    mm_group(bk[:, 0:W], [(wb[:, k, 0:128], H[k][:, 0:W], [wb, H[k]]) for k in range(8)], bk, [])
                elif _mv == "q":
                    for k in range(8):
                        kb.op(DVE, lambda k=k: nc.vector.tensor_copy(P16[k][:, 0:W], X[k][:, 0:W]), reads=[X[k]], writes=[P16[k]])
                    mm_group(bk[:, 0:W], [(wb[:, k, 0:128], P16[k][:, 0:W], [wb, P16[k]]) for k in range(8)], bk, [])
                elif _mv == "n":
                    for k in range(8):
                        kb.op(ACT, lambda k=k: nc.scalar.copy(H[k][:, 0:W], X[k][:, 0:W]), reads=[X[k]], writes=[H[k]])
                    mm_group(bk[:, 0:W], [(wb[:, k, 0:128], H[k][:, 0:W], [wb, H[k]]) for k in range(8)], bk, [])
                elif _mv == "l":
                    mm_group(bk[:, 0:W], [(wb[:, k, 0:128], sq[k][:, 0:W], [wb, sq[k]]) for k in range(8)], bk, [])
                else:
                    mm_group(bk[:, 0:W], [(wb[:, k, 0:128], H[k][:, 0:W], [wb, H[k]]) for k in range(8)], bk, [])
                if n == 0:
                    kb.chk("ip_mm")
                kb.op(DVE, lambda: nc.vector.tensor_copy(UT[n][:, 0:W], bk[:, 0:W]), reads=[bk], writes=[UT[n]])
                if n == 0:
                    kb.chk("ip_act")
                import os
                tv = os.environ.get("KVAR", "")
                if tv == "a":
                    kb.op(DVE, lambda: nc.vector.tensor_copy(P32[5][:, 0:W], bk[:, 0:W]), reads=[bk], writes=[P32[5]])
                elif tv == "c":
                    kb.op(DVE, lambda: nc.vector.tensor_scalar(U32[n][:, 0:W], bk[:, 0:W], 1.0, None, ALU.mult), reads=[bk], writes=[U32[n]])
                elif tv == "d":
                    kb.op(DVE, lambda: nc.vector.tensor_copy(U32[n][:, 0:W], bk[:, 0:W]), reads=[bk], writes=[U32[n]])
                elif tv == "g":
                    ob = kb.banks[(kb.psum_rr - 2) % 8]
                    kb.op(DVE, lambda: nc.vector.tensor_copy(P32[5][:, 0:W], ob[:, 0:W]), reads=[ob], writes=[P32[5]])
                elif tv == "g2":
                    ob = kb.banks[(kb.psum_rr - 2) % 8]
                    kb.op(DVE, lambda: nc.vector.tensor_copy(P32[5][:, 0:W], ob[:, 0:W]), reads=[ob, bk], writes=[P32[5]])
                elif tv == "j":
                    ob = kb.banks[(kb.psum_rr - 2) % 8]
                    kb.op(PE, lambda: nc.tensor.matmul(ob[:, 0:W], ones_bf[:], H[0][:, 0:W], start=True, stop=True), reads=[ones_bf, H[0]], writes=[ob])
                    kb.op(DVE, lambda: nc.vector.tensor_copy(U32[n][:, 0:W], bk[:, 0:W]), reads=[bk], writes=[U32[n]])
                elif tv == "h":
                    kb.op(DVE, lambda: nc.vector.tensor_copy(P32[5][0:64, 0:W], bk[0:64, 0:W]), reads=[bk], writes=[P32[5]])
                elif tv == "i":
                    kb.op(DVE, lambda: nc.vector.tensor_copy(P32[5][:, 0:64], bk[:, 0:64]), reads=[bk], writes=[P32[5]])
                elif tv == "e":
                    kb.op(DVE, lambda: nc.vector.tensor_copy(U32[n][:, 0:W], P32[6][:, 0:W]), reads=[P32[6]], writes=[U32[n]])
                elif tv == "f":
                    kb.op(DVE, lambda: nc.vector.tensor_copy(P32[5][:, 0:W], X[0][:, 0:W]), reads=[X[0]], writes=[P32[5]])
                elif tv == "b":
                    kb.op(DVE, lambda: nc.vector.tensor_copy(U32[n][:, 0:W], X[0][:, 0:W]), reads=[X[0]], writes=[U32[n]])
                else:
                    kb.op(DVE, lambda: nc.vector.tensor_copy(U32[n][:, 0:W], bk[:, 0:W]), reads=[bk], writes=[U32[n]])
                if n == 0:
                    kb.chk("ip_u0")
            kb.chk("ip_u")
            for n in range(4):
                wb = load_w(w_in.ap()[li][:, 512 + n * 128:512 + (n + 1) * 128], 8, 128)
                bk = kb.bank()
                mm_group(bk[:, 0:W], [(wb[:, k, 0:128], H[k][:, 0:W], [wb, H[k]]) for k in range(8)], bk, [])
                kb.op(DVE, lambda: nc.vector.tensor_copy(QT[0][n][0:64, 0:W], bk[0:64, 0:W]), reads=[bk], writes=[QT[0][n]])
                kb.op(DVE, lambda: nc.vector.tensor_copy(QT[1][n][64:128, 0:W], bk[64:128, 0:W]), reads=[bk], writes=[QT[1][n]])

        def attn_prompt(li, W, blk0):
            A = ATT[li]
            nb = W // 128
            for kv in range(2):
                wb = load_w(w_in.ap()[li][:, 1024 + kv * 128:1024 + (kv + 1) * 128], 8, 128)
                bk = kb.bank()
                mm_group(bk[:, 0:W], [(wb[:, k, 0:128], H[k][:, 0:W], [wb, H[k]]) for k in range(8)], bk, [])
                kb.op(DVE, lambda: nc.vector.tensor_copy(A["KK"][kv][:, 128:128 + W], bk[:, 0:W]), reads=[bk], writes=[A["KK"][kv]])
                if kv == 0:
                    kb.op(DVE, lambda: nc.vector.tensor_copy(A["kT32"][0:64, :], bk[0:64, W - 128:W]), reads=[bk], writes=[A["kT32"]])
                else:
                    kb.op(DVE, lambda: nc.vector.tensor_copy(A["kT32"][64:128, :], bk[64:128, W - 128:W]), reads=[bk], writes=[A["kT32"]])
            kb.chk("at_k")
            wv = load_w(w_in.ap()[li][:, 1280:1408], 8, 128)
            for b in range(nb):
                vz = A["Vz"][(blk0 + b) % NVB]
                bk = kb.bank()
                mm_group(bk[:, 0:128], [(H[k][:, b * 128:(b + 1) * 128], wv[:, k, 0:128], [H[k], wv]) for k in range(8)], bk, [])
                kb.op(DVE, lambda: nc.vector.tensor_copy(vz[:, 64:128], bk[:, 0:64]), reads=[bk], writes=[vz])
                kb.op(DVE, lambda: nc.vector.tensor_copy(vz[:, 192:256], bk[:, 64:128]), reads=[bk], writes=[vz])
                if b == nb - 1:
                    kb.op(DVE, lambda: nc.vector.tensor_copy(A["v32"][:, :], bk[:, 0:128]), reads=[bk], writes=[A["v32"]])
            kb.chk("at_v")
            for b in range(nb):
                gb = blk0 + b
                has_prev = gb >= 1
                q0 = b * 128
                pts = []
                for c in range(4):
                    kv = c // 2
                    bk = kb.bank()
                    for e in range(2):
                        kb.op(PE, lambda e=e, bk=bk: nc.tensor.matmul(bk[:, e * 128:(e + 1) * 128],
                                                                      A["KK"][kv][:, 128 + q0:256 + q0],
                                                                      QT[e][c][:, q0:q0 + 128], start=True, stop=True),
                              reads=[A["KK"][kv], QT[e][c]], writes=[bk], inc=(e == 1 and not has_prev))
                    if has_prev:
                        for e in range(2):
                            kb.op(PE, lambda e=e, bk=bk: nc.tensor.matmul(bk[:, 256 + e * 128:256 + (e + 1) * 128],
                                                                          A["KK"][kv][:, q0:q0 + 128],
                                                                          QT[e][c][:, q0:q0 + 128], start=True, stop=True),
                                  reads=[A["KK"][kv], QT[e][c]], writes=[bk], inc=(e == 1))
                    ncol = 512 if has_prev else 256
                    ex = EX[c % 2]
                    kb.chk("at_mm")
                    kb.op(DVE, lambda bk=bk, ex=ex: nc.vector.tensor_copy(ex[:, 0:ncol], bk[:, 0:ncol]), reads=[bk], writes=[ex])
                    kb.chk("at_cp")
                    kb.op(ACT, lambda ex=ex: nc.scalar.activation(ex[:, 0:ncol], ex[:, 0:ncol], AF.Exp, scale=0.125), reads=[ex], writes=[ex])
                    kb.chk("at_ex")
                    pt = PT[c]
                    kb.op(DVE, lambda ex=ex, pt=pt, c=c: nc.vector.tensor_tensor(pt[:, 0:ncol], ex[:, 0:ncol], Etab[c][:, 0:ncol], ALU.mult),
                          reads=[ex, Etab[c]], writes=[pt])
                    pts.append(pt)
                kb.chk("at_s")
                bn, bd = kb.bank(), kb.bank()
                vcur = A["Vz"][gb % NVB]
                vprev = A["Vz"][(gb - 1) % NVB]
                ocur = Oz0 if gb == 0 else Oz
                oprev = Oz0 if gb == 1 else Oz
                for c in range(4):
                    kv = c // 2
                    npairs, dpairs = [], []
                    for e in range(2):
                        vs = (64 + 128 * kv, 192 + 128 * kv) if e == 0 else (128 * kv, 128 + 128 * kv)
                        osl = (64, 192) if e == 0 else (0, 128)
                        npairs.append((vcur[:, vs[0]:vs[1]], pts[c][:, e * 128:(e + 1) * 128], [vcur, pts[c]]))
                        dpairs.append((ocur[:, osl[0]:osl[1]], pts[c][:, e * 128:(e + 1) * 128], [ocur, pts[c]]))
                        if has_prev:
                            npairs.append((vprev[:, vs[0]:vs[1]], pts[c][:, 256 + e * 128:256 + (e + 1) * 128], [vprev, pts[c]]))
                            dpairs.append((oprev[:, osl[0]:osl[1]], pts[c][:, 256 + e * 128:256 + (e + 1) * 128], [oprev, pts[c]]))
                    mm_group(bn[:, c * 128:(c + 1) * 128], npairs, bn, [])
                    mm_group(bd[:, c * 128:(c + 1) * 128], dpairs, bd, [])
                kb.chk("at_pv")
                for c in range(4):
                    kb.op(DVE, lambda bd=bd, c=c: nc.vector.tensor_scalar(dn_t[:, c * 128:(c + 1) * 128], bd[:, c * 128:(c + 1) * 128], EsT[li][:, c:c + 1], None, ALU.add),
                          reads=[bd, EsT[li]], writes=[dn_t])
                kb.op(DVE, lambda: nc.vector.reciprocal(dn_t[:, 0:512], dn_t[:, 0:512]), reads=[dn_t], writes=[dn_t])
                for c in range(4):
                    kb.op(DVE, lambda c=c, bn=bn: nc.vector.tensor_tensor(MIX[4 + c][:, q0:q0 + 128], bn[:, c * 128:(c + 1) * 128],
                                                                          dn_t[:, c * 128:(c + 1) * 128], ALU.mult),
                          reads=[bn, dn_t], writes=[MIX[4 + c]])
            for kv in range(2):
                kb.op(POOL, lambda kv=kv: nc.gpsimd.tensor_copy(A["KK"][kv][:, 0:128], A["KK"][kv][:, W:W + 128]),
                      reads=[A["KK"][kv]], writes=[A["KK"][kv]])

        def out_proj(dram_w, W, nk, src, after=None):
            for n in range(8):
                wb = load_w(dram_w[:, n * 128:(n + 1) * 128], nk, 128)
                bk = kb.bank()
                mm_group(bk[:, 0:W], [(wb[:, k, 0:128], src[k][:, 0:W], [wb, src[k]]) for k in range(nk)], bk, [])
                kb.op(DVE, lambda n=n, bk=bk: nc.vector.tensor_tensor(X[n][:, 0:W], X[n][:, 0:W], bk[:, 0:W], ALU.add),
                      reads=[X[n], bk], writes=[X[n]])

        def conformer(lo, W, nseq, Tt, halo, zero_pad_cols=0):
            C = CONV[lo]
            ext = 30 + Tt
            for c in range(8):
                wb = load_w(w_pw1.ap()[lo][:, c, :], 8, 256)
                ba, bg = kb.bank(), kb.bank()
                mm_group(ba[:, 0:W], [(wb[:, k, 0:128], H[k][:, 0:W], [wb, H[k]]) for k in range(8)], ba, [])
                mm_group(bg[:, 0:W], [(wb[:, k, 128:256], H[k][:, 0:W], [wb, H[k]]) for k in range(8)], bg, [])
                sg = SG[c % 2]
                gl = GLW[c % 2]
                kb.op(DVE, lambda: nc.vector.tensor_copy(sg[:, 0:W], bg[:, 0:W]), reads=[bg], writes=[sg])
                kb.op(ACT, lambda: nc.scalar.activation(sg[:, 0:W], sg[:, 0:W], AF.Sigmoid), reads=[sg], writes=[sg])
                if halo is not None:
                    kb.op(POOL, lambda: nc.gpsimd.tensor_copy(V(gl, 0, 128, 0, [(ext, nseq), (1, 30)]), V(halo, 0, 128, c * nseq * 30, [(30, nseq), (1, 30)])),
                          reads=[halo], writes=[gl])
                else:
                    kb.dma(SP, V(gl, 0, 128, 0, [(ext, nseq), (1, 30)]), st_conv.ap()[lo, c * 128:(c + 1) * 128, :, :], writes=[gl])
                kb.op(DVE, lambda: nc.vector.tensor_tensor(V(gl, 0, 128, 30, [(ext, nseq), (1, Tt)]),
                                                           V(ba, 0, 128, 0, [(Tt, nseq), (1, Tt)]),
                                                           V(sg, 0, 128, 0, [(Tt, nseq), (1, Tt)]), ALU.mult),
                      reads=[ba, sg], writes=[gl])
                if halo is not None:
                    kb.op(POOL, lambda: nc.gpsimd.tensor_copy(V(halo, 0, 128, c * nseq * 30, [(30, nseq), (1, 30)]), V(gl, 0, 128, Tt, [(ext, nseq), (1, 30)])),
                          reads=[gl], writes=[halo])
                else:
                    kb.dma(SP, o_conv_s.ap()[lo, c * 128:(c + 1) * 128, :, :], V(gl, 0, 128, Tt, [(ext, nseq), (1, 30)]), reads=[gl], is_out=True)
                eng_e, eng = (DVE, nc.vector)
                y = LNY[c]
                yv = V(y, 0, 128, 0, [(Tt, nseq), (1, Tt)])
                kb.op(eng_e, lambda: eng.tensor_scalar(yv, V(gl, 0, 128, 0, [(ext, nseq), (1, Tt)]), C["wdw"][:, c, 0:1], C["cv"][:, 0, c:c + 1],
                                                       ALU.mult, ALU.add), reads=[gl, C["wdw"], C["cv"]], writes=[y])
                for k in range(1, 31):
                    kb.op(eng_e, lambda k=k: eng.scalar_tensor_tensor(yv, V(gl, 0, 128, k, [(ext, nseq), (1, Tt)]), C["wdw"][:, c, k:k + 1], yv,
                                                                      ALU.mult, ALU.add), reads=[gl, C["wdw"], y], writes=[y])
            bm, b2 = kb.bank(), kb.bank()
            mm_group(bm[:, 0:W], [(ones_f[:], LNY[c][:, 0:W], [LNY[c]]) for c in range(8)], bm, [ones_f])
            kb.op(DVE, lambda: nc.vector.tensor_scalar(mu_t[:, 0:W], bm[:, 0:W], 1.0 / D, None, ALU.mult), reads=[bm], writes=[mu_t])
            for c in range(8):
                kb.op(DVE, lambda c=c: nc.vector.tensor_tensor(LNY[c][:, 0:W], LNY[c][:, 0:W], mu_t[:, 0:W], ALU.subtract),
                      reads=[LNY[c], mu_t], writes=[LNY[c]])
                sqf = SQF[c % 2]
                kb.op(ACT, lambda c=c, sqf=sqf: nc.scalar.activation(sqf[:, 0:W], LNY[c][:, 0:W], AF.Square), reads=[LNY[c]], writes=[sqf])
                kb.op(PE, lambda c=c, sqf=sqf: nc.tensor.matmul(b2[:, 0:W], ones_f[:], sqf[:, 0:W], start=(c == 0), stop=(c == 7)),
                      reads=[ones_f, sqf], writes=[b2])
            kb.op(DVE, lambda: nc.vector.tensor_scalar(rs[:, 0:W], b2[:, 0:W], 1.0 / D, EPS, ALU.mult, ALU.add), reads=[b2], writes=[rs])
            kb.op(ACT, lambda: nc.scalar.activation(rs[:, 0:W], rs[:, 0:W], AF.Sqrt), reads=[rs], writes=[rs])
            kb.op(DVE, lambda: nc.vector.reciprocal(rinv[:, 0:W], rs[:, 0:W]), reads=[rs], writes=[rinv])
            for c in range(8):
                kb.op(DVE, lambda c=c: nc.vector.scalar_tensor_tensor(LNY[c][:, 0:W], LNY[c][:, 0:W], C["cv"][:, 1, c:c + 1], rinv[:, 0:W],
                                                                      ALU.mult, ALU.mult), reads=[LNY[c], C["cv"], rinv], writes=[LNY[c]])
                kb.op(ACT, lambda c=c: nc.scalar.activation(LNY[c][:, 0:W], LNY[c][:, 0:W], AF.Silu, bias=C["cv"][:, 2, c:c + 1], scale=1.0),
                      reads=[LNY[c], C["cv"]], writes=[LNY[c]])
                kb.op(DVE, lambda c=c: nc.vector.tensor_copy(LNS[c][:, 0:W], LNY[c][:, 0:W]), reads=[LNY[c]], writes=[LNS[c]])
            out_proj(w_pw2.ap()[lo], W, 8, LNS)
            if zero_pad_cols:
                for c in range(8):
                    kb.op(DVE, lambda c=c: nc.vector.memset(X[c][:, 0:zero_pad_cols], 0.0), writes=[X[c]])

        ones_f = kb.sb("ones_f", [128, 128], F32)
        kb.op(DVE, lambda: nc.vector.memset(ones_f[:], 1.0), writes=[ones_f])

        def ffn_down(l, W):
            for n in range(8):
                wd0 = load_wdn(w_dn.ap()[l][0:1408, n * 128:(n + 1) * 128])
                wd1 = load_wdn(w_dn.ap()[l][1408:2816, n * 128:(n + 1) * 128])
                bk = kb.bank()
                mm_group(bk[:, 0:W], [((wd0 if j < 11 else wd1)[:, j % 11, :], YF[j][:, 0:W], [wd0 if j < 11 else wd1, YF[j]]) for j in range(NJ)], bk, [])
                kb.op(DVE, lambda n=n, bk=bk: nc.vector.tensor_tensor(X[n][:, 0:W], X[n][:, 0:W], bk[:, 0:W], ALU.add),
                      reads=[X[n], bk], writes=[X[n]])

        def ffn_up(l, W, nseq, Tt, halo):
            Fd = FFN[l]
            ext = 2 + Tt
            for j in range(NJ):
                wb = load_w(w_up.ap()[l][:, j, :], 8, 256)
                bg, bu = kb.bank(), kb.bank()
                mm_group(bg[:, 0:W], [(wb[:, k, 0:128], H[k][:, 0:W], [wb, H[k]]) for k in range(8)], bg, [])
                mm_group(bu[:, 0:W], [(wb[:, k, 128:256], H[k][:, 0:W], [wb, H[k]]) for k in range(8)], bu, [])
                gr = GR[j % 3]
                kb.op(DVE, lambda: nc.vector.tensor_copy(V(gr, 0, 128, 2, [(ext, nseq), (1, Tt)]), V(bg, 0, 128, 0, [(Tt, nseq), (1, Tt)])),
                      reads=[bg], writes=[gr])
                if halo is not None:
                    kb.op(POOL, lambda: nc.gpsimd.tensor_copy(V(gr, 0, 128, 0, [(ext, nseq), (1, 2)]), V(halo, 0, 128, j * nseq * 2, [(2, nseq), (1, 2)])),
                          reads=[halo], writes=[gr])
                else:
                    kb.dma(SP, V(gr, 0, 128, 0, [(ext, nseq), (1, 2)]), st_ffn.ap()[l, j * 128:(j + 1) * 128, :, :], writes=[gr], slow=True)
                acc = ACC[j % 3]
                av = V(acc, 0, 128, 0, [(Tt, nseq), (1, Tt)])
                kb.op(DVE, lambda: nc.vector.tensor_scalar(av, V(gr, 0, 128, 0, [(ext, nseq), (1, Tt)]), Fd["cw"][:, j, 0:1], Fd["cb"][:, j:j + 1],
                                                           ALU.mult, ALU.add), reads=[gr, Fd["cw"], Fd["cb"]], writes=[acc])
                for k in (1, 2):
                    kb.op(DVE, lambda k=k: nc.vector.scalar_tensor_tensor(av, V(gr, 0, 128, k, [(ext, nseq), (1, Tt)]), Fd["cw"][:, j, k:k + 1], av,
                                                                          ALU.mult, ALU.add), reads=[gr, Fd["cw"], acc], writes=[acc])
                if halo is not None:
                    kb.op(POOL, lambda: nc.gpsimd.tensor_copy(V(halo, 0, 128, j * nseq * 2, [(2, nseq), (1, 2)]), V(gr, 0, 128, Tt, [(ext, nseq), (1, 2)])),
                          reads=[gr], writes=[halo])
                else:
                    kb.dma(SP, o_ffn_s.ap()[l, j * 128:(j + 1) * 128, :, :], V(gr, 0, 128, Tt, [(ext, nseq), (1, 2)]), reads=[gr], is_out=True, slow=True)
                kb.op(ACT, lambda: nc.scalar.activation(acc[:, 0:W], acc[:, 0:W], AF.Gelu_apprx_tanh), reads=[acc], writes=[acc])
                kb.op(DVE, lambda: nc.vector.tensor_tensor(YF[j][:, 0:W], acc[:, 0:W], bu[:, 0:W], ALU.mult), reads=[acc, bu], writes=[YF[j]])

        chunks = [(0, 128)] + [(128 + 512 * i, 512) for i in range(16)]
        chunks = chunks[:nchunk_prompt]
        for ci, (t0, W) in enumerate(chunks):
            blk0 = t0 // 128
            for c in range(8):
                kb.dma(SP, X[c][:, 0:W], xT_p.ap()[c * 128:(c + 1) * 128, t0:t0 + W], writes=[X[c]])
            for l in range(4):
                rmsnorm(X, l, H, W)
                kb.chk("n0")
                if l % 2 == 0:
                    li = l // 2
                    li_cur[0] = li
                    in_proj(li, W, None)
                    kb.chk("inproj")
                    s5_core(S5[li], W, W // 64, 64)
                    kb.chk("s5c")
                    attn_prompt(li, W, blk0)
                    kb.chk("att")
                    if ci == dbg_ci and li == 0:
                        dump16(MIX, W)
                        dump16(UT + QT[0], W)
                        dump(Etab + Etab, 512)
                    out_proj(w_out.ap()[li], W, 8, MIX)
                    if ci == dbg_ci:
                        dump(X, W)
                else:
                    lo = l // 2
                    conformer(lo, W, 1, W, CONV[lo]["halo"], zero_pad_cols=(NPAD if ci == 0 else 0))
                    if ci == dbg_ci:
                        dump(X, W)
                rmsnorm(X, 4 + l, H, W)
                ffn_up(l, W, 1, W, FFN[l]["halo"])
                ffn_down(l, W)
                if ci == dbg_ci:
                    dump(X, W)
                if ci == 0 and l % 2 == 1:
                    pass
            for c in range(8):
                pass
            YO = [P32[c] for c in range(8)]
            rmsnorm(X, 8, None, W, out_f32=YO)
            for c in range(8):
                kb.dma(SP, yT_p.ap()[c * 128:(c + 1) * 128, t0:t0 + W], YO[c][:, 0:W], reads=[YO[c]], is_out=True)
        if nchunk_prompt == 17:
            for li in range(2):
                for ri in range(2):
                    kb.dma(SP, o_ssm_p.ap()[li, ri], S5[li]["car"][ri][:], reads=[S5[li]["car"][ri]], is_out=True)
                kb.dma(SP, o_kT_p.ap()[li], ATT[li]["kT32"][:], reads=[ATT[li]["kT32"]], is_out=True)
                kb.dma(SP, o_v_p.ap()[li], ATT[li]["v32"][:], reads=[ATT[li]["v32"]], is_out=True)
                for c in range(8):
                    kb.dma(SP, o_conv_p.ap()[li, c * 128:(c + 1) * 128, :], CONV[li]["halo"][:, c, :], reads=[CONV[li]["halo"]], is_out=True)
            for l in range(4):
                kb.dma(SP, o_ffn_p.ap()[l].rearrange("(j p) t -> p j t", p=128), FFN[l]["halo"][:], reads=[FFN[l]["halo"]], is_out=True)

        def attn_sample(li):
            A = ATT[li]
            W = 128
            for kv in range(2):
                wb = load_w(w_in.ap()[li][:, 1024 + kv * 128:1024 + (kv + 1) * 128], 8, 128)
                bk = kb.bank()
                mm_group(bk[:, 0:W], [(wb[:, k, 0:128], H[k][:, 0:W], [wb, H[k]]) for k in range(8)], bk, [])
                kb.op(DVE, lambda: nc.vector.tensor_copy(KKs[kv][:, :], bk[:, 0:W]), reads=[bk], writes=[KKs[kv]])
                kb.op(DVE, lambda: nc.vector.tensor_copy(A["kT32"][64 * kv:64 * kv + 64, :], bk[64 * kv:64 * kv + 64, 0:W]), reads=[bk], writes=[A["kT32"]])
            wv = load_w(w_in.ap()[li][:, 1280:1408], 8, 128)
            bk = kb.bank()
            mm_group(bk[:, 0:128], [(H[k][:, 0:128], wv[:, k, 0:128], [H[k], wv]) for k in range(8)], bk, [])
            kb.op(DVE, lambda: nc.vector.tensor_copy(vbf[:, :], bk[:, 0:128]), reads=[bk], writes=[vbf])
            kb.op(DVE, lambda: nc.vector.tensor_copy(A["v32"][:, :], bk[:, 0:128]), reads=[bk], writes=[A["v32"]])
            kb.dma(SP, o_kT_s.ap()[li][:, :, 0:120], st_kT.ap()[li][:, :, 8:128], is_out=True, slow=True)
            kb.dma(SP, o_v_s.ap()[li][:, 0:120, :], st_v.ap()[li][:, 8:128, :], is_out=True)
            kb.dma(SP, o_kT_s.ap()[li][:, :, 120:128].rearrange("s p t -> p s t"), V(A["kT32"], 0, 128, 0, [(8, NSEQ), (1, 8)]),
                   reads=[A["kT32"]], is_out=True, slow=True)
            for s_ in range(NSEQ):
                kb.dma(SP, o_v_s.ap()[li, s_][120:128, :], A["v32"][8 * s_:8 * s_ + 8, :], reads=[A["v32"]], is_out=True)
            bn, bd = kb.bank(), kb.bank()
            sbanks = [kb.bank() for _ in range(4)]
            for sq_i in range(NSEQ):
                kx = [KX[kv][sq_i % 2] for kv in range(2)]
                vz, vb = VZS[sq_i % 2], VBS[sq_i % 2]
                for kv in range(2):
                    for hf in range(2):
                        kb.dma(POOL, kx[kv][64 * hf:64 * hf + 64, 0:128], st_kT.ap()[li, sq_i][64 * kv:64 * kv + 64, :], writes=[kx[kv]])
                    kb.op(ACT, lambda kv=kv: nc.scalar.copy(kx[kv][:, 128:136], KKs[kv][:, 8 * sq_i:8 * sq_i + 8]), reads=[KKs[kv]], writes=[kx[kv]])
                    kb.dma(POOL, vz[0:120, 64 + 128 * kv:128 + 128 * kv], st_v.ap()[li, sq_i][8:128, 64 * kv:64 * kv + 64], writes=[vz])
                    kb.dma(POOL, vb[0:8, 64 + 128 * kv:128 + 128 * kv], st_v.ap()[li, sq_i][0:8, 64 * kv:64 * kv + 64], writes=[vb])
                    kb.dma(POOL, vz[120:128, 64 + 128 * kv:128 + 128 * kv], vbf[8 * sq_i:8 * sq_i + 8, 64 * kv:64 * kv + 64], reads=[vbf], writes=[vz])
                ba, bb = sbanks[2 * (sq_i % 2)], sbanks[2 * (sq_i % 2) + 1]
                for c in range(4):
                    kv = c // 2
                    for e in range(2):
                        col = c * 16 + e * 8
                        kb.op(PE, lambda: nc.tensor.matmul(ba[:, col:col + 8], kx[kv][:, 8:136],
                                                           QT[e][c][:, 8 * sq_i:8 * sq_i + 8], start=True, stop=True),
                              reads=[kx[kv], QT[e][c]], writes=[ba], inc=False)
                        kb.op(PE, lambda: nc.tensor.matmul(bb[0:8, col:col + 8], kx[kv][:, 0:8],
                                                           QT[e][c][:, 8 * sq_i:8 * sq_i + 8], start=True, stop=True),
                              reads=[kx[kv], QT[e][c]], writes=[bb], inc=(c == 3 and e == 1))
                kb.op(DVE, lambda: nc.vector.tensor_copy(EXA[:, :], ba[:, 0:64]), reads=[ba], writes=[EXA])
                kb.op(ACT, lambda: nc.scalar.activation(EXA[:, :], EXA[:, :], AF.Exp, scale=0.125), reads=[EXA], writes=[EXA])
                kb.op(DVE, lambda: nc.vector.tensor_copy(EXB[:, :], bb[0:8, 0:64]), reads=[bb], writes=[EXB])
                kb.op(ACT, lambda: nc.scalar.activation(EXB[:, :], EXB[:, :], AF.Exp, scale=0.125), reads=[EXB], writes=[EXB])
                pa, pb = PAs[sq_i % 2], PBs[sq_i % 2]
                kb.op(DVE, lambda: nc.vector.tensor_tensor(pa[:, :], EXA[:, :], EAt[:, :], ALU.mult), reads=[EXA, EAt], writes=[pa])
                kb.op(DVE, lambda: nc.vector.tensor_tensor(pb[:, :], EXB[:, :], EBt[:, :], ALU.mult), reads=[EXB, EBt], writes=[pb])
                for c in range(4):
                    kv = c // 2
                    npairs, dpairs = [], []
                    for e in range(2):
                        vs = (64 + 128 * kv, 192 + 128 * kv) if e == 0 else (128 * kv, 128 + 128 * kv)
                        osl = (64, 192) if e == 0 else (0, 128)
                        col = c * 16 + e * 8
                        npairs.append((vz[:, vs[0]:vs[1]], pa[:, col:col + 8], [vz, pa]))
                        npairs.append((vb[0:8, vs[0]:vs[1]], pb[0:8, col:col + 8], [vb, pb]))
                        dpairs.append((Oz[:, osl[0]:osl[1]], pa[:, col:col + 8], [Oz, pa]))
                        dpairs.append((Oz[0:8, osl[0]:osl[1]], pb[0:8, col:col + 8], [Oz, pb]))
                    oc = sq_i * 32 + c * 8
                    mm_group(bn[:, oc:oc + 8], npairs, bn, [])
                    mm_group(bd[:, oc:oc + 8], dpairs, bd, [])
            for c in range(4):
                dv = V(dn_t, 0, 128, c * 8, [(32, NSEQ), (1, 8)])
                kb.op(DVE, lambda: nc.vector.tensor_scalar(dv, V(bd, 0, 128, c * 8, [(32, NSEQ), (1, 8)]), EsT[li][:, c:c + 1], None, ALU.add),
                      reads=[bd, EsT[li]], writes=[dn_t])
            kb.op(DVE, lambda: nc.vector.reciprocal(dn_t[:, 0:512], dn_t[:, 0:512]), reads=[dn_t], writes=[dn_t])
            for c in range(4):
                kb.op(DVE, lambda: nc.vector.tensor_tensor(V(MIX[4 + c], 0, 128, 0, [(8, NSEQ), (1, 8)]), V(bn, 0, 128, c * 8, [(32, NSEQ), (1, 8)]),
                                                           V(dn_t, 0, 128, c * 8, [(32, NSEQ), (1, 8)]), ALU.mult),
                      reads=[bn, dn_t], writes=[MIX[4 + c]])

        if do_sample:
            KKs = [kb.sb("KKs%d" % k, [128, 128], BF16) for k in range(2)]
            vbf = kb.sb("vbf", [128, 128], BF16)
            KX = [[kb.sb("KX%d%d" % (k, i), [128, 136], BF16) for i in range(2)] for k in range(2)]
            VZS = [kb.sb("VZS%d" % i, [128, 320], BF16) for i in range(2)]
            VBS = [kb.sb("VBS%d" % i, [8, 320], BF16) for i in range(2)]
            for t_ in VZS + VBS:
                kb.op(POOL, lambda t_=t_: nc.gpsimd.memset(t_[:], 0.0), writes=[t_])
            EXA = kb.sb("EXA", [128, 64], F32)
            EXB = kb.sb("EXB", [8, 64], F32)
            PAs = [kb.sb("PAs%d" % i, [128, 64], BF16) for i in range(2)]
            PBs = [kb.sb("PBs%d" % i, [8, 64], BF16) for i in range(2)]
            W = 128
            for c in range(8):
                kb.dma(SP, X[c][:, 0:W], xT_s.ap()[c * 128:(c + 1) * 128, :], writes=[X[c]])
            for l in range(4):
                rmsnorm(X, l, H, W)
                if l % 2 == 0:
                    li = l // 2
                    li_cur[0] = li
                    in_proj(li, W, None)
                    s5_core(S5[li], W, NSEQ, 8, sample_h0=li)
                    attn_sample(li)
                    if li == 0:
                        dump16(MIX, W)
                        dump([P32[2]] * 8, 128)
                    out_proj(w_out.ap()[li], W, 8, MIX)
                else:
                    lo = l // 2
                    conformer(lo, W, NSEQ, 8, None)
                rmsnorm(X, 4 + l, H, W)
                ffn_up(l, W, NSEQ, 8, None)
                ffn_down(l, W)
            YO = [P32[c] for c in range(8)]
            rmsnorm(X, 8, None, W, out_f32=YO)
            for c in range(8):
                kb.dma(SP, yT_s.ap()[c * 128:(c + 1) * 128, :], YO[c][:, 0:W], reads=[YO[c]], is_out=True)

        kb.finish()
    return kb.nc


_CACHE = {}


def _prep_common(inp):
    f = np.float32
    d = {}
    gv = np.concatenate([inp["g_mix"], inp["g_ffn"], inp["g_final"][None]], 0)
    d["gvec"] = np.ascontiguousarray(gv.reshape(9, 8, 128).transpose(2, 0, 1)).astype(f)
    wi = inp["w_in_mix"]
    u, q = wi[:, :, 0:512], wi[:, :, 512:1024]
    k0, k1, v = wi[:, :, 1024:1088], wi[:, :, 1088:1152], wi[:, :, 1152:1280]
    d["w_in"] = np.ascontiguousarray(np.concatenate([u, q, k0, k0, k1, k1, v], -1)).astype(f)

    def st(a):
        return a.reshape(2, 16, 2, 64).transpose(0, 2, 3, 1).reshape(2, 128, 16)
    ls = np.broadcast_to(inp["ssm_log_step"][:, :, None], (2, 32, 64))
    d["lam"] = np.ascontiguousarray(np.stack([st(inp["ssm_lambda_re"]), st(inp["ssm_lambda_im"]), st(ls)], 1)).astype(f)
    d["ssm_b"] = np.ascontiguousarray(np.stack([inp["ssm_b_re"], inp["ssm_b_im"]], 1)).astype(f)
    d["ssm_c"] = np.ascontiguousarray(np.stack([inp["ssm_c_re"], inp["ssm_c_im"]], 1)).astype(f)
    d["ssm_d"] = np.ascontiguousarray(inp["ssm_d"].reshape(2, 4, 128).transpose(0, 2, 1)).astype(f)
    d["w_glu"] = np.ascontiguousarray(inp["ssm_w_glu"]).astype(f)
    d["b_glu"] = np.ascontiguousarray(inp["ssm_b_glu"].reshape(2, 4, 128).transpose(0, 2, 1)).astype(f)
    d["relb"] = np.ascontiguousarray(inp["rel_bias"]).astype(f)
    sk = inp["attn_sinks"]
    d["sinks"] = np.ascontiguousarray(np.repeat(sk.reshape(2, 4, 2), 64, axis=2).transpose(0, 2, 1)).astype(f)
    d["w_out"] = np.ascontiguousarray(inp["w_out_mix"]).astype(f)
    p1 = inp["conv_w_pw1"]
    a, g = p1[:, :, :1024].reshape(2, 1024, 8, 128), p1[:, :, 1024:].reshape(2, 1024, 8, 128)
    d["w_pw1"] = np.ascontiguousarray(np.concatenate([a, g], -1)).astype(f)
    d["w_dw"] = np.ascontiguousarray(inp["conv_w_dw"].reshape(2, 31, 8, 128).transpose(0, 3, 2, 1)).astype(f)
    cv = np.stack([inp["conv_b_dw"], inp["conv_ln_g"], inp["conv_ln_b"]], 1)
    d["cvec"] = np.ascontiguousarray(cv.reshape(2, 3, 8, 128).transpose(0, 3, 1, 2)).astype(f)
    d["w_pw2"] = np.ascontiguousarray(inp["conv_w_pw2"]).astype(f)
    wu = inp["ffn_w_up"]
    gg, uu = wu[:, :, :DFF].reshape(4, 1024, NJ, 128), wu[:, :, DFF:].reshape(4, 1024, NJ, 128)
    d["w_up"] = np.ascontiguousarray(np.concatenate([gg, uu], -1)).astype(f)
    d["f_cw"] = np.ascontiguousarray(inp["ffn_w_conv"].reshape(4, 3, NJ, 128).transpose(0, 3, 2, 1)).astype(f)
    d["f_cb"] = np.ascontiguousarray(inp["ffn_b_conv"].reshape(4, NJ, 128).transpose(0, 2, 1)).astype(f)
    d["w_dn"] = np.ascontiguousarray(inp["ffn_w_down"]).astype(f)
    m = np.arange(384)
    dist = m - 127
    inside = (dist >= 0) & (dist < 128)
    oh = np.zeros((32, 384), f)
    bk = t5_bucket_np(np.clip(dist, 0, 127))
    oh[bk[inside], m[inside]] = 1.0
    d["oh_bucket"] = oh
    d["msk_ext"] = np.ascontiguousarray(np.broadcast_to(np.where(inside, 0.0, NEG).astype(f)[None], (8, 384)))
    d["antiI"] = np.ascontiguousarray(np.eye(128, dtype=f)[::-1])
    return d


def kernel(**inp):
    inp = {k: np.asarray(v) for k, v in inp.items()}
    f = np.float32
    if "nc" not in _CACHE:
        _CACHE["nc"] = build()
    nc = _CACHE["nc"]
    com = _prep_common(inp)
    in_maps = []
    for c in range(8):
        d = dict(com)
        s = c % 2
        xp = np.concatenate([np.zeros((NPAD, D), f), inp["meta_tokens"], inp["x_prompt"][s]], 0)
        d["xT_p"] = np.ascontiguousarray(xp.T)
        sl = slice(c * NSEQ, (c + 1) * NSEQ)
        d["xT_s"] = np.ascontiguousarray(inp["x_sample"][sl].reshape(NSEQ * 8, D).T)

        def st(a):
            return a.reshape(2, NSEQ, 16, 2, 64).transpose(0, 3, 4, 2, 1).reshape(2, 128, 16, NSEQ)
        d["st_ssm"] = np.ascontiguousarray(np.stack([st(inp["state_ssm_re"][:, sl]), st(inp["state_ssm_im"][:, sl])], 1)).astype(f)
        d["st_kT"] = np.ascontiguousarray(inp["cache_swa_k"][:, sl].reshape(2, NSEQ, 128, 128).transpose(0, 1, 3, 2)).astype(f)
        d["st_v"] = np.ascontiguousarray(inp["cache_swa_v"][:, sl].reshape(2, NSEQ, 128, 128)).astype(f)
        d["st_conv"] = np.ascontiguousarray(inp["state_conv"][:, sl].transpose(0, 3, 1, 2)).astype(f)
        d["st_ffn"] = np.ascontiguousarray(inp["state_ffn"][:, sl].transpose(0, 3, 1, 2)).astype(f)
        in_maps.append(d)
    res = run_bass_kernel_spmd(nc, in_maps, core_ids=list(range(8)))
    R = res.results
    _CACHE["last"] = R
    y_p = np.stack([R[s]["yT_p"][:, 128:].T for s in range(2)], 0)
    y_s = np.concatenate([R[c]["yT_s"].T.reshape(NSEQ, 8, D) for c in range(8)], 0)

    def ust(a):
        return a.reshape(2, 2, 64, 16).transpose(0, 3, 1, 2).reshape(2, 32, 64)
    sr_p = np.stack([ust(R[s]["o_ssm_p"][:, 0]) for s in range(2)], 1)
    si_p = np.stack([ust(R[s]["o_ssm_p"][:, 1]) for s in range(2)], 1)
    k_p = np.stack([R[s]["o_kT_p"].transpose(0, 2, 1).reshape(2, 128, 2, 64) for s in range(2)], 1)
    v_p = np.stack([R[s]["o_v_p"].reshape(2, 128, 2, 64) for s in range(2)], 1)
    c_p = np.stack([R[s]["o_conv_p"].transpose(0, 2, 1) for s in range(2)], 1)
    f_p = np.stack([R[s]["o_ffn_p"].transpose(0, 2, 1) for s in range(2)], 1)

    def usts(a):
        return a.reshape(2, 2, 64, 16, NSEQ).transpose(0, 4, 3, 1, 2).reshape(2, NSEQ, 32, 64)
    sr_s = np.concatenate([usts(R[c]["o_ssm_s"][:, 0]) for c in range(8)], 1)
    si_s = np.concatenate([usts(R[c]["o_ssm_s"][:, 1]) for c in range(8)], 1)
    k_s = np.concatenate([R[c]["o_kT_s"].transpose(0, 1, 3, 2).reshape(2, NSEQ, 128, 2, 64) for c in range(8)], 1)
    v_s = np.concatenate([R[c]["o_v_s"].reshape(2, NSEQ, 128, 2, 64) for c in range(8)], 1)
    c_s = np.concatenate([R[c]["o_conv_s"].transpose(0, 2, 3, 1) for c in range(8)], 1)
    f_s = np.concatenate([R[c]["o_ffn_s"].transpose(0, 2, 3, 1) for c in range(8)], 1)
    outs = (y_p, y_s, sr_p, si_p, k_p, v_p, c_p, f_p, sr_s, si_s, k_s, v_s, c_s, f_s)
    return tuple(np.ascontiguousarray(o).astype(np.float32) for o in outs)
```

```python
import contextlib
import math
import numpy as np
import concourse.bass as bass
import concourse.mybir as mybir
from concourse.bass_utils import run_bass_kernel_spmd

F32 = mybir.dt.float32
BF16 = mybir.dt.bfloat16
AF = mybir.ActivationFunctionType
ALU = mybir.AluOpType

D = 1024
NC8 = 8
DFF = 2816
NJ = 22
NPAD = 112
TPAD = 8320
NBLK = 65
NSEQ = 16
EPS = 1e-6
NEG = -30000.0


def t5_bucket_np(dist):
    n = np.maximum(dist, 0)
    max_exact = 16
    nf = np.maximum(n, max_exact).astype(np.float32)
    large = max_exact + (np.log(nf / max_exact) / math.log(128 / max_exact) * (32 - max_exact)).astype(np.int32)
    large = np.minimum(large, 31)
    return np.where(n < max_exact, n, large)


class Buf:
    __slots__ = ("t", "name", "last_w", "readers", "dsem", "dcnt")

    def __init__(self, t, name):
        self.t = t
        self.name = name
        self.last_w = None
        self.readers = {}
        self.dsem = None
        self.dcnt = 0

    def __getitem__(self, idx):
        return self.t[idx]


class _Stop(Exception):
    pass


_LAST = {}


class KB:
    def __init__(self):
        self.nc = bass.Bass("TRN2", target_bir_lowering=False)
        self.es = contextlib.ExitStack()
        nc = self.nc
        self.eng = {"pe": nc.tensor, "act": nc.scalar, "dve": nc.vector, "pool": nc.gpsimd, "sp": nc.sync}
        self.sem = {}
        self.cnt = {}
        self.seen = {e: {} for e in self.eng}
        self.uid = 0
        self.psum_rr = 0
        self.out_events = []
        self.dead = False

    def start(self):
        import os
        for i in range(int(os.environ.get("KDUMMYSEM", "0"))):
            self.es.enter_context(self.nc.semaphore("dummy%d" % i))
        for e in self.eng:
            self.sem[e] = self.es.enter_context(self.nc.semaphore("prog_" + e))
            self.cnt[e] = 0
        self.banks = []
        for i in range(8):
            t = self.es.enter_context(self.nc.psum_tensor("bank%d" % i, [128, 512], F32))
            self.banks.append(Buf(t, "bank%d" % i))

    def sb(self, name, shape, dtype):
        self.uid += 1
        t = self.es.enter_context(self.nc.sbuf_tensor("%s_%d" % (name, self.uid), list(shape), dtype))
        return Buf(t, name)

    def dram(self, name, shape, dtype, kind):
        return self.nc.dram_tensor(name, list(shape), dtype, kind=kind)

    def bank(self):
        b = self.banks[self.psum_rr % 8]
        self.psum_rr += 1
        return b

    def _waits(self, e, reads, writes):
        deps = []
        for b in reads:
            if b.last_w is not None:
                deps.append((b.last_w, True))
        for b in writes:
            if b.last_w is not None:
                deps.append((b.last_w, False))
            for ev in b.readers.values():
                deps.append((ev, False))
        own = self.sem.get(e)
        for (sem, val), raw in deps:
            if sem is own:
                if e in ("pe", "sp"):
                    continue
            key = id(sem)
            if self.seen[e].get(key, 0) < val:
                self.eng[e].wait_ge(sem, val)
                self.seen[e][key] = val

    def chk(self, tag):
        import os
        if os.environ.get("KSTOP") == tag:
            for i in range(int(os.environ.get("KEXTRA", "0"))):
                tgt = self._xtra if os.environ.get("KXT") else self._misc
                w = 128 if os.environ.get("KXT") else 1
                if os.environ.get("KXT") == "2":
                    self.op("dve", lambda: self.nc.vector.tensor_copy(self._xtra[:, 0:128], self._xtra2[:, 0:128]), reads=[self._xtra2], writes=[self._xtra])
                else:
                    self.op("dve", lambda: self.nc.vector.memset(tgt[:, 0:w], 0.0), writes=[tgt])
            self.dead = True
            if os.environ.get("KRAISE"):
                self.dead = False
                self.finish()
                raise _Stop()

    def op(self, e, fn, reads=(), writes=(), inc=True):
        if self.dead:
            return None
        self._waits(e, reads, writes)
        inst = fn()
        if inc:
            self.cnt[e] += 1
            inst.then_inc(self.sem[e], 1)
            ev = (self.sem[e], self.cnt[e])
        else:
            ev = (self.sem[e], self.cnt[e] + 1)
        for b in writes:
            b.last_w = ev
            b.readers = {}
        for b in reads:
            b.readers[e] = ev
        return inst

    def dma(self, q, out_ap, in_ap, reads=(), writes=(), slow=False, is_out=False):
        if self.dead:
            return None
        self._waits(q, reads, writes)
        kw = {}
        if slow:
            kw["allow_slow_non_contiguous"] = True
        inst = self.eng[q].dma_start(out=out_ap, in_=in_ap, **kw)
        tgt = writes[0] if writes else (reads[0] if reads else None)
        if tgt is None:
            tgt = self._misc
        if tgt.dsem is None:
            self.uid += 1
            tgt.dsem = self.es.enter_context(self.nc.semaphore("d_%s_%d" % (tgt.name, self.uid)))
        tgt.dcnt += 16
        inst.then_inc(tgt.dsem, 16)
        ev = (tgt.dsem, tgt.dcnt)
        for b in writes:
            b.last_w = ev
            b.readers = {}
        for b in reads:
            b.readers[("dma", id(tgt))] = ev
        self.out_events.append(ev)
        return inst

    def finish(self):
        last = {}
        for sem, val in self.out_events:
            k = id(sem)
            if k not in last or last[k][1] < val:
                last[k] = (sem, val)
        for sem, val in last.values():
            self.eng["sp"].wait_ge(sem, val)
        for e in self.eng:
            for e2 in ("pe", "act", "dve", "pool"):
                if e2 != e and self.cnt[e2] > 0:
                    self.eng[e].wait_ge(self.sem[e2], self.cnt[e2])


def V(buf, p0, np_, off, dims):
    t = buf.t
    shape = t.shape
    fsz = 1
    for s in shape[1:]:
        fsz *= s
    return bass.AP(t, p0 * fsz + off, [[fsz, np_]] + [[s, c] for (s, c) in dims])


def build(nchunk_prompt=17, do_sample=True, dbg=False, dbg_ci=1):
    try:
        return _build(nchunk_prompt, do_sample, dbg, dbg_ci)
    except _Stop:
        return _LAST["kb"].nc


def _build(nchunk_prompt=17, do_sample=True, dbg=False, dbg_ci=1):
    kb = KB()
    _LAST["kb"] = kb
    nc = kb.nc
    es = kb.es
    with es:
        kb.start()
        import os as _os
        kb._misc = kb.sb("misc", [128, 1], F32)
        PE, ACT, DVE, POOL, SP = "pe", "act", "dve", "pool", "sp"
        din = {}

        def DI(name, shape):
            din[name] = kb.dram(name, shape, F32, "ExternalInput")
            return din[name]

        dout = {}

        def DO(name, shape):
            dout[name] = kb.dram(name, shape, F32, "ExternalOutput")
            return dout[name]

        xT_p = DI("xT_p", [D, TPAD])
        xT_s = DI("xT_s", [D, 128])
        st_ssm = DI("st_ssm", [2, 2, 128, 16, NSEQ])
        st_kT = DI("st_kT", [2, NSEQ, 128, 128])
        st_v = DI("st_v", [2, NSEQ, 128, 128])
        st_conv = DI("st_conv", [2, D, NSEQ, 30])
        st_ffn = DI("st_ffn", [4, DFF, NSEQ, 2])
        gvec = DI("gvec", [128, 9, 8])
        w_in = DI("w_in", [2, D, 1408])
        lam = DI("lam", [2, 3, 128, 16])
        ssm_b = DI("ssm_b", [2, 2, 32, 64, 16])
        ssm_c = DI("ssm_c", [2, 2, 32, 16, 64])
        ssm_d = DI("ssm_d", [2, 128, 4])
        w_glu = DI("w_glu", [2, 512, 512])
        b_glu = DI("b_glu", [2, 128, 4])
        relb = DI("relb", [32, 8])
        sinks = DI("sinks", [2, 128, 4])
        w_out = DI("w_out", [2, D, D])
        w_pw1 = DI("w_pw1", [2, D, 8, 256])
        w_dw = DI("w_dw", [2, 128, 8, 31])
        cvec = DI("cvec", [2, 128, 3, 8])
        w_pw2 = DI("w_pw2", [2, D, D])
        w_up = DI("w_up", [4, D, NJ, 256])
        f_cw = DI("f_cw", [4, 128, NJ, 3])
        f_cb = DI("f_cb", [4, 128, NJ])
        w_dn = DI("w_dn", [4, DFF, D])
        oh_bucket = DI("oh_bucket", [32, 384])
        msk_ext = DI("msk_ext", [8, 384])
        antiI = DI("antiI", [128, 128])

        yT_p = DO("yT_p", [D, TPAD])
        yT_s = DO("yT_s", [D, 128])
        o_ssm_p = DO("o_ssm_p", [2, 2, 128, 16])
        o_ssm_s = DO("o_ssm_s", [2, 2, 128, 16, NSEQ])
        o_kT_p = DO("o_kT_p", [2, 128, 128])
        o_v_p = DO("o_v_p", [2, 128, 128])
        o_kT_s = DO("o_kT_s", [2, NSEQ, 128, 128])
        o_v_s = DO("o_v_s", [2, NSEQ, 128, 128])
        o_conv_p = DO("o_conv_p", [2, D, 30])
        o_conv_s = DO("o_conv_s", [2, D, NSEQ, 30])
        o_ffn_p = DO("o_ffn_p", [4, DFF, 2])
        o_ffn_s = DO("o_ffn_s", [4, DFF, NSEQ, 2])
        scr = kb.dram("scr_bias", [8, 384], F32, "Internal")
        if dbg:
            dbg_o = DO("dbg", [16, D, 512])
        dbgc = [0]

        def dump16(Xl, W):
            if not dbg:
                return
            for c in range(8):
                kb.dma(POOL, dbg_o.ap()[dbgc[0], c * 128:(c + 1) * 128, 0:W], Xl[c][:, 0:W], reads=[Xl[c]], is_out=True)
            dbgc[0] += 1

        def dump(Xl, W):
            if not dbg:
                return
            for c in range(8):
                kb.dma(SP, dbg_o.ap()[dbgc[0], c * 128:(c + 1) * 128, 0:W], Xl[c][:, 0:W], reads=[Xl[c]], is_out=True)
            dbgc[0] += 1

        ident = kb.sb("ident", [128, 128], F32)
        kb.op(POOL, lambda: nc.gpsimd.memset(ident[:], 1.0), writes=[ident])
        kb.op(POOL, lambda: nc.gpsimd.affine_select(ident[:], ident[:], pattern=[[-1, 128]], compare_op=ALU.is_equal,
                                                     fill=0.0, base=0, channel_multiplier=1), reads=[ident], writes=[ident])
        ones_bf = kb.sb("ones_bf", [128, 128], BF16)
        kb.op(DVE, lambda: nc.vector.memset(ones_bf[:], 1.0), writes=[ones_bf])
        Oz = kb.sb("Oz", [128, 192], BF16)
        Oz0 = kb.sb("Oz0", [128, 192], BF16)
        for o in (Oz, Oz0):
            kb.op(DVE, lambda o=o: nc.vector.memset(o[:], 0.0), writes=[o])
            kb.op(DVE, lambda o=o: nc.vector.memset(o[:, 64:128], 1.0), writes=[o])
        kb.op(DVE, lambda: nc.vector.memset(Oz0[0:NPAD, :], 0.0), writes=[Oz0])
        gv = kb.sb("gv", [128, 9, 8], F32)
        kb.dma(SP, gv[:], gvec.ap(), writes=[gv])
        kb.chk("c0")

        WSL = [kb.sb("wslab%d" % i, [128, 8, 256], BF16) for i in range(2)]
        WDN = [kb.sb("wdn%d" % i, [128, 11, 128], BF16) for i in range(2)]
        wctr = {"a": 0, "b": 0}
        UT = [kb.sb("UT%d" % m, [128, 512], BF16) for m in range(4)]
        U32 = [kb.sb("U32_%d" % m, [128, 512], F32) for m in range(4)]
        QT = [[kb.sb("QT%d_%d" % (e, c), [128, 512], BF16) for c in range(4)] for e in range(2)]
        for e in range(2):
            for c in range(4):
                kb.op(POOL, lambda e=e, c=c: nc.gpsimd.memset(QT[e][c][:], 0.0), writes=[QT[e][c]])

        def load_w(dram_ap, kc, ncols):
            b = WSL[wctr["a"] % len(WSL)]
            wctr["a"] += 1
            import os
            if os.environ.get("KLOADW") == "hw" and ncols == 128:
                src = dram_ap.rearrange("(k p) n -> p k n", p=128)
                for hh in range(0, kc, 4):
                    stg = P32[8 + (hh // 4)]
                    kb.dma(SP, V(stg, 0, 128, 0, [(128, 4), (1, 128)]), src[:, hh:hh + 4, :], writes=[stg])
                    kb.op(DVE, lambda: nc.vector.tensor_copy(b[:, hh:hh + 4, 0:128], V(stg, 0, 128, 0, [(128, 4), (1, 128)])), reads=[stg], writes=[b])
                return b
            kb.dma(POOL, b[:, 0:kc, 0:ncols], dram_ap.rearrange("(k p) n -> p k n", p=128), writes=[b])
            return b

        def load_wdn(dram_ap):
            b = WDN[wctr["b"] % len(WDN)]
            wctr["b"] += 1
            kb.dma(POOL, b[:, :, :], dram_ap.rearrange("(j p) n -> p j n", p=128), writes=[b])
            return b

        def mm_group(out_ap, pairs, bankbuf, rbufs):
            n = len(pairs)
            if _os.environ.get("KHOIST"):
                allr = []
                for (_l, _r, bs_) in pairs:
                    allr += list(bs_)
                kb._waits(PE, allr + list(rbufs), [bankbuf])
            for i, (l, r, bs) in enumerate(pairs):
                kb.op(PE, lambda l=l, r=r, i=i: nc.tensor.matmul(out_ap, l, r, start=(i == 0), stop=(i == n - 1)),
                      reads=list(bs) + list(rbufs), writes=[bankbuf], inc=(i == n - 1))

        P32 = [kb.sb("P32_%d" % i, [128, 608], F32) for i in range(14)]
        P16 = [kb.sb("P16_%d" % i, [128, 512], BF16) for i in range(22)]
        kb._xtra = P32[5]
        kb._xtra2 = P32[6]
        sq = P16[12:20]
        rs = P32[12]
        rinv = P32[13]
        eps_t = kb.sb("eps_t", [128, 1], F32)
        kb.op(DVE, lambda: nc.vector.memset(eps_t[:], EPS), writes=[eps_t])

        def rmsnorm(X, gi, Hout, W, out_f32=None):
            lvl = int(_os.environ.get("KRMS", "9"))
            if lvl < 1:
                return
            for c in range(8):
                kb.op(DVE, lambda c=c: nc.vector.tensor_tensor(sq[c][:, 0:W], X[c][:, 0:W], X[c][:, 0:W], ALU.mult), reads=[X[c]], writes=[sq[c]])
            if lvl < 2:
                return
            bk = kb.bank()
            mm_group(bk[:, 0:W], [(ones_bf[:], sq[c][:, 0:W], [sq[c]]) for c in range(8)], bk, [ones_bf])
            if lvl < 3:
                return
            kb.op(DVE, lambda: nc.vector.tensor_scalar(rs[:, 0:W], bk[:, 0:W], 1.0 / D, EPS, ALU.mult, ALU.add), reads=[bk], writes=[rs])
            kb.op(ACT, lambda: nc.scalar.activation(rs[:, 0:W], rs[:, 0:W], AF.Sqrt), reads=[rs], writes=[rs])
            if lvl < 4:
                return
            kb.op(DVE, lambda: nc.vector.reciprocal(rinv[:, 0:W], rs[:, 0:W]), reads=[rs], writes=[rinv])
            if lvl < 5:
                return
            for c in range(8):
                o = Hout[c] if out_f32 is None else out_f32[c]
                kb.op(DVE, lambda c=c, o=o: nc.vector.scalar_tensor_tensor(o[:, 0:W], X[c][:, 0:W], gv[:, gi, c:c + 1], rinv[:, 0:W],
                                                                          ALU.mult, ALU.mult),
                      reads=[X[c], gv, rinv], writes=[o])

        X = [kb.sb("X%d" % c, [128, 512], F32) for c in range(8)]
        H = [kb.sb("H%d" % c, [128, 512], BF16) for c in range(8)]

        S5 = []
        import os as _os
        for li in range(0 if _os.environ.get("KSKIP_S5") else 2):
            T = {}
            lm = kb.sb("lam", [128, 3, 16], F32)
            kb.dma(SP, lm[:], lam.ap()[li].rearrange("k p j -> p k j"), writes=[lm])
            dt = kb.sb("dt", [128, 16], F32)
            kb.op(ACT, lambda: nc.scalar.activation(dt[:], lm[:, 2, :], AF.Exp), reads=[lm], writes=[dt])
            lrdt = kb.sb("lrdt", [128, 16], F32)
            th = kb.sb("th", [128, 16], F32)
            kb.op(DVE, lambda: nc.vector.tensor_tensor(lrdt[:], lm[:, 0, :], dt[:], ALU.mult), reads=[lm, dt], writes=[lrdt])
            kb.op(DVE, lambda: nc.vector.tensor_tensor(th[:], lm[:, 1, :], dt[:], ALU.mult), reads=[lm, dt], writes=[th])
            rho = kb.sb("rho", [128, 16], F32)
            kb.op(ACT, lambda: nc.scalar.activation(rho[:], lrdt[:], AF.Exp), reads=[lrdt], writes=[rho])
            T["rho"] = rho
            kk = P32[0]
            kb.op(POOL, lambda: nc.gpsimd.iota(kk[:, 0:64], pattern=[[1, 64]], base=1, channel_multiplier=0,
                                               allow_small_or_imprecise_dtypes=True), writes=[kk])
            ctab = kb.sb("ctab", [128, 16, 64], F32)
            stab = kb.sb("stab", [128, 16, 64], F32)
            negpi = kb.sb("negpi", [128, 1], F32)
            ki32 = kb.sb("ki32", [128, 64], mybir.dt.int32)
            kb.op(DVE, lambda: nc.vector.memset(negpi[:], -math.pi), writes=[negpi])
            for j in range(16):
                ang = P32[1 + (j % 2)]
                kb.op(DVE, lambda j=j, ang=ang: nc.vector.tensor_scalar(ang[:, 0:64], kk[:, 0:64], th[:, j:j + 1], None, ALU.mult),
                      reads=[kk, th], writes=[ang])
                for (dst, sh, ti) in ((stab, 0.5, 3), (ctab, 0.75, 5)):
                    tmp = P32[ti + (j % 2)]
                    kb.op(DVE, lambda sh=sh, tmp=tmp, ang=ang: nc.vector.tensor_scalar(tmp[:, 0:64], ang[:, 0:64], 1.0 / (2 * math.pi), sh, ALU.mult, ALU.add),
                          reads=[ang], writes=[tmp])
                    kb.op(DVE, lambda tmp=tmp: nc.vector.tensor_copy(ki32[:, 0:64], tmp[:, 0:64]), reads=[tmp], writes=[ki32])
                    kb.op(DVE, lambda tmp=tmp: nc.vector.tensor_copy(tmp[:, 64:128], ki32[:, 0:64]), reads=[ki32], writes=[tmp])
                    kb.op(DVE, lambda tmp=tmp: nc.vector.tensor_tensor(tmp[:, 0:64], tmp[:, 0:64], tmp[:, 64:128], ALU.subtract), reads=[tmp], writes=[tmp])
                    kb.op(DVE, lambda tmp=tmp: nc.vector.tensor_scalar(tmp[:, 64:128], tmp[:, 0:64], 0.0, None, ALU.is_lt), reads=[tmp], writes=[tmp])
                    kb.op(DVE, lambda tmp=tmp: nc.vector.tensor_tensor(tmp[:, 0:64], tmp[:, 0:64], tmp[:, 64:128], ALU.add), reads=[tmp], writes=[tmp])
                    kb.op(ACT, lambda dst=dst, tmp=tmp, j=j: nc.scalar.activation(dst[:, j, :], tmp[:, 0:64], AF.Sin, bias=negpi[:], scale=2 * math.pi),
                          reads=[tmp, negpi], writes=[dst])
            T["ctab"], T["stab"] = ctab, stab
            kb.chk("s5ang")
            are = kb.sb("are", [128, 16], F32)
            aim = kb.sb("aim", [128, 16], F32)
            kb.op(DVE, lambda: nc.vector.tensor_tensor(are[:], rho[:], ctab[:, :, 0], ALU.mult), reads=[rho, ctab], writes=[are])
            kb.op(DVE, lambda: nc.vector.tensor_tensor(aim[:], rho[:], stab[:, :, 0], ALU.mult), reads=[rho, stab], writes=[aim])
            den = kb.sb("den", [128, 16], F32)
            t1 = kb.sb("t1", [128, 16], F32)
            t2 = kb.sb("t2", [128, 16], F32)
            kb.op(DVE, lambda: nc.vector.tensor_tensor(den[:], lm[:, 0, :], lm[:, 0, :], ALU.mult), reads=[lm], writes=[den])
            kb.op(DVE, lambda: nc.vector.tensor_tensor(t1[:], lm[:, 1, :], lm[:, 1, :], ALU.mult), reads=[lm], writes=[t1])
            kb.op(DVE, lambda: nc.vector.tensor_tensor(den[:], den[:], t1[:], ALU.add), reads=[den, t1], writes=[den])
            rden = kb.sb("rden", [128, 16], F32)
            kb.op(DVE, lambda: nc.vector.reciprocal(rden[:], den[:]), reads=[den], writes=[rden])
            nre = kb.sb("nre", [128, 16], F32)
            kb.op(DVE, lambda: nc.vector.tensor_scalar_add(nre[:], are[:], -1.0), reads=[are], writes=[nre])
            cre = kb.sb("cre", [128, 16], F32)
            cim = kb.sb("cim", [128, 16], F32)
            ncim = kb.sb("ncim", [128, 16], F32)
            kb.op(DVE, lambda: nc.vector.tensor_tensor(t1[:], nre[:], lm[:, 0, :], ALU.mult), reads=[nre, lm], writes=[t1])
            kb.op(DVE, lambda: nc.vector.tensor_tensor(t2[:], aim[:], lm[:, 1, :], ALU.mult), reads=[aim, lm], writes=[t2])
            kb.op(DVE, lambda: nc.vector.tensor_tensor(t1[:], t1[:], t2[:], ALU.add), reads=[t1, t2], writes=[t1])
            kb.op(DVE, lambda: nc.vector.tensor_tensor(cre[:], t1[:], rden[:], ALU.mult), reads=[t1, rden], writes=[cre])
            kb.op(DVE, lambda: nc.vector.tensor_tensor(t1[:], aim[:], lm[:, 0, :], ALU.mult), reads=[aim, lm], writes=[t1])
            kb.op(DVE, lambda: nc.vector.tensor_tensor(t2[:], nre[:], lm[:, 1, :], ALU.mult), reads=[nre, lm], writes=[t2])
            kb.op(DVE, lambda: nc.vector.tensor_tensor(t1[:], t1[:], t2[:], ALU.subtract), reads=[t1, t2], writes=[t1])
            kb.op(DVE, lambda: nc.vector.tensor_tensor(cim[:], t1[:], rden[:], ALU.mult), reads=[t1, rden], writes=[cim])
            kb.op(DVE, lambda: nc.vector.tensor_scalar_mul(ncim[:], cim[:], -1.0), reads=[cim], writes=[ncim])
            kb.chk("s5coef")
            Bl = [kb.sb("Bl_re", [128, 16, 128], BF16), kb.sb("Bl_im", [128, 16, 128], BF16)]
            Cl = [kb.sb("Cl_re", [128, 16, 128], BF16), kb.sb("Cl_im", [128, 16, 128], BF16)]
            for j in range(16):
                o = 128 * (j % 2)
                zb = [P32[7], P32[8]]
                zc = [P32[9], P32[10]]
                zbb = [P32[11], P32[12]]
                for z in zb + zc:
                    kb.op(POOL, lambda z=z, o=o: nc.gpsimd.memset(z[:, o:o + 128], 0.0), writes=[z])
                for ri in range(2):
                    for e in range(2):
                        g = 2 * j + e
                        c0 = 32 * (j % 4) + 16 * e
                        kb.dma(SP, zb[ri][64 * e:64 * e + 64, o + c0:o + c0 + 16], ssm_b.ap()[li, ri, g], writes=[zb[ri]])
                        kb.dma(SP, zc[ri][c0:c0 + 16, o + 64 * e:o + 64 * e + 64], ssm_c.ap()[li, ri, g], writes=[zc[ri]])
                kb.op(DVE, lambda j=j, o=o: nc.vector.tensor_scalar(zbb[0][:, o:o + 128], zb[0][:, o:o + 128], cre[:, j:j + 1], None, ALU.mult),
                      reads=[zb[0], cre], writes=[zbb[0]])
                kb.op(DVE, lambda j=j, o=o: nc.vector.scalar_tensor_tensor(zbb[0][:, o:o + 128], zb[1][:, o:o + 128], ncim[:, j:j + 1], zbb[0][:, o:o + 128],
                                                                      ALU.mult, ALU.add), reads=[zb[1], ncim, zbb[0]], writes=[zbb[0]])
                kb.op(DVE, lambda j=j, o=o: nc.vector.tensor_scalar(zbb[1][:, o:o + 128], zb[1][:, o:o + 128], cre[:, j:j + 1], None, ALU.mult),
                      reads=[zb[1], cre], writes=[zbb[1]])
                kb.op(DVE, lambda j=j, o=o: nc.vector.scalar_tensor_tensor(zbb[1][:, o:o + 128], zb[0][:, o:o + 128], cim[:, j:j + 1], zbb[1][:, o:o + 128],
                                                                      ALU.mult, ALU.add), reads=[zb[0], cim, zbb[1]], writes=[zbb[1]])
                for ri in range(2):
                    bk = kb.bank()
                    kb.op(PE, lambda bk=bk, ri=ri, o=o: nc.tensor.transpose(bk[:, 0:128], zbb[ri][:, o:o + 128], ident[:]),
                          reads=[zbb[ri], ident], writes=[bk])
                    kb.op(PE, lambda bk=bk, ri=ri, o=o: nc.tensor.transpose(bk[:, 128:256], zc[ri][:, o:o + 128], ident[:]),
                          reads=[zc[ri], ident], writes=[bk])
                    kb.op(DVE, lambda bk=bk, ri=ri, j=j: nc.vector.tensor_copy(Bl[ri][:, j, :], bk[:, 0:128]), reads=[bk], writes=[Bl[ri]])
                    if ri == 0:
                        kb.op(DVE, lambda bk=bk, ri=ri, j=j: nc.vector.tensor_copy(Cl[ri][:, j, :], bk[:, 128:256]), reads=[bk], writes=[Cl[ri]])
                    else:
                        kb.op(DVE, lambda bk=bk, ri=ri, j=j: nc.vector.tensor_scalar(Cl[ri][:, j, :], bk[:, 128:256], -1.0, None, ALU.mult), reads=[bk], writes=[Cl[ri]])
            T["Bl"], T["Cl"] = Bl, Cl
            kb.chk("s5bc")
            dsk = kb.sb("dsk", [128, 4], F32)
            bgl = kb.sb("bgl", [128, 4], F32)
            kb.dma(SP, dsk[:], ssm_d.ap()[li], writes=[dsk])
            kb.dma(SP, bgl[:], b_glu.ap()[li], writes=[bgl])
            T["dsk"], T["bgl"] = dsk, bgl
            rho9 = kb.sb("rho9", [128, 16, 9], F32)
            kb.op(DVE, lambda: nc.vector.memset(rho9[:], 0.0), writes=[rho9])
            kb.op(DVE, lambda: nc.vector.tensor_scalar(rho9[:, :, 1:9], V(rho, 0, 128, 0, [(1, 16), (0, 8)]), 1.0, None, ALU.mult),
                  reads=[rho], writes=[rho9])
            T["rho9"] = rho9
            T["car"] = [kb.sb("car_re", [128, 16], F32), kb.sb("car_im", [128, 16], F32)]
            for cbuf in T["car"]:
                kb.op(DVE, lambda cbuf=cbuf: nc.vector.memset(cbuf[:], 0.0), writes=[cbuf])
            S5.append(T)
        kb.chk("s5")

        kb.dead = bool(_os.environ.get("KSKIP_BIAS"))
        rb = kb.sb("rb", [32, 8], F32)
        oh = P32[3]
        kb.dma(SP, rb[:], relb.ap(), writes=[rb])
        kb.dma(SP, oh[0:32, 0:384], oh_bucket.ap(), writes=[oh])
        mk8 = P32[4]
        kb.dma(SP, mk8[0:8, 0:384], msk_ext.ap(), writes=[mk8])
        bk = kb.bank()
        kb.op(PE, lambda: nc.tensor.matmul(bk[0:8, 0:384], rb[:], oh[0:32, 0:384], start=True, stop=True), reads=[rb, oh], writes=[bk])
        bv = P32[5]
        kb.op(DVE, lambda: nc.vector.tensor_tensor(bv[0:8, 0:384], bk[0:8, 0:384], mk8[0:8, 0:384], ALU.add), reads=[bk, mk8], writes=[bv])
        kb.dma(SP, scr.ap(), bv[0:8, 0:384], reads=[bv], writes=[kb._misc])
        aI = kb.sb("aI", [128, 128], F32)
        kb.dma(SP, aI[:], antiI.ap(), writes=[aI])
        Etab = []
        for c in range(4):
            E = kb.sb("Etab%d" % c, [128, 512], F32)
            hank = P32[c % 2]
            kb.dma(SP, hank[:, 0:512], bass.AP(scr, 2 * c * 384, [[1, 128], [384, 2], [1, 256]]), reads=[kb._misc], writes=[hank])
            bk = kb.bank()
            for e in range(2):
                kb.op(PE, lambda e=e, bk=bk, hank=hank: nc.tensor.matmul(bk[:, e * 128:(e + 1) * 128], aI[:], hank[:, e * 256:e * 256 + 128], start=True, stop=True),
                      reads=[aI, hank], writes=[bk])
                kb.op(PE, lambda e=e, bk=bk, hank=hank: nc.tensor.matmul(bk[:, 256 + e * 128:256 + (e + 1) * 128], aI[:], hank[:, e * 256 + 128:e * 256 + 256],
                                                                   start=True, stop=True), reads=[aI, hank], writes=[bk])
            kb.op(DVE, lambda E=E, bk=bk: nc.vector.tensor_copy(E[:], bk[:]), reads=[bk], writes=[E])
            kb.op(ACT, lambda E=E: nc.scalar.activation(E[:], E[:], AF.Exp), reads=[E], writes=[E])
            Etab.append(E)
        EAt = kb.sb("EAt", [128, 64], F32)
        EBt = kb.sb("EBt", [8, 64], F32)
        hk = P32[2]
        kb.dma(SP, hk[:, 0:64], bass.AP(scr, 120, [[1, 128], [384, 8], [1, 8]]), reads=[kb._misc], writes=[hk], slow=True)
        kb.dma(SP, hk[0:8, 64:128], bass.AP(scr, 248, [[1, 8], [384, 8], [1, 8]]), reads=[kb._misc], writes=[hk], slow=True)
        bk = kb.bank()
        kb.op(PE, lambda: nc.tensor.matmul(bk[:, 0:64], aI[:], hk[:, 0:64], start=True, stop=True), reads=[aI, hk], writes=[bk])
        kb.op(PE, lambda: nc.tensor.matmul(bk[0:8, 64:128], aI[0:8, 120:128], hk[0:8, 64:128], start=True, stop=True), reads=[aI, hk], writes=[bk])
        kb.op(DVE, lambda: nc.vector.tensor_copy(EAt[:], bk[:, 0:64]), reads=[bk], writes=[EAt])
        kb.op(ACT, lambda: nc.scalar.activation(EAt[:], EAt[:], AF.Exp), reads=[EAt], writes=[EAt])
        kb.op(DVE, lambda: nc.vector.tensor_copy(EBt[:], bk[0:8, 64:128]), reads=[bk], writes=[EBt])
        kb.op(ACT, lambda: nc.scalar.activation(EBt[:], EBt[:], AF.Exp), reads=[EBt], writes=[EBt])
        EsT = []
        for li in range(2):
            sk = kb.sb("sk", [128, 4], F32)
            kb.dma(SP, sk[:], sinks.ap()[li], writes=[sk])
            kb.op(ACT, lambda sk=sk: nc.scalar.activation(sk[:], sk[:], AF.Exp), reads=[sk], writes=[sk])
            est = sk
            EsT.append(est)

        kb.dead = False
        kb.chk("bias")
        kb.dead = bool(_os.environ.get("KSKIP_STATE"))
        NVB = 5
        ATT = []
        for li in range(2):
            A = {}
            A["KK"] = [kb.sb("KK%d" % k, [128, 128 + 512], BF16) for k in range(2)]
            A["Vz"] = [kb.sb("Vz%d" % i, [128, 320], BF16) for i in range(NVB)]
            for vz in A["Vz"]:
                kb.op(POOL, lambda vz=vz: nc.gpsimd.memset(vz[:], 0.0), writes=[vz])
            A["kT32"] = kb.sb("kT32", [128, 128], F32)
            A["v32"] = kb.sb("v32", [128, 128], F32)
            ATT.append(A)
        CONV = []
        for li in range(2):
            C = {}
            C["halo"] = kb.sb("chalo", [128, 8, 30], F32)
            kb.op(POOL, lambda b=C["halo"]: nc.gpsimd.memset(b[:], 0.0), writes=[C["halo"]])
            C["wdw"] = kb.sb("wdw", [128, 8, 31], F32)
            kb.dma(SP, C["wdw"][:], w_dw.ap()[li], writes=[C["wdw"]])
            C["cv"] = kb.sb("cv", [128, 3, 8], F32)
            kb.dma(SP, C["cv"][:], cvec.ap()[li], writes=[C["cv"]])
            CONV.append(C)
        FFN = []
        for l in range(4):
            Fd = {}
            Fd["halo"] = kb.sb("fhalo", [128, NJ, 2], F32)
            kb.op(POOL, lambda b=Fd["halo"]: nc.gpsimd.memset(b[:], 0.0), writes=[Fd["halo"]])
            Fd["cw"] = kb.sb("fcw", [128, NJ, 3], F32)
            Fd["cb"] = kb.sb("fcb", [128, NJ], F32)
            kb.dma(SP, Fd["cw"][:], f_cw.ap()[l], writes=[Fd["cw"]])
            kb.dma(SP, Fd["cb"][:], f_cb.ap()[l], writes=[Fd["cb"]])
            FFN.append(Fd)

        kb.dead = False
        MIX = P16[16:22] + [kb.sb("MIX%d" % c, [128, 512], BF16) for c in range(2)]
        XR = [[P32[0], P32[1]], [P32[2], P32[3]]]
        GG = [[P32[4], P32[5]], [P32[6], P32[7]]]
        tA = [P32[8], P32[9]]
        tB = [P32[10], P32[11]]
        YS = [P32[12], P32[13]]
        HB = [[P16[0], P16[1], P16[2], P16[3]], [P16[4], P16[5], P16[6], P16[7]]]
        GEL = P16[8:12]
        cfx = [kb.sb("cfx%d" % i, [128, 2], F32) for i in range(4)]
        sso = [kb.sb("sso%d" % i, [128, NSEQ], F32) for i in range(2)]
        m9 = kb.sb("m9", [128, NSEQ * 9], F32)
        PT = P16[12:16]
        EX = [P32[0], P32[1]]
        dn_t = P32[2]
        YF = P16
        GR = [P32[0], P32[1], P32[2]]
        ACC = [P32[3], P32[4], P32[5]]
        SG = [P32[12], P32[13]]
        LNY = P32[0:8]
        GLW = [P32[8], P32[9]]
        SQF = [P32[10], P32[11]]
        mu_t = P32[12]
        LNS = P16[0:8]
        rr = {"t": 0, "pt": 0, "ex": 0, "gr": 0, "acc": 0, "sg": 0, "ys": 0}

        def nxt(lst, key):
            b = lst[rr[key] % len(lst)]
            rr[key] += 1
            return b

        def s5_core(T, W, nrep, L, sample_h0=None, ssm_out=None):
            def v3(b):
                return V(b, 0, 128, 0, [(L, nrep), (1, L)])
            for m in range(4):
                bky = kb.bank()
                cpairs = []
                for half in range(2):
                    for q in range(2):
                        jj = 2 * half + q
                        j = 4 * m + jj
                        bre, bim = kb.bank(), kb.bank()
                        for ri, bkk in ((0, bre), (1, bim)):
                            mm_group(bkk[:, 0:W], [(T["Bl"][ri][:, j, :], UT[m][:, 0:W], [T["Bl"][ri], UT[m]])], bkk, [])
                        cv = V(T["ctab"], 0, 128, j * 64, [(0, nrep), (1, L)])
                        sv = V(T["stab"], 0, 128, j * 64, [(0, nrep), (1, L)])
                        a1, a2 = tA[0], tB[0]
                        kb.op(DVE, lambda: nc.vector.tensor_tensor(v3(a1), v3(bre), cv, ALU.mult), reads=[bre, T["ctab"]], writes=[a1])
                        kb.op(DVE, lambda: nc.vector.tensor_tensor(v3(a2), v3(bim), sv, ALU.mult), reads=[bim, T["stab"]], writes=[a2])
                        kb.op(DVE, lambda: nc.vector.tensor_tensor(XR[0][q][:, 0:W], a1[:, 0:W], a2[:, 0:W], ALU.add),
                              reads=[a1, a2], writes=[XR[0][q]])
                        a1, a2 = tA[1], tB[1]
                        kb.op(DVE, lambda: nc.vector.tensor_tensor(v3(a1), v3(bim), cv, ALU.mult), reads=[bim, T["ctab"]], writes=[a1])
                        kb.op(DVE, lambda: nc.vector.tensor_tensor(v3(a2), v3(bre), sv, ALU.mult), reads=[bre, T["stab"]], writes=[a2])
                        kb.op(DVE, lambda: nc.vector.tensor_tensor(XR[1][q][:, 0:W], a1[:, 0:W], a2[:, 0:W], ALU.subtract),
                              reads=[a1, a2], writes=[XR[1][q]])
                    j0 = 4 * m + 2 * half
                    if sample_h0 is None:
                        nseg = W // 64
                        for sgi in range(nseg):
                            for q in range(2):
                                j = j0 + q
                                for ri in range(2):
                                    kb.op(DVE, lambda q=q, j=j, ri=ri, sgi=sgi: nc.vector.tensor_tensor_scan(
                                        GG[ri][q][:, sgi * 64:(sgi + 1) * 64], V(T["rho"], 0, 128, j, [(0, 64)]),
                                        XR[ri][q][:, sgi * 64:(sgi + 1) * 64], T["car"][ri][:, j:j + 1], ALU.mult, ALU.add),
                                        reads=[T["rho"], XR[ri][q], T["car"][ri]], writes=[GG[ri][q]])
                            col = sgi * 64 + 63
                            for ri in range(2):
                                for q in range(2):
                                    kb.op(DVE, lambda ri=ri, q=q: nc.vector.tensor_copy(cfx[ri][:, q:q + 1], GG[ri][q][:, col:col + 1]),
                                          reads=[GG[ri][q]], writes=[cfx[ri]])
                            cc = V(T["ctab"], 0, 128, j0 * 64 + 63, [(64, 2)])
                            ss = V(T["stab"], 0, 128, j0 * 64 + 63, [(64, 2)])
                            kb.op(DVE, lambda: nc.vector.tensor_tensor(cfx[2][:], cfx[0][:], cc, ALU.mult), reads=[cfx[0], T["ctab"]], writes=[cfx[2]])
                            kb.op(DVE, lambda: nc.vector.tensor_tensor(cfx[3][:], cfx[1][:], ss, ALU.mult), reads=[cfx[1], T["stab"]], writes=[cfx[3]])
                            kb.op(DVE, lambda: nc.vector.tensor_tensor(T["car"][0][:, j0:j0 + 2], cfx[2][:], cfx[3][:], ALU.subtract),
                                  reads=[cfx[2], cfx[3]], writes=[T["car"][0]])
                            kb.op(DVE, lambda: nc.vector.tensor_tensor(cfx[2][:], cfx[0][:], ss, ALU.mult), reads=[cfx[0], T["stab"]], writes=[cfx[2]])
                            kb.op(DVE, lambda: nc.vector.tensor_tensor(cfx[3][:], cfx[1][:], cc, ALU.mult), reads=[cfx[1], T["ctab"]], writes=[cfx[3]])
                            kb.op(DVE, lambda: nc.vector.tensor_tensor(T["car"][1][:, j0:j0 + 2], cfx[2][:], cfx[3][:], ALU.add),
                                  reads=[cfx[2], cfx[3]], writes=[T["car"][1]])
                    else:
                        for q in range(2):
                            j = j0 + q
                            for ri in range(2):
                                x9, g9 = tA[ri], tB[ri]
                                kb.dma(SP, V(x9, 0, 128, 0, [(9, NSEQ)]), st_ssm.ap()[sample_h0, ri][:, j, :], writes=[x9], slow=True)
                                kb.op(DVE, lambda: nc.vector.tensor_copy(V(x9, 0, 128, 1, [(9, NSEQ), (1, 8)]),
                                                                         V(XR[ri][q], 0, 128, 0, [(8, NSEQ), (1, 8)])),
                                      reads=[XR[ri][q]], writes=[x9])
                                if ri == 0:
                                    kb.op(DVE, lambda: nc.vector.tensor_copy(V(m9, 0, 128, 0, [(9, NSEQ), (1, 9)]), V(T["rho9"], 0, 128, j * 9, [(0, NSEQ), (1, 9)])),
                                          reads=[T["rho9"]], writes=[m9])
                                kb.op(DVE, lambda: nc.vector.tensor_tensor_scan(g9[:, 0:NSEQ * 9], m9[:, 0:NSEQ * 9],
                                                                                x9[:, 0:NSEQ * 9], 0.0, ALU.mult, ALU.add),
                                      reads=[m9, x9], writes=[g9])
                                kb.op(DVE, lambda: nc.vector.tensor_copy(V(GG[ri][q], 0, 128, 0, [(8, NSEQ), (1, 8)]),
                                                                         V(g9, 0, 128, 1, [(9, NSEQ), (1, 8)])),
                                      reads=[g9], writes=[GG[ri][q]])
                    for q in range(2):
                        jj = 2 * half + q
                        j = j0 + q
                        cv = V(T["ctab"], 0, 128, j * 64, [(0, nrep), (1, L)])
                        sv = V(T["stab"], 0, 128, j * 64, [(0, nrep), (1, L)])
                        a1, a2 = tA[0], tB[0]
                        kb.op(POOL, lambda: nc.gpsimd.tensor_tensor(v3(a1), v3(GG[0][q]), cv, ALU.mult), reads=[GG[0][q], T["ctab"]], writes=[a1])
                        kb.op(POOL, lambda: nc.gpsimd.tensor_tensor(v3(a2), v3(GG[1][q]), sv, ALU.mult), reads=[GG[1][q], T["stab"]], writes=[a2])
                        kb.op(POOL, lambda: nc.gpsimd.tensor_tensor(HB[0][jj][:, 0:W], a1[:, 0:W], a2[:, 0:W], ALU.subtract),
                              reads=[a1, a2], writes=[HB[0][jj]])
                        if sample_h0 is not None:
                            kb.op(POOL, lambda: nc.gpsimd.tensor_tensor(sso[0][:, :], V(a1, 0, 128, 7, [(8, NSEQ)]), V(a2, 0, 128, 7, [(8, NSEQ)]), ALU.subtract),
                                  reads=[a1, a2], writes=[sso[0]])
                            kb.dma(SP, o_ssm_s.ap()[sample_h0, 0][:, j, :], sso[0][:, :], reads=[sso[0]], is_out=True)
                        a1, a2 = tA[1], tB[1]
                        kb.op(POOL, lambda: nc.gpsimd.tensor_tensor(v3(a1), v3(GG[0][q]), sv, ALU.mult), reads=[GG[0][q], T["stab"]], writes=[a1])
                        kb.op(POOL, lambda: nc.gpsimd.tensor_tensor(v3(a2), v3(GG[1][q]), cv, ALU.mult), reads=[GG[1][q], T["ctab"]], writes=[a2])
                        kb.op(POOL, lambda: nc.gpsimd.tensor_tensor(HB[1][jj][:, 0:W], a1[:, 0:W], a2[:, 0:W], ALU.add),
                              reads=[a1, a2], writes=[HB[1][jj]])
                        if sample_h0 is not None:
                            kb.op(POOL, lambda: nc.gpsimd.tensor_tensor(sso[1][:, :], V(a1, 0, 128, 7, [(8, NSEQ)]), V(a2, 0, 128, 7, [(8, NSEQ)]), ALU.add),
                                  reads=[a1, a2], writes=[sso[1]])
                            kb.dma(SP, o_ssm_s.ap()[sample_h0, 1][:, j, :], sso[1][:, :], reads=[sso[1]], is_out=True)
                        for ri in range(2):
                            cpairs.append((T["Cl"][ri][:, j, :], HB[ri][jj][:, 0:W], [T["Cl"][ri], HB[ri][jj]]))
                mm_group(bky[:, 0:W], cpairs, bky, [])
                ys = U32[m]
                kb.op(DVE, lambda: nc.vector.scalar_tensor_tensor(ys[:, 0:W], U32[m][:, 0:W], T["dsk"][:, m:m + 1], bky[:, 0:W], ALU.mult, ALU.add),
                      reads=[U32[m], T["dsk"], bky], writes=[ys])
                kb.op(ACT, lambda: nc.scalar.activation(U32[m][:, 0:W], ys[:, 0:W], AF.Gelu_apprx_tanh), reads=[ys], writes=[U32[m]])
                kb.op(ACT, lambda: nc.scalar.copy(GEL[m][:, 0:W], U32[m][:, 0:W]), reads=[U32[m]], writes=[GEL[m]])
            for n in range(4):
                wb = load_w(w_glu.ap()[li_cur[0]][:, n * 128:(n + 1) * 128], 4, 128)
                bk = kb.bank()
                mm_group(bk[:, 0:W], [(wb[:, k, 0:128], GEL[k][:, 0:W], [wb, GEL[k]]) for k in range(4)], bk, [])
                sg = SG[n % 2]
                kb.op(DVE, lambda: nc.vector.tensor_scalar(sg[:, 0:W], bk[:, 0:W], T["bgl"][:, n:n + 1], None, ALU.add), reads=[bk, T["bgl"]], writes=[sg])
                kb.op(ACT, lambda: nc.scalar.activation(sg[:, 0:W], sg[:, 0:W], AF.Sigmoid), reads=[sg], writes=[sg])
                kb.op(DVE, lambda: nc.vector.tensor_tensor(MIX[n][:, 0:W], U32[n][:, 0:W], sg[:, 0:W], ALU.mult),
                      reads=[U32[n], sg], writes=[MIX[n]])

        li_cur = [0]

        def in_proj(li, W, want_v_tok_blocks):
            for n in range(4):
                wb = load_w(w_in.ap()[li][:, n * 128:(n + 1) * 128], 8, 128)
                if n == 0:
                    kb.chk("ip_w")
                bk = kb.bank()
                _mv = _os.environ.get("KMM", "")
                if _mv == "k":
                    mm_group(bk[:, 0:W], [(ones_bf[:], H[k][:, 0:W], [ones_bf, H[k]]) for k in range(8)], bk, [])
                elif _mv == "m":
                    for k in range(8):
                        kb.op(DVE, lambda k=k: nc.vector.tensor_copy(H[k][:, 0:W], X[k][:, 0:W]), reads=[X[k]], writes=[H[k]])
                    mm_group(bk[:, 0:W], [(wb[:, k, 0:128], H[k][:, 0:W], [wb, H[k]]) for k in range(8)], bk, [])
                elif _mv == "q":
                    for k in range(8):
                        kb.op(DVE, lambda k=k: nc.vector.tensor_copy(P16[k][:, 0:W], X[k][:, 0:W]), reads=[X[k]], writes=[P16[k]])
                    mm_group(bk[:, 0:W], [(wb[:, k, 0:128], P16[k][:, 0:W], [wb, P16[k]]) for k in range(8)], bk, [])
                elif _mv == "n":
                    for k in range(8):
                        kb.op(ACT, lambda k=k: nc.scalar.copy(H[k][:, 0:W], X[k][:, 0:W]), reads=[X[k]], writes=[H[k]])
                    mm_group(bk[:, 0:W], [(wb[:, k, 0:128], H[k][:, 0:W], [wb, H[k]]) for k in range(8)], bk, [])
                elif _mv == "l":
                    mm_group(bk[:, 0:W], [(wb[:, k, 0:128], sq[k][:, 0:W], [wb, sq[k]]) for k in range(8)], bk, [])
                else:
                    mm_group(bk[:, 0:W], [(wb[:, k, 0:128], H[k][:, 0:W], [wb, H[k]]) for k in range(8)], bk, [])
                if n == 0:
                    kb.chk("ip_mm")
                kb.op(DVE, lambda: nc.vector.tensor_copy(UT[n][:, 0:W], bk[:, 0:W]), reads=[bk], writes=[UT[n]])
                if n == 0:
                    kb.chk("ip_act")
                import os
                tv = os.environ.get("KVAR", "")
                if tv == "a":
                    kb.op(DVE, lambda: nc.vector.tensor_copy(P32[5][:, 0:W], bk[:, 0:W]), reads=[bk], writes=[P32[5]])
                elif tv == "c":
                    kb.op(DVE, lambda: nc.vector.tensor_scalar(U32[n][:, 0:W], bk[:, 0:W], 1.0, None, ALU.mult), reads=[bk], writes=[U32[n]])
                elif tv == "d":
                    kb.op(DVE, lambda: nc.vector.tensor_copy(U32[n][:, 0:W], bk[:, 0:W]), reads=[bk], writes=[U32[n]])
                elif tv == "g":
                    ob = kb.banks[(kb.psum_rr - 2) % 8]
                    kb.op(DVE, lambda: nc.vector.tensor_copy(P32[5][:, 0:W], ob[:, 0:W]), reads=[ob], writes=[P32[5]])
                elif tv == "g2":
                    ob = kb.banks[(kb.psum_rr - 2) % 8]
                    kb.op(DVE, lambda: nc.vector.tensor_copy(P32[5][:, 0:W], ob[:, 0:W]), reads=[ob, bk], writes=[P32[5]])
                elif tv == "j":
                    ob = kb.banks[(kb.psum_rr - 2) % 8]
                    kb.op(PE, lambda: nc.tensor.matmul(ob[:, 0:W], ones_bf[:], H[0][:, 0:W], start=True, stop=True), reads=[ones_bf, H[0]], writes=[ob])
                    kb.op(DVE, lambda: nc.vector.tensor_copy(U32[n][:, 0:W], bk[:, 0:W]), reads=[bk], writes=[U32[n]])
                elif tv == "h":
                    kb.op(DVE, lambda: nc.vector.tensor_copy(P32[5][0:64, 0:W], bk[0:64, 0:W]), reads=[bk], writes=[P32[5]])
                elif tv == "i":
                    kb.op(DVE, lambda: nc.vector.tensor_copy(P32[5][:, 0:64], bk[:, 0:64]), reads=[bk], writes=[P32[5]])
                elif tv == "e":
                    kb.op(DVE, lambda: nc.vector.tensor_copy(U32[n][:, 0:W], P32[6][:, 0:W]), reads=[P32[6]], writes=[U32[n]])
                elif tv == "f":
                    kb.op(DVE, lambda: nc.vector.tensor_copy(P32[5][:, 0:W], X[0][:, 0:W]), reads=[X[0]], writes=[P32[5]])
                elif tv == "b":
                    kb.op(DVE, lambda: nc.vector.tensor_copy(U32[n][:, 0:W], X[0][:, 0:W]), reads=[X[0]], writes=[U32[n]])
                else:
                    kb.op(DVE, lambda: nc.vector.tensor_copy(U32[n][:, 0:W], bk[:, 0:W]), reads=[bk], writes=[U32[n]])
                if n == 0:
                    kb.chk("ip_u0")
            kb.chk("ip_u")
            for n in range(4):
                wb = load_w(w_in.ap()[li][:, 512 + n * 128:512 + (n + 1) * 128], 8, 128)
                bk = kb.bank()
                mm_group(bk[:, 0:W], [(wb[:, k, 0:128], H[k][:, 0:W], [wb, H[k]]) for k in range(8)], bk, [])
                kb.op(DVE, lambda: nc.vector.tensor_copy(QT[0][n][0:64, 0:W], bk[0:64, 0:W]), reads=[bk], writes=[QT[0][n]])
                kb.op(DVE, lambda: nc.vector.tensor_copy(QT[1][n][64:128, 0:W], bk[64:128, 0:W]), reads=[bk], writes=[QT[1][n]])

        def attn_prompt(li, W, blk0):
            A = ATT[li]
            nb = W // 128
            for kv in range(2):
                wb = load_w(w_in.ap()[li][:, 1024 + kv * 128:1024 + (kv + 1) * 128], 8, 128)
                bk = kb.bank()
                mm_group(bk[:, 0:W], [(wb[:, k, 0:128], H[k][:, 0:W], [wb, H[k]]) for k in range(8)], bk, [])
                kb.op(DVE, lambda: nc.vector.tensor_copy(A["KK"][kv][:, 128:128 + W], bk[:, 0:W]), reads=[bk], writes=[A["KK"][kv]])
                if kv == 0:
                    kb.op(DVE, lambda: nc.vector.tensor_copy(A["kT32"][0:64, :], bk[0:64, W - 128:W]), reads=[bk], writes=[A["kT32"]])
                else:
                    kb.op(DVE, lambda: nc.vector.tensor_copy(A["kT32"][64:128, :], bk[64:128, W - 128:W]), reads=[bk], writes=[A["kT32"]])
            kb.chk("at_k")
            wv = load_w(w_in.ap()[li][:, 1280:1408], 8, 128)
            for b in range(nb):
                vz = A["Vz"][(blk0 + b) % NVB]
                bk = kb.bank()
                mm_group(bk[:, 0:128], [(H[k][:, b * 128:(b + 1) * 128], wv[:, k, 0:128], [H[k], wv]) for k in range(8)], bk, [])
                kb.op(DVE, lambda: nc.vector.tensor_copy(vz[:, 64:128], bk[:, 0:64]), reads=[bk], writes=[vz])
                kb.op(DVE, lambda: nc.vector.tensor_copy(vz[:, 192:256], bk[:, 64:128]), reads=[bk], writes=[vz])
                if b == nb - 1:
                    kb.op(DVE, lambda: nc.vector.tensor_copy(A["v32"][:, :], bk[:, 0:128]), reads=[bk], writes=[A["v32"]])
            kb.chk("at_v")
            for b in range(nb):
                gb = blk0 + b
                has_prev = gb >= 1
                q0 = b * 128
                pts = []
                for c in range(4):
                    kv = c // 2
                    bk = kb.bank()
                    for e in range(2):
                        kb.op(PE, lambda e=e, bk=bk: nc.tensor.matmul(bk[:, e * 128:(e + 1) * 128],
                                                                      A["KK"][kv][:, 128 + q0:256 + q0],
                                                                      QT[e][c][:, q0:q0 + 128], start=True, stop=True),
                              reads=[A["KK"][kv], QT[e][c]], writes=[bk], inc=(e == 1 and not has_prev))
                    if has_prev:
                        for e in range(2):
                            kb.op(PE, lambda e=e, bk=bk: nc.tensor.matmul(bk[:, 256 + e * 128:256 + (e + 1) * 128],
                                                                          A["KK"][kv][:, q0:q0 + 128],
                                                                          QT[e][c][:, q0:q0 + 128], start=True, stop=True),
                                  reads=[A["KK"][kv], QT[e][c]], writes=[bk], inc=(e == 1))
                    ncol = 512 if has_prev else 256
                    ex = EX[c % 2]
                    kb.chk("at_mm")
                    kb.op(DVE, lambda bk=bk, ex=ex: nc.vector.tensor_copy(ex[:, 0:ncol], bk[:, 0:ncol]), reads=[bk], writes=[ex])
                    kb.chk("at_cp")
                    kb.op(ACT, lambda ex=ex: nc.scalar.activation(ex[:, 0:ncol], ex[:, 0:ncol], AF.Exp, scale=0.125), reads=[ex], writes=[ex])
                    kb.chk("at_ex")
                    pt = PT[c]
                    kb.op(DVE, lambda ex=ex, pt=pt, c=c: nc.vector.tensor_tensor(pt[:, 0:ncol], ex[:, 0:ncol], Etab[c][:, 0:ncol], ALU.mult),
                          reads=[ex, Etab[c]], writes=[pt])
                    pts.append(pt)
                kb.chk("at_s")
                bn, bd = kb.bank(), kb.bank()
                vcur = A["Vz"][gb % NVB]
                vprev = A["Vz"][(gb - 1) % NVB]
                ocur = Oz0 if gb == 0 else Oz
                oprev = Oz0 if gb == 1 else Oz
                for c in range(4):
                    kv = c // 2
                    npairs, dpairs = [], []
                    for e in range(2):
                        vs = (64 + 128 * kv, 192 + 128 * kv) if e == 0 else (128 * kv, 128 + 128 * kv)
                        osl = (64, 192) if e == 0 else (0, 128)
                        npairs.append((vcur[:, vs[0]:vs[1]], pts[c][:, e * 128:(e + 1) * 128], [vcur, pts[c]]))
                        dpairs.append((ocur[:, osl[0]:osl[1]], pts[c][:, e * 128:(e + 1) * 128], [ocur, pts[c]]))
                        if has_prev:
                            npairs.append((vprev[:, vs[0]:vs[1]], pts[c][:, 256 + e * 128:256 + (e + 1) * 128], [vprev, pts[c]]))
                            dpairs.append((oprev[:, osl[0]:osl[1]], pts[c][:, 256 + e * 128:256 + (e + 1) * 128], [oprev, pts[c]]))
                    mm_group(bn[:, c * 128:(c + 1) * 128], npairs, bn, [])
                    mm_group(bd[:, c * 128:(c + 1) * 128], dpairs, bd, [])
                kb.chk("at_pv")
                for c in range(4):
                    kb.op(DVE, lambda bd=bd, c=c: nc.vector.tensor_scalar(dn_t[:, c * 128:(c + 1) * 128], bd[:, c * 128:(c + 1) * 128], EsT[li][:, c:c + 1], None, ALU.add),
                          reads=[bd, EsT[li]], writes=[dn_t])
                kb.op(DVE, lambda: nc.vector.reciprocal(dn_t[:, 0:512], dn_t[:, 0:512]), reads=[dn_t], writes=[dn_t])
                for c in range(4):
                    kb.op(DVE, lambda c=c, bn=bn: nc.vector.tensor_tensor(MIX[4 + c][:, q0:q0 + 128], bn[:, c * 128:(c + 1) * 128],
                                                                          dn_t[:, c * 128:(c + 1) * 128], ALU.mult),
                          reads=[bn, dn_t], writes=[MIX[4 + c]])
            for kv in range(2):
                kb.op(ACT, lambda kv=kv: nc.scalar.copy(A["KK"][kv][:, 0:128], A["KK"][kv][:, W:W + 128]),
                      reads=[A["KK"][kv]], writes=[A["KK"][kv]])

        def out_proj(dram_w, W, nk, src, after=None):
            wb_next = load_w(dram_w[:, 0:128], nk, 128)
            for n in range(8):
                wb = wb_next
                if n + 1 < 8:
                    wb_next = load_w(dram_w[:, (n + 1) * 128:(n + 2) * 128], nk, 128)
                bk = kb.bank()
                mm_group(bk[:, 0:W], [(wb[:, k, 0:128], src[k][:, 0:W], [wb, src[k]]) for k in range(nk)], bk, [])
                kb.op(DVE, lambda n=n, bk=bk: nc.vector.tensor_tensor(X[n][:, 0:W], X[n][:, 0:W], bk[:, 0:W], ALU.add),
                      reads=[X[n], bk], writes=[X[n]])

        def conformer(lo, W, nseq, Tt, halo, zero_pad_cols=0):
            C = CONV[lo]
            ext = 30 + Tt
            wb_next = load_w(w_pw1.ap()[lo][:, 0, :], 8, 256)
            for c in range(8):
                wb = wb_next
                if c + 1 < 8:
                    wb_next = load_w(w_pw1.ap()[lo][:, c + 1, :], 8, 256)
                ba, bg = kb.bank(), kb.bank()
                mm_group(ba[:, 0:W], [(wb[:, k, 0:128], H[k][:, 0:W], [wb, H[k]]) for k in range(8)], ba, [])
                mm_group(bg[:, 0:W], [(wb[:, k, 128:256], H[k][:, 0:W], [wb, H[k]]) for k in range(8)], bg, [])
                sg = SG[c % 2]
                gl = GLW[c % 2]
                kb.op(DVE, lambda: nc.vector.tensor_copy(sg[:, 0:W], bg[:, 0:W]), reads=[bg], writes=[sg])
                kb.op(ACT, lambda: nc.scalar.activation(sg[:, 0:W], sg[:, 0:W], AF.Sigmoid), reads=[sg], writes=[sg])
                if halo is not None:
                    kb.op(ACT, lambda: nc.scalar.copy(V(gl, 0, 128, 0, [(ext, nseq), (1, 30)]), V(halo, 0, 128, c * nseq * 30, [(30, nseq), (1, 30)])),
                          reads=[halo], writes=[gl])
                else:
                    kb.dma(SP, V(gl, 0, 128, 0, [(ext, nseq), (1, 30)]), st_conv.ap()[lo, c * 128:(c + 1) * 128, :, :], writes=[gl])
                kb.op(DVE, lambda: nc.vector.tensor_tensor(V(gl, 0, 128, 30, [(ext, nseq), (1, Tt)]),
                                                           V(ba, 0, 128, 0, [(Tt, nseq), (1, Tt)]),
                                                           V(sg, 0, 128, 0, [(Tt, nseq), (1, Tt)]), ALU.mult),
                      reads=[ba, sg], writes=[gl])
                if halo is not None:
                    kb.op(ACT, lambda: nc.scalar.copy(V(halo, 0, 128, c * nseq * 30, [(30, nseq), (1, 30)]), V(gl, 0, 128, Tt, [(ext, nseq), (1, 30)])),
                          reads=[gl], writes=[halo])
                else:
                    kb.dma(SP, o_conv_s.ap()[lo, c * 128:(c + 1) * 128, :, :], V(gl, 0, 128, Tt, [(ext, nseq), (1, 30)]), reads=[gl], is_out=True)
                eng_e, eng = (DVE, nc.vector)
                y = LNY[c]
                yv = V(y, 0, 128, 0, [(Tt, nseq), (1, Tt)])
                kb.op(eng_e, lambda: eng.tensor_scalar(yv, V(gl, 0, 128, 0, [(ext, nseq), (1, Tt)]), C["wdw"][:, c, 0:1], C["cv"][:, 0, c:c + 1],
                                                       ALU.mult, ALU.add), reads=[gl, C["wdw"], C["cv"]], writes=[y])
                for k in range(1, 31):
                    kb.op(eng_e, lambda k=k: eng.scalar_tensor_tensor(yv, V(gl, 0, 128, k, [(ext, nseq), (1, Tt)]), C["wdw"][:, c, k:k + 1], yv,
                                                                      ALU.mult, ALU.add), reads=[gl, C["wdw"], y], writes=[y])
            bm, b2 = kb.bank(), kb.bank()
            mm_group(bm[:, 0:W], [(ones_f[:], LNY[c][:, 0:W], [LNY[c]]) for c in range(8)], bm, [ones_f])
            kb.op(DVE, lambda: nc.vector.tensor_scalar(mu_t[:, 0:W], bm[:, 0:W], 1.0 / D, None, ALU.mult), reads=[bm], writes=[mu_t])
            for c in range(8):
                kb.op(DVE, lambda c=c: nc.vector.tensor_tensor(LNY[c][:, 0:W], LNY[c][:, 0:W], mu_t[:, 0:W], ALU.subtract),
                      reads=[LNY[c], mu_t], writes=[LNY[c]])
                sqf = SQF[c % 2]
                kb.op(ACT, lambda c=c, sqf=sqf: nc.scalar.activation(sqf[:, 0:W], LNY[c][:, 0:W], AF.Square), reads=[LNY[c]], writes=[sqf])
                kb.op(PE, lambda c=c, sqf=sqf: nc.tensor.matmul(b2[:, 0:W], ones_f[:], sqf[:, 0:W], start=(c == 0), stop=(c == 7)),
                      reads=[ones_f, sqf], writes=[b2])
            kb.op(DVE, lambda: nc.vector.tensor_scalar(rs[:, 0:W], b2[:, 0:W], 1.0 / D, EPS, ALU.mult, ALU.add), reads=[b2], writes=[rs])
            kb.op(ACT, lambda: nc.scalar.activation(rs[:, 0:W], rs[:, 0:W], AF.Sqrt), reads=[rs], writes=[rs])
            kb.op(DVE, lambda: nc.vector.reciprocal(rinv[:, 0:W], rs[:, 0:W]), reads=[rs], writes=[rinv])
            for c in range(8):
                kb.op(DVE, lambda c=c: nc.vector.scalar_tensor_tensor(LNY[c][:, 0:W], LNY[c][:, 0:W], C["cv"][:, 1, c:c + 1], rinv[:, 0:W],
                                                                      ALU.mult, ALU.mult), reads=[LNY[c], C["cv"], rinv], writes=[LNY[c]])
                kb.op(ACT, lambda c=c: nc.scalar.activation(LNY[c][:, 0:W], LNY[c][:, 0:W], AF.Silu, bias=C["cv"][:, 2, c:c + 1], scale=1.0),
                      reads=[LNY[c], C["cv"]], writes=[LNY[c]])
                kb.op(DVE, lambda c=c: nc.vector.tensor_copy(LNS[c][:, 0:W], LNY[c][:, 0:W]), reads=[LNY[c]], writes=[LNS[c]])
            out_proj(w_pw2.ap()[lo], W, 8, LNS)
            if zero_pad_cols:
                for c in range(8):
                    kb.op(DVE, lambda c=c: nc.vector.memset(X[c][:, 0:zero_pad_cols], 0.0), writes=[X[c]])

        ones_f = kb.sb("ones_f", [128, 128], F32)
        kb.op(DVE, lambda: nc.vector.memset(ones_f[:], 1.0), writes=[ones_f])

        def ffn_down(l, W):
            for n in range(8):
                wd0 = load_wdn(w_dn.ap()[l][0:1408, n * 128:(n + 1) * 128])
                wd1 = load_wdn(w_dn.ap()[l][1408:2816, n * 128:(n + 1) * 128])
                bk = kb.bank()
                mm_group(bk[:, 0:W], [((wd0 if j < 11 else wd1)[:, j % 11, :], YF[j][:, 0:W], [wd0 if j < 11 else wd1, YF[j]]) for j in range(NJ)], bk, [])
                kb.op(DVE, lambda n=n, bk=bk: nc.vector.tensor_tensor(X[n][:, 0:W], X[n][:, 0:W], bk[:, 0:W], ALU.add),
                      reads=[X[n], bk], writes=[X[n]])

        def ffn_up(l, W, nseq, Tt, halo):
            Fd = FFN[l]
            ext = 2 + Tt
            wb_next = load_w(w_up.ap()[l][:, 0, :], 8, 256)
            for j in range(NJ):
                wb = wb_next
                if j + 1 < NJ:
                    wb_next = load_w(w_up.ap()[l][:, j + 1, :], 8, 256)
                bg, bu = kb.bank(), kb.bank()
                mm_group(bg[:, 0:W], [(wb[:, k, 0:128], H[k][:, 0:W], [wb, H[k]]) for k in range(8)], bg, [])
                mm_group(bu[:, 0:W], [(wb[:, k, 128:256], H[k][:, 0:W], [wb, H[k]]) for k in range(8)], bu, [])
                gr = GR[j % 3]
                kb.op(DVE, lambda: nc.vector.tensor_copy(V(gr, 0, 128, 2, [(ext, nseq), (1, Tt)]), V(bg, 0, 128, 0, [(Tt, nseq), (1, Tt)])),
                      reads=[bg], writes=[gr])
                if halo is not None:
                    kb.op(ACT, lambda: nc.scalar.copy(V(gr, 0, 128, 0, [(ext, nseq), (1, 2)]), V(halo, 0, 128, j * nseq * 2, [(2, nseq), (1, 2)])),
                          reads=[halo], writes=[gr])
                else:
                    kb.dma(SP, V(gr, 0, 128, 0, [(ext, nseq), (1, 2)]), st_ffn.ap()[l, j * 128:(j + 1) * 128, :, :], writes=[gr], slow=True)
                acc = ACC[j % 3]
                av = V(acc, 0, 128, 0, [(Tt, nseq), (1, Tt)])
                kb.op(DVE, lambda: nc.vector.tensor_scalar(av, V(gr, 0, 128, 0, [(ext, nseq), (1, Tt)]), Fd["cw"][:, j, 0:1], Fd["cb"][:, j:j + 1],
                                                           ALU.mult, ALU.add), reads=[gr, Fd["cw"], Fd["cb"]], writes=[acc])
                for k in (1, 2):
                    kb.op(DVE, lambda k=k: nc.vector.scalar_tensor_tensor(av, V(gr, 0, 128, k, [(ext, nseq), (1, Tt)]), Fd["cw"][:, j, k:k + 1], av,
                                                                          ALU.mult, ALU.add), reads=[gr, Fd["cw"], acc], writes=[acc])
                if halo is not None:
                    kb.op(ACT, lambda: nc.scalar.copy(V(halo, 0, 128, j * nseq * 2, [(2, nseq), (1, 2)]), V(gr, 0, 128, Tt, [(ext, nseq), (1, 2)])),
                          reads=[gr], writes=[halo])
                else:
                    kb.dma(SP, o_ffn_s.ap()[l, j * 128:(j + 1) * 128, :, :], V(gr, 0, 128, Tt, [(ext, nseq), (1, 2)]), reads=[gr], is_out=True, slow=True)
                kb.op(ACT, lambda: nc.scalar.activation(acc[:, 0:W], acc[:, 0:W], AF.Gelu_apprx_tanh), reads=[acc], writes=[acc])
                kb.op(DVE, lambda: nc.vector.tensor_tensor(YF[j][:, 0:W], acc[:, 0:W], bu[:, 0:W], ALU.mult), reads=[acc, bu], writes=[YF[j]])

        chunks = [(0, 128)] + [(128 + 512 * i, 512) for i in range(16)]
        chunks = chunks[:nchunk_prompt]
        for ci, (t0, W) in enumerate(chunks):
            blk0 = t0 // 128
            for c in range(8):
                kb.dma(SP, X[c][:, 0:W], xT_p.ap()[c * 128:(c + 1) * 128, t0:t0 + W], writes=[X[c]])
            for l in range(4):
                rmsnorm(X, l, H, W)
                kb.chk("n0")
                if l % 2 == 0:
                    li = l // 2
                    li_cur[0] = li
                    in_proj(li, W, None)
                    kb.chk("inproj")
                    s5_core(S5[li], W, W // 64, 64)
                    kb.chk("s5c")
                    attn_prompt(li, W, blk0)
                    kb.chk("att")
                    if ci == dbg_ci and li == 0:
                        dump16(MIX, W)
                        dump16(UT + QT[0], W)
                        dump(Etab + Etab, 512)
                    out_proj(w_out.ap()[li], W, 8, MIX)
                    if ci == dbg_ci:
                        dump(X, W)
                else:
                    lo = l // 2
                    conformer(lo, W, 1, W, CONV[lo]["halo"], zero_pad_cols=(NPAD if ci == 0 else 0))
                    if ci == dbg_ci:
                        dump(X, W)
                rmsnorm(X, 4 + l, H, W)
                ffn_up(l, W, 1, W, FFN[l]["halo"])
                ffn_down(l, W)
                if ci == dbg_ci:
                    dump(X, W)
                if ci == 0 and l % 2 == 1:
                    pass
            for c in range(8):
                pass
            YO = [P32[c] for c in range(8)]
            rmsnorm(X, 8, None, W, out_f32=YO)
            for c in range(8):
                kb.dma(SP, yT_p.ap()[c * 128:(c + 1) * 128, t0:t0 + W], YO[c][:, 0:W], reads=[YO[c]], is_out=True)
        if nchunk_prompt == 17:
            for li in range(2):
                for ri in range(2):
                    kb.dma(SP, o_ssm_p.ap()[li, ri], S5[li]["car"][ri][:], reads=[S5[li]["car"][ri]], is_out=True)
                kb.dma(SP, o_kT_p.ap()[li], ATT[li]["kT32"][:], reads=[ATT[li]["kT32"]], is_out=True)
                kb.dma(SP, o_v_p.ap()[li], ATT[li]["v32"][:], reads=[ATT[li]["v32"]], is_out=True)
                for c in range(8):
                    kb.dma(SP, o_conv_p.ap()[li, c * 128:(c + 1) * 128, :], CONV[li]["halo"][:, c, :], reads=[CONV[li]["halo"]], is_out=True)
            for l in range(4):
                kb.dma(SP, o_ffn_p.ap()[l].rearrange("(j p) t -> p j t", p=128), FFN[l]["halo"][:], reads=[FFN[l]["halo"]], is_out=True)

        def attn_sample(li):
            A = ATT[li]
            W = 128
            for kv in range(2):
                wb = load_w(w_in.ap()[li][:, 1024 + kv * 128:1024 + (kv + 1) * 128], 8, 128)
                bk = kb.bank()
                mm_group(bk[:, 0:W], [(wb[:, k, 0:128], H[k][:, 0:W], [wb, H[k]]) for k in range(8)], bk, [])
                kb.op(DVE, lambda: nc.vector.tensor_copy(KKs[kv][:, :], bk[:, 0:W]), reads=[bk], writes=[KKs[kv]])
                kb.op(DVE, lambda: nc.vector.tensor_copy(A["kT32"][64 * kv:64 * kv + 64, :], bk[64 * kv:64 * kv + 64, 0:W]), reads=[bk], writes=[A["kT32"]])
            wv = load_w(w_in.ap()[li][:, 1280:1408], 8, 128)
            bk = kb.bank()
            mm_group(bk[:, 0:128], [(H[k][:, 0:128], wv[:, k, 0:128], [H[k], wv]) for k in range(8)], bk, [])
            kb.op(DVE, lambda: nc.vector.tensor_copy(vbf[:, :], bk[:, 0:128]), reads=[bk], writes=[vbf])
            kb.op(DVE, lambda: nc.vector.tensor_copy(A["v32"][:, :], bk[:, 0:128]), reads=[bk], writes=[A["v32"]])
            kb.dma(SP, o_kT_s.ap()[li][:, :, 0:120], st_kT.ap()[li][:, :, 8:128], is_out=True, slow=True)
            kb.dma(SP, o_v_s.ap()[li][:, 0:120, :], st_v.ap()[li][:, 8:128, :], is_out=True)
            kb.dma(SP, o_kT_s.ap()[li][:, :, 120:128].rearrange("s p t -> p s t"), V(A["kT32"], 0, 128, 0, [(8, NSEQ), (1, 8)]),
                   reads=[A["kT32"]], is_out=True, slow=True)
            for s_ in range(NSEQ):
                kb.dma(SP, o_v_s.ap()[li, s_][120:128, :], A["v32"][8 * s_:8 * s_ + 8, :], reads=[A["v32"]], is_out=True)
            bn, bd = kb.bank(), kb.bank()
            sbanks = [kb.bank() for _ in range(4)]
            for sq_i in range(NSEQ):
                kx = [KX[kv][sq_i % 2] for kv in range(2)]
                vz, vb = VZS[sq_i % 2], VBS[sq_i % 2]
                for kv in range(2):
                    for hf in range(2):
                        kb.dma(POOL, kx[kv][64 * hf:64 * hf + 64, 0:128], st_kT.ap()[li, sq_i][64 * kv:64 * kv + 64, :], writes=[kx[kv]])
                    kb.op(ACT, lambda kv=kv: nc.scalar.copy(kx[kv][:, 128:136], KKs[kv][:, 8 * sq_i:8 * sq_i + 8]), reads=[KKs[kv]], writes=[kx[kv]])
                    kb.dma(POOL, vz[0:120, 64 + 128 * kv:128 + 128 * kv], st_v.ap()[li, sq_i][8:128, 64 * kv:64 * kv + 64], writes=[vz])
                    kb.dma(POOL, vb[0:8, 64 + 128 * kv:128 + 128 * kv], st_v.ap()[li, sq_i][0:8, 64 * kv:64 * kv + 64], writes=[vb])
                    kb.dma(POOL, vz[120:128, 64 + 128 * kv:128 + 128 * kv], vbf[8 * sq_i:8 * sq_i + 8, 64 * kv:64 * kv + 64], reads=[vbf], writes=[vz])
                ba, bb = sbanks[2 * (sq_i % 2)], sbanks[2 * (sq_i % 2) + 1]
                for c in range(4):
                    kv = c // 2
                    for e in range(2):
                        col = c * 16 + e * 8
                        kb.op(PE, lambda: nc.tensor.matmul(ba[:, col:col + 8], kx[kv][:, 8:136],
                                                           QT[e][c][:, 8 * sq_i:8 * sq_i + 8], start=True, stop=True),
                              reads=[kx[kv], QT[e][c]], writes=[ba], inc=False)
                        kb.op(PE, lambda: nc.tensor.matmul(bb[0:8, col:col + 8], kx[kv][:, 0:8],
                                                           QT[e][c][:, 8 * sq_i:8 * sq_i + 8], start=True, stop=True),
                              reads=[kx[kv], QT[e][c]], writes=[bb], inc=(c == 3 and e == 1))
                kb.op(DVE, lambda: nc.vector.tensor_copy(EXA[:, :], ba[:, 0:64]), reads=[ba], writes=[EXA])
                kb.op(ACT, lambda: nc.scalar.activation(EXA[:, :], EXA[:, :], AF.Exp, scale=0.125), reads=[EXA], writes=[EXA])
                kb.op(DVE, lambda: nc.vector.tensor_copy(EXB[:, :], bb[0:8, 0:64]), reads=[bb], writes=[EXB])
                kb.op(ACT, lambda: nc.scalar.activation(EXB[:, :], EXB[:, :], AF.Exp, scale=0.125), reads=[EXB], writes=[EXB])
                pa, pb = PAs[sq_i % 2], PBs[sq_i % 2]
                kb.op(DVE, lambda: nc.vector.tensor_tensor(pa[:, :], EXA[:, :], EAt[:, :], ALU.mult), reads=[EXA, EAt], writes=[pa])
                kb.op(DVE, lambda: nc.vector.tensor_tensor(pb[:, :], EXB[:, :], EBt[:, :], ALU.mult), reads=[EXB, EBt], writes=[pb])
                for c in range(4):
                    kv = c // 2
                    npairs, dpairs = [], []
                    for e in range(2):
                        vs = (64 + 128 * kv, 192 + 128 * kv) if e == 0 else (128 * kv, 128 + 128 * kv)
                        osl = (64, 192) if e == 0 else (0, 128)
                        col = c * 16 + e * 8
                        npairs.append((vz[:, vs[0]:vs[1]], pa[:, col:col + 8], [vz, pa]))
                        npairs.append((vb[0:8, vs[0]:vs[1]], pb[0:8, col:col + 8], [vb, pb]))
                        dpairs.append((Oz[:, osl[0]:osl[1]], pa[:, col:col + 8], [Oz, pa]))
                        dpairs.append((Oz[0:8, osl[0]:osl[1]], pb[0:8, col:col + 8], [Oz, pb]))
                    oc = sq_i * 32 + c * 8
                    mm_group(bn[:, oc:oc + 8], npairs, bn, [])
                    mm_group(bd[:, oc:oc + 8], dpairs, bd, [])
            for c in range(4):
                dv = V(dn_t, 0, 128, c * 8, [(32, NSEQ), (1, 8)])
                kb.op(DVE, lambda: nc.vector.tensor_scalar(dv, V(bd, 0, 128, c * 8, [(32, NSEQ), (1, 8)]), EsT[li][:, c:c + 1], None, ALU.add),
                      reads=[bd, EsT[li]], writes=[dn_t])
            kb.op(DVE, lambda: nc.vector.reciprocal(dn_t[:, 0:512], dn_t[:, 0:512]), reads=[dn_t], writes=[dn_t])
            for c in range(4):
                kb.op(DVE, lambda: nc.vector.tensor_tensor(V(MIX[4 + c], 0, 128, 0, [(8, NSEQ), (1, 8)]), V(bn, 0, 128, c * 8, [(32, NSEQ), (1, 8)]),
                                                           V(dn_t, 0, 128, c * 8, [(32, NSEQ), (1, 8)]), ALU.mult),
                      reads=[bn, dn_t], writes=[MIX[4 + c]])

        if do_sample:
            KKs = [kb.sb("KKs%d" % k, [128, 128], BF16) for k in range(2)]
            vbf = kb.sb("vbf", [128, 128], BF16)
            KX = [[kb.sb("KX%d%d" % (k, i), [128, 136], BF16) for i in range(2)] for k in range(2)]
            VZS = [kb.sb("VZS%d" % i, [128, 320], BF16) for i in range(2)]
            VBS = [kb.sb("VBS%d" % i, [8, 320], BF16) for i in range(2)]
            for t_ in VZS + VBS:
                kb.op(POOL, lambda t_=t_: nc.gpsimd.memset(t_[:], 0.0), writes=[t_])
            EXA = kb.sb("EXA", [128, 64], F32)
            EXB = kb.sb("EXB", [8, 64], F32)
            PAs = [kb.sb("PAs%d" % i, [128, 64], BF16) for i in range(2)]
            PBs = [kb.sb("PBs%d" % i, [8, 64], BF16) for i in range(2)]
            W = 128
            for c in range(8):
                kb.dma(SP, X[c][:, 0:W], xT_s.ap()[c * 128:(c + 1) * 128, :], writes=[X[c]])
            for l in range(4):
                rmsnorm(X, l, H, W)
                if l % 2 == 0:
                    li = l // 2
                    li_cur[0] = li
                    in_proj(li, W, None)
                    s5_core(S5[li], W, NSEQ, 8, sample_h0=li)
                    attn_sample(li)
                    if li == 0:
                        dump16(MIX, W)
                        dump([P32[2]] * 8, 128)
                    out_proj(w_out.ap()[li], W, 8, MIX)
                else:
                    lo = l // 2
                    conformer(lo, W, NSEQ, 8, None)
                rmsnorm(X, 4 + l, H, W)
                ffn_up(l, W, NSEQ, 8, None)
                ffn_down(l, W)
            YO = [P32[c] for c in range(8)]
            rmsnorm(X, 8, None, W, out_f32=YO)
            for c in range(8):
                kb.dma(SP, yT_s.ap()[c * 128:(c + 1) * 128, :], YO[c][:, 0:W], reads=[YO[c]], is_out=True)

        kb.finish()
    return kb.nc


_CACHE = {}


def _prep_common(inp):
    f = np.float32
    d = {}
    gv = np.concatenate([inp["g_mix"], inp["g_ffn"], inp["g_final"][None]], 0)
    d["gvec"] = np.ascontiguousarray(gv.reshape(9, 8, 128).transpose(2, 0, 1)).astype(f)
    wi = inp["w_in_mix"]
    u, q = wi[:, :, 0:512], wi[:, :, 512:1024]
    k0, k1, v = wi[:, :, 1024:1088], wi[:, :, 1088:1152], wi[:, :, 1152:1280]
    d["w_in"] = np.ascontiguousarray(np.concatenate([u, q, k0, k0, k1, k1, v], -1)).astype(f)

    def st(a):
        return a.reshape(2, 16, 2, 64).transpose(0, 2, 3, 1).reshape(2, 128, 16)
    ls = np.broadcast_to(inp["ssm_log_step"][:, :, None], (2, 32, 64))
    d["lam"] = np.ascontiguousarray(np.stack([st(inp["ssm_lambda_re"]), st(inp["ssm_lambda_im"]), st(ls)], 1)).astype(f)
    d["ssm_b"] = np.ascontiguousarray(np.stack([inp["ssm_b_re"], inp["ssm_b_im"]], 1)).astype(f)
    d["ssm_c"] = np.ascontiguousarray(np.stack([inp["ssm_c_re"], inp["ssm_c_im"]], 1)).astype(f)
    d["ssm_d"] = np.ascontiguousarray(inp["ssm_d"].reshape(2, 4, 128).transpose(0, 2, 1)).astype(f)
    d["w_glu"] = np.ascontiguousarray(inp["ssm_w_glu"]).astype(f)
    d["b_glu"] = np.ascontiguousarray(inp["ssm_b_glu"].reshape(2, 4, 128).transpose(0, 2, 1)).astype(f)
    d["relb"] = np.ascontiguousarray(inp["rel_bias"]).astype(f)
    sk = inp["attn_sinks"]
    d["sinks"] = np.ascontiguousarray(np.repeat(sk.reshape(2, 4, 2), 64, axis=2).transpose(0, 2, 1)).astype(f)
    d["w_out"] = np.ascontiguousarray(inp["w_out_mix"]).astype(f)
    p1 = inp["conv_w_pw1"]
    a, g = p1[:, :, :1024].reshape(2, 1024, 8, 128), p1[:, :, 1024:].reshape(2, 1024, 8, 128)
    d["w_pw1"] = np.ascontiguousarray(np.concatenate([a, g], -1)).astype(f)
    d["w_dw"] = np.ascontiguousarray(inp["conv_w_dw"].reshape(2, 31, 8, 128).transpose(0, 3, 2, 1)).astype(f)
    cv = np.stack([inp["conv_b_dw"], inp["conv_ln_g"], inp["conv_ln_b"]], 1)
    d["cvec"] = np.ascontiguousarray(cv.reshape(2, 3, 8, 128).transpose(0, 3, 1, 2)).astype(f)
    d["w_pw2"] = np.ascontiguousarray(inp["conv_w_pw2"]).astype(f)
    wu = inp["ffn_w_up"]
    gg, uu = wu[:, :, :DFF].reshape(4, 1024, NJ, 128), wu[:, :, DFF:].reshape(4, 1024, NJ, 128)
    d["w_up"] = np.ascontiguousarray(np.concatenate([gg, uu], -1)).astype(f)
    d["f_cw"] = np.ascontiguousarray(inp["ffn_w_conv"].reshape(4, 3, NJ, 128).transpose(0, 3, 2, 1)).astype(f)
    d["f_cb"] = np.ascontiguousarray(inp["ffn_b_conv"].reshape(4, NJ, 128).transpose(0, 2, 1)).astype(f)
    d["w_dn"] = np.ascontiguousarray(inp["ffn_w_down"]).astype(f)
    m = np.arange(384)
    dist = m - 127
    inside = (dist >= 0) & (dist < 128)
    oh = np.zeros((32, 384), f)
    bk = t5_bucket_np(np.clip(dist, 0, 127))
    oh[bk[inside], m[inside]] = 1.0
    d["oh_bucket"] = oh
    d["msk_ext"] = np.ascontiguousarray(np.broadcast_to(np.where(inside, 0.0, NEG).astype(f)[None], (8, 384)))
    d["antiI"] = np.ascontiguousarray(np.eye(128, dtype=f)[::-1])
    return d


def kernel(**inp):
    inp = {k: np.asarray(v) for k, v in inp.items()}
    f = np.float32
    if "nc" not in _CACHE:
        _CACHE["nc"] = build()
    nc = _CACHE["nc"]
    com = _prep_common(inp)
    in_maps = []
    for c in range(8):
        d = dict(com)
        s = c % 2
        xp = np.concatenate([np.zeros((NPAD, D), f), inp["meta_tokens"], inp["x_prompt"][s]], 0)
        d["xT_p"] = np.ascontiguousarray(xp.T)
        sl = slice(c * NSEQ, (c + 1) * NSEQ)
        d["xT_s"] = np.ascontiguousarray(inp["x_sample"][sl].reshape(NSEQ * 8, D).T)

        def st(a):
            return a.reshape(2, NSEQ, 16, 2, 64).transpose(0, 3, 4, 2, 1).reshape(2, 128, 16, NSEQ)
        d["st_ssm"] = np.ascontiguousarray(np.stack([st(inp["state_ssm_re"][:, sl]), st(inp["state_ssm_im"][:, sl])], 1)).astype(f)
        d["st_kT"] = np.ascontiguousarray(inp["cache_swa_k"][:, sl].reshape(2, NSEQ, 128, 128).transpose(0, 1, 3, 2)).astype(f)
        d["st_v"] = np.ascontiguousarray(inp["cache_swa_v"][:, sl].reshape(2, NSEQ, 128, 128)).astype(f)
        d["st_conv"] = np.ascontiguousarray(inp["state_conv"][:, sl].transpose(0, 3, 1, 2)).astype(f)
        d["st_ffn"] = np.ascontiguousarray(inp["state_ffn"][:, sl].transpose(0, 3, 1, 2)).astype(f)
        in_maps.append(d)
    res = run_bass_kernel_spmd(nc, in_maps, core_ids=list(range(8)))
    R = res.results
    _CACHE["last"] = R
    y_p = np.stack([R[s]["yT_p"][:, 128:].T for s in range(2)], 0)
    y_s = np.concatenate([R[c]["yT_s"].T.reshape(NSEQ, 8, D) for c in range(8)], 0)

    def ust(a):
        return a.reshape(2, 2, 64, 16).transpose(0, 3, 1, 2).reshape(2, 32, 64)
    sr_p = np.stack([ust(R[s]["o_ssm_p"][:, 0]) for s in range(2)], 1)
    si_p = np.stack([ust(R[s]["o_ssm_p"][:, 1]) for s in range(2)], 1)
    k_p = np.stack([R[s]["o_kT_p"].transpose(0, 2, 1).reshape(2, 128, 2, 64) for s in range(2)], 1)
    v_p = np.stack([R[s]["o_v_p"].reshape(2, 128, 2, 64) for s in range(2)], 1)
    c_p = np.stack([R[s]["o_conv_p"].transpose(0, 2, 1) for s in range(2)], 1)
    f_p = np.stack([R[s]["o_ffn_p"].transpose(0, 2, 1) for s in range(2)], 1)

    def usts(a):
        return a.reshape(2, 2, 64, 16, NSEQ).transpose(0, 4, 3, 1, 2).reshape(2, NSEQ, 32, 64)
    sr_s = np.concatenate([usts(R[c]["o_ssm_s"][:, 0]) for c in range(8)], 1)
    si_s = np.concatenate([usts(R[c]["o_ssm_s"][:, 1]) for c in range(8)], 1)
    k_s = np.concatenate([R[c]["o_kT_s"].transpose(0, 1, 3, 2).reshape(2, NSEQ, 128, 2, 64) for c in range(8)], 1)
    v_s = np.concatenate([R[c]["o_v_s"].reshape(2, NSEQ, 128, 2, 64) for c in range(8)], 1)
    c_s = np.concatenate([R[c]["o_conv_s"].transpose(0, 2, 3, 1) for c in range(8)], 1)
    f_s = np.concatenate([R[c]["o_ffn_s"].transpose(0, 2, 3, 1) for c in range(8)], 1)
    outs = (y_p, y_s, sr_p, si_p, k_p, v_p, c_p, f_p, sr_s, si_s, k_s, v_s, c_s, f_s)
    return tuple(np.ascontiguousarray(o).astype(np.float32) for o in outs)
```

```python
import contextlib
import math
import numpy as np
import concourse.bass as bass
import concourse.mybir as mybir
from concourse.bass_utils import run_bass_kernel_spmd

F32 = mybir.dt.float32
BF16 = mybir.dt.bfloat16
AF = mybir.ActivationFunctionType
ALU = mybir.AluOpType

D = 1024
NC8 = 8
DFF = 2816
NJ = 22
NPAD = 112
TPAD = 8320
NBLK = 65
NSEQ = 16
EPS = 1e-6
NEG = -30000.0


def t5_bucket_np(dist):
    n = np.maximum(dist, 0)
    max_exact = 16
    nf = np.maximum(n, max_exact).astype(np.float32)
    large = max_exact + (np.log(nf / max_exact) / math.log(128 / max_exact) * (32 - max_exact)).astype(np.int32)
    large = np.minimum(large, 31)
    return np.where(n < max_exact, n, large)


class Buf:
    __slots__ = ("t", "name", "last_w", "readers", "dsem", "dcnt", "wn")

    def __init__(self, t, name):
        self.t = t
        self.name = name
        self.last_w = None
        self.readers = {}
        self.dsem = None
        self.dcnt = 0
        self.wn = None

    def __getitem__(self, idx):
        if self.wn is not None and isinstance(idx, tuple) and len(idx) == 3 and isinstance(idx[1], int):
            cs = idx[2]
            return V(self, 0, 128, idx[1] * self.wn + cs.start, [(1, cs.stop - cs.start)])
        return self.t[idx]


class _Stop(Exception):
    pass


_LAST = {}


class KB:
    def __init__(self):
        self.nc = bass.Bass("TRN2", target_bir_lowering=False)
        self.es = contextlib.ExitStack()
        nc = self.nc
        self.eng = {"pe": nc.tensor, "act": nc.scalar, "dve": nc.vector, "pool": nc.gpsimd, "sp": nc.sync}
        self.sem = {}
        self.cnt = {}
        self.seen = {e: {} for e in self.eng}
        self.uid = 0
        self.psum_rr = 0
        self.out_events = []
        self.dead = False

    def start(self):
        import os
        for i in range(int(os.environ.get("KDUMMYSEM", "0"))):
            self.es.enter_context(self.nc.semaphore("dummy%d" % i))
        for e in self.eng:
            self.sem[e] = self.es.enter_context(self.nc.semaphore("prog_" + e))
            self.cnt[e] = 0
        self.banks = []
        for i in range(8):
            t = self.es.enter_context(self.nc.psum_tensor("bank%d" % i, [128, 512], F32))
            self.banks.append(Buf(t, "bank%d" % i))

    def sb(self, name, shape, dtype):
        self.uid += 1
        t = self.es.enter_context(self.nc.sbuf_tensor("%s_%d" % (name, self.uid), list(shape), dtype))
        return Buf(t, name)

    def dram(self, name, shape, dtype, kind):
        return self.nc.dram_tensor(name, list(shape), dtype, kind=kind)

    def bank(self):
        b = self.banks[self.psum_rr % 8]
        self.psum_rr += 1
        return b

    def _waits(self, e, reads, writes):
        deps = []
        for b in reads:
            if b.last_w is not None:
                deps.append((b.last_w, True))
        for b in writes:
            if b.last_w is not None:
                deps.append((b.last_w, False))
            for ev in b.readers.values():
                deps.append((ev, False))
        own = self.sem.get(e)
        for (sem, val), raw in deps:
            if sem is own:
                if e in ("pe", "sp"):
                    continue
            key = id(sem)
            if self.seen[e].get(key, 0) < val:
                self.eng[e].wait_ge(sem, val)
                self.seen[e][key] = val

    def chk(self, tag):
        import os
        if os.environ.get("KSTOP") == tag:
            for i in range(int(os.environ.get("KEXTRA", "0"))):
                tgt = self._xtra if os.environ.get("KXT") else self._misc
                w = 128 if os.environ.get("KXT") else 1
                if os.environ.get("KXT") == "2":
                    self.op("dve", lambda: self.nc.vector.tensor_copy(self._xtra[:, 0:128], self._xtra2[:, 0:128]), reads=[self._xtra2], writes=[self._xtra])
                else:
                    self.op("dve", lambda: self.nc.vector.memset(tgt[:, 0:w], 0.0), writes=[tgt])
            self.dead = True
            if os.environ.get("KRAISE"):
                self.dead = False
                self.finish()
                raise _Stop()

    def op(self, e, fn, reads=(), writes=(), inc=True):
        if self.dead:
            return None
        self._waits(e, reads, writes)
        inst = fn()
        if inc:
            self.cnt[e] += 1
            inst.then_inc(self.sem[e], 1)
            ev = (self.sem[e], self.cnt[e])
        else:
            ev = (self.sem[e], self.cnt[e] + 1)
        for b in writes:
            b.last_w = ev
            b.readers = {}
        for b in reads:
            b.readers[e] = ev
        return inst

    def dma(self, q, out_ap, in_ap, reads=(), writes=(), slow=False, is_out=False):
        if self.dead:
            return None
        self._waits(q, reads, writes)
        kw = {}
        if slow:
            kw["allow_slow_non_contiguous"] = True
        inst = self.eng[q].dma_start(out=out_ap, in_=in_ap, **kw)
        tgt = writes[0] if writes else (reads[0] if reads else None)
        if tgt is None:
            tgt = self._misc
        if tgt.dsem is None:
            self.uid += 1
            tgt.dsem = self.es.enter_context(self.nc.semaphore("d_%s_%d" % (tgt.name, self.uid)))
        tgt.dcnt += 16
        inst.then_inc(tgt.dsem, 16)
        ev = (tgt.dsem, tgt.dcnt)
        for b in writes:
            b.last_w = ev
            b.readers = {}
        for b in reads:
            b.readers[("dma", id(tgt))] = ev
        self.out_events.append(ev)
        return inst

    def finish(self):
        last = {}
        for sem, val in self.out_events:
            k = id(sem)
            if k not in last or last[k][1] < val:
                last[k] = (sem, val)
        for sem, val in last.values():
            self.eng["sp"].wait_ge(sem, val)
        for e in self.eng:
            for e2 in ("pe", "act", "dve", "pool"):
                if e2 != e and self.cnt[e2] > 0:
                    self.eng[e].wait_ge(self.sem[e2], self.cnt[e2])


def V(buf, p0, np_, off, dims):
    t = buf.t
    shape = t.shape
    fsz = 1
    for s in shape[1:]:
        fsz *= s
    return bass.AP(t, p0 * fsz + off, [[fsz, np_]] + [[s, c] for (s, c) in dims])


def build(nchunk_prompt=17, do_sample=True, dbg=False, dbg_ci=1):
    try:
        return _build(nchunk_prompt, do_sample, dbg, dbg_ci)
    except _Stop:
        return _LAST["kb"].nc


def _build(nchunk_prompt=17, do_sample=True, dbg=False, dbg_ci=1):
    kb = KB()
    _LAST["kb"] = kb
    nc = kb.nc
    es = kb.es
    with es:
        kb.start()
        import os as _os
        kb._misc = kb.sb("misc", [128, 1], F32)
        PE, ACT, DVE, POOL, SP = "pe", "act", "dve", "pool", "sp"
        din = {}

        def DI(name, shape):
            din[name] = kb.dram(name, shape, F32, "ExternalInput")
            return din[name]

        dout = {}

        def DO(name, shape):
            dout[name] = kb.dram(name, shape, F32, "ExternalOutput")
            return dout[name]

        xT_p = DI("xT_p", [D, TPAD])
        xT_s = DI("xT_s", [D, 128])
        st_ssm = DI("st_ssm", [2, 2, 128, 16, NSEQ])
        st_kT = DI("st_kT", [2, NSEQ, 128, 128])
        st_v = DI("st_v", [2, NSEQ, 128, 128])
        st_conv = DI("st_conv", [2, D, NSEQ, 30])
        st_ffn = DI("st_ffn", [4, DFF, NSEQ, 2])
        gvec = DI("gvec", [128, 9, 8])
        w_in = DI("w_in", [2, 11, 128, 1024])
        lam = DI("lam", [2, 3, 128, 16])
        ssm_b = DI("ssm_b", [2, 2, 32, 64, 16])
        ssm_c = DI("ssm_c", [2, 2, 32, 16, 64])
        ssm_d = DI("ssm_d", [2, 128, 4])
        w_glu = DI("w_glu", [2, 4, 128, 512])
        b_glu = DI("b_glu", [2, 128, 4])
        relb = DI("relb", [32, 8])
        sinks = DI("sinks", [2, 128, 4])
        w_out = DI("w_out", [2, 8, 128, 1024])
        w_pw1 = DI("w_pw1", [2, 8, 128, 2048])
        w_dw = DI("w_dw", [2, 128, 8, 31])
        cvec = DI("cvec", [2, 128, 3, 8])
        w_pw2 = DI("w_pw2", [2, 8, 128, 1024])
        w_up = DI("w_up", [4, NJ, 128, 2048])
        f_cw = DI("f_cw", [4, 128, NJ, 3])
        f_cb = DI("f_cb", [4, 128, NJ])
        w_dn = DI("w_dn", [4, 8, 2, 128, 1408])
        oh_bucket = DI("oh_bucket", [32, 384])
        msk_ext = DI("msk_ext", [8, 384])
        antiI = DI("antiI", [128, 128])

        yT_p = DO("yT_p", [D, TPAD])
        yT_s = DO("yT_s", [D, 128])
        o_ssm_p = DO("o_ssm_p", [2, 2, 128, 16])
        o_ssm_s = DO("o_ssm_s", [2, 2, 128, 16, NSEQ])
        o_kT_p = DO("o_kT_p", [2, 128, 128])
        o_v_p = DO("o_v_p", [2, 128, 128])
        o_kT_s = DO("o_kT_s", [2, NSEQ, 128, 128])
        o_v_s = DO("o_v_s", [2, NSEQ, 128, 128])
        o_conv_p = DO("o_conv_p", [2, D, 30])
        o_conv_s = DO("o_conv_s", [2, D, NSEQ, 30])
        o_ffn_p = DO("o_ffn_p", [4, DFF, 2])
        o_ffn_s = DO("o_ffn_s", [4, DFF, NSEQ, 2])
        scr = kb.dram("scr_bias", [8, 384], F32, "Internal")
        if dbg:
            dbg_o = DO("dbg", [16, D, 512])
        dbgc = [0]

        def dump16(Xl, W):
            if not dbg:
                return
            for c in range(8):
                kb.dma(POOL, dbg_o.ap()[dbgc[0], c * 128:(c + 1) * 128, 0:W], Xl[c][:, 0:W], reads=[Xl[c]], is_out=True)
            dbgc[0] += 1

        def dump(Xl, W):
            if not dbg:
                return
            for c in range(8):
                kb.dma(SP, dbg_o.ap()[dbgc[0], c * 128:(c + 1) * 128, 0:W], Xl[c][:, 0:W], reads=[Xl[c]], is_out=True)
            dbgc[0] += 1

        ident = kb.sb("ident", [128, 128], F32)
        kb.op(POOL, lambda: nc.gpsimd.memset(ident[:], 1.0), writes=[ident])
        kb.op(POOL, lambda: nc.gpsimd.affine_select(ident[:], ident[:], pattern=[[-1, 128]], compare_op=ALU.is_equal,
                                                     fill=0.0, base=0, channel_multiplier=1), reads=[ident], writes=[ident])
        ones_bf = kb.sb("ones_bf", [128, 128], BF16)
        kb.op(DVE, lambda: nc.vector.memset(ones_bf[:], 1.0), writes=[ones_bf])
        Oz = kb.sb("Oz", [128, 192], BF16)
        Oz0 = kb.sb("Oz0", [128, 192], BF16)
        for o in (Oz, Oz0):
            kb.op(DVE, lambda o=o: nc.vector.memset(o[:], 0.0), writes=[o])
            kb.op(DVE, lambda o=o: nc.vector.memset(o[:, 64:128], 1.0), writes=[o])
        kb.op(DVE, lambda: nc.vector.memset(Oz0[0:NPAD, :], 0.0), writes=[Oz0])
        gv = kb.sb("gv", [128, 9, 8], F32)
        kb.dma(SP, gv[:], gvec.ap(), writes=[gv])
        kb.chk("c0")

        WSL = [kb.sb("wslab%d" % i, [128, 8, 256], BF16) for i in range(2)]
        WDN = [kb.sb("wdn%d" % i, [128, 11, 128], BF16) for i in range(2)]
        wctr = {"a": 0, "b": 0}
        UT = [kb.sb("UT%d" % m, [128, 512], BF16) for m in range(4)]
        U32 = [kb.sb("U32_%d" % m, [128, 512], F32) for m in range(4)]
        QT = [[kb.sb("QT%d_%d" % (e, c), [128, 512], BF16) for c in range(4)] for e in range(2)]
        for e in range(2):
            for c in range(4):
                kb.op(POOL, lambda e=e, c=c: nc.gpsimd.memset(QT[e][c][:], 0.0), writes=[QT[e][c]])

        def load_w(dram_ap, kc, ncols):
            b = WSL[wctr["a"] % len(WSL)]
            wctr["a"] += 1
            b.wn = ncols
            kb.dma(POOL, V(b, 0, 128, 0, [(1, kc * ncols)]), dram_ap, writes=[b])
            return b

        def load_wdn(dram_ap):
            b = WDN[wctr["b"] % len(WDN)]
            wctr["b"] += 1
            kb.dma(POOL, V(b, 0, 128, 0, [(1, 11 * 128)]), dram_ap, writes=[b])
            return b

        def mm_group(out_ap, pairs, bankbuf, rbufs):
            n = len(pairs)
            if _os.environ.get("KHOIST"):
                allr = []
                for (_l, _r, bs_) in pairs:
                    allr += list(bs_)
                kb._waits(PE, allr + list(rbufs), [bankbuf])
            for i, (l, r, bs) in enumerate(pairs):
                kb.op(PE, lambda l=l, r=r, i=i: nc.tensor.matmul(out_ap, l, r, start=(i == 0), stop=(i == n - 1)),
                      reads=list(bs) + list(rbufs), writes=[bankbuf], inc=(i == n - 1))

        P32 = [kb.sb("P32_%d" % i, [128, 608], F32) for i in range(14)]
        P16 = [kb.sb("P16_%d" % i, [128, 512], BF16) for i in range(22)]
        kb._xtra = P32[5]
        kb._xtra2 = P32[6]
        sq = P16[12:20]
        rs = P32[12]
        rinv = P32[13]
        eps_t = kb.sb("eps_t", [128, 1], F32)
        kb.op(DVE, lambda: nc.vector.memset(eps_t[:], EPS), writes=[eps_t])

        def rmsnorm(X, gi, Hout, W, out_f32=None):
            lvl = int(_os.environ.get("KRMS", "9"))
            if lvl < 1:
                return
            for c in range(8):
                kb.op(DVE, lambda c=c: nc.vector.tensor_tensor(sq[c][:, 0:W], X[c][:, 0:W], X[c][:, 0:W], ALU.mult), reads=[X[c]], writes=[sq[c]])
            if lvl < 2:
                return
            bk = kb.bank()
            mm_group(bk[:, 0:W], [(ones_bf[:], sq[c][:, 0:W], [sq[c]]) for c in range(8)], bk, [ones_bf])
            if lvl < 3:
                return
            kb.op(DVE, lambda: nc.vector.tensor_scalar(rs[:, 0:W], bk[:, 0:W], 1.0 / D, EPS, ALU.mult, ALU.add), reads=[bk], writes=[rs])
            kb.op(ACT, lambda: nc.scalar.activation(rs[:, 0:W], rs[:, 0:W], AF.Sqrt), reads=[rs], writes=[rs])
            if lvl < 4:
                return
            kb.op(DVE, lambda: nc.vector.reciprocal(rinv[:, 0:W], rs[:, 0:W]), reads=[rs], writes=[rinv])
            if lvl < 5:
                return
            for c in range(8):
                o = Hout[c] if out_f32 is None else out_f32[c]
                kb.op(DVE, lambda c=c, o=o: nc.vector.scalar_tensor_tensor(o[:, 0:W], X[c][:, 0:W], gv[:, gi, c:c + 1], rinv[:, 0:W],
                                                                          ALU.mult, ALU.mult),
                      reads=[X[c], gv, rinv], writes=[o])

        X = [kb.sb("X%d" % c, [128, 512], F32) for c in range(8)]
        H = [kb.sb("H%d" % c, [128, 512], BF16) for c in range(8)]

        S5 = []
        import os as _os
        for li in range(0 if _os.environ.get("KSKIP_S5") else 2):
            T = {}
            lm = kb.sb("lam", [128, 3, 16], F32)
            kb.dma(SP, lm[:], lam.ap()[li].rearrange("k p j -> p k j"), writes=[lm])
            dt = kb.sb("dt", [128, 16], F32)
            kb.op(ACT, lambda: nc.scalar.activation(dt[:], lm[:, 2, :], AF.Exp), reads=[lm], writes=[dt])
            lrdt = kb.sb("lrdt", [128, 16], F32)
            th = kb.sb("th", [128, 16], F32)
            kb.op(DVE, lambda: nc.vector.tensor_tensor(lrdt[:], lm[:, 0, :], dt[:], ALU.mult), reads=[lm, dt], writes=[lrdt])
            kb.op(DVE, lambda: nc.vector.tensor_tensor(th[:], lm[:, 1, :], dt[:], ALU.mult), reads=[lm, dt], writes=[th])
            rho = kb.sb("rho", [128, 16], F32)
            kb.op(ACT, lambda: nc.scalar.activation(rho[:], lrdt[:], AF.Exp), reads=[lrdt], writes=[rho])
            T["rho"] = rho
            kk = P32[0]
            kb.op(POOL, lambda: nc.gpsimd.iota(kk[:, 0:64], pattern=[[1, 64]], base=1, channel_multiplier=0,
                                               allow_small_or_imprecise_dtypes=True), writes=[kk])
            ctab = kb.sb("ctab", [128, 16, 64], F32)
            stab = kb.sb("stab", [128, 16, 64], F32)
            negpi = kb.sb("negpi", [128, 1], F32)
            ki32 = kb.sb("ki32", [128, 64], mybir.dt.int32)
            kb.op(DVE, lambda: nc.vector.memset(negpi[:], -math.pi), writes=[negpi])
            for j in range(16):
                ang = P32[1 + (j % 2)]
                kb.op(DVE, lambda j=j, ang=ang: nc.vector.tensor_scalar(ang[:, 0:64], kk[:, 0:64], th[:, j:j + 1], None, ALU.mult),
                      reads=[kk, th], writes=[ang])
                for (dst, sh, ti) in ((stab, 0.5, 3), (ctab, 0.75, 5)):
                    tmp = P32[ti + (j % 2)]
                    kb.op(DVE, lambda sh=sh, tmp=tmp, ang=ang: nc.vector.tensor_scalar(tmp[:, 0:64], ang[:, 0:64], 1.0 / (2 * math.pi), sh, ALU.mult, ALU.add),
                          reads=[ang], writes=[tmp])
                    kb.op(DVE, lambda tmp=tmp: nc.vector.tensor_copy(ki32[:, 0:64], tmp[:, 0:64]), reads=[tmp], writes=[ki32])
                    kb.op(DVE, lambda tmp=tmp: nc.vector.tensor_copy(tmp[:, 64:128], ki32[:, 0:64]), reads=[ki32], writes=[tmp])
                    kb.op(DVE, lambda tmp=tmp: nc.vector.tensor_tensor(tmp[:, 0:64], tmp[:, 0:64], tmp[:, 64:128], ALU.subtract), reads=[tmp], writes=[tmp])
                    kb.op(DVE, lambda tmp=tmp: nc.vector.tensor_scalar(tmp[:, 64:128], tmp[:, 0:64], 0.0, None, ALU.is_lt), reads=[tmp], writes=[tmp])
                    kb.op(DVE, lambda tmp=tmp: nc.vector.tensor_tensor(tmp[:, 0:64], tmp[:, 0:64], tmp[:, 64:128], ALU.add), reads=[tmp], writes=[tmp])
                    kb.op(ACT, lambda dst=dst, tmp=tmp, j=j: nc.scalar.activation(dst[:, j, :], tmp[:, 0:64], AF.Sin, bias=negpi[:], scale=2 * math.pi),
                          reads=[tmp, negpi], writes=[dst])
            T["ctab"], T["stab"] = ctab, stab
            kb.chk("s5ang")
            are = kb.sb("are", [128, 16], F32)
            aim = kb.sb("aim", [128, 16], F32)
            kb.op(DVE, lambda: nc.vector.tensor_tensor(are[:], rho[:], ctab[:, :, 0], ALU.mult), reads=[rho, ctab], writes=[are])
            kb.op(DVE, lambda: nc.vector.tensor_tensor(aim[:], rho[:], stab[:, :, 0], ALU.mult), reads=[rho, stab], writes=[aim])
            den = kb.sb("den", [128, 16], F32)
            t1 = kb.sb("t1", [128, 16], F32)
            t2 = kb.sb("t2", [128, 16], F32)
            kb.op(DVE, lambda: nc.vector.tensor_tensor(den[:], lm[:, 0, :], lm[:, 0, :], ALU.mult), reads=[lm], writes=[den])
            kb.op(DVE, lambda: nc.vector.tensor_tensor(t1[:], lm[:, 1, :], lm[:, 1, :], ALU.mult), reads=[lm], writes=[t1])
            kb.op(DVE, lambda: nc.vector.tensor_tensor(den[:], den[:], t1[:], ALU.add), reads=[den, t1], writes=[den])
            rden = kb.sb("rden", [128, 16], F32)
            kb.op(DVE, lambda: nc.vector.reciprocal(rden[:], den[:]), reads=[den], writes=[rden])
            nre = kb.sb("nre", [128, 16], F32)
            kb.op(DVE, lambda: nc.vector.tensor_scalar_add(nre[:], are[:], -1.0), reads=[are], writes=[nre])
            cre = kb.sb("cre", [128, 16], F32)
            cim = kb.sb("cim", [128, 16], F32)
            ncim = kb.sb("ncim", [128, 16], F32)
            kb.op(DVE, lambda: nc.vector.tensor_tensor(t1[:], nre[:], lm[:, 0, :], ALU.mult), reads=[nre, lm], writes=[t1])
            kb.op(DVE, lambda: nc.vector.tensor_tensor(t2[:], aim[:], lm[:, 1, :], ALU.mult), reads=[aim, lm], writes=[t2])
            kb.op(DVE, lambda: nc.vector.tensor_tensor(t1[:], t1[:], t2[:], ALU.add), reads=[t1, t2], writes=[t1])
            kb.op(DVE, lambda: nc.vector.tensor_tensor(cre[:], t1[:], rden[:], ALU.mult), reads=[t1, rden], writes=[cre])
            kb.op(DVE, lambda: nc.vector.tensor_tensor(t1[:], aim[:], lm[:, 0, :], ALU.mult), reads=[aim, lm], writes=[t1])
            kb.op(DVE, lambda: nc.vector.tensor_tensor(t2[:], nre[:], lm[:, 1, :], ALU.mult), reads=[nre, lm], writes=[t2])
            kb.op(DVE, lambda: nc.vector.tensor_tensor(t1[:], t1[:], t2[:], ALU.subtract), reads=[t1, t2], writes=[t1])
            kb.op(DVE, lambda: nc.vector.tensor_tensor(cim[:], t1[:], rden[:], ALU.mult), reads=[t1, rden], writes=[cim])
            kb.op(DVE, lambda: nc.vector.tensor_scalar_mul(ncim[:], cim[:], -1.0), reads=[cim], writes=[ncim])
            kb.chk("s5coef")
            Bl = [kb.sb("Bl_re", [128, 16, 128], BF16), kb.sb("Bl_im", [128, 16, 128], BF16)]
            Cl = [kb.sb("Cl_re", [128, 16, 128], BF16), kb.sb("Cl_im", [128, 16, 128], BF16)]
            for j in range(16):
                o = 128 * (j % 2)
                zb = [P32[7], P32[8]]
                zc = [P32[9], P32[10]]
                zbb = [P32[11], P32[12]]
                for z in zb + zc:
                    kb.op(POOL, lambda z=z, o=o: nc.gpsimd.memset(z[:, o:o + 128], 0.0), writes=[z])
                for ri in range(2):
                    for e in range(2):
                        g = 2 * j + e
                        c0 = 32 * (j % 4) + 16 * e
                        kb.dma(SP, zb[ri][64 * e:64 * e + 64, o + c0:o + c0 + 16], ssm_b.ap()[li, ri, g], writes=[zb[ri]])
                        kb.dma(SP, zc[ri][c0:c0 + 16, o + 64 * e:o + 64 * e + 64], ssm_c.ap()[li, ri, g], writes=[zc[ri]])
                kb.op(DVE, lambda j=j, o=o: nc.vector.tensor_scalar(zbb[0][:, o:o + 128], zb[0][:, o:o + 128], cre[:, j:j + 1], None, ALU.mult),
                      reads=[zb[0], cre], writes=[zbb[0]])
                kb.op(DVE, lambda j=j, o=o: nc.vector.scalar_tensor_tensor(zbb[0][:, o:o + 128], zb[1][:, o:o + 128], ncim[:, j:j + 1], zbb[0][:, o:o + 128],
                                                                      ALU.mult, ALU.add), reads=[zb[1], ncim, zbb[0]], writes=[zbb[0]])
                kb.op(DVE, lambda j=j, o=o: nc.vector.tensor_scalar(zbb[1][:, o:o + 128], zb[1][:, o:o + 128], cre[:, j:j + 1], None, ALU.mult),
                      reads=[zb[1], cre], writes=[zbb[1]])
                kb.op(DVE, lambda j=j, o=o: nc.vector.scalar_tensor_tensor(zbb[1][:, o:o + 128], zb[0][:, o:o + 128], cim[:, j:j + 1], zbb[1][:, o:o + 128],
                                                                      ALU.mult, ALU.add), reads=[zb[0], cim, zbb[1]], writes=[zbb[1]])
                for ri in range(2):
                    bk = kb.bank()
                    kb.op(PE, lambda bk=bk, ri=ri, o=o: nc.tensor.transpose(bk[:, 0:128], zbb[ri][:, o:o + 128], ident[:]),
                          reads=[zbb[ri], ident], writes=[bk])
                    kb.op(PE, lambda bk=bk, ri=ri, o=o: nc.tensor.transpose(bk[:, 128:256], zc[ri][:, o:o + 128], ident[:]),
                          reads=[zc[ri], ident], writes=[bk])
                    kb.op(DVE, lambda bk=bk, ri=ri, j=j: nc.vector.tensor_copy(Bl[ri][:, j, :], bk[:, 0:128]), reads=[bk], writes=[Bl[ri]])
                    if ri == 0:
                        kb.op(DVE, lambda bk=bk, ri=ri, j=j: nc.vector.tensor_copy(Cl[ri][:, j, :], bk[:, 128:256]), reads=[bk], writes=[Cl[ri]])
                    else:
                        kb.op(DVE, lambda bk=bk, ri=ri, j=j: nc.vector.tensor_scalar(Cl[ri][:, j, :], bk[:, 128:256], -1.0, None, ALU.mult), reads=[bk], writes=[Cl[ri]])
            T["Bl"], T["Cl"] = Bl, Cl
            kb.chk("s5bc")
            dsk = kb.sb("dsk", [128, 4], F32)
            bgl = kb.sb("bgl", [128, 4], F32)
            kb.dma(SP, dsk[:], ssm_d.ap()[li], writes=[dsk])
            kb.dma(SP, bgl[:], b_glu.ap()[li], writes=[bgl])
            T["dsk"], T["bgl"] = dsk, bgl
            rho9 = kb.sb("rho9", [128, 16, 9], F32)
            kb.op(DVE, lambda: nc.vector.memset(rho9[:], 0.0), writes=[rho9])
            kb.op(DVE, lambda: nc.vector.tensor_scalar(rho9[:, :, 1:9], V(rho, 0, 128, 0, [(1, 16), (0, 8)]), 1.0, None, ALU.mult),
                  reads=[rho], writes=[rho9])
            T["rho9"] = rho9
            T["car"] = [kb.sb("car_re", [128, 16], F32), kb.sb("car_im", [128, 16], F32)]
            for cbuf in T["car"]:
                kb.op(DVE, lambda cbuf=cbuf: nc.vector.memset(cbuf[:], 0.0), writes=[cbuf])
            S5.append(T)
        kb.chk("s5")

        kb.dead = bool(_os.environ.get("KSKIP_BIAS"))
        rb = kb.sb("rb", [32, 8], F32)
        oh = P32[3]
        kb.dma(SP, rb[:], relb.ap(), writes=[rb])
        kb.dma(SP, oh[0:32, 0:384], oh_bucket.ap(), writes=[oh])
        mk8 = P32[4]
        kb.dma(SP, mk8[0:8, 0:384], msk_ext.ap(), writes=[mk8])
        bk = kb.bank()
        kb.op(PE, lambda: nc.tensor.matmul(bk[0:8, 0:384], rb[:], oh[0:32, 0:384], start=True, stop=True), reads=[rb, oh], writes=[bk])
        bv = P32[5]
        kb.op(DVE, lambda: nc.vector.tensor_tensor(bv[0:8, 0:384], bk[0:8, 0:384], mk8[0:8, 0:384], ALU.add), reads=[bk, mk8], writes=[bv])
        kb.dma(SP, scr.ap(), bv[0:8, 0:384], reads=[bv], writes=[kb._misc])
        aI = kb.sb("aI", [128, 128], F32)
        kb.dma(SP, aI[:], antiI.ap(), writes=[aI])
        Etab = []
        for c in range(4):
            E = kb.sb("Etab%d" % c, [128, 512], F32)
            hank = P32[c % 2]
            kb.dma(SP, hank[:, 0:512], bass.AP(scr, 2 * c * 384, [[1, 128], [384, 2], [1, 256]]), reads=[kb._misc], writes=[hank])
            bk = kb.bank()
            for e in range(2):
                kb.op(PE, lambda e=e, bk=bk, hank=hank: nc.tensor.matmul(bk[:, e * 128:(e + 1) * 128], aI[:], hank[:, e * 256:e * 256 + 128], start=True, stop=True),
                      reads=[aI, hank], writes=[bk])
                kb.op(PE, lambda e=e, bk=bk, hank=hank: nc.tensor.matmul(bk[:, 256 + e * 128:256 + (e + 1) * 128], aI[:], hank[:, e * 256 + 128:e * 256 + 256],
                                                                   start=True, stop=True), reads=[aI, hank], writes=[bk])
            kb.op(DVE, lambda E=E, bk=bk: nc.vector.tensor_copy(E[:], bk[:]), reads=[bk], writes=[E])
            kb.op(ACT, lambda E=E: nc.scalar.activation(E[:], E[:], AF.Exp), reads=[E], writes=[E])
            Etab.append(E)
        EAt = kb.sb("EAt", [128, 64], F32)
        EBt = kb.sb("EBt", [8, 64], F32)
        hk = P32[2]
        kb.dma(SP, hk[:, 0:64], bass.AP(scr, 120, [[1, 128], [384, 8], [1, 8]]), reads=[kb._misc], writes=[hk], slow=True)
        kb.dma(SP, hk[0:8, 64:128], bass.AP(scr, 248, [[1, 8], [384, 8], [1, 8]]), reads=[kb._misc], writes=[hk], slow=True)
        bk = kb.bank()
        kb.op(PE, lambda: nc.tensor.matmul(bk[:, 0:64], aI[:], hk[:, 0:64], start=True, stop=True), reads=[aI, hk], writes=[bk])
        kb.op(PE, lambda: nc.tensor.matmul(bk[0:8, 64:128], aI[0:8, 120:128], hk[0:8, 64:128], start=True, stop=True), reads=[aI, hk], writes=[bk])
        kb.op(DVE, lambda: nc.vector.tensor_copy(EAt[:], bk[:, 0:64]), reads=[bk], writes=[EAt])
        kb.op(ACT, lambda: nc.scalar.activation(EAt[:], EAt[:], AF.Exp), reads=[EAt], writes=[EAt])
        kb.op(DVE, lambda: nc.vector.tensor_copy(EBt[:], bk[0:8, 64:128]), reads=[bk], writes=[EBt])
        kb.op(ACT, lambda: nc.scalar.activation(EBt[:], EBt[:], AF.Exp), reads=[EBt], writes=[EBt])
        EsT = []
        for li in range(2):
            sk = kb.sb("sk", [128, 4], F32)
            kb.dma(SP, sk[:], sinks.ap()[li], writes=[sk])
            kb.op(ACT, lambda sk=sk: nc.scalar.activation(sk[:], sk[:], AF.Exp), reads=[sk], writes=[sk])
            est = sk
            EsT.append(est)

        kb.dead = False
        kb.chk("bias")
        kb.dead = bool(_os.environ.get("KSKIP_STATE"))
        NVB = 5
        ATT = []
        for li in range(2):
            A = {}
            A["KK"] = [kb.sb("KK%d" % k, [128, 128 + 512], BF16) for k in range(2)]
            A["Vz"] = [kb.sb("Vz%d" % i, [128, 320], BF16) for i in range(NVB)]
            for vz in A["Vz"]:
                kb.op(POOL, lambda vz=vz: nc.gpsimd.memset(vz[:], 0.0), writes=[vz])
            A["kT32"] = kb.sb("kT32", [128, 128], F32)
            A["v32"] = kb.sb("v32", [128, 128], F32)
            ATT.append(A)
        CONV = []
        for li in range(2):
            C = {}
            C["halo"] = kb.sb("chalo", [128, 8, 30], F32)
            kb.op(POOL, lambda b=C["halo"]: nc.gpsimd.memset(b[:], 0.0), writes=[C["halo"]])
            C["wdw"] = kb.sb("wdw", [128, 8, 31], F32)
            kb.dma(SP, C["wdw"][:], w_dw.ap()[li], writes=[C["wdw"]])
            C["cv"] = kb.sb("cv", [128, 3, 8], F32)
            kb.dma(SP, C["cv"][:], cvec.ap()[li], writes=[C["cv"]])
            CONV.append(C)
        FFN = []
        for l in range(4):
            Fd = {}
            Fd["halo"] = kb.sb("fhalo", [128, NJ, 2], F32)
            kb.op(POOL, lambda b=Fd["halo"]: nc.gpsimd.memset(b[:], 0.0), writes=[Fd["halo"]])
            Fd["cw"] = kb.sb("fcw", [128, NJ, 3], F32)
            Fd["cb"] = kb.sb("fcb", [128, NJ], F32)
            kb.dma(SP, Fd["cw"][:], f_cw.ap()[l], writes=[Fd["cw"]])
            kb.dma(SP, Fd["cb"][:], f_cb.ap()[l], writes=[Fd["cb"]])
            FFN.append(Fd)

        kb.dead = False
        MIX = P16[16:22] + [kb.sb("MIX%d" % c, [128, 512], BF16) for c in range(2)]
        XR = [[P32[0], P32[1]], [P32[2], P32[3]]]
        GG = [[P32[4], P32[5]], [P32[6], P32[7]]]
        tA = [P32[8], P32[9]]
        tB = [P32[10], P32[11]]
        YS = [P32[12], P32[13]]
        HB = [[P16[0], P16[1], P16[2], P16[3]], [P16[4], P16[5], P16[6], P16[7]]]
        GEL = P16[8:12]
        cfx = [kb.sb("cfx%d" % i, [128, 2], F32) for i in range(4)]
        sso = [kb.sb("sso%d" % i, [128, NSEQ], F32) for i in range(2)]
        m9 = kb.sb("m9", [128, NSEQ * 9], F32)
        PT = P16[12:16]
        EX = [P32[0], P32[1]]
        dn_t = P32[2]
        YF = P16
        GR = [P32[0], P32[1], P32[2]]
        ACC = [P32[3], P32[4], P32[5]]
        SG = [P32[12], P32[13]]
        LNY = P32[0:8]
        GLW = [P32[8], P32[9]]
        SQF = [P32[10], P32[11]]
        mu_t = P32[12]
        LNS = P16[0:8]
        rr = {"t": 0, "pt": 0, "ex": 0, "gr": 0, "acc": 0, "sg": 0, "ys": 0}

        def nxt(lst, key):
            b = lst[rr[key] % len(lst)]
            rr[key] += 1
            return b

        def s5_core(T, W, nrep, L, sample_h0=None, ssm_out=None):
            def v3(b):
                return V(b, 0, 128, 0, [(L, nrep), (1, L)])
            for m in range(4):
                bky = kb.bank()
                cpairs = []
                for half in range(2):
                    for q in range(2):
                        jj = 2 * half + q
                        j = 4 * m + jj
                        bre, bim = kb.bank(), kb.bank()
                        for ri, bkk in ((0, bre), (1, bim)):
                            mm_group(bkk[:, 0:W], [(T["Bl"][ri][:, j, :], UT[m][:, 0:W], [T["Bl"][ri], UT[m]])], bkk, [])
                        cv = V(T["ctab"], 0, 128, j * 64, [(0, nrep), (1, L)])
                        sv = V(T["stab"], 0, 128, j * 64, [(0, nrep), (1, L)])
                        a1, a2 = tA[0], tB[0]
                        kb.op(DVE, lambda: nc.vector.tensor_tensor(v3(a1), v3(bre), cv, ALU.mult), reads=[bre, T["ctab"]], writes=[a1])
                        kb.op(DVE, lambda: nc.vector.tensor_tensor(v3(a2), v3(bim), sv, ALU.mult), reads=[bim, T["stab"]], writes=[a2])
                        kb.op(DVE, lambda: nc.vector.tensor_tensor(XR[0][q][:, 0:W], a1[:, 0:W], a2[:, 0:W], ALU.add),
                              reads=[a1, a2], writes=[XR[0][q]])
                        a1, a2 = tA[1], tB[1]
                        kb.op(DVE, lambda: nc.vector.tensor_tensor(v3(a1), v3(bim), cv, ALU.mult), reads=[bim, T["ctab"]], writes=[a1])
                        kb.op(DVE, lambda: nc.vector.tensor_tensor(v3(a2), v3(bre), sv, ALU.mult), reads=[bre, T["stab"]], writes=[a2])
                        kb.op(DVE, lambda: nc.vector.tensor_tensor(XR[1][q][:, 0:W], a1[:, 0:W], a2[:, 0:W], ALU.subtract),
                              reads=[a1, a2], writes=[XR[1][q]])
                    j0 = 4 * m + 2 * half
                    if sample_h0 is None:
                        nseg = W // 64
                        for sgi in range(nseg):
                            for q in range(2):
                                j = j0 + q
                                for ri in range(2):
                                    kb.op(DVE, lambda q=q, j=j, ri=ri, sgi=sgi: nc.vector.tensor_tensor_scan(
                                        GG[ri][q][:, sgi * 64:(sgi + 1) * 64], V(T["rho"], 0, 128, j, [(0, 64)]),
                                        XR[ri][q][:, sgi * 64:(sgi + 1) * 64], T["car"][ri][:, j:j + 1], ALU.mult, ALU.add),
                                        reads=[T["rho"], XR[ri][q], T["car"][ri]], writes=[GG[ri][q]])
                            col = sgi * 64 + 63
                            for ri in range(2):
                                for q in range(2):
                                    kb.op(DVE, lambda ri=ri, q=q: nc.vector.tensor_copy(cfx[ri][:, q:q + 1], GG[ri][q][:, col:col + 1]),
                                          reads=[GG[ri][q]], writes=[cfx[ri]])
                            cc = V(T["ctab"], 0, 128, j0 * 64 + 63, [(64, 2)])
                            ss = V(T["stab"], 0, 128, j0 * 64 + 63, [(64, 2)])
                            kb.op(DVE, lambda: nc.vector.tensor_tensor(cfx[2][:], cfx[0][:], cc, ALU.mult), reads=[cfx[0], T["ctab"]], writes=[cfx[2]])
                            kb.op(DVE, lambda: nc.vector.tensor_tensor(cfx[3][:], cfx[1][:], ss, ALU.mult), reads=[cfx[1], T["stab"]], writes=[cfx[3]])
                            kb.op(DVE, lambda: nc.vector.tensor_tensor(T["car"][0][:, j0:j0 + 2], cfx[2][:], cfx[3][:], ALU.subtract),
                                  reads=[cfx[2], cfx[3]], writes=[T["car"][0]])
                            kb.op(DVE, lambda: nc.vector.tensor_tensor(cfx[2][:], cfx[0][:], ss, ALU.mult), reads=[cfx[0], T["stab"]], writes=[cfx[2]])
                            kb.op(DVE, lambda: nc.vector.tensor_tensor(cfx[3][:], cfx[1][:], cc, ALU.mult), reads=[cfx[1], T["ctab"]], writes=[cfx[3]])
                            kb.op(DVE, lambda: nc.vector.tensor_tensor(T["car"][1][:, j0:j0 + 2], cfx[2][:], cfx[3][:], ALU.add),
                                  reads=[cfx[2], cfx[3]], writes=[T["car"][1]])
                    else:
                        for q in range(2):
                            j = j0 + q
                            for ri in range(2):
                                x9, g9 = tA[ri], tB[ri]
                                kb.dma(SP, V(x9, 0, 128, 0, [(9, NSEQ)]), st_ssm.ap()[sample_h0, ri][:, j, :], writes=[x9], slow=True)
                                kb.op(DVE, lambda: nc.vector.tensor_copy(V(x9, 0, 128, 1, [(9, NSEQ), (1, 8)]),
                                                                         V(XR[ri][q], 0, 128, 0, [(8, NSEQ), (1, 8)])),
                                      reads=[XR[ri][q]], writes=[x9])
                                if ri == 0:
                                    kb.op(DVE, lambda: nc.vector.tensor_copy(V(m9, 0, 128, 0, [(9, NSEQ), (1, 9)]), V(T["rho9"], 0, 128, j * 9, [(0, NSEQ), (1, 9)])),
                                          reads=[T["rho9"]], writes=[m9])
                                kb.op(DVE, lambda: nc.vector.tensor_tensor_scan(g9[:, 0:NSEQ * 9], m9[:, 0:NSEQ * 9],
                                                                                x9[:, 0:NSEQ * 9], 0.0, ALU.mult, ALU.add),
                                      reads=[m9, x9], writes=[g9])
                                kb.op(DVE, lambda: nc.vector.tensor_copy(V(GG[ri][q], 0, 128, 0, [(8, NSEQ), (1, 8)]),
                                                                         V(g9, 0, 128, 1, [(9, NSEQ), (1, 8)])),
                                      reads=[g9], writes=[GG[ri][q]])
                    for q in range(2):
                        jj = 2 * half + q
                        j = j0 + q
                        cv = V(T["ctab"], 0, 128, j * 64, [(0, nrep), (1, L)])
                        sv = V(T["stab"], 0, 128, j * 64, [(0, nrep), (1, L)])
                        a1, a2 = tA[0], tB[0]
                        kb.op(POOL, lambda: nc.gpsimd.tensor_tensor(v3(a1), v3(GG[0][q]), cv, ALU.mult), reads=[GG[0][q], T["ctab"]], writes=[a1])
                        kb.op(POOL, lambda: nc.gpsimd.tensor_tensor(v3(a2), v3(GG[1][q]), sv, ALU.mult), reads=[GG[1][q], T["stab"]], writes=[a2])
                        kb.op(POOL, lambda: nc.gpsimd.tensor_tensor(HB[0][jj][:, 0:W], a1[:, 0:W], a2[:, 0:W], ALU.subtract),
                              reads=[a1, a2], writes=[HB[0][jj]])
                        if sample_h0 is not None:
                            kb.op(POOL, lambda: nc.gpsimd.tensor_tensor(sso[0][:, :], V(a1, 0, 128, 7, [(8, NSEQ)]), V(a2, 0, 128, 7, [(8, NSEQ)]), ALU.subtract),
                                  reads=[a1, a2], writes=[sso[0]])
                            kb.dma(SP, o_ssm_s.ap()[sample_h0, 0][:, j, :], sso[0][:, :], reads=[sso[0]], is_out=True)
                        a1, a2 = tA[1], tB[1]
                        kb.op(POOL, lambda: nc.gpsimd.tensor_tensor(v3(a1), v3(GG[0][q]), sv, ALU.mult), reads=[GG[0][q], T["stab"]], writes=[a1])
                        kb.op(POOL, lambda: nc.gpsimd.tensor_tensor(v3(a2), v3(GG[1][q]), cv, ALU.mult), reads=[GG[1][q], T["ctab"]], writes=[a2])
                        kb.op(POOL, lambda: nc.gpsimd.tensor_tensor(HB[1][jj][:, 0:W], a1[:, 0:W], a2[:, 0:W], ALU.add),
                              reads=[a1, a2], writes=[HB[1][jj]])
                        if sample_h0 is not None:
                            kb.op(POOL, lambda: nc.gpsimd.tensor_tensor(sso[1][:, :], V(a1, 0, 128, 7, [(8, NSEQ)]), V(a2, 0, 128, 7, [(8, NSEQ)]), ALU.add),
                                  reads=[a1, a2], writes=[sso[1]])
                            kb.dma(SP, o_ssm_s.ap()[sample_h0, 1][:, j, :], sso[1][:, :], reads=[sso[1]], is_out=True)
                        for ri in range(2):
                            cpairs.append((T["Cl"][ri][:, j, :], HB[ri][jj][:, 0:W], [T["Cl"][ri], HB[ri][jj]]))
                mm_group(bky[:, 0:W], cpairs, bky, [])
                ys = U32[m]
                kb.op(DVE, lambda: nc.vector.scalar_tensor_tensor(ys[:, 0:W], U32[m][:, 0:W], T["dsk"][:, m:m + 1], bky[:, 0:W], ALU.mult, ALU.add),
                      reads=[U32[m], T["dsk"], bky], writes=[ys])
                kb.op(ACT, lambda: nc.scalar.activation(U32[m][:, 0:W], ys[:, 0:W], AF.Gelu_apprx_tanh), reads=[ys], writes=[U32[m]])
                kb.op(ACT, lambda: nc.scalar.copy(GEL[m][:, 0:W], U32[m][:, 0:W]), reads=[U32[m]], writes=[GEL[m]])
            for n in range(4):
                wb = load_w(w_glu.ap()[li_cur[0], n], 4, 128)
                bk = kb.bank()
                mm_group(bk[:, 0:W], [(wb[:, k, 0:128], GEL[k][:, 0:W], [wb, GEL[k]]) for k in range(4)], bk, [])
                sg = SG[n % 2]
                kb.op(DVE, lambda: nc.vector.tensor_scalar(sg[:, 0:W], bk[:, 0:W], T["bgl"][:, n:n + 1], None, ALU.add), reads=[bk, T["bgl"]], writes=[sg])
                kb.op(ACT, lambda: nc.scalar.activation(sg[:, 0:W], sg[:, 0:W], AF.Sigmoid), reads=[sg], writes=[sg])
                kb.op(DVE, lambda: nc.vector.tensor_tensor(MIX[n][:, 0:W], U32[n][:, 0:W], sg[:, 0:W], ALU.mult),
                      reads=[U32[n], sg], writes=[MIX[n]])

        li_cur = [0]

        def in_proj(li, W, want_v_tok_blocks):
            for n in range(4):
                wb = load_w(w_in.ap()[li, n], 8, 128)
                if n == 0:
                    kb.chk("ip_w")
                bk = kb.bank()
                _mv = _os.environ.get("KMM", "")
                if _mv == "k":
                    mm_group(bk[:, 0:W], [(ones_bf[:], H[k][:, 0:W], [ones_bf, H[k]]) for k in range(8)], bk, [])
                elif _mv == "m":
                    for k in range(8):
                        kb.op(DVE, lambda k=k: nc.vector.tensor_copy(H[k][:, 0:W], X[k][:, 0:W]), reads=[X[k]], writes=[H[k]])
                    mm_group(bk[:, 0:W], [(wb[:, k, 0:128], H[k][:, 0:W], [wb, H[k]]) for k in range(8)], bk, [])
                elif _mv == "q":
                    for k in range(8):
                        kb.op(DVE, lambda k=k: nc.vector.tensor_copy(P16[k][:, 0:W], X[k][:, 0:W]), reads=[X[k]], writes=[P16[k]])
                    mm_group(bk[:, 0:W], [(wb[:, k, 0:128], P16[k][:, 0:W], [wb, P16[k]]) for k in range(8)], bk, [])
                elif _mv == "n":
                    for k in range(8):
                        kb.op(ACT, lambda k=k: nc.scalar.copy(H[k][:, 0:W], X[k][:, 0:W]), reads=[X[k]], writes=[H[k]])
                    mm_group(bk[:, 0:W], [(wb[:, k, 0:128], H[k][:, 0:W], [wb, H[k]]) for k in range(8)], bk, [])
                elif _mv == "l":
                    mm_group(bk[:, 0:W], [(wb[:, k, 0:128], sq[k][:, 0:W], [wb, sq[k]]) for k in range(8)], bk, [])
                else:
                    mm_group(bk[:, 0:W], [(wb[:, k, 0:128], H[k][:, 0:W], [wb, H[k]]) for k in range(8)], bk, [])
                if n == 0:
                    kb.chk("ip_mm")
                kb.op(DVE, lambda: nc.vector.tensor_copy(UT[n][:, 0:W], bk[:, 0:W]), reads=[bk], writes=[UT[n]])
                if n == 0:
                    kb.chk("ip_act")
                import os
                tv = os.environ.get("KVAR", "")
                if tv == "a":
                    kb.op(DVE, lambda: nc.vector.tensor_copy(P32[5][:, 0:W], bk[:, 0:W]), reads=[bk], writes=[P32[5]])
                elif tv == "c":
                    kb.op(DVE, lambda: nc.vector.tensor_scalar(U32[n][:, 0:W], bk[:, 0:W], 1.0, None, ALU.mult), reads=[bk], writes=[U32[n]])
                elif tv == "d":
                    kb.op(DVE, lambda: nc.vector.tensor_copy(U32[n][:, 0:W], bk[:, 0:W]), reads=[bk], writes=[U32[n]])
                elif tv == "g":
                    ob = kb.banks[(kb.psum_rr - 2) % 8]
                    kb.op(DVE, lambda: nc.vector.tensor_copy(P32[5][:, 0:W], ob[:, 0:W]), reads=[ob], writes=[P32[5]])
                elif tv == "g2":
                    ob = kb.banks[(kb.psum_rr - 2) % 8]
                    kb.op(DVE, lambda: nc.vector.tensor_copy(P32[5][:, 0:W], ob[:, 0:W]), reads=[ob, bk], writes=[P32[5]])
                elif tv == "j":
                    ob = kb.banks[(kb.psum_rr - 2) % 8]
                    kb.op(PE, lambda: nc.tensor.matmul(ob[:, 0:W], ones_bf[:], H[0][:, 0:W], start=True, stop=True), reads=[ones_bf, H[0]], writes=[ob])
                    kb.op(DVE, lambda: nc.vector.tensor_copy(U32[n][:, 0:W], bk[:, 0:W]), reads=[bk], writes=[U32[n]])
                elif tv == "h":
                    kb.op(DVE, lambda: nc.vector.tensor_copy(P32[5][0:64, 0:W], bk[0:64, 0:W]), reads=[bk], writes=[P32[5]])
                elif tv == "i":
                    kb.op(DVE, lambda: nc.vector.tensor_copy(P32[5][:, 0:64], bk[:, 0:64]), reads=[bk], writes=[P32[5]])
                elif tv == "e":
                    kb.op(DVE, lambda: nc.vector.tensor_copy(U32[n][:, 0:W], P32[6][:, 0:W]), reads=[P32[6]], writes=[U32[n]])
                elif tv == "f":
                    kb.op(DVE, lambda: nc.vector.tensor_copy(P32[5][:, 0:W], X[0][:, 0:W]), reads=[X[0]], writes=[P32[5]])
                elif tv == "b":
                    kb.op(DVE, lambda: nc.vector.tensor_copy(U32[n][:, 0:W], X[0][:, 0:W]), reads=[X[0]], writes=[U32[n]])
                else:
                    kb.op(DVE, lambda: nc.vector.tensor_copy(U32[n][:, 0:W], bk[:, 0:W]), reads=[bk], writes=[U32[n]])
                if n == 0:
                    kb.chk("ip_u0")
            kb.chk("ip_u")
            for n in range(4):
                wb = load_w(w_in.ap()[li, 4 + n], 8, 128)
                bk = kb.bank()
                mm_group(bk[:, 0:W], [(wb[:, k, 0:128], H[k][:, 0:W], [wb, H[k]]) for k in range(8)], bk, [])
                kb.op(DVE, lambda: nc.vector.tensor_copy(QT[0][n][0:64, 0:W], bk[0:64, 0:W]), reads=[bk], writes=[QT[0][n]])
                kb.op(DVE, lambda: nc.vector.tensor_copy(QT[1][n][64:128, 0:W], bk[64:128, 0:W]), reads=[bk], writes=[QT[1][n]])

        def attn_prompt(li, W, blk0):
            A = ATT[li]
            nb = W // 128
            for kv in range(2):
                wb = load_w(w_in.ap()[li, 8 + kv], 8, 128)
                bk = kb.bank()
                mm_group(bk[:, 0:W], [(wb[:, k, 0:128], H[k][:, 0:W], [wb, H[k]]) for k in range(8)], bk, [])
                kb.op(DVE, lambda: nc.vector.tensor_copy(A["KK"][kv][:, 128:128 + W], bk[:, 0:W]), reads=[bk], writes=[A["KK"][kv]])
                if kv == 0:
                    kb.op(DVE, lambda: nc.vector.tensor_copy(A["kT32"][0:64, :], bk[0:64, W - 128:W]), reads=[bk], writes=[A["kT32"]])
                else:
                    kb.op(DVE, lambda: nc.vector.tensor_copy(A["kT32"][64:128, :], bk[64:128, W - 128:W]), reads=[bk], writes=[A["kT32"]])
            kb.chk("at_k")
            wv = load_w(w_in.ap()[li, 10], 8, 128)
            for b in range(nb):
                vz = A["Vz"][(blk0 + b) % NVB]
                bk = kb.bank()
                mm_group(bk[:, 0:128], [(H[k][:, b * 128:(b + 1) * 128], wv[:, k, 0:128], [H[k], wv]) for k in range(8)], bk, [])
                kb.op(DVE, lambda: nc.vector.tensor_copy(vz[:, 64:128], bk[:, 0:64]), reads=[bk], writes=[vz])
                kb.op(DVE, lambda: nc.vector.tensor_copy(vz[:, 192:256], bk[:, 64:128]), reads=[bk], writes=[vz])
                if b == nb - 1:
                    kb.op(DVE, lambda: nc.vector.tensor_copy(A["v32"][:, :], bk[:, 0:128]), reads=[bk], writes=[A["v32"]])
            kb.chk("at_v")
            for b in range(nb):
                gb = blk0 + b
                has_prev = gb >= 1
                q0 = b * 128
                pts = []
                for c in range(4):
                    kv = c // 2
                    bk = kb.bank()
                    for e in range(2):
                        kb.op(PE, lambda e=e, bk=bk: nc.tensor.matmul(bk[:, e * 128:(e + 1) * 128],
                                                                      A["KK"][kv][:, 128 + q0:256 + q0],
                                                                      QT[e][c][:, q0:q0 + 128], start=True, stop=True),
                              reads=[A["KK"][kv], QT[e][c]], writes=[bk], inc=(e == 1 and not has_prev))
                    if has_prev:
                        for e in range(2):
                            kb.op(PE, lambda e=e, bk=bk: nc.tensor.matmul(bk[:, 256 + e * 128:256 + (e + 1) * 128],
                                                                          A["KK"][kv][:, q0:q0 + 128],
                                                                          QT[e][c][:, q0:q0 + 128], start=True, stop=True),
                                  reads=[A["KK"][kv], QT[e][c]], writes=[bk], inc=(e == 1))
                    ncol = 512 if has_prev else 256
                    ex = EX[c % 2]
                    kb.chk("at_mm")
                    kb.op(DVE, lambda bk=bk, ex=ex: nc.vector.tensor_copy(ex[:, 0:ncol], bk[:, 0:ncol]), reads=[bk], writes=[ex])
                    kb.chk("at_cp")
                    kb.op(ACT, lambda ex=ex: nc.scalar.activation(ex[:, 0:ncol], ex[:, 0:ncol], AF.Exp, scale=0.125), reads=[ex], writes=[ex])
                    kb.chk("at_ex")
                    pt = PT[c]
                    kb.op(DVE, lambda ex=ex, pt=pt, c=c: nc.vector.tensor_tensor(pt[:, 0:ncol], ex[:, 0:ncol], Etab[c][:, 0:ncol], ALU.mult),
                          reads=[ex, Etab[c]], writes=[pt])
                    pts.append(pt)
                kb.chk("at_s")
                bn, bd = kb.bank(), kb.bank()
                vcur = A["Vz"][gb % NVB]
                vprev = A["Vz"][(gb - 1) % NVB]
                ocur = Oz0 if gb == 0 else Oz
                oprev = Oz0 if gb == 1 else Oz
                for c in range(4):
                    kv = c // 2
                    npairs, dpairs = [], []
                    for e in range(2):
                        vs = (64 + 128 * kv, 192 + 128 * kv) if e == 0 else (128 * kv, 128 + 128 * kv)
                        osl = (64, 192) if e == 0 else (0, 128)
                        npairs.append((vcur[:, vs[0]:vs[1]], pts[c][:, e * 128:(e + 1) * 128], [vcur, pts[c]]))
                        dpairs.append((ocur[:, osl[0]:osl[1]], pts[c][:, e * 128:(e + 1) * 128], [ocur, pts[c]]))
                        if has_prev:
                            npairs.append((vprev[:, vs[0]:vs[1]], pts[c][:, 256 + e * 128:256 + (e + 1) * 128], [vprev, pts[c]]))
                            dpairs.append((oprev[:, osl[0]:osl[1]], pts[c][:, 256 + e * 128:256 + (e + 1) * 128], [oprev, pts[c]]))
                    mm_group(bn[:, c * 128:(c + 1) * 128], npairs, bn, [])
                    mm_group(bd[:, c * 128:(c + 1) * 128], dpairs, bd, [])
                kb.chk("at_pv")
                for c in range(4):
                    kb.op(DVE, lambda bd=bd, c=c: nc.vector.tensor_scalar(dn_t[:, c * 128:(c + 1) * 128], bd[:, c * 128:(c + 1) * 128], EsT[li][:, c:c + 1], None, ALU.add),
                          reads=[bd, EsT[li]], writes=[dn_t])
                kb.op(DVE, lambda: nc.vector.reciprocal(dn_t[:, 0:512], dn_t[:, 0:512]), reads=[dn_t], writes=[dn_t])
                for c in range(4):
                    kb.op(DVE, lambda c=c, bn=bn: nc.vector.tensor_tensor(MIX[4 + c][:, q0:q0 + 128], bn[:, c * 128:(c + 1) * 128],
                                                                          dn_t[:, c * 128:(c + 1) * 128], ALU.mult),
                          reads=[bn, dn_t], writes=[MIX[4 + c]])
            for kv in range(2):
                kb.op(ACT, lambda kv=kv: nc.scalar.copy(A["KK"][kv][:, 0:128], A["KK"][kv][:, W:W + 128]),
                      reads=[A["KK"][kv]], writes=[A["KK"][kv]])

        def out_proj(dram_w, W, nk, src, after=None):
            wb_next = load_w(dram_w[0], nk, 128)
            for n in range(8):
                wb = wb_next
                if n + 1 < 8:
                    wb_next = load_w(dram_w[n + 1], nk, 128)
                bk = kb.bank()
                mm_group(bk[:, 0:W], [(wb[:, k, 0:128], src[k][:, 0:W], [wb, src[k]]) for k in range(nk)], bk, [])
                kb.op(DVE, lambda n=n, bk=bk: nc.vector.tensor_tensor(X[n][:, 0:W], X[n][:, 0:W], bk[:, 0:W], ALU.add),
                      reads=[X[n], bk], writes=[X[n]])

        def conformer(lo, W, nseq, Tt, halo, zero_pad_cols=0):
            C = CONV[lo]
            ext = 30 + Tt
            wb_next = load_w(w_pw1.ap()[lo, 0], 8, 256)
            for c in range(8):
                wb = wb_next
                if c + 1 < 8:
                    wb_next = load_w(w_pw1.ap()[lo, c + 1], 8, 256)
                ba, bg = kb.bank(), kb.bank()
                mm_group(ba[:, 0:W], [(wb[:, k, 0:128], H[k][:, 0:W], [wb, H[k]]) for k in range(8)], ba, [])
                mm_group(bg[:, 0:W], [(wb[:, k, 128:256], H[k][:, 0:W], [wb, H[k]]) for k in range(8)], bg, [])
                sg = SG[c % 2]
                gl = GLW[c % 2]
                kb.op(DVE, lambda: nc.vector.tensor_copy(sg[:, 0:W], bg[:, 0:W]), reads=[bg], writes=[sg])
                kb.op(ACT, lambda: nc.scalar.activation(sg[:, 0:W], sg[:, 0:W], AF.Sigmoid), reads=[sg], writes=[sg])
                if halo is not None:
                    kb.op(ACT, lambda: nc.scalar.copy(V(gl, 0, 128, 0, [(ext, nseq), (1, 30)]), V(halo, 0, 128, c * nseq * 30, [(30, nseq), (1, 30)])),
                          reads=[halo], writes=[gl])
                else:
                    kb.dma(SP, V(gl, 0, 128, 0, [(ext, nseq), (1, 30)]), st_conv.ap()[lo, c * 128:(c + 1) * 128, :, :], writes=[gl])
                kb.op(DVE, lambda: nc.vector.tensor_tensor(V(gl, 0, 128, 30, [(ext, nseq), (1, Tt)]),
                                                           V(ba, 0, 128, 0, [(Tt, nseq), (1, Tt)]),
                                                           V(sg, 0, 128, 0, [(Tt, nseq), (1, Tt)]), ALU.mult),
                      reads=[ba, sg], writes=[gl])
                if halo is not None:
                    kb.op(ACT, lambda: nc.scalar.copy(V(halo, 0, 128, c * nseq * 30, [(30, nseq), (1, 30)]), V(gl, 0, 128, Tt, [(ext, nseq), (1, 30)])),
                          reads=[gl], writes=[halo])
                else:
                    kb.dma(SP, o_conv_s.ap()[lo, c * 128:(c + 1) * 128, :, :], V(gl, 0, 128, Tt, [(ext, nseq), (1, 30)]), reads=[gl], is_out=True)
                eng_e, eng = (DVE, nc.vector)
                y = LNY[c]
                yv = V(y, 0, 128, 0, [(Tt, nseq), (1, Tt)])
                kb.op(eng_e, lambda: eng.tensor_scalar(yv, V(gl, 0, 128, 0, [(ext, nseq), (1, Tt)]), C["wdw"][:, c, 0:1], C["cv"][:, 0, c:c + 1],
                                                       ALU.mult, ALU.add), reads=[gl, C["wdw"], C["cv"]], writes=[y])
                for k in range(1, 31):
                    kb.op(eng_e, lambda k=k: eng.scalar_tensor_tensor(yv, V(gl, 0, 128, k, [(ext, nseq), (1, Tt)]), C["wdw"][:, c, k:k + 1], yv,
                                                                      ALU.mult, ALU.add), reads=[gl, C["wdw"], y], writes=[y])
            bm, b2 = kb.bank(), kb.bank()
            mm_group(bm[:, 0:W], [(ones_f[:], LNY[c][:, 0:W], [LNY[c]]) for c in range(8)], bm, [ones_f])
            kb.op(DVE, lambda: nc.vector.tensor_scalar(mu_t[:, 0:W], bm[:, 0:W], 1.0 / D, None, ALU.mult), reads=[bm], writes=[mu_t])
            for c in range(8):
                kb.op(DVE, lambda c=c: nc.vector.tensor_tensor(LNY[c][:, 0:W], LNY[c][:, 0:W], mu_t[:, 0:W], ALU.subtract),
                      reads=[LNY[c], mu_t], writes=[LNY[c]])
                sqf = SQF[c % 2]
                kb.op(ACT, lambda c=c, sqf=sqf: nc.scalar.activation(sqf[:, 0:W], LNY[c][:, 0:W], AF.Square), reads=[LNY[c]], writes=[sqf])
                kb.op(PE, lambda c=c, sqf=sqf: nc.tensor.matmul(b2[:, 0:W], ones_f[:], sqf[:, 0:W], start=(c == 0), stop=(c == 7)),
                      reads=[ones_f, sqf], writes=[b2])
            kb.op(DVE, lambda: nc.vector.tensor_scalar(rs[:, 0:W], b2[:, 0:W], 1.0 / D, EPS, ALU.mult, ALU.add), reads=[b2], writes=[rs])
            kb.op(ACT, lambda: nc.scalar.activation(rs[:, 0:W], rs[:, 0:W], AF.Sqrt), reads=[rs], writes=[rs])
            kb.op(DVE, lambda: nc.vector.reciprocal(rinv[:, 0:W], rs[:, 0:W]), reads=[rs], writes=[rinv])
            for c in range(8):
                kb.op(DVE, lambda c=c: nc.vector.scalar_tensor_tensor(LNY[c][:, 0:W], LNY[c][:, 0:W], C["cv"][:, 1, c:c + 1], rinv[:, 0:W],
                                                                      ALU.mult, ALU.mult), reads=[LNY[c], C["cv"], rinv], writes=[LNY[c]])
                kb.op(ACT, lambda c=c: nc.scalar.activation(LNY[c][:, 0:W], LNY[c][:, 0:W], AF.Silu, bias=C["cv"][:, 2, c:c + 1], scale=1.0),
                      reads=[LNY[c], C["cv"]], writes=[LNY[c]])
                kb.op(DVE, lambda c=c: nc.vector.tensor_copy(LNS[c][:, 0:W], LNY[c][:, 0:W]), reads=[LNY[c]], writes=[LNS[c]])
            out_proj(w_pw2.ap()[lo], W, 8, LNS)
            if zero_pad_cols:
                for c in range(8):
                    kb.op(DVE, lambda c=c: nc.vector.memset(X[c][:, 0:zero_pad_cols], 0.0), writes=[X[c]])

        ones_f = kb.sb("ones_f", [128, 128], F32)
        kb.op(DVE, lambda: nc.vector.memset(ones_f[:], 1.0), writes=[ones_f])

        def ffn_down(l, W):
            for n in range(8):
                wd0 = load_wdn(w_dn.ap()[l, n, 0])
                wd1 = load_wdn(w_dn.ap()[l, n, 1])
                bk = kb.bank()
                mm_group(bk[:, 0:W], [((wd0 if j < 11 else wd1)[:, j % 11, :], YF[j][:, 0:W], [wd0 if j < 11 else wd1, YF[j]]) for j in range(NJ)], bk, [])
                kb.op(DVE, lambda n=n, bk=bk: nc.vector.tensor_tensor(X[n][:, 0:W], X[n][:, 0:W], bk[:, 0:W], ALU.add),
                      reads=[X[n], bk], writes=[X[n]])

        def ffn_up(l, W, nseq, Tt, halo):
            Fd = FFN[l]
            ext = 2 + Tt
            wb_next = load_w(w_up.ap()[l, 0], 8, 256)
            for j in range(NJ):
                wb = wb_next
                if j + 1 < NJ:
                    wb_next = load_w(w_up.ap()[l, j + 1], 8, 256)
                bg, bu = kb.bank(), kb.bank()
                mm_group(bg[:, 0:W], [(wb[:, k, 0:128], H[k][:, 0:W], [wb, H[k]]) for k in range(8)], bg, [])
                mm_group(bu[:, 0:W], [(wb[:, k, 128:256], H[k][:, 0:W], [wb, H[k]]) for k in range(8)], bu, [])
                gr = GR[j % 3]
                kb.op(DVE, lambda: nc.vector.tensor_copy(V(gr, 0, 128, 2, [(ext, nseq), (1, Tt)]), V(bg, 0, 128, 0, [(Tt, nseq), (1, Tt)])),
                      reads=[bg], writes=[gr])
                if halo is not None:
                    kb.op(ACT, lambda: nc.scalar.copy(V(gr, 0, 128, 0, [(ext, nseq), (1, 2)]), V(halo, 0, 128, j * nseq * 2, [(2, nseq), (1, 2)])),
                          reads=[halo], writes=[gr])
                else:
                    kb.dma(SP, V(gr, 0, 128, 0, [(ext, nseq), (1, 2)]), st_ffn.ap()[l, j * 128:(j + 1) * 128, :, :], writes=[gr], slow=True)
                acc = ACC[j % 3]
                av = V(acc, 0, 128, 0, [(Tt, nseq), (1, Tt)])
                kb.op(DVE, lambda: nc.vector.tensor_scalar(av, V(gr, 0, 128, 0, [(ext, nseq), (1, Tt)]), Fd["cw"][:, j, 0:1], Fd["cb"][:, j:j + 1],
                                                           ALU.mult, ALU.add), reads=[gr, Fd["cw"], Fd["cb"]], writes=[acc])
                for k in (1, 2):
                    kb.op(DVE, lambda k=k: nc.vector.scalar_tensor_tensor(av, V(gr, 0, 128, k, [(ext, nseq), (1, Tt)]), Fd["cw"][:, j, k:k + 1], av,
                                                                          ALU.mult, ALU.add), reads=[gr, Fd["cw"], acc], writes=[acc])
                if halo is not None:
                    kb.op(ACT, lambda: nc.scalar.copy(V(halo, 0, 128, j * nseq * 2, [(2, nseq), (1, 2)]), V(gr, 0, 128, Tt, [(ext, nseq), (1, 2)])),
                          reads=[gr], writes=[halo])
                else:
                    kb.dma(SP, o_ffn_s.ap()[l, j * 128:(j + 1) * 128, :, :], V(gr, 0, 128, Tt, [(ext, nseq), (1, 2)]), reads=[gr], is_out=True, slow=True)
                kb.op(ACT, lambda: nc.scalar.activation(acc[:, 0:W], acc[:, 0:W], AF.Gelu_apprx_tanh), reads=[acc], writes=[acc])
                kb.op(DVE, lambda: nc.vector.tensor_tensor(YF[j][:, 0:W], acc[:, 0:W], bu[:, 0:W], ALU.mult), reads=[acc, bu], writes=[YF[j]])

        chunks = [(0, 128)] + [(128 + 512 * i, 512) for i in range(16)]
        chunks = chunks[:nchunk_prompt]
        for ci, (t0, W) in enumerate(chunks):
            blk0 = t0 // 128
            for c in range(8):
                kb.dma(SP, X[c][:, 0:W], xT_p.ap()[c * 128:(c + 1) * 128, t0:t0 + W], writes=[X[c]])
            for l in range(4):
                rmsnorm(X, l, H, W)
                kb.chk("n0")
                if l % 2 == 0:
                    li = l // 2
                    li_cur[0] = li
                    in_proj(li, W, None)
                    kb.chk("inproj")
                    s5_core(S5[li], W, W // 64, 64)
                    kb.chk("s5c")
                    attn_prompt(li, W, blk0)
                    kb.chk("att")
                    if ci == dbg_ci and li == 0:
                        dump16(MIX, W)
                        dump16(UT + QT[0], W)
                        dump(Etab + Etab, 512)
                    out_proj(w_out.ap()[li], W, 8, MIX)
                    if ci == dbg_ci:
                        dump(X, W)
                else:
                    lo = l // 2
                    conformer(lo, W, 1, W, CONV[lo]["halo"], zero_pad_cols=(NPAD if ci == 0 else 0))
                    if ci == dbg_ci:
                        dump(X, W)
                rmsnorm(X, 4 + l, H, W)
                ffn_up(l, W, 1, W, FFN[l]["halo"])
                ffn_down(l, W)
                if ci == dbg_ci:
                    dump(X, W)
                if ci == 0 and l % 2 == 1:
                    pass
            for c in range(8):
                pass
            YO = [P32[c] for c in range(8)]
            rmsnorm(X, 8, None, W, out_f32=YO)
            for c in range(8):
                kb.dma(SP, yT_p.ap()[c * 128:(c + 1) * 128, t0:t0 + W], YO[c][:, 0:W], reads=[YO[c]], is_out=True)
        if nchunk_prompt == 17:
            for li in range(2):
                for ri in range(2):
                    kb.dma(SP, o_ssm_p.ap()[li, ri], S5[li]["car"][ri][:], reads=[S5[li]["car"][ri]], is_out=True)
                kb.dma(SP, o_kT_p.ap()[li], ATT[li]["kT32"][:], reads=[ATT[li]["kT32"]], is_out=True)
                kb.dma(SP, o_v_p.ap()[li], ATT[li]["v32"][:], reads=[ATT[li]["v32"]], is_out=True)
                for c in range(8):
                    kb.dma(SP, o_conv_p.ap()[li, c * 128:(c + 1) * 128, :], CONV[li]["halo"][:, c, :], reads=[CONV[li]["halo"]], is_out=True)
            for l in range(4):
                kb.dma(SP, o_ffn_p.ap()[l].rearrange("(j p) t -> p j t", p=128), FFN[l]["halo"][:], reads=[FFN[l]["halo"]], is_out=True)

        def attn_sample(li):
            A = ATT[li]
            W = 128
            for kv in range(2):
                wb = load_w(w_in.ap()[li, 8 + kv], 8, 128)
                bk = kb.bank()
                mm_group(bk[:, 0:W], [(wb[:, k, 0:128], H[k][:, 0:W], [wb, H[k]]) for k in range(8)], bk, [])
                kb.op(DVE, lambda: nc.vector.tensor_copy(KKs[kv][:, :], bk[:, 0:W]), reads=[bk], writes=[KKs[kv]])
                kb.op(DVE, lambda: nc.vector.tensor_copy(A["kT32"][64 * kv:64 * kv + 64, :], bk[64 * kv:64 * kv + 64, 0:W]), reads=[bk], writes=[A["kT32"]])
            wv = load_w(w_in.ap()[li, 10], 8, 128)
            bk = kb.bank()
            mm_group(bk[:, 0:128], [(H[k][:, 0:128], wv[:, k, 0:128], [H[k], wv]) for k in range(8)], bk, [])
            kb.op(DVE, lambda: nc.vector.tensor_copy(vbf[:, :], bk[:, 0:128]), reads=[bk], writes=[vbf])
            kb.op(DVE, lambda: nc.vector.tensor_copy(A["v32"][:, :], bk[:, 0:128]), reads=[bk], writes=[A["v32"]])
            kb.dma(SP, o_kT_s.ap()[li][:, :, 0:120], st_kT.ap()[li][:, :, 8:128], is_out=True, slow=True)
            kb.dma(SP, o_v_s.ap()[li][:, 0:120, :], st_v.ap()[li][:, 8:128, :], is_out=True)
            kb.dma(SP, o_kT_s.ap()[li][:, :, 120:128].rearrange("s p t -> p s t"), V(A["kT32"], 0, 128, 0, [(8, NSEQ), (1, 8)]),
                   reads=[A["kT32"]], is_out=True, slow=True)
            for s_ in range(NSEQ):
                kb.dma(SP, o_v_s.ap()[li, s_][120:128, :], A["v32"][8 * s_:8 * s_ + 8, :], reads=[A["v32"]], is_out=True)
            bn, bd = kb.bank(), kb.bank()
            sbanks = [kb.bank() for _ in range(4)]
            for sq_i in range(NSEQ):
                kx = [KX[kv][sq_i % 2] for kv in range(2)]
                vz, vb = VZS[sq_i % 2], VBS[sq_i % 2]
                for kv in range(2):
                    for hf in range(2):
                        kb.dma(POOL, kx[kv][64 * hf:64 * hf + 64, 0:128], st_kT.ap()[li, sq_i][64 * kv:64 * kv + 64, :], writes=[kx[kv]])
                    kb.op(ACT, lambda kv=kv: nc.scalar.copy(kx[kv][:, 128:136], KKs[kv][:, 8 * sq_i:8 * sq_i + 8]), reads=[KKs[kv]], writes=[kx[kv]])
                    kb.dma(POOL, vz[0:120, 64 + 128 * kv:128 + 128 * kv], st_v.ap()[li, sq_i][8:128, 64 * kv:64 * kv + 64], writes=[vz])
                    kb.dma(POOL, vb[0:8, 64 + 128 * kv:128 + 128 * kv], st_v.ap()[li, sq_i][0:8, 64 * kv:64 * kv + 64], writes=[vb])
                    kb.dma(POOL, vz[120:128, 64 + 128 * kv:128 + 128 * kv], vbf[8 * sq_i:8 * sq_i + 8, 64 * kv:64 * kv + 64], reads=[vbf], writes=[vz])
                ba, bb = sbanks[2 * (sq_i % 2)], sbanks[2 * (sq_i % 2) + 1]
                for c in range(4):
                    kv = c // 2
                    for e in range(2):
                        col = c * 16 + e * 8
                        kb.op(PE, lambda: nc.tensor.matmul(ba[:, col:col + 8], kx[kv][:, 8:136],
                                                           QT[e][c][:, 8 * sq_i:8 * sq_i + 8], start=True, stop=True),
                              reads=[kx[kv], QT[e][c]], writes=[ba], inc=False)
                        kb.op(PE, lambda: nc.tensor.matmul(bb[0:8, col:col + 8], kx[kv][:, 0:8],
                                                           QT[e][c][:, 8 * sq_i:8 * sq_i + 8], start=True, stop=True),
                              reads=[kx[kv], QT[e][c]], writes=[bb], inc=(c == 3 and e == 1))
                kb.op(DVE, lambda: nc.vector.tensor_copy(EXA[:, :], ba[:, 0:64]), reads=[ba], writes=[EXA])
                kb.op(ACT, lambda: nc.scalar.activation(EXA[:, :], EXA[:, :], AF.Exp, scale=0.125), reads=[EXA], writes=[EXA])
                kb.op(DVE, lambda: nc.vector.tensor_copy(EXB[:, :], bb[0:8, 0:64]), reads=[bb], writes=[EXB])
                kb.op(ACT, lambda: nc.scalar.activation(EXB[:, :], EXB[:, :], AF.Exp, scale=0.125), reads=[EXB], writes=[EXB])
                pa, pb = PAs[sq_i % 2], PBs[sq_i % 2]
                kb.op(DVE, lambda: nc.vector.tensor_tensor(pa[:, :], EXA[:, :], EAt[:, :], ALU.mult), reads=[EXA, EAt], writes=[pa])
                kb.op(DVE, lambda: nc.vector.tensor_tensor(pb[:, :], EXB[:, :], EBt[:, :], ALU.mult), reads=[EXB, EBt], writes=[pb])
                for c in range(4):
                    kv = c // 2
                    npairs, dpairs = [], []
                    for e in range(2):
                        vs = (64 + 128 * kv, 192 + 128 * kv) if e == 0 else (128 * kv, 128 + 128 * kv)
                        osl = (64, 192) if e == 0 else (0, 128)
                        col = c * 16 + e * 8
                        npairs.append((vz[:, vs[0]:vs[1]], pa[:, col:col + 8], [vz, pa]))
                        npairs.append((vb[0:8, vs[0]:vs[1]], pb[0:8, col:col + 8], [vb, pb]))
                        dpairs.append((Oz[:, osl[0]:osl[1]], pa[:, col:col + 8], [Oz, pa]))
                        dpairs.append((Oz[0:8, osl[0]:osl[1]], pb[0:8, col:col + 8], [Oz, pb]))
                    oc = sq_i * 32 + c * 8
                    mm_group(bn[:, oc:oc + 8], npairs, bn, [])
                    mm_group(bd[:, oc:oc + 8], dpairs, bd, [])
            for c in range(4):
                dv = V(dn_t, 0, 128, c * 8, [(32, NSEQ), (1, 8)])
                kb.op(DVE, lambda: nc.vector.tensor_scalar(dv, V(bd, 0, 128, c * 8, [(32, NSEQ), (1, 8)]), EsT[li][:, c:c + 1], None, ALU.add),
                      reads=[bd, EsT[li]], writes=[dn_t])
            kb.op(DVE, lambda: nc.vector.reciprocal(dn_t[:, 0:512], dn_t[:, 0:512]), reads=[dn_t], writes=[dn_t])
            for c in range(4):
                kb.op(DVE, lambda: nc.vector.tensor_tensor(V(MIX[4 + c], 0, 128, 0, [(8, NSEQ), (1, 8)]), V(bn, 0, 128, c * 8, [(32, NSEQ), (1, 8)]),
                                                           V(dn_t, 0, 128, c * 8, [(32, NSEQ), (1, 8)]), ALU.mult),
                      reads=[bn, dn_t], writes=[MIX[4 + c]])

        if do_sample:
            KKs = [kb.sb("KKs%d" % k, [128, 128], BF16) for k in range(2)]
            vbf = kb.sb("vbf", [128, 128], BF16)
            KX = [[kb.sb("KX%d%d" % (k, i), [128, 136], BF16) for i in range(2)] for k in range(2)]
            VZS = [kb.sb("VZS%d" % i, [128, 320], BF16) for i in range(2)]
            VBS = [kb.sb("VBS%d" % i, [8, 320], BF16) for i in range(2)]
            for t_ in VZS + VBS:
                kb.op(POOL, lambda t_=t_: nc.gpsimd.memset(t_[:], 0.0), writes=[t_])
            EXA = kb.sb("EXA", [128, 64], F32)
            EXB = kb.sb("EXB", [8, 64], F32)
            PAs = [kb.sb("PAs%d" % i, [128, 64], BF16) for i in range(2)]
            PBs = [kb.sb("PBs%d" % i, [8, 64], BF16) for i in range(2)]
            W = 128
            for c in range(8):
                kb.dma(SP, X[c][:, 0:W], xT_s.ap()[c * 128:(c + 1) * 128, :], writes=[X[c]])
            for l in range(4):
                rmsnorm(X, l, H, W)
                if l % 2 == 0:
                    li = l // 2
                    li_cur[0] = li
                    in_proj(li, W, None)
                    s5_core(S5[li], W, NSEQ, 8, sample_h0=li)
                    attn_sample(li)
                    if li == 0:
                        dump16(MIX, W)
                        dump([P32[2]] * 8, 128)
                    out_proj(w_out.ap()[li], W, 8, MIX)
                else:
                    lo = l // 2
                    conformer(lo, W, NSEQ, 8, None)
                rmsnorm(X, 4 + l, H, W)
                ffn_up(l, W, NSEQ, 8, None)
                ffn_down(l, W)
            YO = [P32[c] for c in range(8)]
            rmsnorm(X, 8, None, W, out_f32=YO)
            for c in range(8):
                kb.dma(SP, yT_s.ap()[c * 128:(c + 1) * 128, :], YO[c][:, 0:W], reads=[YO[c]], is_out=True)

        kb.finish()
    return kb.nc


_CACHE = {}


def _prep_common(inp):
    f = np.float32
    d = {}
    gv = np.concatenate([inp["g_mix"], inp["g_ffn"], inp["g_final"][None]], 0)
    d["gvec"] = np.ascontiguousarray(gv.reshape(9, 8, 128).transpose(2, 0, 1)).astype(f)
    wi = inp["w_in_mix"]
    u, q = wi[:, :, 0:512], wi[:, :, 512:1024]
    k0, k1, v = wi[:, :, 1024:1088], wi[:, :, 1088:1152], wi[:, :, 1152:1280]
    def tile_w(w, ncols):
        L, K, N = w.shape
        t = w.reshape(L, K // 128, 128, N // ncols, ncols).transpose(0, 3, 2, 1, 4)
        return np.ascontiguousarray(t.reshape(L, N // ncols, 128, (K // 128) * ncols)).astype(f)
    d["w_in"] = tile_w(np.concatenate([u, q, k0, k0, k1, k1, v], -1), 128)

    def st(a):
        return a.reshape(2, 16, 2, 64).transpose(0, 2, 3, 1).reshape(2, 128, 16)
    ls = np.broadcast_to(inp["ssm_log_step"][:, :, None], (2, 32, 64))
    d["lam"] = np.ascontiguousarray(np.stack([st(inp["ssm_lambda_re"]), st(inp["ssm_lambda_im"]), st(ls)], 1)).astype(f)
    d["ssm_b"] = np.ascontiguousarray(np.stack([inp["ssm_b_re"], inp["ssm_b_im"]], 1)).astype(f)
    d["ssm_c"] = np.ascontiguousarray(np.stack([inp["ssm_c_re"], inp["ssm_c_im"]], 1)).astype(f)
    d["ssm_d"] = np.ascontiguousarray(inp["ssm_d"].reshape(2, 4, 128).transpose(0, 2, 1)).astype(f)
    d["w_glu"] = tile_w(inp["ssm_w_glu"], 128)
    d["b_glu"] = np.ascontiguousarray(inp["ssm_b_glu"].reshape(2, 4, 128).transpose(0, 2, 1)).astype(f)
    d["relb"] = np.ascontiguousarray(inp["rel_bias"]).astype(f)
    sk = inp["attn_sinks"]
    d["sinks"] = np.ascontiguousarray(np.repeat(sk.reshape(2, 4, 2), 64, axis=2).transpose(0, 2, 1)).astype(f)
    d["w_out"] = tile_w(inp["w_out_mix"], 128)
    p1 = inp["conv_w_pw1"]
    a, g = p1[:, :, :1024].reshape(2, 1024, 8, 128), p1[:, :, 1024:].reshape(2, 1024, 8, 128)
    d["w_pw1"] = tile_w(np.concatenate([a, g], -1).reshape(2, 1024, 2048), 256)
    d["w_dw"] = np.ascontiguousarray(inp["conv_w_dw"].reshape(2, 31, 8, 128).transpose(0, 3, 2, 1)).astype(f)
    cv = np.stack([inp["conv_b_dw"], inp["conv_ln_g"], inp["conv_ln_b"]], 1)
    d["cvec"] = np.ascontiguousarray(cv.reshape(2, 3, 8, 128).transpose(0, 3, 1, 2)).astype(f)
    d["w_pw2"] = tile_w(inp["conv_w_pw2"], 128)
    wu = inp["ffn_w_up"]
    gg, uu = wu[:, :, :DFF].reshape(4, 1024, NJ, 128), wu[:, :, DFF:].reshape(4, 1024, NJ, 128)
    d["w_up"] = tile_w(np.concatenate([gg, uu], -1).reshape(4, 1024, NJ * 256), 256)
    d["f_cw"] = np.ascontiguousarray(inp["ffn_w_conv"].reshape(4, 3, NJ, 128).transpose(0, 3, 2, 1)).astype(f)
    d["f_cb"] = np.ascontiguousarray(inp["ffn_b_conv"].reshape(4, NJ, 128).transpose(0, 2, 1)).astype(f)
    wd = inp["ffn_w_down"].reshape(4, 2, 11, 128, 8, 128).transpose(0, 4, 1, 3, 2, 5)
    d["w_dn"] = np.ascontiguousarray(wd.reshape(4, 8, 2, 128, 11 * 128)).astype(f)
    m = np.arange(384)
    dist = m - 127
    inside = (dist >= 0) & (dist < 128)
    oh = np.zeros((32, 384), f)
    bk = t5_bucket_np(np.clip(dist, 0, 127))
    oh[bk[inside], m[inside]] = 1.0
    d["oh_bucket"] = oh
    d["msk_ext"] = np.ascontiguousarray(np.broadcast_to(np.where(inside, 0.0, NEG).astype(f)[None], (8, 384)))
    d["antiI"] = np.ascontiguousarray(np.eye(128, dtype=f)[::-1])
    return d


def kernel(**inp):
    inp = {k: np.asarray(v) for k, v in inp.items()}
    f = np.float32
    if "nc" not in _CACHE:
        _CACHE["nc"] = build()
    nc = _CACHE["nc"]
    com = _prep_common(inp)
    in_maps = []
    for c in range(8):
        d = dict(com)
        s = c % 2
        xp = np.concatenate([np.zeros((NPAD, D), f), inp["meta_tokens"], inp["x_prompt"][s]], 0)
        d["xT_p"] = np.ascontiguousarray(xp.T)
        sl = slice(c * NSEQ, (c + 1) * NSEQ)
        d["xT_s"] = np.ascontiguousarray(inp["x_sample"][sl].reshape(NSEQ * 8, D).T)

        def st(a):
            return a.reshape(2, NSEQ, 16, 2, 64).transpose(0, 3, 4, 2, 1).reshape(2, 128, 16, NSEQ)
        d["st_ssm"] = np.ascontiguousarray(np.stack([st(inp["state_ssm_re"][:, sl]), st(inp["state_ssm_im"][:, sl])], 1)).astype(f)
        d["st_kT"] = np.ascontiguousarray(inp["cache_swa_k"][:, sl].reshape(2, NSEQ, 128, 128).transpose(0, 1, 3, 2)).astype(f)
        d["st_v"] = np.ascontiguousarray(inp["cache_swa_v"][:, sl].reshape(2, NSEQ, 128, 128)).astype(f)
        d["st_conv"] = np.ascontiguousarray(inp["state_conv"][:, sl].transpose(0, 3, 1, 2)).astype(f)
        d["st_ffn"] = np.ascontiguousarray(inp["state_ffn"][:, sl].transpose(0, 3, 1, 2)).astype(f)
        in_maps.append(d)
    res = run_bass_kernel_spmd(nc, in_maps, core_ids=list(range(8)))
    R = res.results
    _CACHE["last"] = R
    y_p = np.stack([R[s]["yT_p"][:, 128:].T for s in range(2)], 0)
    y_s = np.concatenate([R[c]["yT_s"].T.reshape(NSEQ, 8, D) for c in range(8)], 0)

    def ust(a):
        return a.reshape(2, 2, 64, 16).transpose(0, 3, 1, 2).reshape(2, 32, 64)
    sr_p = np.stack([ust(R[s]["o_ssm_p"][:, 0]) for s in range(2)], 1)
    si_p = np.stack([ust(R[s]["o_ssm_p"][:, 1]) for s in range(2)], 1)
    k_p = np.stack([R[s]["o_kT_p"].transpose(0, 2, 1).reshape(2, 128, 2, 64) for s in range(2)], 1)
    v_p = np.stack([R[s]["o_v_p"].reshape(2, 128, 2, 64) for s in range(2)], 1)
    c_p = np.stack([R[s]["o_conv_p"].transpose(0, 2, 1) for s in range(2)], 1)
    f_p = np.stack([R[s]["o_ffn_p"].transpose(0, 2, 1) for s in range(2)], 1)

    def usts(a):
        return a.reshape(2, 2, 64, 16, NSEQ).transpose(0, 4, 3, 1, 2).reshape(2, NSEQ, 32, 64)
    sr_s = np.concatenate([usts(R[c]["o_ssm_s"][:, 0]) for c in range(8)], 1)
    si_s = np.concatenate([usts(R[c]["o_ssm_s"][:, 1]) for c in range(8)], 1)
    k_s = np.concatenate([R[c]["o_kT_s"].transpose(0, 1, 3, 2).reshape(2, NSEQ, 128, 2, 64) for c in range(8)], 1)
    v_s = np.concatenate([R[c]["o_v_s"].reshape(2, NSEQ, 128, 2, 64) for c in range(8)], 1)
    c_s = np.concatenate([R[c]["o_conv_s"].transpose(0, 2, 3, 1) for c in range(8)], 1)
    f_s = np.concatenate([R[c]["o_ffn_s"].transpose(0, 2, 3, 1) for c in range(8)], 1)
    outs = (y_p, y_s, sr_p, si_p, k_p, v_p, c_p, f_p, sr_s, si_s, k_s, v_s, c_s, f_s)
    return tuple(np.ascontiguousarray(o).astype(np.float32) for o in outs)
```

```python
import contextlib
import math
import numpy as np
import concourse.bass as bass
import concourse.mybir as mybir
from concourse.bass_utils import run_bass_kernel_spmd

F32 = mybir.dt.float32
BF16 = mybir.dt.bfloat16
AF = mybir.ActivationFunctionType
ALU = mybir.AluOpType

D = 1024
NC8 = 8
DFF = 2816
NJ = 22
NPAD = 112
TPAD = 8320
NBLK = 65
NSEQ = 16
EPS = 1e-6
NEG = -30000.0


def t5_bucket_np(dist):
    n = np.maximum(dist, 0)
    max_exact = 16
    nf = np.maximum(n, max_exact).astype(np.float32)
    large = max_exact + (np.log(nf / max_exact) / math.log(128 / max_exact) * (32 - max_exact)).astype(np.int32)
    large = np.minimum(large, 31)
    return np.where(n < max_exact, n, large)


class Buf:
    __slots__ = ("t", "name", "last_w", "readers", "dsem", "dcnt", "wn")

    def __init__(self, t, name):
        self.t = t
        self.name = name
        self.last_w = None
        self.readers = {}
        self.dsem = None
        self.dcnt = 0
        self.wn = None

    def __getitem__(self, idx):
        if self.wn is not None and isinstance(idx, tuple) and len(idx) == 3 and isinstance(idx[1], int):
            cs = idx[2]
            return V(self, 0, 128, idx[1] * self.wn + cs.start, [(1, cs.stop - cs.start)])
        return self.t[idx]


class _Stop(Exception):
    pass


_LAST = {}


class KB:
    def __init__(self):
        self.nc = bass.Bass("TRN2", target_bir_lowering=False)
        self.es = contextlib.ExitStack()
        nc = self.nc
        self.eng = {"pe": nc.tensor, "act": nc.scalar, "dve": nc.vector, "pool": nc.gpsimd, "sp": nc.sync}
        self.sem = {}
        self.cnt = {}
        self.seen = {e: {} for e in self.eng}
        self.uid = 0
        self.psum_rr = 0
        self.out_events = []
        self.dead = False

    def start(self):
        import os
        for i in range(int(os.environ.get("KDUMMYSEM", "0"))):
            self.es.enter_context(self.nc.semaphore("dummy%d" % i))
        for e in self.eng:
            self.sem[e] = self.es.enter_context(self.nc.semaphore("prog_" + e))
            self.cnt[e] = 0
        self.banks = []
        for i in range(8):
            t = self.es.enter_context(self.nc.psum_tensor("bank%d" % i, [128, 512], F32))
            self.banks.append(Buf(t, "bank%d" % i))

    def sb(self, name, shape, dtype):
        self.uid += 1
        t = self.es.enter_context(self.nc.sbuf_tensor("%s_%d" % (name, self.uid), list(shape), dtype))
        return Buf(t, name)

    def dram(self, name, shape, dtype, kind):
        return self.nc.dram_tensor(name, list(shape), dtype, kind=kind)

    def bank(self):
        b = self.banks[self.psum_rr % 8]
        self.psum_rr += 1
        return b

    def _waits(self, e, reads, writes):
        deps = []
        for b in reads:
            if b.last_w is not None:
                deps.append((b.last_w, True))
        for b in writes:
            if b.last_w is not None:
                deps.append((b.last_w, False))
            for ev in b.readers.values():
                deps.append((ev, False))
        own = self.sem.get(e)
        for (sem, val), raw in deps:
            if sem is own:
                if e in ("pe", "sp"):
                    continue
            key = id(sem)
            if self.seen[e].get(key, 0) < val:
                self.eng[e].wait_ge(sem, val)
                self.seen[e][key] = val

    def chk(self, tag):
        import os
        if os.environ.get("KSTOP") == tag:
            for i in range(int(os.environ.get("KEXTRA", "0"))):
                tgt = self._xtra if os.environ.get("KXT") else self._misc
                w = 128 if os.environ.get("KXT") else 1
                if os.environ.get("KXT") == "2":
                    self.op("dve", lambda: self.nc.vector.tensor_copy(self._xtra[:, 0:128], self._xtra2[:, 0:128]), reads=[self._xtra2], writes=[self._xtra])
                else:
                    self.op("dve", lambda: self.nc.vector.memset(tgt[:, 0:w], 0.0), writes=[tgt])
            self.dead = True
            if os.environ.get("KRAISE"):
                self.dead = False
                self.finish()
                raise _Stop()

    def op(self, e, fn, reads=(), writes=(), inc=True):
        if self.dead:
            return None
        self._waits(e, reads, writes)
        inst = fn()
        if inc:
            self.cnt[e] += 1
            inst.then_inc(self.sem[e], 1)
            ev = (self.sem[e], self.cnt[e])
        else:
            ev = (self.sem[e], self.cnt[e] + 1)
        for b in writes:
            b.last_w = ev
            b.readers = {}
        for b in reads:
            b.readers[e] = ev
        return inst

    def dma(self, q, out_ap, in_ap, reads=(), writes=(), slow=False, is_out=False):
        if self.dead:
            return None
        self._waits(q, reads, writes)
        kw = {}
        if slow:
            kw["allow_slow_non_contiguous"] = True
        inst = self.eng[q].dma_start(out=out_ap, in_=in_ap, **kw)
        tgt = writes[0] if writes else (reads[0] if reads else None)
        if tgt is None:
            tgt = self._misc
        if tgt.dsem is None:
            self.uid += 1
            tgt.dsem = self.es.enter_context(self.nc.semaphore("d_%s_%d" % (tgt.name, self.uid)))
        tgt.dcnt += 16
        inst.then_inc(tgt.dsem, 16)
        ev = (tgt.dsem, tgt.dcnt)
        for b in writes:
            b.last_w = ev
            b.readers = {}
        for b in reads:
            b.readers[("dma", id(tgt))] = ev
        self.out_events.append(ev)
        return inst

    def finish(self):
        last = {}
        for sem, val in self.out_events:
            k = id(sem)
            if k not in last or last[k][1] < val:
                last[k] = (sem, val)
        for sem, val in last.values():
            self.eng["sp"].wait_ge(sem, val)
        for e in self.eng:
            for e2 in ("pe", "act", "dve", "pool"):
                if e2 != e and self.cnt[e2] > 0:
                    self.eng[e].wait_ge(self.sem[e2], self.cnt[e2])


def V(buf, p0, np_, off, dims):
    t = buf.t
    shape = t.shape
    fsz = 1
    for s in shape[1:]:
        fsz *= s
    return bass.AP(t, p0 * fsz + off, [[fsz, np_]] + [[s, c] for (s, c) in dims])


def build(nchunk_prompt=17, do_sample=True, dbg=False, dbg_ci=1):
    try:
        return _build(nchunk_prompt, do_sample, dbg, dbg_ci)
    except _Stop:
        return _LAST["kb"].nc


def _build(nchunk_prompt=17, do_sample=True, dbg=False, dbg_ci=1):
    kb = KB()
    _LAST["kb"] = kb
    nc = kb.nc
    es = kb.es
    with es:
        kb.start()
        import os as _os
        kb._misc = kb.sb("misc", [128, 1], F32)
        PE, ACT, DVE, POOL, SP = "pe", "act", "dve", "pool", "sp"
        din = {}

        def DI(name, shape):
            din[name] = kb.dram(name, shape, F32, "ExternalInput")
            return din[name]

        dout = {}

        def DO(name, shape):
            dout[name] = kb.dram(name, shape, F32, "ExternalOutput")
            return dout[name]

        xT_p = DI("xT_p", [D, TPAD])
        xT_s = DI("xT_s", [D, 128])
        st_ssm = DI("st_ssm", [2, 2, 128, 16, NSEQ])
        st_kT = DI("st_kT", [2, NSEQ, 128, 128])
        st_v = DI("st_v", [2, NSEQ, 128, 128])
        st_conv = DI("st_conv", [2, D, NSEQ, 30])
        st_ffn = DI("st_ffn", [4, DFF, NSEQ, 2])
        gvec = DI("gvec", [128, 9, 8])
        w_in = DI("w_in", [2, 11, 128, 1024])
        lam = DI("lam", [2, 3, 128, 16])
        ssm_b = DI("ssm_b", [2, 2, 32, 64, 16])
        ssm_c = DI("ssm_c", [2, 2, 32, 16, 64])
        ssm_d = DI("ssm_d", [2, 128, 4])
        w_glu = DI("w_glu", [2, 4, 128, 512])
        b_glu = DI("b_glu", [2, 128, 4])
        relb = DI("relb", [32, 8])
        sinks = DI("sinks", [2, 128, 4])
        w_out = DI("w_out", [2, 8, 128, 1024])
        w_pw1 = DI("w_pw1", [2, 8, 128, 2048])
        w_dw = DI("w_dw", [2, 128, 8, 31])
        cvec = DI("cvec", [2, 128, 3, 8])
        w_pw2 = DI("w_pw2", [2, 8, 128, 1024])
        w_up = DI("w_up", [4, NJ, 128, 2048])
        f_cw = DI("f_cw", [4, 128, NJ, 3])
        f_cb = DI("f_cb", [4, 128, NJ])
        w_dn = DI("w_dn", [4, 8, 2, 128, 1408])
        oh_bucket = DI("oh_bucket", [32, 384])
        msk_ext = DI("msk_ext", [8, 384])
        antiI = DI("antiI", [128, 128])

        yT_p = DO("yT_p", [D, TPAD])
        yT_s = DO("yT_s", [D, 128])
        o_ssm_p = DO("o_ssm_p", [2, 2, 128, 16])
        o_ssm_s = DO("o_ssm_s", [2, 2, 128, 16, NSEQ])
        o_kT_p = DO("o_kT_p", [2, 128, 128])
        o_v_p = DO("o_v_p", [2, 128, 128])
        o_kT_s = DO("o_kT_s", [2, NSEQ, 128, 128])
        o_v_s = DO("o_v_s", [2, NSEQ, 128, 128])
        o_conv_p = DO("o_conv_p", [2, D, 30])
        o_conv_s = DO("o_conv_s", [2, D, NSEQ, 30])
        o_ffn_p = DO("o_ffn_p", [4, DFF, 2])
        o_ffn_s = DO("o_ffn_s", [4, DFF, NSEQ, 2])
        scr = kb.dram("scr_bias", [8, 384], F32, "Internal")
        if dbg:
            dbg_o = DO("dbg", [16, D, 512])
        dbgc = [0]

        def dump16(Xl, W):
            if not dbg:
                return
            for c in range(8):
                kb.dma(POOL, dbg_o.ap()[dbgc[0], c * 128:(c + 1) * 128, 0:W], Xl[c][:, 0:W], reads=[Xl[c]], is_out=True)
            dbgc[0] += 1

        def dump(Xl, W):
            if not dbg:
                return
            for c in range(8):
                kb.dma(SP, dbg_o.ap()[dbgc[0], c * 128:(c + 1) * 128, 0:W], Xl[c][:, 0:W], reads=[Xl[c]], is_out=True)
            dbgc[0] += 1

        ident = kb.sb("ident", [128, 128], F32)
        kb.op(POOL, lambda: nc.gpsimd.memset(ident[:], 1.0), writes=[ident])
        kb.op(POOL, lambda: nc.gpsimd.affine_select(ident[:], ident[:], pattern=[[-1, 128]], compare_op=ALU.is_equal,
                                                     fill=0.0, base=0, channel_multiplier=1), reads=[ident], writes=[ident])
        ones_bf = kb.sb("ones_bf", [128, 128], BF16)
        kb.op(DVE, lambda: nc.vector.memset(ones_bf[:], 1.0), writes=[ones_bf])
        Oz = kb.sb("Oz", [128, 192], BF16)
        Oz0 = kb.sb("Oz0", [128, 192], BF16)
        for o in (Oz, Oz0):
            kb.op(DVE, lambda o=o: nc.vector.memset(o[:], 0.0), writes=[o])
            kb.op(DVE, lambda o=o: nc.vector.memset(o[:, 64:128], 1.0), writes=[o])
        kb.op(DVE, lambda: nc.vector.memset(Oz0[0:NPAD, :], 0.0), writes=[Oz0])
        gv = kb.sb("gv", [128, 9, 8], F32)
        kb.dma(SP, gv[:], gvec.ap(), writes=[gv])
        kb.chk("c0")

        WSL = [kb.sb("wslab%d" % i, [128, 8, 256], BF16) for i in range(2)]
        WDN = [kb.sb("wdn%d" % i, [128, 11, 128], BF16) for i in range(2)]
        wctr = {"a": 0, "b": 0}
        UT = [kb.sb("UT%d" % m, [128, 512], BF16) for m in range(4)]
        U32 = [kb.sb("U32_%d" % m, [128, 512], F32) for m in range(4)]
        QT = [[kb.sb("QT%d_%d" % (e, c), [128, 512], BF16) for c in range(4)] for e in range(2)]
        for e in range(2):
            for c in range(4):
                kb.op(POOL, lambda e=e, c=c: nc.gpsimd.memset(QT[e][c][:], 0.0), writes=[QT[e][c]])

        wconv = kb.sb("wconv", [128, 1], F32)
        WB = {}
        for nm, src in (("w_in", w_in), ("w_glu", w_glu), ("w_out", w_out), ("w_pw1", w_pw1), ("w_pw2", w_pw2), ("w_up", w_up), ("w_dn", w_dn)):
            shp = list(src.shape)
            dst = kb.dram(nm + "_bf", shp, BF16, "Internal")
            WB[nm] = dst
            lead = 1
            for d_ in shp[:-2]:
                lead *= d_
            sflat = src.ap().flatten_outer_dims() if len(shp) > 2 else src.ap()
            dflat = dst.ap().flatten_outer_dims() if len(shp) > 2 else dst.ap()
            for i_ in range(lead):
                kb.dma(POOL, dflat[i_ * 128:(i_ + 1) * 128, :], sflat[i_ * 128:(i_ + 1) * 128, :], writes=[wconv])

        def load_w(dram_ap, kc, ncols):
            b = WSL[wctr["a"] % len(WSL)]
            wctr["a"] += 1
            b.wn = ncols
            kb.dma(SP, V(b, 0, 128, 0, [(1, kc * ncols)]), dram_ap, reads=[wconv], writes=[b])
            return b

        def load_wdn(dram_ap):
            b = WDN[wctr["b"] % len(WDN)]
            wctr["b"] += 1
            kb.dma(SP, V(b, 0, 128, 0, [(1, 11 * 128)]), dram_ap, reads=[wconv], writes=[b])
            return b

        def mm_group(out_ap, pairs, bankbuf, rbufs):
            n = len(pairs)
            if _os.environ.get("KHOIST"):
                allr = []
                for (_l, _r, bs_) in pairs:
                    allr += list(bs_)
                kb._waits(PE, allr + list(rbufs), [bankbuf])
            for i, (l, r, bs) in enumerate(pairs):
                kb.op(PE, lambda l=l, r=r, i=i: nc.tensor.matmul(out_ap, l, r, start=(i == 0), stop=(i == n - 1)),
                      reads=list(bs) + list(rbufs), writes=[bankbuf], inc=(i == n - 1))

        P32 = [kb.sb("P32_%d" % i, [128, 608], F32) for i in range(14)]
        P16 = [kb.sb("P16_%d" % i, [128, 512], BF16) for i in range(22)]
        kb._xtra = P32[5]
        kb._xtra2 = P32[6]
        sq = P16[12:20]
        rs = P32[12]
        rinv = P32[13]
        eps_t = kb.sb("eps_t", [128, 1], F32)
        kb.op(DVE, lambda: nc.vector.memset(eps_t[:], EPS), writes=[eps_t])

        def rmsnorm(X, gi, Hout, W, out_f32=None):
            lvl = int(_os.environ.get("KRMS", "9"))
            if lvl < 1:
                return
            for c in range(8):
                kb.op(DVE, lambda c=c: nc.vector.tensor_tensor(sq[c][:, 0:W], X[c][:, 0:W], X[c][:, 0:W], ALU.mult), reads=[X[c]], writes=[sq[c]])
            if lvl < 2:
                return
            bk = kb.bank()
            mm_group(bk[:, 0:W], [(ones_bf[:], sq[c][:, 0:W], [sq[c]]) for c in range(8)], bk, [ones_bf])
            if lvl < 3:
                return
            kb.op(DVE, lambda: nc.vector.tensor_scalar(rs[:, 0:W], bk[:, 0:W], 1.0 / D, EPS, ALU.mult, ALU.add), reads=[bk], writes=[rs])
            kb.op(ACT, lambda: nc.scalar.activation(rs[:, 0:W], rs[:, 0:W], AF.Sqrt), reads=[rs], writes=[rs])
            if lvl < 4:
                return
            kb.op(DVE, lambda: nc.vector.reciprocal(rinv[:, 0:W], rs[:, 0:W]), reads=[rs], writes=[rinv])
            if lvl < 5:
                return
            for c in range(8):
                o = Hout[c] if out_f32 is None else out_f32[c]
                kb.op(DVE, lambda c=c, o=o: nc.vector.scalar_tensor_tensor(o[:, 0:W], X[c][:, 0:W], gv[:, gi, c:c + 1], rinv[:, 0:W],
                                                                          ALU.mult, ALU.mult),
                      reads=[X[c], gv, rinv], writes=[o])

        X = [kb.sb("X%d" % c, [128, 512], F32) for c in range(8)]
        H = [kb.sb("H%d" % c, [128, 512], BF16) for c in range(8)]

        S5 = []
        import os as _os
        for li in range(0 if _os.environ.get("KSKIP_S5") else 2):
            T = {}
            lm = kb.sb("lam", [128, 3, 16], F32)
            kb.dma(SP, lm[:], lam.ap()[li].rearrange("k p j -> p k j"), writes=[lm])
            dt = kb.sb("dt", [128, 16], F32)
            kb.op(ACT, lambda: nc.scalar.activation(dt[:], lm[:, 2, :], AF.Exp), reads=[lm], writes=[dt])
            lrdt = kb.sb("lrdt", [128, 16], F32)
            th = kb.sb("th", [128, 16], F32)
            kb.op(DVE, lambda: nc.vector.tensor_tensor(lrdt[:], lm[:, 0, :], dt[:], ALU.mult), reads=[lm, dt], writes=[lrdt])
            kb.op(DVE, lambda: nc.vector.tensor_tensor(th[:], lm[:, 1, :], dt[:], ALU.mult), reads=[lm, dt], writes=[th])
            rho = kb.sb("rho", [128, 16], F32)
            kb.op(ACT, lambda: nc.scalar.activation(rho[:], lrdt[:], AF.Exp), reads=[lrdt], writes=[rho])
            T["rho"] = rho
            kk = P32[0]
            kb.op(POOL, lambda: nc.gpsimd.iota(kk[:, 0:64], pattern=[[1, 64]], base=1, channel_multiplier=0,
                                               allow_small_or_imprecise_dtypes=True), writes=[kk])
            ctab = kb.sb("ctab", [128, 16, 64], F32)
            stab = kb.sb("stab", [128, 16, 64], F32)
            negpi = kb.sb("negpi", [128, 1], F32)
            ki32 = kb.sb("ki32", [128, 64], mybir.dt.int32)
            kb.op(DVE, lambda: nc.vector.memset(negpi[:], -math.pi), writes=[negpi])
            for j in range(16):
                ang = P32[1 + (j % 2)]
                kb.op(DVE, lambda j=j, ang=ang: nc.vector.tensor_scalar(ang[:, 0:64], kk[:, 0:64], th[:, j:j + 1], None, ALU.mult),
                      reads=[kk, th], writes=[ang])
                for (dst, sh, ti) in ((stab, 0.5, 3), (ctab, 0.75, 5)):
                    tmp = P32[ti + (j % 2)]
                    kb.op(DVE, lambda sh=sh, tmp=tmp, ang=ang: nc.vector.tensor_scalar(tmp[:, 0:64], ang[:, 0:64], 1.0 / (2 * math.pi), sh, ALU.mult, ALU.add),
                          reads=[ang], writes=[tmp])
                    kb.op(DVE, lambda tmp=tmp: nc.vector.tensor_copy(ki32[:, 0:64], tmp[:, 0:64]), reads=[tmp], writes=[ki32])
                    kb.op(DVE, lambda tmp=tmp: nc.vector.tensor_copy(tmp[:, 64:128], ki32[:, 0:64]), reads=[ki32], writes=[tmp])
                    kb.op(DVE, lambda tmp=tmp: nc.vector.tensor_tensor(tmp[:, 0:64], tmp[:, 0:64], tmp[:, 64:128], ALU.subtract), reads=[tmp], writes=[tmp])
                    kb.op(DVE, lambda tmp=tmp: nc.vector.tensor_scalar(tmp[:, 64:128], tmp[:, 0:64], 0.0, None, ALU.is_lt), reads=[tmp], writes=[tmp])
                    kb.op(DVE, lambda tmp=tmp: nc.vector.tensor_tensor(tmp[:, 0:64], tmp[:, 0:64], tmp[:, 64:128], ALU.add), reads=[tmp], writes=[tmp])
                    kb.op(ACT, lambda dst=dst, tmp=tmp, j=j: nc.scalar.activation(dst[:, j, :], tmp[:, 0:64], AF.Sin, bias=negpi[:], scale=2 * math.pi),
                          reads=[tmp, negpi], writes=[dst])
            T["ctab"], T["stab"] = ctab, stab
            kb.chk("s5ang")
            are = kb.sb("are", [128, 16], F32)
            aim = kb.sb("aim", [128, 16], F32)
            kb.op(DVE, lambda: nc.vector.tensor_tensor(are[:], rho[:], ctab[:, :, 0], ALU.mult), reads=[rho, ctab], writes=[are])
            kb.op(DVE, lambda: nc.vector.tensor_tensor(aim[:], rho[:], stab[:, :, 0], ALU.mult), reads=[rho, stab], writes=[aim])
            den = kb.sb("den", [128, 16], F32)
            t1 = kb.sb("t1", [128, 16], F32)
            t2 = kb.sb("t2", [128, 16], F32)
            kb.op(DVE, lambda: nc.vector.tensor_tensor(den[:], lm[:, 0, :], lm[:, 0, :], ALU.mult), reads=[lm], writes=[den])
            kb.op(DVE, lambda: nc.vector.tensor_tensor(t1[:], lm[:, 1, :], lm[:, 1, :], ALU.mult), reads=[lm], writes=[t1])
            kb.op(DVE, lambda: nc.vector.tensor_tensor(den[:], den[:], t1[:], ALU.add), reads=[den, t1], writes=[den])
            rden = kb.sb("rden", [128, 16], F32)
            kb.op(DVE, lambda: nc.vector.reciprocal(rden[:], den[:]), reads=[den], writes=[rden])
            nre = kb.sb("nre", [128, 16], F32)
            kb.op(DVE, lambda: nc.vector.tensor_scalar_add(nre[:], are[:], -1.0), reads=[are], writes=[nre])
            cre = kb.sb("cre", [128, 16], F32)
            cim = kb.sb("cim", [128, 16], F32)
            ncim = kb.sb("ncim", [128, 16], F32)
            kb.op(DVE, lambda: nc.vector.tensor_tensor(t1[:], nre[:], lm[:, 0, :], ALU.mult), reads=[nre, lm], writes=[t1])
            kb.op(DVE, lambda: nc.vector.tensor_tensor(t2[:], aim[:], lm[:, 1, :], ALU.mult), reads=[aim, lm], writes=[t2])
            kb.op(DVE, lambda: nc.vector.tensor_tensor(t1[:], t1[:], t2[:], ALU.add), reads=[t1, t2], writes=[t1])
            kb.op(DVE, lambda: nc.vector.tensor_tensor(cre[:], t1[:], rden[:], ALU.mult), reads=[t1, rden], writes=[cre])
            kb.op(DVE, lambda: nc.vector.tensor_tensor(t1[:], aim[:], lm[:, 0, :], ALU.mult), reads=[aim, lm], writes=[t1])
            kb.op(DVE, lambda: nc.vector.tensor_tensor(t2[:], nre[:], lm[:, 1, :], ALU.mult), reads=[nre, lm], writes=[t2])
            kb.op(DVE, lambda: nc.vector.tensor_tensor(t1[:], t1[:], t2[:], ALU.subtract), reads=[t1, t2], writes=[t1])
            kb.op(DVE, lambda: nc.vector.tensor_tensor(cim[:], t1[:], rden[:], ALU.mult), reads=[t1, rden], writes=[cim])
            kb.op(DVE, lambda: nc.vector.tensor_scalar_mul(ncim[:], cim[:], -1.0), reads=[cim], writes=[ncim])
            kb.chk("s5coef")
            Bl = [kb.sb("Bl_re", [128, 16, 128], BF16), kb.sb("Bl_im", [128, 16, 128], BF16)]
            Cl = [kb.sb("Cl_re", [128, 16, 128], BF16), kb.sb("Cl_im", [128, 16, 128], BF16)]
            for j in range(16):
                o = 128 * (j % 2)
                zb = [P32[7], P32[8]]
                zc = [P32[9], P32[10]]
                zbb = [P32[11], P32[12]]
                for z in zb + zc:
                    kb.op(POOL, lambda z=z, o=o: nc.gpsimd.memset(z[:, o:o + 128], 0.0), writes=[z])
                for ri in range(2):
                    for e in range(2):
                        g = 2 * j + e
                        c0 = 32 * (j % 4) + 16 * e
                        kb.dma(SP, zb[ri][64 * e:64 * e + 64, o + c0:o + c0 + 16], ssm_b.ap()[li, ri, g], writes=[zb[ri]])
                        kb.dma(SP, zc[ri][c0:c0 + 16, o + 64 * e:o + 64 * e + 64], ssm_c.ap()[li, ri, g], writes=[zc[ri]])
                kb.op(DVE, lambda j=j, o=o: nc.vector.tensor_scalar(zbb[0][:, o:o + 128], zb[0][:, o:o + 128], cre[:, j:j + 1], None, ALU.mult),
                      reads=[zb[0], cre], writes=[zbb[0]])
                kb.op(DVE, lambda j=j, o=o: nc.vector.scalar_tensor_tensor(zbb[0][:, o:o + 128], zb[1][:, o:o + 128], ncim[:, j:j + 1], zbb[0][:, o:o + 128],
                                                                      ALU.mult, ALU.add), reads=[zb[1], ncim, zbb[0]], writes=[zbb[0]])
                kb.op(DVE, lambda j=j, o=o: nc.vector.tensor_scalar(zbb[1][:, o:o + 128], zb[1][:, o:o + 128], cre[:, j:j + 1], None, ALU.mult),
                      reads=[zb[1], cre], writes=[zbb[1]])
                kb.op(DVE, lambda j=j, o=o: nc.vector.scalar_tensor_tensor(zbb[1][:, o:o + 128], zb[0][:, o:o + 128], cim[:, j:j + 1], zbb[1][:, o:o + 128],
                                                                      ALU.mult, ALU.add), reads=[zb[0], cim, zbb[1]], writes=[zbb[1]])
                for ri in range(2):
                    bk = kb.bank()
                    kb.op(PE, lambda bk=bk, ri=ri, o=o: nc.tensor.transpose(bk[:, 0:128], zbb[ri][:, o:o + 128], ident[:]),
                          reads=[zbb[ri], ident], writes=[bk])
                    kb.op(PE, lambda bk=bk, ri=ri, o=o: nc.tensor.transpose(bk[:, 128:256], zc[ri][:, o:o + 128], ident[:]),
                          reads=[zc[ri], ident], writes=[bk])
                    kb.op(DVE, lambda bk=bk, ri=ri, j=j: nc.vector.tensor_copy(Bl[ri][:, j, :], bk[:, 0:128]), reads=[bk], writes=[Bl[ri]])
                    if ri == 0:
                        kb.op(DVE, lambda bk=bk, ri=ri, j=j: nc.vector.tensor_copy(Cl[ri][:, j, :], bk[:, 128:256]), reads=[bk], writes=[Cl[ri]])
                    else:
                        kb.op(DVE, lambda bk=bk, ri=ri, j=j: nc.vector.tensor_scalar(Cl[ri][:, j, :], bk[:, 128:256], -1.0, None, ALU.mult), reads=[bk], writes=[Cl[ri]])
            T["Bl"], T["Cl"] = Bl, Cl
            kb.chk("s5bc")
            dsk = kb.sb("dsk", [128, 4], F32)
            bgl = kb.sb("bgl", [128, 4], F32)
            kb.dma(SP, dsk[:], ssm_d.ap()[li], writes=[dsk])
            kb.dma(SP, bgl[:], b_glu.ap()[li], writes=[bgl])
            T["dsk"], T["bgl"] = dsk, bgl
            rho9 = kb.sb("rho9", [128, 16, 9], F32)
            kb.op(DVE, lambda: nc.vector.memset(rho9[:], 0.0), writes=[rho9])
            kb.op(DVE, lambda: nc.vector.tensor_scalar(rho9[:, :, 1:9], V(rho, 0, 128, 0, [(1, 16), (0, 8)]), 1.0, None, ALU.mult),
                  reads=[rho], writes=[rho9])
            T["rho9"] = rho9
            T["car"] = [kb.sb("car_re", [128, 16], F32), kb.sb("car_im", [128, 16], F32)]
            for cbuf in T["car"]:
                kb.op(DVE, lambda cbuf=cbuf: nc.vector.memset(cbuf[:], 0.0), writes=[cbuf])
            S5.append(T)
        kb.chk("s5")

        kb.dead = bool(_os.environ.get("KSKIP_BIAS"))
        rb = kb.sb("rb", [32, 8], F32)
        oh = P32[3]
        kb.dma(SP, rb[:], relb.ap(), writes=[rb])
        kb.dma(SP, oh[0:32, 0:384], oh_bucket.ap(), writes=[oh])
        mk8 = P32[4]
        kb.dma(SP, mk8[0:8, 0:384], msk_ext.ap(), writes=[mk8])
        bk = kb.bank()
        kb.op(PE, lambda: nc.tensor.matmul(bk[0:8, 0:384], rb[:], oh[0:32, 0:384], start=True, stop=True), reads=[rb, oh], writes=[bk])
        bv = P32[5]
        kb.op(DVE, lambda: nc.vector.tensor_tensor(bv[0:8, 0:384], bk[0:8, 0:384], mk8[0:8, 0:384], ALU.add), reads=[bk, mk8], writes=[bv])
        kb.dma(SP, scr.ap(), bv[0:8, 0:384], reads=[bv], writes=[kb._misc])
        aI = kb.sb("aI", [128, 128], F32)
        kb.dma(SP, aI[:], antiI.ap(), writes=[aI])
        Etab = []
        for c in range(4):
            E = kb.sb("Etab%d" % c, [128, 512], F32)
            hank = P32[c % 2]
            kb.dma(SP, hank[:, 0:512], bass.AP(scr, 2 * c * 384, [[1, 128], [384, 2], [1, 256]]), reads=[kb._misc], writes=[hank])
            bk = kb.bank()
            for e in range(2):
                kb.op(PE, lambda e=e, bk=bk, hank=hank: nc.tensor.matmul(bk[:, e * 128:(e + 1) * 128], aI[:], hank[:, e * 256:e * 256 + 128], start=True, stop=True),
                      reads=[aI, hank], writes=[bk])
                kb.op(PE, lambda e=e, bk=bk, hank=hank: nc.tensor.matmul(bk[:, 256 + e * 128:256 + (e + 1) * 128], aI[:], hank[:, e * 256 + 128:e * 256 + 256],
                                                                   start=True, stop=True), reads=[aI, hank], writes=[bk])
            kb.op(DVE, lambda E=E, bk=bk: nc.vector.tensor_copy(E[:], bk[:]), reads=[bk], writes=[E])
            kb.op(ACT, lambda E=E: nc.scalar.activation(E[:], E[:], AF.Exp), reads=[E], writes=[E])
            Etab.append(E)
        EAt = kb.sb("EAt", [128, 64], F32)
        EBt = kb.sb("EBt", [8, 64], F32)
        hk = P32[2]
        kb.dma(SP, hk[:, 0:64], bass.AP(scr, 120, [[1, 128], [384, 8], [1, 8]]), reads=[kb._misc], writes=[hk], slow=True)
        kb.dma(SP, hk[0:8, 64:128], bass.AP(scr, 248, [[1, 8], [384, 8], [1, 8]]), reads=[kb._misc], writes=[hk], slow=True)
        bk = kb.bank()
        kb.op(PE, lambda: nc.tensor.matmul(bk[:, 0:64], aI[:], hk[:, 0:64], start=True, stop=True), reads=[aI, hk], writes=[bk])
        kb.op(PE, lambda: nc.tensor.matmul(bk[0:8, 64:128], aI[0:8, 120:128], hk[0:8, 64:128], start=True, stop=True), reads=[aI, hk], writes=[bk])
        kb.op(DVE, lambda: nc.vector.tensor_copy(EAt[:], bk[:, 0:64]), reads=[bk], writes=[EAt])
        kb.op(ACT, lambda: nc.scalar.activation(EAt[:], EAt[:], AF.Exp), reads=[EAt], writes=[EAt])
        kb.op(DVE, lambda: nc.vector.tensor_copy(EBt[:], bk[0:8, 64:128]), reads=[bk], writes=[EBt])
        kb.op(ACT, lambda: nc.scalar.activation(EBt[:], EBt[:], AF.Exp), reads=[EBt], writes=[EBt])
        EsT = []
        for li in range(2):
            sk = kb.sb("sk", [128, 4], F32)
            kb.dma(SP, sk[:], sinks.ap()[li], writes=[sk])
            kb.op(ACT, lambda sk=sk: nc.scalar.activation(sk[:], sk[:], AF.Exp), reads=[sk], writes=[sk])
            est = sk
            EsT.append(est)

        kb.dead = False
        kb.chk("bias")
        kb.dead = bool(_os.environ.get("KSKIP_STATE"))
        NVB = 5
        ATT = []
        for li in range(2):
            A = {}
            A["KK"] = [kb.sb("KK%d" % k, [128, 128 + 512], BF16) for k in range(2)]
            A["Vz"] = [kb.sb("Vz%d" % i, [128, 320], BF16) for i in range(NVB)]
            for vz in A["Vz"]:
                kb.op(POOL, lambda vz=vz: nc.gpsimd.memset(vz[:], 0.0), writes=[vz])
            A["kT32"] = kb.sb("kT32", [128, 128], F32)
            A["v32"] = kb.sb("v32", [128, 128], F32)
            ATT.append(A)
        CONV = []
        for li in range(2):
            C = {}
            C["halo"] = kb.sb("chalo", [128, 8, 30], F32)
            kb.op(POOL, lambda b=C["halo"]: nc.gpsimd.memset(b[:], 0.0), writes=[C["halo"]])
            C["wdw"] = kb.sb("wdw", [128, 8, 31], F32)
            kb.dma(SP, C["wdw"][:], w_dw.ap()[li], writes=[C["wdw"]])
            C["cv"] = kb.sb("cv", [128, 3, 8], F32)
            kb.dma(SP, C["cv"][:], cvec.ap()[li], writes=[C["cv"]])
            CONV.append(C)
        FFN = []
        for l in range(4):
            Fd = {}
            Fd["halo"] = kb.sb("fhalo", [128, NJ, 2], F32)
            kb.op(POOL, lambda b=Fd["halo"]: nc.gpsimd.memset(b[:], 0.0), writes=[Fd["halo"]])
            Fd["cw"] = kb.sb("fcw", [128, NJ, 3], F32)
            Fd["cb"] = kb.sb("fcb", [128, NJ], F32)
            kb.dma(SP, Fd["cw"][:], f_cw.ap()[l], writes=[Fd["cw"]])
            kb.dma(SP, Fd["cb"][:], f_cb.ap()[l], writes=[Fd["cb"]])
            FFN.append(Fd)

        kb.dead = False
        MIX = P16[16:22] + [kb.sb("MIX%d" % c, [128, 512], BF16) for c in range(2)]
        XR = [[P32[0], P32[1]], [P32[2], P32[3]]]
        GG = [[P32[4], P32[5]], [P32[6], P32[7]]]
        tA = [P32[8], P32[9]]
        tB = [P32[10], P32[11]]
        YS = [P32[12], P32[13]]
        HB = [[P16[0], P16[1], P16[2], P16[3]], [P16[4], P16[5], P16[6], P16[7]]]
        GEL = P16[8:12]
        cfx = [kb.sb("cfx%d" % i, [128, 2], F32) for i in range(4)]
        sso = [kb.sb("sso%d" % i, [128, NSEQ], F32) for i in range(2)]
        m9 = kb.sb("m9", [128, NSEQ * 9], F32)
        PT = P16[12:16]
        EX = [P32[0], P32[1]]
        dn_t = P32[2]
        YF = P16
        GR = [P32[0], P32[1], P32[2]]
        ACC = [P32[3], P32[4], P32[5]]
        SG = [P32[12], P32[13]]
        LNY = P32[0:8]
        GLW = [P32[8], P32[9]]
        SQF = [P32[10], P32[11]]
        mu_t = P32[12]
        LNS = P16[0:8]
        rr = {"t": 0, "pt": 0, "ex": 0, "gr": 0, "acc": 0, "sg": 0, "ys": 0}

        def nxt(lst, key):
            b = lst[rr[key] % len(lst)]
            rr[key] += 1
            return b

        def s5_core(T, W, nrep, L, sample_h0=None, ssm_out=None):
            def v3(b):
                return V(b, 0, 128, 0, [(L, nrep), (1, L)])
            for m in range(4):
                bky = kb.bank()
                cpairs = []
                for half in range(2):
                    for q in range(2):
                        jj = 2 * half + q
                        j = 4 * m + jj
                        bre, bim = kb.bank(), kb.bank()
                        for ri, bkk in ((0, bre), (1, bim)):
                            mm_group(bkk[:, 0:W], [(T["Bl"][ri][:, j, :], UT[m][:, 0:W], [T["Bl"][ri], UT[m]])], bkk, [])
                        cv = V(T["ctab"], 0, 128, j * 64, [(0, nrep), (1, L)])
                        sv = V(T["stab"], 0, 128, j * 64, [(0, nrep), (1, L)])
                        a1, a2 = tA[0], tB[0]
                        kb.op(DVE, lambda: nc.vector.tensor_tensor(v3(a1), v3(bre), cv, ALU.mult), reads=[bre, T["ctab"]], writes=[a1])
                        kb.op(DVE, lambda: nc.vector.tensor_tensor(v3(a2), v3(bim), sv, ALU.mult), reads=[bim, T["stab"]], writes=[a2])
                        kb.op(DVE, lambda: nc.vector.tensor_tensor(XR[0][q][:, 0:W], a1[:, 0:W], a2[:, 0:W], ALU.add),
                              reads=[a1, a2], writes=[XR[0][q]])
                        a1, a2 = tA[1], tB[1]
                        kb.op(DVE, lambda: nc.vector.tensor_tensor(v3(a1), v3(bim), cv, ALU.mult), reads=[bim, T["ctab"]], writes=[a1])
                        kb.op(DVE, lambda: nc.vector.tensor_tensor(v3(a2), v3(bre), sv, ALU.mult), reads=[bre, T["stab"]], writes=[a2])
                        kb.op(DVE, lambda: nc.vector.tensor_tensor(XR[1][q][:, 0:W], a1[:, 0:W], a2[:, 0:W], ALU.subtract),
                              reads=[a1, a2], writes=[XR[1][q]])
                    j0 = 4 * m + 2 * half
                    if sample_h0 is None:
                        nseg = W // 64
                        for sgi in range(nseg):
                            for q in range(2):
                                j = j0 + q
                                for ri in range(2):
                                    kb.op(DVE, lambda q=q, j=j, ri=ri, sgi=sgi: nc.vector.tensor_tensor_scan(
                                        GG[ri][q][:, sgi * 64:(sgi + 1) * 64], V(T["rho"], 0, 128, j, [(0, 64)]),
                                        XR[ri][q][:, sgi * 64:(sgi + 1) * 64], T["car"][ri][:, j:j + 1], ALU.mult, ALU.add),
                                        reads=[T["rho"], XR[ri][q], T["car"][ri]], writes=[GG[ri][q]])
                            col = sgi * 64 + 63
                            for ri in range(2):
                                for q in range(2):
                                    kb.op(DVE, lambda ri=ri, q=q: nc.vector.tensor_copy(cfx[ri][:, q:q + 1], GG[ri][q][:, col:col + 1]),
                                          reads=[GG[ri][q]], writes=[cfx[ri]])
                            cc = V(T["ctab"], 0, 128, j0 * 64 + 63, [(64, 2)])
                            ss = V(T["stab"], 0, 128, j0 * 64 + 63, [(64, 2)])
                            kb.op(DVE, lambda: nc.vector.tensor_tensor(cfx[2][:], cfx[0][:], cc, ALU.mult), reads=[cfx[0], T["ctab"]], writes=[cfx[2]])
                            kb.op(DVE, lambda: nc.vector.tensor_tensor(cfx[3][:], cfx[1][:], ss, ALU.mult), reads=[cfx[1], T["stab"]], writes=[cfx[3]])
                            kb.op(DVE, lambda: nc.vector.tensor_tensor(T["car"][0][:, j0:j0 + 2], cfx[2][:], cfx[3][:], ALU.subtract),
                                  reads=[cfx[2], cfx[3]], writes=[T["car"][0]])
                            kb.op(DVE, lambda: nc.vector.tensor_tensor(cfx[2][:], cfx[0][:], ss, ALU.mult), reads=[cfx[0], T["stab"]], writes=[cfx[2]])
                            kb.op(DVE, lambda: nc.vector.tensor_tensor(cfx[3][:], cfx[1][:], cc, ALU.mult), reads=[cfx[1], T["ctab"]], writes=[cfx[3]])
                            kb.op(DVE, lambda: nc.vector.tensor_tensor(T["car"][1][:, j0:j0 + 2], cfx[2][:], cfx[3][:], ALU.add),
                                  reads=[cfx[2], cfx[3]], writes=[T["car"][1]])
                    else:
                        for q in range(2):
                            j = j0 + q
                            for ri in range(2):
                                x9, g9 = tA[ri], tB[ri]
                                kb.dma(SP, V(x9, 0, 128, 0, [(9, NSEQ)]), st_ssm.ap()[sample_h0, ri][:, j, :], writes=[x9], slow=True)
                                kb.op(DVE, lambda: nc.vector.tensor_copy(V(x9, 0, 128, 1, [(9, NSEQ), (1, 8)]),
                                                                         V(XR[ri][q], 0, 128, 0, [(8, NSEQ), (1, 8)])),
                                      reads=[XR[ri][q]], writes=[x9])
                                if ri == 0:
                                    kb.op(DVE, lambda: nc.vector.tensor_copy(V(m9, 0, 128, 0, [(9, NSEQ), (1, 9)]), V(T["rho9"], 0, 128, j * 9, [(0, NSEQ), (1, 9)])),
                                          reads=[T["rho9"]], writes=[m9])
                                kb.op(DVE, lambda: nc.vector.tensor_tensor_scan(g9[:, 0:NSEQ * 9], m9[:, 0:NSEQ * 9],
                                                                                x9[:, 0:NSEQ * 9], 0.0, ALU.mult, ALU.add),
                                      reads=[m9, x9], writes=[g9])
                                kb.op(DVE, lambda: nc.vector.tensor_copy(V(GG[ri][q], 0, 128, 0, [(8, NSEQ), (1, 8)]),
                                                                         V(g9, 0, 128, 1, [(9, NSEQ), (1, 8)])),
                                      reads=[g9], writes=[GG[ri][q]])
                    for q in range(2):
                        jj = 2 * half + q
                        j = j0 + q
                        cv = V(T["ctab"], 0, 128, j * 64, [(0, nrep), (1, L)])
                        sv = V(T["stab"], 0, 128, j * 64, [(0, nrep), (1, L)])
                        a1, a2 = tA[0], tB[0]
                        kb.op(POOL, lambda: nc.gpsimd.tensor_tensor(v3(a1), v3(GG[0][q]), cv, ALU.mult), reads=[GG[0][q], T["ctab"]], writes=[a1])
                        kb.op(POOL, lambda: nc.gpsimd.tensor_tensor(v3(a2), v3(GG[1][q]), sv, ALU.mult), reads=[GG[1][q], T["stab"]], writes=[a2])
                        kb.op(POOL, lambda: nc.gpsimd.tensor_tensor(HB[0][jj][:, 0:W], a1[:, 0:W], a2[:, 0:W], ALU.subtract),
                              reads=[a1, a2], writes=[HB[0][jj]])
                        if sample_h0 is not None:
                            kb.op(POOL, lambda: nc.gpsimd.tensor_tensor(sso[0][:, :], V(a1, 0, 128, 7, [(8, NSEQ)]), V(a2, 0, 128, 7, [(8, NSEQ)]), ALU.subtract),
                                  reads=[a1, a2], writes=[sso[0]])
                            kb.dma(SP, o_ssm_s.ap()[sample_h0, 0][:, j, :], sso[0][:, :], reads=[sso[0]], is_out=True)
                        a1, a2 = tA[1], tB[1]
                        kb.op(POOL, lambda: nc.gpsimd.tensor_tensor(v3(a1), v3(GG[0][q]), sv, ALU.mult), reads=[GG[0][q], T["stab"]], writes=[a1])
                        kb.op(POOL, lambda: nc.gpsimd.tensor_tensor(v3(a2), v3(GG[1][q]), cv, ALU.mult), reads=[GG[1][q], T["ctab"]], writes=[a2])
                        kb.op(POOL, lambda: nc.gpsimd.tensor_tensor(HB[1][jj][:, 0:W], a1[:, 0:W], a2[:, 0:W], ALU.add),
                              reads=[a1, a2], writes=[HB[1][jj]])
                        if sample_h0 is not None:
                            kb.op(POOL, lambda: nc.gpsimd.tensor_tensor(sso[1][:, :], V(a1, 0, 128, 7, [(8, NSEQ)]), V(a2, 0, 128, 7, [(8, NSEQ)]), ALU.add),
                                  reads=[a1, a2], writes=[sso[1]])
                            kb.dma(SP, o_ssm_s.ap()[sample_h0, 1][:, j, :], sso[1][:, :], reads=[sso[1]], is_out=True)
                        for ri in range(2):
                            cpairs.append((T["Cl"][ri][:, j, :], HB[ri][jj][:, 0:W], [T["Cl"][ri], HB[ri][jj]]))
                mm_group(bky[:, 0:W], cpairs, bky, [])
                ys = U32[m]
                kb.op(DVE, lambda: nc.vector.scalar_tensor_tensor(ys[:, 0:W], U32[m][:, 0:W], T["dsk"][:, m:m + 1], bky[:, 0:W], ALU.mult, ALU.add),
                      reads=[U32[m], T["dsk"], bky], writes=[ys])
                kb.op(ACT, lambda: nc.scalar.activation(U32[m][:, 0:W], ys[:, 0:W], AF.Gelu_apprx_tanh), reads=[ys], writes=[U32[m]])
                kb.op(ACT, lambda: nc.scalar.copy(GEL[m][:, 0:W], U32[m][:, 0:W]), reads=[U32[m]], writes=[GEL[m]])
            for n in range(4):
                wb = load_w(WB["w_glu"].ap()[li_cur[0], n], 4, 128)
                bk = kb.bank()
                mm_group(bk[:, 0:W], [(wb[:, k, 0:128], GEL[k][:, 0:W], [wb, GEL[k]]) for k in range(4)], bk, [])
                sg = SG[n % 2]
                kb.op(DVE, lambda: nc.vector.tensor_scalar(sg[:, 0:W], bk[:, 0:W], T["bgl"][:, n:n + 1], None, ALU.add), reads=[bk, T["bgl"]], writes=[sg])
                kb.op(ACT, lambda: nc.scalar.activation(sg[:, 0:W], sg[:, 0:W], AF.Sigmoid), reads=[sg], writes=[sg])
                kb.op(DVE, lambda: nc.vector.tensor_tensor(MIX[n][:, 0:W], U32[n][:, 0:W], sg[:, 0:W], ALU.mult),
                      reads=[U32[n], sg], writes=[MIX[n]])

        li_cur = [0]

        def in_proj(li, W, want_v_tok_blocks):
            for n in range(4):
                wb = load_w(WB["w_in"].ap()[li, n], 8, 128)
                if n == 0:
                    kb.chk("ip_w")
                bk = kb.bank()
                _mv = _os.environ.get("KMM", "")
                if _mv == "k":
                    mm_group(bk[:, 0:W], [(ones_bf[:], H[k][:, 0:W], [ones_bf, H[k]]) for k in range(8)], bk, [])
                elif _mv == "m":
                    for k in range(8):
                        kb.op(DVE, lambda k=k: nc.vector.tensor_copy(H[k][:, 0:W], X[k][:, 0:W]), reads=[X[k]], writes=[H[k]])
                    mm_group(bk[:, 0:W], [(wb[:, k, 0:128], H[k][:, 0:W], [wb, H[k]]) for k in range(8)], bk, [])
                elif _mv == "q":
                    for k in range(8):
                        kb.op(DVE, lambda k=k: nc.vector.tensor_copy(P16[k][:, 0:W], X[k][:, 0:W]), reads=[X[k]], writes=[P16[k]])
                    mm_group(bk[:, 0:W], [(wb[:, k, 0:128], P16[k][:, 0:W], [wb, P16[k]]) for k in range(8)], bk, [])
                elif _mv == "n":
                    for k in range(8):
                        kb.op(ACT, lambda k=k: nc.scalar.copy(H[k][:, 0:W], X[k][:, 0:W]), reads=[X[k]], writes=[H[k]])
                    mm_group(bk[:, 0:W], [(wb[:, k, 0:128], H[k][:, 0:W], [wb, H[k]]) for k in range(8)], bk, [])
                elif _mv == "l":
                    mm_group(bk[:, 0:W], [(wb[:, k, 0:128], sq[k][:, 0:W], [wb, sq[k]]) for k in range(8)], bk, [])
                else:
                    mm_group(bk[:, 0:W], [(wb[:, k, 0:128], H[k][:, 0:W], [wb, H[k]]) for k in range(8)], bk, [])
                if n == 0:
                    kb.chk("ip_mm")
                kb.op(DVE, lambda: nc.vector.tensor_copy(UT[n][:, 0:W], bk[:, 0:W]), reads=[bk], writes=[UT[n]])
                if n == 0:
                    kb.chk("ip_act")
                import os
                tv = os.environ.get("KVAR", "")
                if tv == "a":
                    kb.op(DVE, lambda: nc.vector.tensor_copy(P32[5][:, 0:W], bk[:, 0:W]), reads=[bk], writes=[P32[5]])
                elif tv == "c":
                    kb.op(DVE, lambda: nc.vector.tensor_scalar(U32[n][:, 0:W], bk[:, 0:W], 1.0, None, ALU.mult), reads=[bk], writes=[U32[n]])
                elif tv == "d":
                    kb.op(DVE, lambda: nc.vector.tensor_copy(U32[n][:, 0:W], bk[:, 0:W]), reads=[bk], writes=[U32[n]])
                elif tv == "g":
                    ob = kb.banks[(kb.psum_rr - 2) % 8]
                    kb.op(DVE, lambda: nc.vector.tensor_copy(P32[5][:, 0:W], ob[:, 0:W]), reads=[ob], writes=[P32[5]])
                elif tv == "g2":
                    ob = kb.banks[(kb.psum_rr - 2) % 8]
                    kb.op(DVE, lambda: nc.vector.tensor_copy(P32[5][:, 0:W], ob[:, 0:W]), reads=[ob, bk], writes=[P32[5]])
                elif tv == "j":
                    ob = kb.banks[(kb.psum_rr - 2) % 8]
                    kb.op(PE, lambda: nc.tensor.matmul(ob[:, 0:W], ones_bf[:], H[0][:, 0:W], start=True, stop=True), reads=[ones_bf, H[0]], writes=[ob])
                    kb.op(DVE, lambda: nc.vector.tensor_copy(U32[n][:, 0:W], bk[:, 0:W]), reads=[bk], writes=[U32[n]])
                elif tv == "h":
                    kb.op(DVE, lambda: nc.vector.tensor_copy(P32[5][0:64, 0:W], bk[0:64, 0:W]), reads=[bk], writes=[P32[5]])
                elif tv == "i":
                    kb.op(DVE, lambda: nc.vector.tensor_copy(P32[5][:, 0:64], bk[:, 0:64]), reads=[bk], writes=[P32[5]])
                elif tv == "e":
                    kb.op(DVE, lambda: nc.vector.tensor_copy(U32[n][:, 0:W], P32[6][:, 0:W]), reads=[P32[6]], writes=[U32[n]])
                elif tv == "f":
                    kb.op(DVE, lambda: nc.vector.tensor_copy(P32[5][:, 0:W], X[0][:, 0:W]), reads=[X[0]], writes=[P32[5]])
                elif tv == "b":
                    kb.op(DVE, lambda: nc.vector.tensor_copy(U32[n][:, 0:W], X[0][:, 0:W]), reads=[X[0]], writes=[U32[n]])
                else:
                    kb.op(DVE, lambda: nc.vector.tensor_copy(U32[n][:, 0:W], bk[:, 0:W]), reads=[bk], writes=[U32[n]])
                if n == 0:
                    kb.chk("ip_u0")
            kb.chk("ip_u")
            for n in range(4):
                wb = load_w(WB["w_in"].ap()[li, 4 + n], 8, 128)
                bk = kb.bank()
                mm_group(bk[:, 0:W], [(wb[:, k, 0:128], H[k][:, 0:W], [wb, H[k]]) for k in range(8)], bk, [])
                kb.op(DVE, lambda: nc.vector.tensor_copy(QT[0][n][0:64, 0:W], bk[0:64, 0:W]), reads=[bk], writes=[QT[0][n]])
                kb.op(DVE, lambda: nc.vector.tensor_copy(QT[1][n][64:128, 0:W], bk[64:128, 0:W]), reads=[bk], writes=[QT[1][n]])

        def attn_prompt(li, W, blk0):
            A = ATT[li]
            nb = W // 128
            for kv in range(2):
                wb = load_w(WB["w_in"].ap()[li, 8 + kv], 8, 128)
                bk = kb.bank()
                mm_group(bk[:, 0:W], [(wb[:, k, 0:128], H[k][:, 0:W], [wb, H[k]]) for k in range(8)], bk, [])
                kb.op(DVE, lambda: nc.vector.tensor_copy(A["KK"][kv][:, 128:128 + W], bk[:, 0:W]), reads=[bk], writes=[A["KK"][kv]])
                if kv == 0:
                    kb.op(DVE, lambda: nc.vector.tensor_copy(A["kT32"][0:64, :], bk[0:64, W - 128:W]), reads=[bk], writes=[A["kT32"]])
                else:
                    kb.op(DVE, lambda: nc.vector.tensor_copy(A["kT32"][64:128, :], bk[64:128, W - 128:W]), reads=[bk], writes=[A["kT32"]])
            kb.chk("at_k")
            wv = load_w(WB["w_in"].ap()[li, 10], 8, 128)
            for b in range(nb):
                vz = A["Vz"][(blk0 + b) % NVB]
                bk = kb.bank()
                mm_group(bk[:, 0:128], [(H[k][:, b * 128:(b + 1) * 128], wv[:, k, 0:128], [H[k], wv]) for k in range(8)], bk, [])
                kb.op(DVE, lambda: nc.vector.tensor_copy(vz[:, 64:128], bk[:, 0:64]), reads=[bk], writes=[vz])
                kb.op(DVE, lambda: nc.vector.tensor_copy(vz[:, 192:256], bk[:, 64:128]), reads=[bk], writes=[vz])
                if b == nb - 1:
                    kb.op(DVE, lambda: nc.vector.tensor_copy(A["v32"][:, :], bk[:, 0:128]), reads=[bk], writes=[A["v32"]])
            kb.chk("at_v")
            for b in range(nb):
                gb = blk0 + b
                has_prev = gb >= 1
                q0 = b * 128
                pts = []
                for c in range(4):
                    kv = c // 2
                    bk = kb.bank()
                    for e in range(2):
                        kb.op(PE, lambda e=e, bk=bk: nc.tensor.matmul(bk[:, e * 128:(e + 1) * 128],
                                                                      A["KK"][kv][:, 128 + q0:256 + q0],
                                                                      QT[e][c][:, q0:q0 + 128], start=True, stop=True),
                              reads=[A["KK"][kv], QT[e][c]], writes=[bk], inc=(e == 1 and not has_prev))
                    if has_prev:
                        for e in range(2):
                            kb.op(PE, lambda e=e, bk=bk: nc.tensor.matmul(bk[:, 256 + e * 128:256 + (e + 1) * 128],
                                                                          A["KK"][kv][:, q0:q0 + 128],
                                                                          QT[e][c][:, q0:q0 + 128], start=True, stop=True),
                                  reads=[A["KK"][kv], QT[e][c]], writes=[bk], inc=(e == 1))
                    ncol = 512 if has_prev else 256
                    ex = EX[c % 2]
                    kb.chk("at_mm")
                    kb.op(DVE, lambda bk=bk, ex=ex: nc.vector.tensor_copy(ex[:, 0:ncol], bk[:, 0:ncol]), reads=[bk], writes=[ex])
                    kb.chk("at_cp")
                    kb.op(ACT, lambda ex=ex: nc.scalar.activation(ex[:, 0:ncol], ex[:, 0:ncol], AF.Exp, scale=0.125), reads=[ex], writes=[ex])
                    kb.chk("at_ex")
                    pt = PT[c]
                    kb.op(DVE, lambda ex=ex, pt=pt, c=c: nc.vector.tensor_tensor(pt[:, 0:ncol], ex[:, 0:ncol], Etab[c][:, 0:ncol], ALU.mult),
                          reads=[ex, Etab[c]], writes=[pt])
                    pts.append(pt)
                kb.chk("at_s")
                bn, bd = kb.bank(), kb.bank()
                vcur = A["Vz"][gb % NVB]
                vprev = A["Vz"][(gb - 1) % NVB]
                ocur = Oz0 if gb == 0 else Oz
                oprev = Oz0 if gb == 1 else Oz
                for c in range(4):
                    kv = c // 2
                    npairs, dpairs = [], []
                    for e in range(2):
                        vs = (64 + 128 * kv, 192 + 128 * kv) if e == 0 else (128 * kv, 128 + 128 * kv)
                        osl = (64, 192) if e == 0 else (0, 128)
                        npairs.append((vcur[:, vs[0]:vs[1]], pts[c][:, e * 128:(e + 1) * 128], [vcur, pts[c]]))
                        dpairs.append((ocur[:, osl[0]:osl[1]], pts[c][:, e * 128:(e + 1) * 128], [ocur, pts[c]]))
                        if has_prev:
                            npairs.append((vprev[:, vs[0]:vs[1]], pts[c][:, 256 + e * 128:256 + (e + 1) * 128], [vprev, pts[c]]))
                            dpairs.append((oprev[:, osl[0]:osl[1]], pts[c][:, 256 + e * 128:256 + (e + 1) * 128], [oprev, pts[c]]))
                    mm_group(bn[:, c * 128:(c + 1) * 128], npairs, bn, [])
                    mm_group(bd[:, c * 128:(c + 1) * 128], dpairs, bd, [])
                kb.chk("at_pv")
                for c in range(4):
                    kb.op(DVE, lambda bd=bd, c=c: nc.vector.tensor_scalar(dn_t[:, c * 128:(c + 1) * 128], bd[:, c * 128:(c + 1) * 128], EsT[li][:, c:c + 1], None, ALU.add),
                          reads=[bd, EsT[li]], writes=[dn_t])
                kb.op(DVE, lambda: nc.vector.reciprocal(dn_t[:, 0:512], dn_t[:, 0:512]), reads=[dn_t], writes=[dn_t])
                for c in range(4):
                    kb.op(DVE, lambda c=c, bn=bn: nc.vector.tensor_tensor(MIX[4 + c][:, q0:q0 + 128], bn[:, c * 128:(c + 1) * 128],
                                                                          dn_t[:, c * 128:(c + 1) * 128], ALU.mult),
                          reads=[bn, dn_t], writes=[MIX[4 + c]])
            for kv in range(2):
                kb.op(ACT, lambda kv=kv: nc.scalar.copy(A["KK"][kv][:, 0:128], A["KK"][kv][:, W:W + 128]),
                      reads=[A["KK"][kv]], writes=[A["KK"][kv]])

        def out_proj(dram_w, W, nk, src, after=None):
            wb_next = load_w(dram_w[0], nk, 128)
            for n in range(8):
                wb = wb_next
                if n + 1 < 8:
                    wb_next = load_w(dram_w[n + 1], nk, 128)
                bk = kb.bank()
                mm_group(bk[:, 0:W], [(wb[:, k, 0:128], src[k][:, 0:W], [wb, src[k]]) for k in range(nk)], bk, [])
                kb.op(DVE, lambda n=n, bk=bk: nc.vector.tensor_tensor(X[n][:, 0:W], X[n][:, 0:W], bk[:, 0:W], ALU.add),
                      reads=[X[n], bk], writes=[X[n]])

        def conformer(lo, W, nseq, Tt, halo, zero_pad_cols=0):
            C = CONV[lo]
            ext = 30 + Tt
            wb_next = load_w(WB["w_pw1"].ap()[lo, 0], 8, 256)
            for c in range(8):
                wb = wb_next
                if c + 1 < 8:
                    wb_next = load_w(WB["w_pw1"].ap()[lo, c + 1], 8, 256)
                ba, bg = kb.bank(), kb.bank()
                mm_group(ba[:, 0:W], [(wb[:, k, 0:128], H[k][:, 0:W], [wb, H[k]]) for k in range(8)], ba, [])
                mm_group(bg[:, 0:W], [(wb[:, k, 128:256], H[k][:, 0:W], [wb, H[k]]) for k in range(8)], bg, [])
                sg = SG[c % 2]
                gl = GLW[c % 2]
                kb.op(DVE, lambda: nc.vector.tensor_copy(sg[:, 0:W], bg[:, 0:W]), reads=[bg], writes=[sg])
                kb.op(ACT, lambda: nc.scalar.activation(sg[:, 0:W], sg[:, 0:W], AF.Sigmoid), reads=[sg], writes=[sg])
                if halo is not None:
                    kb.op(ACT, lambda: nc.scalar.copy(V(gl, 0, 128, 0, [(ext, nseq), (1, 30)]), V(halo, 0, 128, c * nseq * 30, [(30, nseq), (1, 30)])),
                          reads=[halo], writes=[gl])
                else:
                    kb.dma(SP, V(gl, 0, 128, 0, [(ext, nseq), (1, 30)]), st_conv.ap()[lo, c * 128:(c + 1) * 128, :, :], writes=[gl])
                kb.op(DVE, lambda: nc.vector.tensor_tensor(V(gl, 0, 128, 30, [(ext, nseq), (1, Tt)]),
                                                           V(ba, 0, 128, 0, [(Tt, nseq), (1, Tt)]),
                                                           V(sg, 0, 128, 0, [(Tt, nseq), (1, Tt)]), ALU.mult),
                      reads=[ba, sg], writes=[gl])
                if halo is not None:
                    kb.op(ACT, lambda: nc.scalar.copy(V(halo, 0, 128, c * nseq * 30, [(30, nseq), (1, 30)]), V(gl, 0, 128, Tt, [(ext, nseq), (1, 30)])),
                          reads=[gl], writes=[halo])
                else:
                    kb.dma(SP, o_conv_s.ap()[lo, c * 128:(c + 1) * 128, :, :], V(gl, 0, 128, Tt, [(ext, nseq), (1, 30)]), reads=[gl], is_out=True)
                eng_e, eng = (DVE, nc.vector)
                y = LNY[c]
                yv = V(y, 0, 128, 0, [(Tt, nseq), (1, Tt)])
                kb.op(eng_e, lambda: eng.tensor_scalar(yv, V(gl, 0, 128, 0, [(ext, nseq), (1, Tt)]), C["wdw"][:, c, 0:1], C["cv"][:, 0, c:c + 1],
                                                       ALU.mult, ALU.add), reads=[gl, C["wdw"], C["cv"]], writes=[y])
                for k in range(1, 31):
                    kb.op(eng_e, lambda k=k: eng.scalar_tensor_tensor(yv, V(gl, 0, 128, k, [(ext, nseq), (1, Tt)]), C["wdw"][:, c, k:k + 1], yv,
                                                                      ALU.mult, ALU.add), reads=[gl, C["wdw"], y], writes=[y])
            bm, b2 = kb.bank(), kb.bank()
            mm_group(bm[:, 0:W], [(ones_f[:], LNY[c][:, 0:W], [LNY[c]]) for c in range(8)], bm, [ones_f])
            kb.op(DVE, lambda: nc.vector.tensor_scalar(mu_t[:, 0:W], bm[:, 0:W], 1.0 / D, None, ALU.mult), reads=[bm], writes=[mu_t])
            for c in range(8):
                kb.op(DVE, lambda c=c: nc.vector.tensor_tensor(LNY[c][:, 0:W], LNY[c][:, 0:W], mu_t[:, 0:W], ALU.subtract),
                      reads=[LNY[c], mu_t], writes=[LNY[c]])
                sqf = SQF[c % 2]
                kb.op(ACT, lambda c=c, sqf=sqf: nc.scalar.activation(sqf[:, 0:W], LNY[c][:, 0:W], AF.Square), reads=[LNY[c]], writes=[sqf])
                kb.op(PE, lambda c=c, sqf=sqf: nc.tensor.matmul(b2[:, 0:W], ones_f[:], sqf[:, 0:W], start=(c == 0), stop=(c == 7)),
                      reads=[ones_f, sqf], writes=[b2])
            kb.op(DVE, lambda: nc.vector.tensor_scalar(rs[:, 0:W], b2[:, 0:W], 1.0 / D, EPS, ALU.mult, ALU.add), reads=[b2], writes=[rs])
            kb.op(ACT, lambda: nc.scalar.activation(rs[:, 0:W], rs[:, 0:W], AF.Sqrt), reads=[rs], writes=[rs])
            kb.op(DVE, lambda: nc.vector.reciprocal(rinv[:, 0:W], rs[:, 0:W]), reads=[rs], writes=[rinv])
            for c in range(8):
                kb.op(DVE, lambda c=c: nc.vector.scalar_tensor_tensor(LNY[c][:, 0:W], LNY[c][:, 0:W], C["cv"][:, 1, c:c + 1], rinv[:, 0:W],
                                                                      ALU.mult, ALU.mult), reads=[LNY[c], C["cv"], rinv], writes=[LNY[c]])
                kb.op(ACT, lambda c=c: nc.scalar.activation(LNY[c][:, 0:W], LNY[c][:, 0:W], AF.Silu, bias=C["cv"][:, 2, c:c + 1], scale=1.0),
                      reads=[LNY[c], C["cv"]], writes=[LNY[c]])
                kb.op(DVE, lambda c=c: nc.vector.tensor_copy(LNS[c][:, 0:W], LNY[c][:, 0:W]), reads=[LNY[c]], writes=[LNS[c]])
            out_proj(WB["w_pw2"].ap()[lo], W, 8, LNS)
            if zero_pad_cols:
                for c in range(8):
                    kb.op(DVE, lambda c=c: nc.vector.memset(X[c][:, 0:zero_pad_cols], 0.0), writes=[X[c]])

        ones_f = kb.sb("ones_f", [128, 128], F32)
        kb.op(DVE, lambda: nc.vector.memset(ones_f[:], 1.0), writes=[ones_f])

        def ffn_down(l, W):
            for n in range(8):
                wd0 = load_wdn(WB["w_dn"].ap()[l, n, 0])
                wd1 = load_wdn(WB["w_dn"].ap()[l, n, 1])
                bk = kb.bank()
                mm_group(bk[:, 0:W], [((wd0 if j < 11 else wd1)[:, j % 11, :], YF[j][:, 0:W], [wd0 if j < 11 else wd1, YF[j]]) for j in range(NJ)], bk, [])
                kb.op(DVE, lambda n=n, bk=bk: nc.vector.tensor_tensor(X[n][:, 0:W], X[n][:, 0:W], bk[:, 0:W], ALU.add),
                      reads=[X[n], bk], writes=[X[n]])

        def ffn_up(l, W, nseq, Tt, halo):
            Fd = FFN[l]
            ext = 2 + Tt
            wb_next = load_w(WB["w_up"].ap()[l, 0], 8, 256)
            for j in range(NJ):
                wb = wb_next
                if j + 1 < NJ:
                    wb_next = load_w(WB["w_up"].ap()[l, j + 1], 8, 256)
                bg, bu = kb.bank(), kb.bank()
                mm_group(bg[:, 0:W], [(wb[:, k, 0:128], H[k][:, 0:W], [wb, H[k]]) for k in range(8)], bg, [])
                mm_group(bu[:, 0:W], [(wb[:, k, 128:256], H[k][:, 0:W], [wb, H[k]]) for k in range(8)], bu, [])
                gr = GR[j % 3]
                kb.op(DVE, lambda: nc.vector.tensor_copy(V(gr, 0, 128, 2, [(ext, nseq), (1, Tt)]), V(bg, 0, 128, 0, [(Tt, nseq), (1, Tt)])),
                      reads=[bg], writes=[gr])
                if halo is not None:
                    kb.op(ACT, lambda: nc.scalar.copy(V(gr, 0, 128, 0, [(ext, nseq), (1, 2)]), V(halo, 0, 128, j * nseq * 2, [(2, nseq), (1, 2)])),
                          reads=[halo], writes=[gr])
                else:
                    kb.dma(SP, V(gr, 0, 128, 0, [(ext, nseq), (1, 2)]), st_ffn.ap()[l, j * 128:(j + 1) * 128, :, :], writes=[gr], slow=True)
                acc = ACC[j % 3]
                av = V(acc, 0, 128, 0, [(Tt, nseq), (1, Tt)])
                kb.op(DVE, lambda: nc.vector.tensor_scalar(av, V(gr, 0, 128, 0, [(ext, nseq), (1, Tt)]), Fd["cw"][:, j, 0:1], Fd["cb"][:, j:j + 1],
                                                           ALU.mult, ALU.add), reads=[gr, Fd["cw"], Fd["cb"]], writes=[acc])
                for k in (1, 2):
                    kb.op(DVE, lambda k=k: nc.vector.scalar_tensor_tensor(av, V(gr, 0, 128, k, [(ext, nseq), (1, Tt)]), Fd["cw"][:, j, k:k + 1], av,
                                                                          ALU.mult, ALU.add), reads=[gr, Fd["cw"], acc], writes=[acc])
                if halo is not None:
                    kb.op(ACT, lambda: nc.scalar.copy(V(halo, 0, 128, j * nseq * 2, [(2, nseq), (1, 2)]), V(gr, 0, 128, Tt, [(ext, nseq), (1, 2)])),
                          reads=[gr], writes=[halo])
                else:
                    kb.dma(SP, o_ffn_s.ap()[l, j * 128:(j + 1) * 128, :, :], V(gr, 0, 128, Tt, [(ext, nseq), (1, 2)]), reads=[gr], is_out=True, slow=True)
                kb.op(ACT, lambda: nc.scalar.activation(acc[:, 0:W], acc[:, 0:W], AF.Gelu_apprx_tanh), reads=[acc], writes=[acc])
                kb.op(DVE, lambda: nc.vector.tensor_tensor(YF[j][:, 0:W], acc[:, 0:W], bu[:, 0:W], ALU.mult), reads=[acc, bu], writes=[YF[j]])

        chunks = [(0, 128)] + [(128 + 512 * i, 512) for i in range(16)]
        chunks = chunks[:nchunk_prompt]
        for ci, (t0, W) in enumerate(chunks):
            blk0 = t0 // 128
            for c in range(8):
                kb.dma(SP, X[c][:, 0:W], xT_p.ap()[c * 128:(c + 1) * 128, t0:t0 + W], writes=[X[c]])
            for l in range(4):
                rmsnorm(X, l, H, W)
                kb.chk("n0")
                if l % 2 == 0:
                    li = l // 2
                    li_cur[0] = li
                    in_proj(li, W, None)
                    kb.chk("inproj")
                    s5_core(S5[li], W, W // 64, 64)
                    kb.chk("s5c")
                    attn_prompt(li, W, blk0)
                    kb.chk("att")
                    if ci == dbg_ci and li == 0:
                        dump16(MIX, W)
                        dump16(UT + QT[0], W)
                        dump(Etab + Etab, 512)
                    out_proj(WB["w_out"].ap()[li], W, 8, MIX)
                    if ci == dbg_ci:
                        dump(X, W)
                else:
                    lo = l // 2
                    conformer(lo, W, 1, W, CONV[lo]["halo"], zero_pad_cols=(NPAD if ci == 0 else 0))
                    if ci == dbg_ci:
                        dump(X, W)
                rmsnorm(X, 4 + l, H, W)
                ffn_up(l, W, 1, W, FFN[l]["halo"])
                ffn_down(l, W)
                if ci == dbg_ci:
                    dump(X, W)
                if ci == 0 and l % 2 == 1:
                    pass
            for c in range(8):
                pass
            YO = [P32[c] for c in range(8)]
            rmsnorm(X, 8, None, W, out_f32=YO)
            for c in range(8):
                kb.dma(SP, yT_p.ap()[c * 128:(c + 1) * 128, t0:t0 + W], YO[c][:, 0:W], reads=[YO[c]], is_out=True)
        if nchunk_prompt == 17:
            for li in range(2):
                for ri in range(2):
                    kb.dma(SP, o_ssm_p.ap()[li, ri], S5[li]["car"][ri][:], reads=[S5[li]["car"][ri]], is_out=True)
                kb.dma(SP, o_kT_p.ap()[li], ATT[li]["kT32"][:], reads=[ATT[li]["kT32"]], is_out=True)
                kb.dma(SP, o_v_p.ap()[li], ATT[li]["v32"][:], reads=[ATT[li]["v32"]], is_out=True)
                for c in range(8):
                    kb.dma(SP, o_conv_p.ap()[li, c * 128:(c + 1) * 128, :], CONV[li]["halo"][:, c, :], reads=[CONV[li]["halo"]], is_out=True)
            for l in range(4):
                kb.dma(SP, o_ffn_p.ap()[l].rearrange("(j p) t -> p j t", p=128), FFN[l]["halo"][:], reads=[FFN[l]["halo"]], is_out=True)

        def attn_sample(li):
            A = ATT[li]
            W = 128
            for kv in range(2):
                wb = load_w(WB["w_in"].ap()[li, 8 + kv], 8, 128)
                bk = kb.bank()
                mm_group(bk[:, 0:W], [(wb[:, k, 0:128], H[k][:, 0:W], [wb, H[k]]) for k in range(8)], bk, [])
                kb.op(DVE, lambda: nc.vector.tensor_copy(KKs[kv][:, :], bk[:, 0:W]), reads=[bk], writes=[KKs[kv]])
                kb.op(DVE, lambda: nc.vector.tensor_copy(A["kT32"][64 * kv:64 * kv + 64, :], bk[64 * kv:64 * kv + 64, 0:W]), reads=[bk], writes=[A["kT32"]])
            wv = load_w(WB["w_in"].ap()[li, 10], 8, 128)
            bk = kb.bank()
            mm_group(bk[:, 0:128], [(H[k][:, 0:128], wv[:, k, 0:128], [H[k], wv]) for k in range(8)], bk, [])
            kb.op(DVE, lambda: nc.vector.tensor_copy(vbf[:, :], bk[:, 0:128]), reads=[bk], writes=[vbf])
            kb.op(DVE, lambda: nc.vector.tensor_copy(A["v32"][:, :], bk[:, 0:128]), reads=[bk], writes=[A["v32"]])
            kb.dma(SP, o_kT_s.ap()[li][:, :, 0:120], st_kT.ap()[li][:, :, 8:128], is_out=True, slow=True)
            kb.dma(SP, o_v_s.ap()[li][:, 0:120, :], st_v.ap()[li][:, 8:128, :], is_out=True)
            kb.dma(SP, o_kT_s.ap()[li][:, :, 120:128].rearrange("s p t -> p s t"), V(A["kT32"], 0, 128, 0, [(8, NSEQ), (1, 8)]),
                   reads=[A["kT32"]], is_out=True, slow=True)
            for s_ in range(NSEQ):
                kb.dma(SP, o_v_s.ap()[li, s_][120:128, :], A["v32"][8 * s_:8 * s_ + 8, :], reads=[A["v32"]], is_out=True)
            bn, bd = kb.bank(), kb.bank()
            sbanks = [kb.bank() for _ in range(4)]
            for sq_i in range(NSEQ):
                kx = [KX[kv][sq_i % 2] for kv in range(2)]
                vz, vb = VZS[sq_i % 2], VBS[sq_i % 2]
                for kv in range(2):
                    for hf in range(2):
                        kb.dma(POOL, kx[kv][64 * hf:64 * hf + 64, 0:128], st_kT.ap()[li, sq_i][64 * kv:64 * kv + 64, :], writes=[kx[kv]])
                    kb.op(ACT, lambda kv=kv: nc.scalar.copy(kx[kv][:, 128:136], KKs[kv][:, 8 * sq_i:8 * sq_i + 8]), reads=[KKs[kv]], writes=[kx[kv]])
                    kb.dma(POOL, vz[0:120, 64 + 128 * kv:128 + 128 * kv], st_v.ap()[li, sq_i][8:128, 64 * kv:64 * kv + 64], writes=[vz])
                    kb.dma(POOL, vb[0:8, 64 + 128 * kv:128 + 128 * kv], st_v.ap()[li, sq_i][0:8, 64 * kv:64 * kv + 64], writes=[vb])
                    kb.dma(POOL, vz[120:128, 64 + 128 * kv:128 + 128 * kv], vbf[8 * sq_i:8 * sq_i + 8, 64 * kv:64 * kv + 64], reads=[vbf], writes=[vz])
                ba, bb = sbanks[2 * (sq_i % 2)], sbanks[2 * (sq_i % 2) + 1]
                for c in range(4):
                    kv = c // 2
                    for e in range(2):
                        col = c * 16 + e * 8
                        kb.op(PE, lambda: nc.tensor.matmul(ba[:, col:col + 8], kx[kv][:, 8:136],
                                                           QT[e][c][:, 8 * sq_i:8 * sq_i + 8], start=True, stop=True),
                              reads=[kx[kv], QT[e][c]], writes=[ba], inc=False)
                        kb.op(PE, lambda: nc.tensor.matmul(bb[0:8, col:col + 8], kx[kv][:, 0:8],
                                                           QT[e][c][:, 8 * sq_i:8 * sq_i + 8], start=True, stop=True),
                              reads=[kx[kv], QT[e][c]], writes=[bb], inc=(c == 3 and e == 1))
                kb.op(DVE, lambda: nc.vector.tensor_copy(EXA[:, :], ba[:, 0:64]), reads=[ba], writes=[EXA])
                kb.op(ACT, lambda: nc.scalar.activation(EXA[:, :], EXA[:, :], AF.Exp, scale=0.125), reads=[EXA], writes=[EXA])
                kb.op(DVE, lambda: nc.vector.tensor_copy(EXB[:, :], bb[0:8, 0:64]), reads=[bb], writes=[EXB])
                kb.op(ACT, lambda: nc.scalar.activation(EXB[:, :], EXB[:, :], AF.Exp, scale=0.125), reads=[EXB], writes=[EXB])
                pa, pb = PAs[sq_i % 2], PBs[sq_i % 2]
                kb.op(DVE, lambda: nc.vector.tensor_tensor(pa[:, :], EXA[:, :], EAt[:, :], ALU.mult), reads=[EXA, EAt], writes=[pa])
                kb.op(DVE, lambda: nc.vector.tensor_tensor(pb[:, :], EXB[:, :], EBt[:, :], ALU.mult), reads=[EXB, EBt], writes=[pb])
                for c in range(4):
                    kv = c // 2
                    npairs, dpairs = [], []
                    for e in range(2):
                        vs = (64 + 128 * kv, 192 + 128 * kv) if e == 0 else (128 * kv, 128 + 128 * kv)
                        osl = (64, 192) if e == 0 else (0, 128)
                        col = c * 16 + e * 8
                        npairs.append((vz[:, vs[0]:vs[1]], pa[:, col:col + 8], [vz, pa]))
                        npairs.append((vb[0:8, vs[0]:vs[1]], pb[0:8, col:col + 8], [vb, pb]))
                        dpairs.append((Oz[:, osl[0]:osl[1]], pa[:, col:col + 8], [Oz, pa]))
                        dpairs.append((Oz[0:8, osl[0]:osl[1]], pb[0:8, col:col + 8], [Oz, pb]))
                    oc = sq_i * 32 + c * 8
                    mm_group(bn[:, oc:oc + 8], npairs, bn, [])
                    mm_group(bd[:, oc:oc + 8], dpairs, bd, [])
            for c in range(4):
                dv = V(dn_t, 0, 128, c * 8, [(32, NSEQ), (1, 8)])
                kb.op(DVE, lambda: nc.vector.tensor_scalar(dv, V(bd, 0, 128, c * 8, [(32, NSEQ), (1, 8)]), EsT[li][:, c:c + 1], None, ALU.add),
                      reads=[bd, EsT[li]], writes=[dn_t])
            kb.op(DVE, lambda: nc.vector.reciprocal(dn_t[:, 0:512], dn_t[:, 0:512]), reads=[dn_t], writes=[dn_t])
            for c in range(4):
                kb.op(DVE, lambda: nc.vector.tensor_tensor(V(MIX[4 + c], 0, 128, 0, [(8, NSEQ), (1, 8)]), V(bn, 0, 128, c * 8, [(32, NSEQ), (1, 8)]),
                                                           V(dn_t, 0, 128, c * 8, [(32, NSEQ), (1, 8)]), ALU.mult),
                      reads=[bn, dn_t], writes=[MIX[4 + c]])

        if do_sample:
            KKs = [kb.sb("KKs%d" % k, [128, 128], BF16) for k in range(2)]
            vbf = kb.sb("vbf", [128, 128], BF16)
            KX = [[kb.sb("KX%d%d" % (k, i), [128, 136], BF16) for i in range(2)] for k in range(2)]
            VZS = [kb.sb("VZS%d" % i, [128, 320], BF16) for i in range(2)]
            VBS = [kb.sb("VBS%d" % i, [8, 320], BF16) for i in range(2)]
            for t_ in VZS + VBS:
                kb.op(POOL, lambda t_=t_: nc.gpsimd.memset(t_[:], 0.0), writes=[t_])
            EXA = kb.sb("EXA", [128, 64], F32)
            EXB = kb.sb("EXB", [8, 64], F32)
            PAs = [kb.sb("PAs%d" % i, [128, 64], BF16) for i in range(2)]
            PBs = [kb.sb("PBs%d" % i, [8, 64], BF16) for i in range(2)]
            W = 128
            for c in range(8):
                kb.dma(SP, X[c][:, 0:W], xT_s.ap()[c * 128:(c + 1) * 128, :], writes=[X[c]])
            for l in range(4):
                rmsnorm(X, l, H, W)
                if l % 2 == 0:
                    li = l // 2
                    li_cur[0] = li
                    in_proj(li, W, None)
                    s5_core(S5[li], W, NSEQ, 8, sample_h0=li)
                    attn_sample(li)
                    if li == 0:
                        dump16(MIX, W)
                        dump([P32[2]] * 8, 128)
                    out_proj(WB["w_out"].ap()[li], W, 8, MIX)
                else:
                    lo = l // 2
                    conformer(lo, W, NSEQ, 8, None)
                rmsnorm(X, 4 + l, H, W)
                ffn_up(l, W, NSEQ, 8, None)
                ffn_down(l, W)
            YO = [P32[c] for c in range(8)]
            rmsnorm(X, 8, None, W, out_f32=YO)
            for c in range(8):
                kb.dma(SP, yT_s.ap()[c * 128:(c + 1) * 128, :], YO[c][:, 0:W], reads=[YO[c]], is_out=True)

        kb.finish()
    return kb.nc


_CACHE = {}


def _prep_common(inp):
    f = np.float32
    d = {}
    gv = np.concatenate([inp["g_mix"], inp["g_ffn"], inp["g_final"][None]], 0)
    d["gvec"] = np.ascontiguousarray(gv.reshape(9, 8, 128).transpose(2, 0, 1)).astype(f)
    wi = inp["w_in_mix"]
    u, q = wi[:, :, 0:512], wi[:, :, 512:1024]
    k0, k1, v = wi[:, :, 1024:1088], wi[:, :, 1088:1152], wi[:, :, 1152:1280]
    def tile_w(w, ncols):
        L, K, N = w.shape
        t = w.reshape(L, K // 128, 128, N // ncols, ncols).transpose(0, 3, 2, 1, 4)
        return np.ascontiguousarray(t.reshape(L, N // ncols, 128, (K // 128) * ncols)).astype(f)
    d["w_in"] = tile_w(np.concatenate([u, q, k0, k0, k1, k1, v], -1), 128)

    def st(a):
        return a.reshape(2, 16, 2, 64).transpose(0, 2, 3, 1).reshape(2, 128, 16)
    ls = np.broadcast_to(inp["ssm_log_step"][:, :, None], (2, 32, 64))
    d["lam"] = np.ascontiguousarray(np.stack([st(inp["ssm_lambda_re"]), st(inp["ssm_lambda_im"]), st(ls)], 1)).astype(f)
    d["ssm_b"] = np.ascontiguousarray(np.stack([inp["ssm_b_re"], inp["ssm_b_im"]], 1)).astype(f)
    d["ssm_c"] = np.ascontiguousarray(np.stack([inp["ssm_c_re"], inp["ssm_c_im"]], 1)).astype(f)
    d["ssm_d"] = np.ascontiguousarray(inp["ssm_d"].reshape(2, 4, 128).transpose(0, 2, 1)).astype(f)
    d["w_glu"] = tile_w(inp["ssm_w_glu"], 128)
    d["b_glu"] = np.ascontiguousarray(inp["ssm_b_glu"].reshape(2, 4, 128).transpose(0, 2, 1)).astype(f)
    d["relb"] = np.ascontiguousarray(inp["rel_bias"]).astype(f)
    sk = inp["attn_sinks"]
    d["sinks"] = np.ascontiguousarray(np.repeat(sk.reshape(2, 4, 2), 64, axis=2).transpose(0, 2, 1)).astype(f)
    d["w_out"] = tile_w(inp["w_out_mix"], 128)
    p1 = inp["conv_w_pw1"]
    a, g = p1[:, :, :1024].reshape(2, 1024, 8, 128), p1[:, :, 1024:].reshape(2, 1024, 8, 128)
    d["w_pw1"] = tile_w(np.concatenate([a, g], -1).reshape(2, 1024, 2048), 256)
    d["w_dw"] = np.ascontiguousarray(inp["conv_w_dw"].reshape(2, 31, 8, 128).transpose(0, 3, 2, 1)).astype(f)
    cv = np.stack([inp["conv_b_dw"], inp["conv_ln_g"], inp["conv_ln_b"]], 1)
    d["cvec"] = np.ascontiguousarray(cv.reshape(2, 3, 8, 128).transpose(0, 3, 1, 2)).astype(f)
    d["w_pw2"] = tile_w(inp["conv_w_pw2"], 128)
    wu = inp["ffn_w_up"]
    gg, uu = wu[:, :, :DFF].reshape(4, 1024, NJ, 128), wu[:, :, DFF:].reshape(4, 1024, NJ, 128)
    d["w_up"] = tile_w(np.concatenate([gg, uu], -1).reshape(4, 1024, NJ * 256), 256)
    d["f_cw"] = np.ascontiguousarray(inp["ffn_w_conv"].reshape(4, 3, NJ, 128).transpose(0, 3, 2, 1)).astype(f)
    d["f_cb"] = np.ascontiguousarray(inp["ffn_b_conv"].reshape(4, NJ, 128).transpose(0, 2, 1)).astype(f)
    wd = inp["ffn_w_down"].reshape(4, 2, 11, 128, 8, 128).transpose(0, 4, 1, 3, 2, 5)
    d["w_dn"] = np.ascontiguousarray(wd.reshape(4, 8, 2, 128, 11 * 128)).astype(f)
    m = np.arange(384)
    dist = m - 127
    inside = (dist >= 0) & (dist < 128)
    oh = np.zeros((32, 384), f)
    bk = t5_bucket_np(np.clip(dist, 0, 127))
    oh[bk[inside], m[inside]] = 1.0
    d["oh_bucket"] = oh
    d["msk_ext"] = np.ascontiguousarray(np.broadcast_to(np.where(inside, 0.0, NEG).astype(f)[None], (8, 384)))
    d["antiI"] = np.ascontiguousarray(np.eye(128, dtype=f)[::-1])
    return d


def kernel(**inp):
    inp = {k: np.asarray(v) for k, v in inp.items()}
    f = np.float32
    if "nc" not in _CACHE:
        _CACHE["nc"] = build()
    nc = _CACHE["nc"]
    com = _prep_common(inp)
    in_maps = []
    for c in range(8):
        d = dict(com)
        s = c % 2
        xp = np.concatenate([np.zeros((NPAD, D), f), inp["meta_tokens"], inp["x_prompt"][s]], 0)
        d["xT_p"] = np.ascontiguousarray(xp.T)
        sl = slice(c * NSEQ, (c + 1) * NSEQ)
        d["xT_s"] = np.ascontiguousarray(inp["x_sample"][sl].reshape(NSEQ * 8, D).T)

        def st(a):
            return a.reshape(2, NSEQ, 16, 2, 64).transpose(0, 3, 4, 2, 1).reshape(2, 128, 16, NSEQ)
        d["st_ssm"] = np.ascontiguousarray(np.stack([st(inp["state_ssm_re"][:, sl]), st(inp["state_ssm_im"][:, sl])], 1)).astype(f)
        d["st_kT"] = np.ascontiguousarray(inp["cache_swa_k"][:, sl].reshape(2, NSEQ, 128, 128).transpose(0, 1, 3, 2)).astype(f)
        d["st_v"] = np.ascontiguousarray(inp["cache_swa_v"][:, sl].reshape(2, NSEQ, 128, 128)).astype(f)
        d["st_conv"] = np.ascontiguousarray(inp["state_conv"][:, sl].transpose(0, 3, 1, 2)).astype(f)
        d["st_ffn"] = np.ascontiguousarray(inp["state_ffn"][:, sl].transpose(0, 3, 1, 2)).astype(f)
        in_maps.append(d)
    res = run_bass_kernel_spmd(nc, in_maps, core_ids=list(range(8)))
    R = res.results
    _CACHE["last"] = R
    y_p = np.stack([R[s]["yT_p"][:, 128:].T for s in range(2)], 0)
    y_s = np.concatenate([R[c]["yT_s"].T.reshape(NSEQ, 8, D) for c in range(8)], 0)

    def ust(a):
        return a.reshape(2, 2, 64, 16).transpose(0, 3, 1, 2).reshape(2, 32, 64)
    sr_p = np.stack([ust(R[s]["o_ssm_p"][:, 0]) for s in range(2)], 1)
    si_p = np.stack([ust(R[s]["o_ssm_p"][:, 1]) for s in range(2)], 1)
    k_p = np.stack([R[s]["o_kT_p"].transpose(0, 2, 1).reshape(2, 128, 2, 64) for s in range(2)], 1)
    v_p = np.stack([R[s]["o_v_p"].reshape(2, 128, 2, 64) for s in range(2)], 1)
    c_p = np.stack([R[s]["o_conv_p"].transpose(0, 2, 1) for s in range(2)], 1)
    f_p = np.stack([R[s]["o_ffn_p"].transpose(0, 2, 1) for s in range(2)], 1)

    def usts(a):
        return a.reshape(2, 2, 64, 16, NSEQ).transpose(0, 4, 3, 1, 2).reshape(2, NSEQ, 32, 64)
    sr_s = np.concatenate([usts(R[c]["o_ssm_s"][:, 0]) for c in range(8)], 1)
    si_s = np.concatenate([usts(R[c]["o_ssm_s"][:, 1]) for c in range(8)], 1)
    k_s = np.concatenate([R[c]["o_kT_s"].transpose(0, 1, 3, 2).reshape(2, NSEQ, 128, 2, 64) for c in range(8)], 1)
    v_s = np.concatenate([R[c]["o_v_s"].reshape(2, NSEQ, 128, 2, 64) for c in range(8)], 1)
    c_s = np.concatenate([R[c]["o_conv_s"].transpose(0, 2, 3, 1) for c in range(8)], 1)
    f_s = np.concatenate([R[c]["o_ffn_s"].transpose(0, 2, 3, 1) for c in range(8)], 1)
    outs = (y_p, y_s, sr_p, si_p, k_p, v_p, c_p, f_p, sr_s, si_s, k_s, v_s, c_s, f_s)
    return tuple(np.ascontiguousarray(o).astype(np.float32) for o in outs)
```

```python
import contextlib
import math
import numpy as np
import concourse.bass as bass
import concourse.mybir as mybir
from concourse.bass_utils import run_bass_kernel_spmd

F32 = mybir.dt.float32
BF16 = mybir.dt.bfloat16
AF = mybir.ActivationFunctionType
ALU = mybir.AluOpType

D = 1024
NC8 = 8
DFF = 2816
NJ = 22
NPAD = 112
TPAD = 8320
NBLK = 65
NSEQ = 16
EPS = 1e-6
NEG = -30000.0


def t5_bucket_np(dist):
    n = np.maximum(dist, 0)
    max_exact = 16
    nf = np.maximum(n, max_exact).astype(np.float32)
    large = max_exact + (np.log(nf / max_exact) / math.log(128 / max_exact) * (32 - max_exact)).astype(np.int32)
    large = np.minimum(large, 31)
    return np.where(n < max_exact, n, large)


class Buf:
    __slots__ = ("t", "name", "last_w", "readers", "dsem", "dcnt", "wn")

    def __init__(self, t, name):
        self.t = t
        self.name = name
        self.last_w = None
        self.readers = {}
        self.dsem = None
        self.dcnt = 0
        self.wn = None

    def __getitem__(self, idx):
        if self.wn is not None and isinstance(idx, tuple) and len(idx) == 3 and isinstance(idx[1], int):
            cs = idx[2]
            return V(self, 0, 128, idx[1] * self.wn + cs.start, [(1, cs.stop - cs.start)])
        return self.t[idx]


class _Stop(Exception):
    pass


_LAST = {}


class KB:
    def __init__(self):
        self.nc = bass.Bass("TRN2", target_bir_lowering=False)
        self.es = contextlib.ExitStack()
        nc = self.nc
        self.eng = {"pe": nc.tensor, "act": nc.scalar, "dve": nc.vector, "pool": nc.gpsimd, "sp": nc.sync}
        self.sem = {}
        self.cnt = {}
        self.seen = {e: {} for e in self.eng}
        self.uid = 0
        self.psum_rr = 0
        self.out_events = []
        self.dead = False

    def start(self):
        import os
        for i in range(int(os.environ.get("KDUMMYSEM", "0"))):
            self.es.enter_context(self.nc.semaphore("dummy%d" % i))
        for e in self.eng:
            self.sem[e] = self.es.enter_context(self.nc.semaphore("prog_" + e))
            self.cnt[e] = 0
        self.banks = []
        for i in range(8):
            t = self.es.enter_context(self.nc.psum_tensor("bank%d" % i, [128, 512], F32))
            self.banks.append(Buf(t, "bank%d" % i))

    def sb(self, name, shape, dtype):
        self.uid += 1
        t = self.es.enter_context(self.nc.sbuf_tensor("%s_%d" % (name, self.uid), list(shape), dtype))
        return Buf(t, name)

    def dram(self, name, shape, dtype, kind):
        return self.nc.dram_tensor(name, list(shape), dtype, kind=kind)

    def bank(self):
        b = self.banks[self.psum_rr % 8]
        self.psum_rr += 1
        return b

    def _waits(self, e, reads, writes):
        deps = []
        for b in reads:
            if b.last_w is not None:
                deps.append((b.last_w, True))
        for b in writes:
            if b.last_w is not None:
                deps.append((b.last_w, False))
            for ev in b.readers.values():
                deps.append((ev, False))
        own = self.sem.get(e)
        for (sem, val), raw in deps:
            if sem is own:
                if e in ("pe", "sp"):
                    continue
            key = id(sem)
            if self.seen[e].get(key, 0) < val:
                self.eng[e].wait_ge(sem, val)
                self.seen[e][key] = val

    def chk(self, tag):
        import os
        if os.environ.get("KSTOP") == tag:
            for i in range(int(os.environ.get("KEXTRA", "0"))):
                tgt = self._xtra if os.environ.get("KXT") else self._misc
                w = 128 if os.environ.get("KXT") else 1
                if os.environ.get("KXT") == "2":
                    self.op("dve", lambda: self.nc.vector.tensor_copy(self._xtra[:, 0:128], self._xtra2[:, 0:128]), reads=[self._xtra2], writes=[self._xtra])
                else:
                    self.op("dve", lambda: self.nc.vector.memset(tgt[:, 0:w], 0.0), writes=[tgt])
            self.dead = True
            if os.environ.get("KRAISE"):
                self.dead = False
                self.finish()
                raise _Stop()

    def op(self, e, fn, reads=(), writes=(), inc=True):
        if self.dead:
            return None
        self._waits(e, reads, writes)
        inst = fn()
        if inc:
            self.cnt[e] += 1
            inst.then_inc(self.sem[e], 1)
            ev = (self.sem[e], self.cnt[e])
        else:
            ev = (self.sem[e], self.cnt[e] + 1)
        for b in writes:
            b.last_w = ev
            b.readers = {}
        for b in reads:
            b.readers[e] = ev
        return inst

    def dma(self, q, out_ap, in_ap, reads=(), writes=(), slow=False, is_out=False):
        if self.dead:
            return None
        self._waits(q, reads, writes)
        kw = {}
        if slow:
            kw["allow_slow_non_contiguous"] = True
        inst = self.eng[q].dma_start(out=out_ap, in_=in_ap, **kw)
        tgt = writes[0] if writes else (reads[0] if reads else None)
        if tgt is None:
            tgt = self._misc
        if tgt.dsem is None:
            self.uid += 1
            tgt.dsem = self.es.enter_context(self.nc.semaphore("d_%s_%d" % (tgt.name, self.uid)))
        tgt.dcnt += 16
        inst.then_inc(tgt.dsem, 16)
        ev = (tgt.dsem, tgt.dcnt)
        for b in writes:
            b.last_w = ev
            b.readers = {}
        for b in reads:
            b.readers[("dma", id(tgt))] = ev
        self.out_events.append(ev)
        return inst

    def finish(self):
        last = {}
        for sem, val in self.out_events:
            k = id(sem)
            if k not in last or last[k][1] < val:
                last[k] = (sem, val)
        for sem, val in last.values():
            self.eng["sp"].wait_ge(sem, val)
        for e in self.eng:
            for e2 in ("pe", "act", "dve", "pool"):
                if e2 != e and self.cnt[e2] > 0:
                    self.eng[e].wait_ge(self.sem[e2], self.cnt[e2])


def V(buf, p0, np_, off, dims):
    t = buf.t
    shape = t.shape
    fsz = 1
    for s in shape[1:]:
        fsz *= s
    return bass.AP(t, p0 * fsz + off, [[fsz, np_]] + [[s, c] for (s, c) in dims])


def build(nchunk_prompt=17, do_sample=True, dbg=False, dbg_ci=1):
    try:
        return _build(nchunk_prompt, do_sample, dbg, dbg_ci)
    except _Stop:
        return _LAST["kb"].nc


def _build(nchunk_prompt=17, do_sample=True, dbg=False, dbg_ci=1):
    kb = KB()
    _LAST["kb"] = kb
    nc = kb.nc
    es = kb.es
    with es:
        kb.start()
        import os as _os
        kb._misc = kb.sb("misc", [128, 1], F32)
        PE, ACT, DVE, POOL, SP = "pe", "act", "dve", "pool", "sp"
        din = {}

        def DI(name, shape):
            din[name] = kb.dram(name, shape, F32, "ExternalInput")
            return din[name]

        dout = {}

        def DO(name, shape):
            dout[name] = kb.dram(name, shape, F32, "ExternalOutput")
            return dout[name]

        xT_p = DI("xT_p", [D, TPAD])
        xT_s = DI("xT_s", [D, 128])
        st_ssm = DI("st_ssm", [2, 2, 128, 16, NSEQ])
        st_kT = DI("st_kT", [2, NSEQ, 128, 128])
        st_v = DI("st_v", [2, NSEQ, 128, 128])
        st_conv = DI("st_conv", [2, D, NSEQ, 30])
        st_ffn = DI("st_ffn", [4, DFF, NSEQ, 2])
        gvec = DI("gvec", [128, 9, 8])
        w_in = DI("w_in", [2, 11, 128, 1024])
        lam = DI("lam", [2, 3, 128, 16])
        ssm_b = DI("ssm_b", [2, 2, 32, 64, 16])
        ssm_c = DI("ssm_c", [2, 2, 32, 16, 64])
        ssm_d = DI("ssm_d", [2, 128, 4])
        w_glu = DI("w_glu", [2, 4, 128, 512])
        b_glu = DI("b_glu", [2, 128, 4])
        relb = DI("relb", [32, 8])
        sinks = DI("sinks", [2, 128, 4])
        w_out = DI("w_out", [2, 8, 128, 1024])
        w_pw1 = DI("w_pw1", [2, 8, 128, 2048])
        w_dw = DI("w_dw", [2, 128, 8, 31])
        cvec = DI("cvec", [2, 128, 3, 8])
        w_pw2 = DI("w_pw2", [2, 8, 128, 1024])
        w_up = DI("w_up", [4, NJ, 128, 2048])
        f_cw = DI("f_cw", [4, 128, NJ, 3])
        f_cb = DI("f_cb", [4, 128, NJ])
        w_dn = DI("w_dn", [4, 8, 2, 128, 1408])
        oh_bucket = DI("oh_bucket", [32, 384])
        msk_ext = DI("msk_ext", [8, 384])
        antiI = DI("antiI", [128, 128])

        yT_p = DO("yT_p", [D, TPAD])
        yT_s = DO("yT_s", [D, 128])
        o_ssm_p = DO("o_ssm_p", [2, 2, 128, 16])
        o_ssm_s = DO("o_ssm_s", [2, 2, 128, 16, NSEQ])
        o_kT_p = DO("o_kT_p", [2, 128, 128])
        o_v_p = DO("o_v_p", [2, 128, 128])
        o_kT_s = DO("o_kT_s", [2, NSEQ, 128, 128])
        o_v_s = DO("o_v_s", [2, NSEQ, 128, 128])
        o_conv_p = DO("o_conv_p", [2, D, 30])
        o_conv_s = DO("o_conv_s", [2, D, NSEQ, 30])
        o_ffn_p = DO("o_ffn_p", [4, DFF, 2])
        o_ffn_s = DO("o_ffn_s", [4, DFF, NSEQ, 2])
        scr = kb.dram("scr_bias", [8, 384], F32, "Internal")
        if dbg:
            dbg_o = DO("dbg", [16, D, 512])
        dbgc = [0]

        def dump16(Xl, W):
            if not dbg:
                return
            for c in range(8):
                kb.dma(POOL, dbg_o.ap()[dbgc[0], c * 128:(c + 1) * 128, 0:W], Xl[c][:, 0:W], reads=[Xl[c]], is_out=True)
            dbgc[0] += 1

        def dump(Xl, W):
            if not dbg:
                return
            for c in range(8):
                kb.dma(SP, dbg_o.ap()[dbgc[0], c * 128:(c + 1) * 128, 0:W], Xl[c][:, 0:W], reads=[Xl[c]], is_out=True)
            dbgc[0] += 1

        ident = kb.sb("ident", [128, 128], F32)
        kb.op(POOL, lambda: nc.gpsimd.memset(ident[:], 1.0), writes=[ident])
        kb.op(POOL, lambda: nc.gpsimd.affine_select(ident[:], ident[:], pattern=[[-1, 128]], compare_op=ALU.is_equal,
                                                     fill=0.0, base=0, channel_multiplier=1), reads=[ident], writes=[ident])
        ones_bf = kb.sb("ones_bf", [128, 128], BF16)
        kb.op(DVE, lambda: nc.vector.memset(ones_bf[:], 1.0), writes=[ones_bf])
        Oz = kb.sb("Oz", [128, 192], BF16)
        Oz0 = kb.sb("Oz0", [128, 192], BF16)
        for o in (Oz, Oz0):
            kb.op(DVE, lambda o=o: nc.vector.memset(o[:], 0.0), writes=[o])
            kb.op(DVE, lambda o=o: nc.vector.memset(o[:, 64:128], 1.0), writes=[o])
        kb.op(DVE, lambda: nc.vector.memset(Oz0[0:NPAD, :], 0.0), writes=[Oz0])
        gv = kb.sb("gv", [128, 9, 8], F32)
        kb.dma(SP, gv[:], gvec.ap(), writes=[gv])
        kb.chk("c0")

        WSL = [kb.sb("wslab%d" % i, [128, 8, 256], BF16) for i in range(2)]
        WDN = [kb.sb("wdn%d" % i, [128, 11, 128], BF16) for i in range(2)]
        wctr = {"a": 0, "b": 0}
        UT = [kb.sb("UT%d" % m, [128, 512], BF16) for m in range(4)]
        U32 = [kb.sb("U32_%d" % m, [128, 512], F32) for m in range(4)]
        QT = [[kb.sb("QT%d_%d" % (e, c), [128, 512], BF16) for c in range(4)] for e in range(2)]
        for e in range(2):
            for c in range(4):
                kb.op(POOL, lambda e=e, c=c: nc.gpsimd.memset(QT[e][c][:], 0.0), writes=[QT[e][c]])

        wconv = kb.sb("wconv", [128, 1], F32)
        WB = {}
        for nm, src in (("w_in", w_in), ("w_glu", w_glu), ("w_out", w_out), ("w_pw1", w_pw1), ("w_pw2", w_pw2), ("w_up", w_up), ("w_dn", w_dn)):
            shp = list(src.shape)
            dst = kb.dram(nm + "_bf", shp, BF16, "Internal")
            WB[nm] = dst
            lead = 1
            for d_ in shp[:-2]:
                lead *= d_
            sflat = src.ap().flatten_outer_dims() if len(shp) > 2 else src.ap()
            dflat = dst.ap().flatten_outer_dims() if len(shp) > 2 else dst.ap()
            for i_ in range(lead):
                kb.dma(POOL, dflat[i_ * 128:(i_ + 1) * 128, :], sflat[i_ * 128:(i_ + 1) * 128, :], writes=[wconv])

        def load_w(dram_ap, kc, ncols):
            b = WSL[wctr["a"] % len(WSL)]
            wctr["a"] += 1
            b.wn = ncols
            kb.dma(SP, V(b, 0, 128, 0, [(1, kc * ncols)]), dram_ap, reads=[wconv], writes=[b])
            return b

        def load_wdn(dram_ap):
            b = WDN[wctr["b"] % len(WDN)]
            wctr["b"] += 1
            kb.dma(SP, V(b, 0, 128, 0, [(1, 11 * 128)]), dram_ap, reads=[wconv], writes=[b])
            return b

        def mm_group(out_ap, pairs, bankbuf, rbufs):
            n = len(pairs)
            if _os.environ.get("KHOIST"):
                allr = []
                for (_l, _r, bs_) in pairs:
                    allr += list(bs_)
                kb._waits(PE, allr + list(rbufs), [bankbuf])
            for i, (l, r, bs) in enumerate(pairs):
                kb.op(PE, lambda l=l, r=r, i=i: nc.tensor.matmul(out_ap, l, r, start=(i == 0), stop=(i == n - 1)),
                      reads=list(bs) + list(rbufs), writes=[bankbuf], inc=(i == n - 1))

        P32 = [kb.sb("P32_%d" % i, [128, 608], F32) for i in range(14)]
        P16 = [kb.sb("P16_%d" % i, [128, 512], BF16) for i in range(22)]
        kb._xtra = P32[5]
        kb._xtra2 = P32[6]
        sq = P16[12:20]
        rs = P32[12]
        rinv = P32[13]
        eps_t = kb.sb("eps_t", [128, 1], F32)
        kb.op(DVE, lambda: nc.vector.memset(eps_t[:], EPS), writes=[eps_t])

        def rmsnorm(X, gi, Hout, W, out_f32=None):
            lvl = int(_os.environ.get("KRMS", "9"))
            if lvl < 1:
                return
            for c in range(8):
                kb.op(DVE, lambda c=c: nc.vector.tensor_tensor(sq[c][:, 0:W], X[c][:, 0:W], X[c][:, 0:W], ALU.mult), reads=[X[c]], writes=[sq[c]])
            if lvl < 2:
                return
            bk = kb.bank()
            mm_group(bk[:, 0:W], [(ones_bf[:], sq[c][:, 0:W], [sq[c]]) for c in range(8)], bk, [ones_bf])
            if lvl < 3:
                return
            kb.op(DVE, lambda: nc.vector.tensor_scalar(rs[:, 0:W], bk[:, 0:W], 1.0 / D, EPS, ALU.mult, ALU.add), reads=[bk], writes=[rs])
            kb.op(ACT, lambda: nc.scalar.activation(rs[:, 0:W], rs[:, 0:W], AF.Sqrt), reads=[rs], writes=[rs])
            if lvl < 4:
                return
            kb.op(DVE, lambda: nc.vector.reciprocal(rinv[:, 0:W], rs[:, 0:W]), reads=[rs], writes=[rinv])
            if lvl < 5:
                return
            for c in range(8):
                o = Hout[c] if out_f32 is None else out_f32[c]
                kb.op(DVE, lambda c=c, o=o: nc.vector.scalar_tensor_tensor(o[:, 0:W], X[c][:, 0:W], gv[:, gi, c:c + 1], rinv[:, 0:W],
                                                                          ALU.mult, ALU.mult),
                      reads=[X[c], gv, rinv], writes=[o])

        X = [kb.sb("X%d" % c, [128, 512], F32) for c in range(8)]
        H = [kb.sb("H%d" % c, [128, 512], BF16) for c in range(8)]

        S5 = []
        import os as _os
        for li in range(0 if _os.environ.get("KSKIP_S5") else 2):
            T = {}
            lm = kb.sb("lam", [128, 3, 16], F32)
            kb.dma(SP, lm[:], lam.ap()[li].rearrange("k p j -> p k j"), writes=[lm])
            dt = kb.sb("dt", [128, 16], F32)
            kb.op(ACT, lambda: nc.scalar.activation(dt[:], lm[:, 2, :], AF.Exp), reads=[lm], writes=[dt])
            lrdt = kb.sb("lrdt", [128, 16], F32)
            th = kb.sb("th", [128, 16], F32)
            kb.op(DVE, lambda: nc.vector.tensor_tensor(lrdt[:], lm[:, 0, :], dt[:], ALU.mult), reads=[lm, dt], writes=[lrdt])
            kb.op(DVE, lambda: nc.vector.tensor_tensor(th[:], lm[:, 1, :], dt[:], ALU.mult), reads=[lm, dt], writes=[th])
            rho = kb.sb("rho", [128, 16], F32)
            kb.op(ACT, lambda: nc.scalar.activation(rho[:], lrdt[:], AF.Exp), reads=[lrdt], writes=[rho])
            T["rho"] = rho
            kk = P32[0]
            kb.op(POOL, lambda: nc.gpsimd.iota(kk[:, 0:64], pattern=[[1, 64]], base=1, channel_multiplier=0,
                                               allow_small_or_imprecise_dtypes=True), writes=[kk])
            ctab = kb.sb("ctab", [128, 16, 64], F32)
            stab = kb.sb("stab", [128, 16, 64], F32)
            negpi = kb.sb("negpi", [128, 1], F32)
            ki32 = kb.sb("ki32", [128, 64], mybir.dt.int32)
            kb.op(DVE, lambda: nc.vector.memset(negpi[:], -math.pi), writes=[negpi])
            for j in range(16):
                ang = P32[1 + (j % 2)]
                kb.op(DVE, lambda j=j, ang=ang: nc.vector.tensor_scalar(ang[:, 0:64], kk[:, 0:64], th[:, j:j + 1], None, ALU.mult),
                      reads=[kk, th], writes=[ang])
                for (dst, sh, ti) in ((stab, 0.5, 3), (ctab, 0.75, 5)):
                    tmp = P32[ti + (j % 2)]
                    kb.op(DVE, lambda sh=sh, tmp=tmp, ang=ang: nc.vector.tensor_scalar(tmp[:, 0:64], ang[:, 0:64], 1.0 / (2 * math.pi), sh, ALU.mult, ALU.add),
                          reads=[ang], writes=[tmp])
                    kb.op(DVE, lambda tmp=tmp: nc.vector.tensor_copy(ki32[:, 0:64], tmp[:, 0:64]), reads=[tmp], writes=[ki32])
                    kb.op(DVE, lambda tmp=tmp: nc.vector.tensor_copy(tmp[:, 64:128], ki32[:, 0:64]), reads=[ki32], writes=[tmp])
                    kb.op(DVE, lambda tmp=tmp: nc.vector.tensor_tensor(tmp[:, 0:64], tmp[:, 0:64], tmp[:, 64:128], ALU.subtract), reads=[tmp], writes=[tmp])
                    kb.op(DVE, lambda tmp=tmp: nc.vector.tensor_scalar(tmp[:, 64:128], tmp[:, 0:64], 0.0, None, ALU.is_lt), reads=[tmp], writes=[tmp])
                    kb.op(DVE, lambda tmp=tmp: nc.vector.tensor_tensor(tmp[:, 0:64], tmp[:, 0:64], tmp[:, 64:128], ALU.add), reads=[tmp], writes=[tmp])
                    kb.op(ACT, lambda dst=dst, tmp=tmp, j=j: nc.scalar.activation(dst[:, j, :], tmp[:, 0:64], AF.Sin, bias=negpi[:], scale=2 * math.pi),
                          reads=[tmp, negpi], writes=[dst])
            T["ctab"], T["stab"] = ctab, stab
            kb.chk("s5ang")
            are = kb.sb("are", [128, 16], F32)
            aim = kb.sb("aim", [128, 16], F32)
            kb.op(DVE, lambda: nc.vector.tensor_tensor(are[:], rho[:], ctab[:, :, 0], ALU.mult), reads=[rho, ctab], writes=[are])
            kb.op(DVE, lambda: nc.vector.tensor_tensor(aim[:], rho[:], stab[:, :, 0], ALU.mult), reads=[rho, stab], writes=[aim])
            den = kb.sb("den", [128, 16], F32)
            t1 = kb.sb("t1", [128, 16], F32)
            t2 = kb.sb("t2", [128, 16], F32)
            kb.op(DVE, lambda: nc.vector.tensor_tensor(den[:], lm[:, 0, :], lm[:, 0, :], ALU.mult), reads=[lm], writes=[den])
            kb.op(DVE, lambda: nc.vector.tensor_tensor(t1[:], lm[:, 1, :], lm[:, 1, :], ALU.mult), reads=[lm], writes=[t1])
            kb.op(DVE, lambda: nc.vector.tensor_tensor(den[:], den[:], t1[:], ALU.add), reads=[den, t1], writes=[den])
            rden = kb.sb("rden", [128, 16], F32)
            kb.op(DVE, lambda: nc.vector.reciprocal(rden[:], den[:]), reads=[den], writes=[rden])
            nre = kb.sb("nre", [128, 16], F32)
            kb.op(DVE, lambda: nc.vector.tensor_scalar_add(nre[:], are[:], -1.0), reads=[are], writes=[nre])
            cre = kb.sb("cre", [128, 16], F32)
            cim = kb.sb("cim", [128, 16], F32)
            ncim = kb.sb("ncim", [128, 16], F32)
            kb.op(DVE, lambda: nc.vector.tensor_tensor(t1[:], nre[:], lm[:, 0, :], ALU.mult), reads=[nre, lm], writes=[t1])
            kb.op(DVE, lambda: nc.vector.tensor_tensor(t2[:], aim[:], lm[:, 1, :], ALU.mult), reads=[aim, lm], writes=[t2])
            kb.op(DVE, lambda: nc.vector.tensor_tensor(t1[:], t1[:], t2[:], ALU.add), reads=[t1, t2], writes=[t1])
            kb.op(DVE, lambda: nc.vector.tensor_tensor(cre[:], t1[:], rden[:], ALU.mult), reads=[t1, rden], writes=[cre])
            kb.op(DVE, lambda: nc.vector.tensor_tensor(t1[:], aim[:], lm[:, 0, :], ALU.mult), reads=[aim, lm], writes=[t1])
            kb.op(DVE, lambda: nc.vector.tensor_tensor(t2[:], nre[:], lm[:, 1, :], ALU.mult), reads=[nre, lm], writes=[t2])
            kb.op(DVE, lambda: nc.vector.tensor_tensor(t1[:], t1[:], t2[:], ALU.subtract), reads=[t1, t2], writes=[t1])
            kb.op(DVE, lambda: nc.vector.tensor_tensor(cim[:], t1[:], rden[:], ALU.mult), reads=[t1, rden], writes=[cim])
            kb.op(DVE, lambda: nc.vector.tensor_scalar_mul(ncim[:], cim[:], -1.0), reads=[cim], writes=[ncim])
            kb.chk("s5coef")
            Bl = [kb.sb("Bl_re", [128, 16, 128], BF16), kb.sb("Bl_im", [128, 16, 128], BF16)]
            Cl = [kb.sb("Cl_re", [128, 16, 128], BF16), kb.sb("Cl_im", [128, 16, 128], BF16)]
            for j in range(16):
                o = 128 * (j % 2)
                zb = [P32[7], P32[8]]
                zc = [P32[9], P32[10]]
                zbb = [P32[11], P32[12]]
                for z in zb + zc:
                    kb.op(POOL, lambda z=z, o=o: nc.gpsimd.memset(z[:, o:o + 128], 0.0), writes=[z])
                for ri in range(2):
                    for e in range(2):
                        g = 2 * j + e
                        c0 = 32 * (j % 4) + 16 * e
                        kb.dma(SP, zb[ri][64 * e:64 * e + 64, o + c0:o + c0 + 16], ssm_b.ap()[li, ri, g], writes=[zb[ri]])
                        kb.dma(SP, zc[ri][c0:c0 + 16, o + 64 * e:o + 64 * e + 64], ssm_c.ap()[li, ri, g], writes=[zc[ri]])
                kb.op(DVE, lambda j=j, o=o: nc.vector.tensor_scalar(zbb[0][:, o:o + 128], zb[0][:, o:o + 128], cre[:, j:j + 1], None, ALU.mult),
                      reads=[zb[0], cre], writes=[zbb[0]])
                kb.op(DVE, lambda j=j, o=o: nc.vector.scalar_tensor_tensor(zbb[0][:, o:o + 128], zb[1][:, o:o + 128], ncim[:, j:j + 1], zbb[0][:, o:o + 128],
                                                                      ALU.mult, ALU.add), reads=[zb[1], ncim, zbb[0]], writes=[zbb[0]])
                kb.op(DVE, lambda j=j, o=o: nc.vector.tensor_scalar(zbb[1][:, o:o + 128], zb[1][:, o:o + 128], cre[:, j:j + 1], None, ALU.mult),
                      reads=[zb[1], cre], writes=[zbb[1]])
                kb.op(DVE, lambda j=j, o=o: nc.vector.scalar_tensor_tensor(zbb[1][:, o:o + 128], zb[0][:, o:o + 128], cim[:, j:j + 1], zbb[1][:, o:o + 128],
                                                                      ALU.mult, ALU.add), reads=[zb[0], cim, zbb[1]], writes=[zbb[1]])
                for ri in range(2):
                    bk = kb.bank()
                    kb.op(PE, lambda bk=bk, ri=ri, o=o: nc.tensor.transpose(bk[:, 0:128], zbb[ri][:, o:o + 128], ident[:]),
                          reads=[zbb[ri], ident], writes=[bk])
                    kb.op(PE, lambda bk=bk, ri=ri, o=o: nc.tensor.transpose(bk[:, 128:256], zc[ri][:, o:o + 128], ident[:]),
                          reads=[zc[ri], ident], writes=[bk])
                    kb.op(DVE, lambda bk=bk, ri=ri, j=j: nc.vector.tensor_copy(Bl[ri][:, j, :], bk[:, 0:128]), reads=[bk], writes=[Bl[ri]])
                    if ri == 0:
                        kb.op(DVE, lambda bk=bk, ri=ri, j=j: nc.vector.tensor_copy(Cl[ri][:, j, :], bk[:, 128:256]), reads=[bk], writes=[Cl[ri]])
                    else:
                        kb.op(DVE, lambda bk=bk, ri=ri, j=j: nc.vector.tensor_scalar(Cl[ri][:, j, :], bk[:, 128:256], -1.0, None, ALU.mult), reads=[bk], writes=[Cl[ri]])
            T["Bl"], T["Cl"] = Bl, Cl
            kb.chk("s5bc")
            dsk = kb.sb("dsk", [128, 4], F32)
            bgl = kb.sb("bgl", [128, 4], F32)
            kb.dma(SP, dsk[:], ssm_d.ap()[li], writes=[dsk])
            kb.dma(SP, bgl[:], b_glu.ap()[li], writes=[bgl])
            T["dsk"], T["bgl"] = dsk, bgl
            rho9 = kb.sb("rho9", [128, 16, 9], F32)
            kb.op(DVE, lambda: nc.vector.memset(rho9[:], 0.0), writes=[rho9])
            kb.op(DVE, lambda: nc.vector.tensor_scalar(rho9[:, :, 1:9], V(rho, 0, 128, 0, [(1, 16), (0, 8)]), 1.0, None, ALU.mult),
                  reads=[rho], writes=[rho9])
            T["rho9"] = rho9
            T["car"] = [kb.sb("car_re", [128, 16], F32), kb.sb("car_im", [128, 16], F32)]
            for cbuf in T["car"]:
                kb.op(DVE, lambda cbuf=cbuf: nc.vector.memset(cbuf[:], 0.0), writes=[cbuf])
            S5.append(T)
        kb.chk("s5")

        kb.dead = bool(_os.environ.get("KSKIP_BIAS"))
        rb = kb.sb("rb", [32, 8], F32)
        oh = P32[3]
        kb.dma(SP, rb[:], relb.ap(), writes=[rb])
        kb.dma(SP, oh[0:32, 0:384], oh_bucket.ap(), writes=[oh])
        mk8 = P32[4]
        kb.dma(SP, mk8[0:8, 0:384], msk_ext.ap(), writes=[mk8])
        bk = kb.bank()
        kb.op(PE, lambda: nc.tensor.matmul(bk[0:8, 0:384], rb[:], oh[0:32, 0:384], start=True, stop=True), reads=[rb, oh], writes=[bk])
        bv = P32[5]
        kb.op(DVE, lambda: nc.vector.tensor_tensor(bv[0:8, 0:384], bk[0:8, 0:384], mk8[0:8, 0:384], ALU.add), reads=[bk, mk8], writes=[bv])
        kb.dma(SP, scr.ap(), bv[0:8, 0:384], reads=[bv], writes=[kb._misc])
        aI = kb.sb("aI", [128, 128], F32)
        kb.dma(SP, aI[:], antiI.ap(), writes=[aI])
        Etab = []
        for c in range(4):
            E = kb.sb("Etab%d" % c, [128, 512], F32)
            hank = P32[c % 2]
            kb.dma(SP, hank[:, 0:512], bass.AP(scr, 2 * c * 384, [[1, 128], [384, 2], [1, 256]]), reads=[kb._misc], writes=[hank])
            bk = kb.bank()
            for e in range(2):
                kb.op(PE, lambda e=e, bk=bk, hank=hank: nc.tensor.matmul(bk[:, e * 128:(e + 1) * 128], aI[:], hank[:, e * 256:e * 256 + 128], start=True, stop=True),
                      reads=[aI, hank], writes=[bk])
                kb.op(PE, lambda e=e, bk=bk, hank=hank: nc.tensor.matmul(bk[:, 256 + e * 128:256 + (e + 1) * 128], aI[:], hank[:, e * 256 + 128:e * 256 + 256],
                                                                   start=True, stop=True), reads=[aI, hank], writes=[bk])
            kb.op(DVE, lambda E=E, bk=bk: nc.vector.tensor_copy(E[:], bk[:]), reads=[bk], writes=[E])
            kb.op(ACT, lambda E=E: nc.scalar.activation(E[:], E[:], AF.Exp), reads=[E], writes=[E])
            Etab.append(E)
        EAt = kb.sb("EAt", [128, 64], F32)
        EBt = kb.sb("EBt", [8, 64], F32)
        hk = P32[2]
        kb.dma(SP, hk[:, 0:64], bass.AP(scr, 120, [[1, 128], [384, 8], [1, 8]]), reads=[kb._misc], writes=[hk], slow=True)
        kb.dma(SP, hk[0:8, 64:128], bass.AP(scr, 248, [[1, 8], [384, 8], [1, 8]]), reads=[kb._misc], writes=[hk], slow=True)
        bk = kb.bank()
        kb.op(PE, lambda: nc.tensor.matmul(bk[:, 0:64], aI[:], hk[:, 0:64], start=True, stop=True), reads=[aI, hk], writes=[bk])
        kb.op(PE, lambda: nc.tensor.matmul(bk[0:8, 64:128], aI[0:8, 120:128], hk[0:8, 64:128], start=True, stop=True), reads=[aI, hk], writes=[bk])
        kb.op(DVE, lambda: nc.vector.tensor_copy(EAt[:], bk[:, 0:64]), reads=[bk], writes=[EAt])
        kb.op(ACT, lambda: nc.scalar.activation(EAt[:], EAt[:], AF.Exp), reads=[EAt], writes=[EAt])
        kb.op(DVE, lambda: nc.vector.tensor_copy(EBt[:], bk[0:8, 64:128]), reads=[bk], writes=[EBt])
        kb.op(ACT, lambda: nc.scalar.activation(EBt[:], EBt[:], AF.Exp), reads=[EBt], writes=[EBt])
        EsT = []
        for li in range(2):
            sk = kb.sb("sk", [128, 4], F32)
            kb.dma(SP, sk[:], sinks.ap()[li], writes=[sk])
            kb.op(ACT, lambda sk=sk: nc.scalar.activation(sk[:], sk[:], AF.Exp), reads=[sk], writes=[sk])
            est = sk
            EsT.append(est)

        kb.dead = False
        kb.chk("bias")
        kb.dead = bool(_os.environ.get("KSKIP_STATE"))
        NVB = 5
        ATT = []
        for li in range(2):
            A = {}
            A["KK"] = [kb.sb("KK%d" % k, [128, 128 + 512], BF16) for k in range(2)]
            A["Vz"] = [kb.sb("Vz%d" % i, [128, 320], BF16) for i in range(NVB)]
            for vz in A["Vz"]:
                kb.op(POOL, lambda vz=vz: nc.gpsimd.memset(vz[:], 0.0), writes=[vz])
            A["kT32"] = kb.sb("kT32", [128, 128], F32)
            A["v32"] = kb.sb("v32", [128, 128], F32)
            ATT.append(A)
        CONV = []
        for li in range(2):
            C = {}
            C["halo"] = kb.sb("chalo", [128, 8, 30], F32)
            kb.op(POOL, lambda b=C["halo"]: nc.gpsimd.memset(b[:], 0.0), writes=[C["halo"]])
            C["wdw"] = kb.sb("wdw", [128, 8, 31], F32)
            kb.dma(SP, C["wdw"][:], w_dw.ap()[li], writes=[C["wdw"]])
            C["cv"] = kb.sb("cv", [128, 3, 8], F32)
            kb.dma(SP, C["cv"][:], cvec.ap()[li], writes=[C["cv"]])
            CONV.append(C)
        FFN = []
        for l in range(4):
            Fd = {}
            Fd["halo"] = kb.sb("fhalo", [128, NJ, 2], F32)
            kb.op(POOL, lambda b=Fd["halo"]: nc.gpsimd.memset(b[:], 0.0), writes=[Fd["halo"]])
            Fd["cw"] = kb.sb("fcw", [128, NJ, 3], F32)
            Fd["cb"] = kb.sb("fcb", [128, NJ], F32)
            kb.dma(SP, Fd["cw"][:], f_cw.ap()[l], writes=[Fd["cw"]])
            kb.dma(SP, Fd["cb"][:], f_cb.ap()[l], writes=[Fd["cb"]])
            FFN.append(Fd)

        kb.dead = False
        MIX = P16[16:22] + [kb.sb("MIX%d" % c, [128, 512], BF16) for c in range(2)]
        XR = [[P32[0], P32[1]], [P32[2], P32[3]]]
        GG = [[P32[4], P32[5]], [P32[6], P32[7]]]
        tA = [P32[8], P32[9]]
        tB = [P32[10], P32[11]]
        YS = [P32[12], P32[13]]
        HB = [[P16[0], P16[1], P16[2], P16[3]], [P16[4], P16[5], P16[6], P16[7]]]
        GEL = P16[8:12]
        cfx = [kb.sb("cfx%d" % i, [128, 2], F32) for i in range(4)]
        sso = [kb.sb("sso%d" % i, [128, NSEQ], F32) for i in range(2)]
        m9 = kb.sb("m9", [128, NSEQ * 9], F32)
        PT = P16[12:16]
        EX = [P32[0], P32[1]]
        dn_t = P32[2]
        YF = P16
        GR = [P32[0], P32[1], P32[2]]
        ACC = [P32[3], P32[4], P32[5]]
        SG = [P32[12], P32[13]]
        LNY = P32[0:8]
        GLW = [P32[8], P32[9]]
        SQF = [P32[10], P32[11]]
        mu_t = P32[12]
        LNS = P16[0:8]
        rr = {"t": 0, "pt": 0, "ex": 0, "gr": 0, "acc": 0, "sg": 0, "ys": 0}

        def nxt(lst, key):
            b = lst[rr[key] % len(lst)]
            rr[key] += 1
            return b

        def s5_core(T, W, nrep, L, sample_h0=None, ssm_out=None):
            def v3(b):
                return V(b, 0, 128, 0, [(L, nrep), (1, L)])
            for m in range(4):
                bky = kb.bank()
                cpairs = []
                for half in range(2):
                    _set = (2 * m + half) % 2
                    XR = [[P32[4 * _set + 0], P32[4 * _set + 1]], [P32[4 * _set + 2], P32[4 * _set + 3]]]
                    GG = XR
                    for q in range(2):
                        jj = 2 * half + q
                        j = 4 * m + jj
                        bre, bim = kb.bank(), kb.bank()
                        for ri, bkk in ((0, bre), (1, bim)):
                            mm_group(bkk[:, 0:W], [(T["Bl"][ri][:, j, :], UT[m][:, 0:W], [T["Bl"][ri], UT[m]])], bkk, [])
                        cv = V(T["ctab"], 0, 128, j * 64, [(0, nrep), (1, L)])
                        sv = V(T["stab"], 0, 128, j * 64, [(0, nrep), (1, L)])
                        a1, a2 = tA[0], tB[0]
                        kb.op(DVE, lambda: nc.vector.tensor_tensor(v3(a1), v3(bre), cv, ALU.mult), reads=[bre, T["ctab"]], writes=[a1])
                        kb.op(DVE, lambda: nc.vector.tensor_tensor(v3(a2), v3(bim), sv, ALU.mult), reads=[bim, T["stab"]], writes=[a2])
                        kb.op(DVE, lambda: nc.vector.tensor_tensor(XR[0][q][:, 0:W], a1[:, 0:W], a2[:, 0:W], ALU.add),
                              reads=[a1, a2], writes=[XR[0][q]])
                        a1, a2 = tA[1], tB[1]
                        kb.op(DVE, lambda: nc.vector.tensor_tensor(v3(a1), v3(bim), cv, ALU.mult), reads=[bim, T["ctab"]], writes=[a1])
                        kb.op(DVE, lambda: nc.vector.tensor_tensor(v3(a2), v3(bre), sv, ALU.mult), reads=[bre, T["stab"]], writes=[a2])
                        kb.op(DVE, lambda: nc.vector.tensor_tensor(XR[1][q][:, 0:W], a1[:, 0:W], a2[:, 0:W], ALU.subtract),
                              reads=[a1, a2], writes=[XR[1][q]])
                    j0 = 4 * m + 2 * half
                    if sample_h0 is None:
                        nseg = W // 64
                        for sgi in range(nseg):
                            for q in range(2):
                                j = j0 + q
                                for ri in range(2):
                                    kb.op(DVE, lambda q=q, j=j, ri=ri, sgi=sgi: nc.vector.tensor_tensor_scan(
                                        GG[ri][q][:, sgi * 64:(sgi + 1) * 64], V(T["rho"], 0, 128, j, [(0, 64)]),
                                        XR[ri][q][:, sgi * 64:(sgi + 1) * 64], T["car"][ri][:, j:j + 1], ALU.mult, ALU.add),
                                        reads=[T["rho"], XR[ri][q], T["car"][ri]], writes=[GG[ri][q]])
                            col = sgi * 64 + 63
                            for ri in range(2):
                                for q in range(2):
                                    kb.op(DVE, lambda ri=ri, q=q: nc.vector.tensor_copy(cfx[ri][:, q:q + 1], GG[ri][q][:, col:col + 1]),
                                          reads=[GG[ri][q]], writes=[cfx[ri]])
                            cc = V(T["ctab"], 0, 128, j0 * 64 + 63, [(64, 2)])
                            ss = V(T["stab"], 0, 128, j0 * 64 + 63, [(64, 2)])
                            kb.op(DVE, lambda: nc.vector.tensor_tensor(cfx[2][:], cfx[0][:], cc, ALU.mult), reads=[cfx[0], T["ctab"]], writes=[cfx[2]])
                            kb.op(DVE, lambda: nc.vector.tensor_tensor(cfx[3][:], cfx[1][:], ss, ALU.mult), reads=[cfx[1], T["stab"]], writes=[cfx[3]])
                            kb.op(DVE, lambda: nc.vector.tensor_tensor(T["car"][0][:, j0:j0 + 2], cfx[2][:], cfx[3][:], ALU.subtract),
                                  reads=[cfx[2], cfx[3]], writes=[T["car"][0]])
                            kb.op(DVE, lambda: nc.vector.tensor_tensor(cfx[2][:], cfx[0][:], ss, ALU.mult), reads=[cfx[0], T["stab"]], writes=[cfx[2]])
                            kb.op(DVE, lambda: nc.vector.tensor_tensor(cfx[3][:], cfx[1][:], cc, ALU.mult), reads=[cfx[1], T["ctab"]], writes=[cfx[3]])
                            kb.op(DVE, lambda: nc.vector.tensor_tensor(T["car"][1][:, j0:j0 + 2], cfx[2][:], cfx[3][:], ALU.add),
                                  reads=[cfx[2], cfx[3]], writes=[T["car"][1]])
                    else:
                        for q in range(2):
                            j = j0 + q
                            for ri in range(2):
                                x9, g9 = tA[ri], tB[ri]
                                kb.dma(SP, V(x9, 0, 128, 0, [(9, NSEQ)]), st_ssm.ap()[sample_h0, ri][:, j, :], writes=[x9], slow=True)
                                kb.op(DVE, lambda: nc.vector.tensor_copy(V(x9, 0, 128, 1, [(9, NSEQ), (1, 8)]),
                                                                         V(XR[ri][q], 0, 128, 0, [(8, NSEQ), (1, 8)])),
                                      reads=[XR[ri][q]], writes=[x9])
                                if ri == 0:
                                    kb.op(DVE, lambda: nc.vector.tensor_copy(V(m9, 0, 128, 0, [(9, NSEQ), (1, 9)]), V(T["rho9"], 0, 128, j * 9, [(0, NSEQ), (1, 9)])),
                                          reads=[T["rho9"]], writes=[m9])
                                kb.op(DVE, lambda: nc.vector.tensor_tensor_scan(g9[:, 0:NSEQ * 9], m9[:, 0:NSEQ * 9],
                                                                                x9[:, 0:NSEQ * 9], 0.0, ALU.mult, ALU.add),
                                      reads=[m9, x9], writes=[g9])
                                kb.op(DVE, lambda: nc.vector.tensor_copy(V(GG[ri][q], 0, 128, 0, [(8, NSEQ), (1, 8)]),
                                                                         V(g9, 0, 128, 1, [(9, NSEQ), (1, 8)])),
                                      reads=[g9], writes=[GG[ri][q]])
                    for q in range(2):
                        jj = 2 * half + q
                        j = j0 + q
                        cv = V(T["ctab"], 0, 128, j * 64, [(0, nrep), (1, L)])
                        sv = V(T["stab"], 0, 128, j * 64, [(0, nrep), (1, L)])
                        a1, a2 = P32[12], P32[13]
                        kb.op(POOL, lambda: nc.gpsimd.tensor_tensor(v3(a1), v3(GG[0][q]), cv, ALU.mult), reads=[GG[0][q], T["ctab"]], writes=[a1])
                        kb.op(POOL, lambda: nc.gpsimd.tensor_tensor(v3(a2), v3(GG[1][q]), sv, ALU.mult), reads=[GG[1][q], T["stab"]], writes=[a2])
                        kb.op(POOL, lambda: nc.gpsimd.tensor_tensor(HB[0][jj][:, 0:W], a1[:, 0:W], a2[:, 0:W], ALU.subtract),
                              reads=[a1, a2], writes=[HB[0][jj]])
                        if sample_h0 is not None:
                            kb.op(POOL, lambda: nc.gpsimd.tensor_tensor(sso[0][:, :], V(a1, 0, 128, 7, [(8, NSEQ)]), V(a2, 0, 128, 7, [(8, NSEQ)]), ALU.subtract),
                                  reads=[a1, a2], writes=[sso[0]])
                            kb.dma(SP, o_ssm_s.ap()[sample_h0, 0][:, j, :], sso[0][:, :], reads=[sso[0]], is_out=True)
                        a1, a2 = P32[12], P32[13]
                        kb.op(POOL, lambda: nc.gpsimd.tensor_tensor(v3(a1), v3(GG[0][q]), sv, ALU.mult), reads=[GG[0][q], T["stab"]], writes=[a1])
                        kb.op(POOL, lambda: nc.gpsimd.tensor_tensor(v3(a2), v3(GG[1][q]), cv, ALU.mult), reads=[GG[1][q], T["ctab"]], writes=[a2])
                        kb.op(POOL, lambda: nc.gpsimd.tensor_tensor(HB[1][jj][:, 0:W], a1[:, 0:W], a2[:, 0:W], ALU.add),
                              reads=[a1, a2], writes=[HB[1][jj]])
                        if sample_h0 is not None:
                            kb.op(POOL, lambda: nc.gpsimd.tensor_tensor(sso[1][:, :], V(a1, 0, 128, 7, [(8, NSEQ)]), V(a2, 0, 128, 7, [(8, NSEQ)]), ALU.add),
                                  reads=[a1, a2], writes=[sso[1]])
                            kb.dma(SP, o_ssm_s.ap()[sample_h0, 1][:, j, :], sso[1][:, :], reads=[sso[1]], is_out=True)
                        for ri in range(2):
                            cpairs.append((T["Cl"][ri][:, j, :], HB[ri][jj][:, 0:W], [T["Cl"][ri], HB[ri][jj]]))
                mm_group(bky[:, 0:W], cpairs, bky, [])
                ys = U32[m]
                kb.op(DVE, lambda: nc.vector.scalar_tensor_tensor(ys[:, 0:W], U32[m][:, 0:W], T["dsk"][:, m:m + 1], bky[:, 0:W], ALU.mult, ALU.add),
                      reads=[U32[m], T["dsk"], bky], writes=[ys])
                kb.op(ACT, lambda: nc.scalar.activation(U32[m][:, 0:W], ys[:, 0:W], AF.Gelu_apprx_tanh), reads=[ys], writes=[U32[m]])
                kb.op(ACT, lambda: nc.scalar.copy(GEL[m][:, 0:W], U32[m][:, 0:W]), reads=[U32[m]], writes=[GEL[m]])
            for n in range(4):
                wb = load_w(WB["w_glu"].ap()[li_cur[0], n], 4, 128)
                bk = kb.bank()
                mm_group(bk[:, 0:W], [(wb[:, k, 0:128], GEL[k][:, 0:W], [wb, GEL[k]]) for k in range(4)], bk, [])
                sg = SG[n % 2]
                kb.op(DVE, lambda: nc.vector.tensor_scalar(sg[:, 0:W], bk[:, 0:W], T["bgl"][:, n:n + 1], None, ALU.add), reads=[bk, T["bgl"]], writes=[sg])
                kb.op(ACT, lambda: nc.scalar.activation(sg[:, 0:W], sg[:, 0:W], AF.Sigmoid), reads=[sg], writes=[sg])
                kb.op(DVE, lambda: nc.vector.tensor_tensor(MIX[n][:, 0:W], U32[n][:, 0:W], sg[:, 0:W], ALU.mult),
                      reads=[U32[n], sg], writes=[MIX[n]])

        li_cur = [0]

        def in_proj(li, W, want_v_tok_blocks):
            for n in range(4):
                wb = load_w(WB["w_in"].ap()[li, n], 8, 128)
                if n == 0:
                    kb.chk("ip_w")
                bk = kb.bank()
                _mv = _os.environ.get("KMM", "")
                if _mv == "k":
                    mm_group(bk[:, 0:W], [(ones_bf[:], H[k][:, 0:W], [ones_bf, H[k]]) for k in range(8)], bk, [])
                elif _mv == "m":
                    for k in range(8):
                        kb.op(DVE, lambda k=k: nc.vector.tensor_copy(H[k][:, 0:W], X[k][:, 0:W]), reads=[X[k]], writes=[H[k]])
                    mm_group(bk[:, 0:W], [(wb[:, k, 0:128], H[k][:, 0:W], [wb, H[k]]) for k in range(8)], bk, [])
                elif _mv == "q":
                    for k in range(8):
                        kb.op(DVE, lambda k=k: nc.vector.tensor_copy(P16[k][:, 0:W], X[k][:, 0:W]), reads=[X[k]], writes=[P16[k]])
                    mm_group(bk[:, 0:W], [(wb[:, k, 0:128], P16[k][:, 0:W], [wb, P16[k]]) for k in range(8)], bk, [])
                elif _mv == "n":
                    for k in range(8):
                        kb.op(ACT, lambda k=k: nc.scalar.copy(H[k][:, 0:W], X[k][:, 0:W]), reads=[X[k]], writes=[H[k]])
                    mm_group(bk[:, 0:W], [(wb[:, k, 0:128], H[k][:, 0:W], [wb, H[k]]) for k in range(8)], bk, [])
                elif _mv == "l":
                    mm_group(bk[:, 0:W], [(wb[:, k, 0:128], sq[k][:, 0:W], [wb, sq[k]]) for k in range(8)], bk, [])
                else:
                    mm_group(bk[:, 0:W], [(wb[:, k, 0:128], H[k][:, 0:W], [wb, H[k]]) for k in range(8)], bk, [])
                if n == 0:
                    kb.chk("ip_mm")
                kb.op(DVE, lambda: nc.vector.tensor_copy(UT[n][:, 0:W], bk[:, 0:W]), reads=[bk], writes=[UT[n]])
                if n == 0:
                    kb.chk("ip_act")
                import os
                tv = os.environ.get("KVAR", "")
                if tv == "a":
                    kb.op(DVE, lambda: nc.vector.tensor_copy(P32[5][:, 0:W], bk[:, 0:W]), reads=[bk], writes=[P32[5]])
                elif tv == "c":
                    kb.op(DVE, lambda: nc.vector.tensor_scalar(U32[n][:, 0:W], bk[:, 0:W], 1.0, None, ALU.mult), reads=[bk], writes=[U32[n]])
                elif tv == "d":
                    kb.op(DVE, lambda: nc.vector.tensor_copy(U32[n][:, 0:W], bk[:, 0:W]), reads=[bk], writes=[U32[n]])
                elif tv == "g":
                    ob = kb.banks[(kb.psum_rr - 2) % 8]
                    kb.op(DVE, lambda: nc.vector.tensor_copy(P32[5][:, 0:W], ob[:, 0:W]), reads=[ob], writes=[P32[5]])
                elif tv == "g2":
                    ob = kb.banks[(kb.psum_rr - 2) % 8]
                    kb.op(DVE, lambda: nc.vector.tensor_copy(P32[5][:, 0:W], ob[:, 0:W]), reads=[ob, bk], writes=[P32[5]])
                elif tv == "j":
                    ob = kb.banks[(kb.psum_rr - 2) % 8]
                    kb.op(PE, lambda: nc.tensor.matmul(ob[:, 0:W], ones_bf[:], H[0][:, 0:W], start=True, stop=True), reads=[ones_bf, H[0]], writes=[ob])
                    kb.op(DVE, lambda: nc.vector.tensor_copy(U32[n][:, 0:W], bk[:, 0:W]), reads=[bk], writes=[U32[n]])
                elif tv == "h":
                    kb.op(DVE, lambda: nc.vector.tensor_copy(P32[5][0:64, 0:W], bk[0:64, 0:W]), reads=[bk], writes=[P32[5]])
                elif tv == "i":
                    kb.op(DVE, lambda: nc.vector.tensor_copy(P32[5][:, 0:64], bk[:, 0:64]), reads=[bk], writes=[P32[5]])
                elif tv == "e":
                    kb.op(DVE, lambda: nc.vector.tensor_copy(U32[n][:, 0:W], P32[6][:, 0:W]), reads=[P32[6]], writes=[U32[n]])
                elif tv == "f":
                    kb.op(DVE, lambda: nc.vector.tensor_copy(P32[5][:, 0:W], X[0][:, 0:W]), reads=[X[0]], writes=[P32[5]])
                elif tv == "b":
                    kb.op(DVE, lambda: nc.vector.tensor_copy(U32[n][:, 0:W], X[0][:, 0:W]), reads=[X[0]], writes=[U32[n]])
                else:
                    kb.op(DVE, lambda: nc.vector.tensor_copy(U32[n][:, 0:W], bk[:, 0:W]), reads=[bk], writes=[U32[n]])
                if n == 0:
                    kb.chk("ip_u0")
            kb.chk("ip_u")
            for n in range(4):
                wb = load_w(WB["w_in"].ap()[li, 4 + n], 8, 128)
                bk = kb.bank()
                mm_group(bk[:, 0:W], [(wb[:, k, 0:128], H[k][:, 0:W], [wb, H[k]]) for k in range(8)], bk, [])
                kb.op(DVE, lambda: nc.vector.tensor_copy(QT[0][n][0:64, 0:W], bk[0:64, 0:W]), reads=[bk], writes=[QT[0][n]])
                kb.op(DVE, lambda: nc.vector.tensor_copy(QT[1][n][64:128, 0:W], bk[64:128, 0:W]), reads=[bk], writes=[QT[1][n]])

        def attn_prompt(li, W, blk0):
            A = ATT[li]
            nb = W // 128
            for kv in range(2):
                wb = load_w(WB["w_in"].ap()[li, 8 + kv], 8, 128)
                bk = kb.bank()
                mm_group(bk[:, 0:W], [(wb[:, k, 0:128], H[k][:, 0:W], [wb, H[k]]) for k in range(8)], bk, [])
                kb.op(DVE, lambda: nc.vector.tensor_copy(A["KK"][kv][:, 128:128 + W], bk[:, 0:W]), reads=[bk], writes=[A["KK"][kv]])
                if kv == 0:
                    kb.op(DVE, lambda: nc.vector.tensor_copy(A["kT32"][0:64, :], bk[0:64, W - 128:W]), reads=[bk], writes=[A["kT32"]])
                else:
                    kb.op(DVE, lambda: nc.vector.tensor_copy(A["kT32"][64:128, :], bk[64:128, W - 128:W]), reads=[bk], writes=[A["kT32"]])
            kb.chk("at_k")
            wv = load_w(WB["w_in"].ap()[li, 10], 8, 128)
            for b in range(nb):
                vz = A["Vz"][(blk0 + b) % NVB]
                bk = kb.bank()
                mm_group(bk[:, 0:128], [(H[k][:, b * 128:(b + 1) * 128], wv[:, k, 0:128], [H[k], wv]) for k in range(8)], bk, [])
                kb.op(DVE, lambda: nc.vector.tensor_copy(vz[:, 64:128], bk[:, 0:64]), reads=[bk], writes=[vz])
                kb.op(DVE, lambda: nc.vector.tensor_copy(vz[:, 192:256], bk[:, 64:128]), reads=[bk], writes=[vz])
                if b == nb - 1:
                    kb.op(DVE, lambda: nc.vector.tensor_copy(A["v32"][:, :], bk[:, 0:128]), reads=[bk], writes=[A["v32"]])
            kb.chk("at_v")
            for b in range(nb):
                gb = blk0 + b
                has_prev = gb >= 1
                q0 = b * 128
                pts = []
                for c in range(4):
                    kv = c // 2
                    bk = kb.bank()
                    for e in range(2):
                        kb.op(PE, lambda e=e, bk=bk: nc.tensor.matmul(bk[:, e * 128:(e + 1) * 128],
                                                                      A["KK"][kv][:, 128 + q0:256 + q0],
                                                                      QT[e][c][:, q0:q0 + 128], start=True, stop=True),
                              reads=[A["KK"][kv], QT[e][c]], writes=[bk], inc=(e == 1 and not has_prev))
                    if has_prev:
                        for e in range(2):
                            kb.op(PE, lambda e=e, bk=bk: nc.tensor.matmul(bk[:, 256 + e * 128:256 + (e + 1) * 128],
                                                                          A["KK"][kv][:, q0:q0 + 128],
                                                                          QT[e][c][:, q0:q0 + 128], start=True, stop=True),
                                  reads=[A["KK"][kv], QT[e][c]], writes=[bk], inc=(e == 1))
                    ncol = 512 if has_prev else 256
                    ex = EX[c % 2]
                    kb.chk("at_mm")
                    kb.op(DVE, lambda bk=bk, ex=ex: nc.vector.tensor_copy(ex[:, 0:ncol], bk[:, 0:ncol]), reads=[bk], writes=[ex])
                    kb.chk("at_cp")
                    kb.op(ACT, lambda ex=ex: nc.scalar.activation(ex[:, 0:ncol], ex[:, 0:ncol], AF.Exp, scale=0.125), reads=[ex], writes=[ex])
                    kb.chk("at_ex")
                    pt = PT[c]
                    kb.op(DVE, lambda ex=ex, pt=pt, c=c: nc.vector.tensor_tensor(pt[:, 0:ncol], ex[:, 0:ncol], Etab[c][:, 0:ncol], ALU.mult),
                          reads=[ex, Etab[c]], writes=[pt])
                    pts.append(pt)
                kb.chk("at_s")
                bn, bd = kb.bank(), kb.bank()
                vcur = A["Vz"][gb % NVB]
                vprev = A["Vz"][(gb - 1) % NVB]
                ocur = Oz0 if gb == 0 else Oz
                oprev = Oz0 if gb == 1 else Oz
                for c in range(4):
                    kv = c // 2
                    npairs, dpairs = [], []
                    for e in range(2):
                        vs = (64 + 128 * kv, 192 + 128 * kv) if e == 0 else (128 * kv, 128 + 128 * kv)
                        osl = (64, 192) if e == 0 else (0, 128)
                        npairs.append((vcur[:, vs[0]:vs[1]], pts[c][:, e * 128:(e + 1) * 128], [vcur, pts[c]]))
                        dpairs.append((ocur[:, osl[0]:osl[1]], pts[c][:, e * 128:(e + 1) * 128], [ocur, pts[c]]))
                        if has_prev:
                            npairs.append((vprev[:, vs[0]:vs[1]], pts[c][:, 256 + e * 128:256 + (e + 1) * 128], [vprev, pts[c]]))
                            dpairs.append((oprev[:, osl[0]:osl[1]], pts[c][:, 256 + e * 128:256 + (e + 1) * 128], [oprev, pts[c]]))
                    mm_group(bn[:, c * 128:(c + 1) * 128], npairs, bn, [])
                    mm_group(bd[:, c * 128:(c + 1) * 128], dpairs, bd, [])
                kb.chk("at_pv")
                for c in range(4):
                    kb.op(DVE, lambda bd=bd, c=c: nc.vector.tensor_scalar(dn_t[:, c * 128:(c + 1) * 128], bd[:, c * 128:(c + 1) * 128], EsT[li][:, c:c + 1], None, ALU.add),
                          reads=[bd, EsT[li]], writes=[dn_t])
                kb.op(DVE, lambda: nc.vector.reciprocal(dn_t[:, 0:512], dn_t[:, 0:512]), reads=[dn_t], writes=[dn_t])
                for c in range(4):
                    kb.op(DVE, lambda c=c, bn=bn: nc.vector.tensor_tensor(MIX[4 + c][:, q0:q0 + 128], bn[:, c * 128:(c + 1) * 128],
                                                                          dn_t[:, c * 128:(c + 1) * 128], ALU.mult),
                          reads=[bn, dn_t], writes=[MIX[4 + c]])
            for kv in range(2):
                kb.op(ACT, lambda kv=kv: nc.scalar.copy(A["KK"][kv][:, 0:128], A["KK"][kv][:, W:W + 128]),
                      reads=[A["KK"][kv]], writes=[A["KK"][kv]])

        def out_proj(dram_w, W, nk, src, after=None):
            wb_next = load_w(dram_w[0], nk, 128)
            for n in range(8):
                wb = wb_next
                if n + 1 < 8:
                    wb_next = load_w(dram_w[n + 1], nk, 128)
                bk = kb.bank()
                mm_group(bk[:, 0:W], [(wb[:, k, 0:128], src[k][:, 0:W], [wb, src[k]]) for k in range(nk)], bk, [])
                kb.op(DVE, lambda n=n, bk=bk: nc.vector.tensor_tensor(X[n][:, 0:W], X[n][:, 0:W], bk[:, 0:W], ALU.add),
                      reads=[X[n], bk], writes=[X[n]])

        def conformer(lo, W, nseq, Tt, halo, zero_pad_cols=0):
            C = CONV[lo]
            ext = 30 + Tt
            wb_next = load_w(WB["w_pw1"].ap()[lo, 0], 8, 256)
            for c in range(8):
                wb = wb_next
                if c + 1 < 8:
                    wb_next = load_w(WB["w_pw1"].ap()[lo, c + 1], 8, 256)
                ba, bg = kb.bank(), kb.bank()
                mm_group(ba[:, 0:W], [(wb[:, k, 0:128], H[k][:, 0:W], [wb, H[k]]) for k in range(8)], ba, [])
                mm_group(bg[:, 0:W], [(wb[:, k, 128:256], H[k][:, 0:W], [wb, H[k]]) for k in range(8)], bg, [])
                sg = SG[c % 2]
                gl = GLW[c % 2]
                kb.op(DVE, lambda: nc.vector.tensor_copy(sg[:, 0:W], bg[:, 0:W]), reads=[bg], writes=[sg])
                kb.op(ACT, lambda: nc.scalar.activation(sg[:, 0:W], sg[:, 0:W], AF.Sigmoid), reads=[sg], writes=[sg])
                if halo is not None:
                    kb.op(ACT, lambda: nc.scalar.copy(V(gl, 0, 128, 0, [(ext, nseq), (1, 30)]), V(halo, 0, 128, c * nseq * 30, [(30, nseq), (1, 30)])),
                          reads=[halo], writes=[gl])
                else:
                    kb.dma(SP, V(gl, 0, 128, 0, [(ext, nseq), (1, 30)]), st_conv.ap()[lo, c * 128:(c + 1) * 128, :, :], writes=[gl])
                kb.op(DVE, lambda: nc.vector.tensor_tensor(V(gl, 0, 128, 30, [(ext, nseq), (1, Tt)]),
                                                           V(ba, 0, 128, 0, [(Tt, nseq), (1, Tt)]),
                                                           V(sg, 0, 128, 0, [(Tt, nseq), (1, Tt)]), ALU.mult),
                      reads=[ba, sg], writes=[gl])
                if halo is not None:
                    kb.op(ACT, lambda: nc.scalar.copy(V(halo, 0, 128, c * nseq * 30, [(30, nseq), (1, 30)]), V(gl, 0, 128, Tt, [(ext, nseq), (1, 30)])),
                          reads=[gl], writes=[halo])
                else:
                    kb.dma(SP, o_conv_s.ap()[lo, c * 128:(c + 1) * 128, :, :], V(gl, 0, 128, Tt, [(ext, nseq), (1, 30)]), reads=[gl], is_out=True)
                eng_e, eng = (DVE, nc.vector)
                y = LNY[c]
                yv = V(y, 0, 128, 0, [(Tt, nseq), (1, Tt)])
                kb.op(eng_e, lambda: eng.tensor_scalar(yv, V(gl, 0, 128, 0, [(ext, nseq), (1, Tt)]), C["wdw"][:, c, 0:1], C["cv"][:, 0, c:c + 1],
                                                       ALU.mult, ALU.add), reads=[gl, C["wdw"], C["cv"]], writes=[y])
                for k in range(1, 31):
                    kb.op(eng_e, lambda k=k: eng.scalar_tensor_tensor(yv, V(gl, 0, 128, k, [(ext, nseq), (1, Tt)]), C["wdw"][:, c, k:k + 1], yv,
                                                                      ALU.mult, ALU.add), reads=[gl, C["wdw"], y], writes=[y])
            bm, b2 = kb.bank(), kb.bank()
            mm_group(bm[:, 0:W], [(ones_f[:], LNY[c][:, 0:W], [LNY[c]]) for c in range(8)], bm, [ones_f])
            kb.op(DVE, lambda: nc.vector.tensor_scalar(mu_t[:, 0:W], bm[:, 0:W], 1.0 / D, None, ALU.mult), reads=[bm], writes=[mu_t])
            for c in range(8):
                kb.op(DVE, lambda c=c: nc.vector.tensor_tensor(LNY[c][:, 0:W], LNY[c][:, 0:W], mu_t[:, 0:W], ALU.subtract),
                      reads=[LNY[c], mu_t], writes=[LNY[c]])
                sqf = SQF[c % 2]
                kb.op(ACT, lambda c=c, sqf=sqf: nc.scalar.activation(sqf[:, 0:W], LNY[c][:, 0:W], AF.Square), reads=[LNY[c]], writes=[sqf])
                kb.op(PE, lambda c=c, sqf=sqf: nc.tensor.matmul(b2[:, 0:W], ones_f[:], sqf[:, 0:W], start=(c == 0), stop=(c == 7)),
                      reads=[ones_f, sqf], writes=[b2])
            kb.op(DVE, lambda: nc.vector.tensor_scalar(rs[:, 0:W], b2[:, 0:W], 1.0 / D, EPS, ALU.mult, ALU.add), reads=[b2], writes=[rs])
            kb.op(ACT, lambda: nc.scalar.activation(rs[:, 0:W], rs[:, 0:W], AF.Sqrt), reads=[rs], writes=[rs])
            kb.op(DVE, lambda: nc.vector.reciprocal(rinv[:, 0:W], rs[:, 0:W]), reads=[rs], writes=[rinv])
            for c in range(8):
                kb.op(DVE, lambda c=c: nc.vector.scalar_tensor_tensor(LNY[c][:, 0:W], LNY[c][:, 0:W], C["cv"][:, 1, c:c + 1], rinv[:, 0:W],
                                                                      ALU.mult, ALU.mult), reads=[LNY[c], C["cv"], rinv], writes=[LNY[c]])
                kb.op(ACT, lambda c=c: nc.scalar.activation(LNY[c][:, 0:W], LNY[c][:, 0:W], AF.Silu, bias=C["cv"][:, 2, c:c + 1], scale=1.0),
                      reads=[LNY[c], C["cv"]], writes=[LNY[c]])
                kb.op(DVE, lambda c=c: nc.vector.tensor_copy(LNS[c][:, 0:W], LNY[c][:, 0:W]), reads=[LNY[c]], writes=[LNS[c]])
            out_proj(WB["w_pw2"].ap()[lo], W, 8, LNS)
            if zero_pad_cols:
                for c in range(8):
                    kb.op(DVE, lambda c=c: nc.vector.memset(X[c][:, 0:zero_pad_cols], 0.0), writes=[X[c]])

        ones_f = kb.sb("ones_f", [128, 128], F32)
        kb.op(DVE, lambda: nc.vector.memset(ones_f[:], 1.0), writes=[ones_f])

        def ffn_down(l, W):
            for n in range(8):
                wd0 = load_wdn(WB["w_dn"].ap()[l, n, 0])
                wd1 = load_wdn(WB["w_dn"].ap()[l, n, 1])
                bk = kb.bank()
                mm_group(bk[:, 0:W], [((wd0 if j < 11 else wd1)[:, j % 11, :], YF[j][:, 0:W], [wd0 if j < 11 else wd1, YF[j]]) for j in range(NJ)], bk, [])
                kb.op(DVE, lambda n=n, bk=bk: nc.vector.tensor_tensor(X[n][:, 0:W], X[n][:, 0:W], bk[:, 0:W], ALU.add),
                      reads=[X[n], bk], writes=[X[n]])

        def ffn_up(l, W, nseq, Tt, halo):
            Fd = FFN[l]
            ext = 2 + Tt
            wb_next = load_w(WB["w_up"].ap()[l, 0], 8, 256)
            for j in range(NJ):
                wb = wb_next
                if j + 1 < NJ:
                    wb_next = load_w(WB["w_up"].ap()[l, j + 1], 8, 256)
                bg, bu = kb.bank(), kb.bank()
                mm_group(bg[:, 0:W], [(wb[:, k, 0:128], H[k][:, 0:W], [wb, H[k]]) for k in range(8)], bg, [])
                mm_group(bu[:, 0:W], [(wb[:, k, 128:256], H[k][:, 0:W], [wb, H[k]]) for k in range(8)], bu, [])
                gr = GR[j % 3]
                kb.op(DVE, lambda: nc.vector.tensor_copy(V(gr, 0, 128, 2, [(ext, nseq), (1, Tt)]), V(bg, 0, 128, 0, [(Tt, nseq), (1, Tt)])),
                      reads=[bg], writes=[gr])
                if halo is not None:
                    kb.op(ACT, lambda: nc.scalar.copy(V(gr, 0, 128, 0, [(ext, nseq), (1, 2)]), V(halo, 0, 128, j * nseq * 2, [(2, nseq), (1, 2)])),
                          reads=[halo], writes=[gr])
                else:
                    kb.dma(SP, V(gr, 0, 128, 0, [(ext, nseq), (1, 2)]), st_ffn.ap()[l, j * 128:(j + 1) * 128, :, :], writes=[gr], slow=True)
                acc = ACC[j % 3]
                av = V(acc, 0, 128, 0, [(Tt, nseq), (1, Tt)])
                kb.op(DVE, lambda: nc.vector.tensor_scalar(av, V(gr, 0, 128, 0, [(ext, nseq), (1, Tt)]), Fd["cw"][:, j, 0:1], Fd["cb"][:, j:j + 1],
                                                           ALU.mult, ALU.add), reads=[gr, Fd["cw"], Fd["cb"]], writes=[acc])
                for k in (1, 2):
                    kb.op(DVE, lambda k=k: nc.vector.scalar_tensor_tensor(av, V(gr, 0, 128, k, [(ext, nseq), (1, Tt)]), Fd["cw"][:, j, k:k + 1], av,
                                                                          ALU.mult, ALU.add), reads=[gr, Fd["cw"], acc], writes=[acc])
                if halo is not None:
                    kb.op(ACT, lambda: nc.scalar.copy(V(halo, 0, 128, j * nseq * 2, [(2, nseq), (1, 2)]), V(gr, 0, 128, Tt, [(ext, nseq), (1, 2)])),
                          reads=[gr], writes=[halo])
                else:
                    kb.dma(SP, o_ffn_s.ap()[l, j * 128:(j + 1) * 128, :, :], V(gr, 0, 128, Tt, [(ext, nseq), (1, 2)]), reads=[gr], is_out=True, slow=True)
                kb.op(ACT, lambda: nc.scalar.activation(acc[:, 0:W], acc[:, 0:W], AF.Gelu_apprx_tanh), reads=[acc], writes=[acc])
                kb.op(DVE, lambda: nc.vector.tensor_tensor(YF[j][:, 0:W], acc[:, 0:W], bu[:, 0:W], ALU.mult), reads=[acc, bu], writes=[YF[j]])

        chunks = [(0, 128)] + [(128 + 512 * i, 512) for i in range(16)]
        chunks = chunks[:nchunk_prompt]
        for ci, (t0, W) in enumerate(chunks):
            blk0 = t0 // 128
            for c in range(8):
                kb.dma(SP, X[c][:, 0:W], xT_p.ap()[c * 128:(c + 1) * 128, t0:t0 + W], writes=[X[c]])
            for l in range(4):
                rmsnorm(X, l, H, W)
                kb.chk("n0")
                if l % 2 == 0:
                    li = l // 2
                    li_cur[0] = li
                    in_proj(li, W, None)
                    kb.chk("inproj")
                    s5_core(S5[li], W, W // 64, 64)
                    kb.chk("s5c")
                    attn_prompt(li, W, blk0)
                    kb.chk("att")
                    if ci == dbg_ci and li == 0:
                        dump16(MIX, W)
                        dump16(UT + QT[0], W)
                        dump(Etab + Etab, 512)
                    out_proj(WB["w_out"].ap()[li], W, 8, MIX)
                    if ci == dbg_ci:
                        dump(X, W)
                else:
                    lo = l // 2
                    conformer(lo, W, 1, W, CONV[lo]["halo"], zero_pad_cols=(NPAD if ci == 0 else 0))
                    if ci == dbg_ci:
                        dump(X, W)
                rmsnorm(X, 4 + l, H, W)
                ffn_up(l, W, 1, W, FFN[l]["halo"])
                ffn_down(l, W)
                if ci == dbg_ci:
                    dump(X, W)
                if ci == 0 and l % 2 == 1:
                    pass
            for c in range(8):
                pass
            YO = [P32[c] for c in range(8)]
            rmsnorm(X, 8, None, W, out_f32=YO)
            for c in range(8):
                kb.dma(SP, yT_p.ap()[c * 128:(c + 1) * 128, t0:t0 + W], YO[c][:, 0:W], reads=[YO[c]], is_out=True)
        if nchunk_prompt == 17:
            for li in range(2):
                for ri in range(2):
                    kb.dma(SP, o_ssm_p.ap()[li, ri], S5[li]["car"][ri][:], reads=[S5[li]["car"][ri]], is_out=True)
                kb.dma(SP, o_kT_p.ap()[li], ATT[li]["kT32"][:], reads=[ATT[li]["kT32"]], is_out=True)
                kb.dma(SP, o_v_p.ap()[li], ATT[li]["v32"][:], reads=[ATT[li]["v32"]], is_out=True)
                for c in range(8):
                    kb.dma(SP, o_conv_p.ap()[li, c * 128:(c + 1) * 128, :], CONV[li]["halo"][:, c, :], reads=[CONV[li]["halo"]], is_out=True)
            for l in range(4):
                kb.dma(SP, o_ffn_p.ap()[l].rearrange("(j p) t -> p j t", p=128), FFN[l]["halo"][:], reads=[FFN[l]["halo"]], is_out=True)

        def attn_sample(li):
            A = ATT[li]
            W = 128
            for kv in range(2):
                wb = load_w(WB["w_in"].ap()[li, 8 + kv], 8, 128)
                bk = kb.bank()
                mm_group(bk[:, 0:W], [(wb[:, k, 0:128], H[k][:, 0:W], [wb, H[k]]) for k in range(8)], bk, [])
                kb.op(DVE, lambda: nc.vector.tensor_copy(KKs[kv][:, :], bk[:, 0:W]), reads=[bk], writes=[KKs[kv]])
                kb.op(DVE, lambda: nc.vector.tensor_copy(A["kT32"][64 * kv:64 * kv + 64, :], bk[64 * kv:64 * kv + 64, 0:W]), reads=[bk], writes=[A["kT32"]])
            wv = load_w(WB["w_in"].ap()[li, 10], 8, 128)
            bk = kb.bank()
            mm_group(bk[:, 0:128], [(H[k][:, 0:128], wv[:, k, 0:128], [H[k], wv]) for k in range(8)], bk, [])
            kb.op(DVE, lambda: nc.vector.tensor_copy(vbf[:, :], bk[:, 0:128]), reads=[bk], writes=[vbf])
            kb.op(DVE, lambda: nc.vector.tensor_copy(A["v32"][:, :], bk[:, 0:128]), reads=[bk], writes=[A["v32"]])
            kb.dma(SP, o_kT_s.ap()[li][:, :, 0:120], st_kT.ap()[li][:, :, 8:128], is_out=True, slow=True)
            kb.dma(SP, o_v_s.ap()[li][:, 0:120, :], st_v.ap()[li][:, 8:128, :], is_out=True)
            kb.dma(SP, o_kT_s.ap()[li][:, :, 120:128].rearrange("s p t -> p s t"), V(A["kT32"], 0, 128, 0, [(8, NSEQ), (1, 8)]),
                   reads=[A["kT32"]], is_out=True, slow=True)
            for s_ in range(NSEQ):
                kb.dma(SP, o_v_s.ap()[li, s_][120:128, :], A["v32"][8 * s_:8 * s_ + 8, :], reads=[A["v32"]], is_out=True)
            bn, bd = kb.bank(), kb.bank()
            sbanks = [kb.bank() for _ in range(4)]
            for sq_i in range(NSEQ):
                kx = [KX[kv][sq_i % 2] for kv in range(2)]
                vz, vb = VZS[sq_i % 2], VBS[sq_i % 2]
                for kv in range(2):
                    for hf in range(2):
                        kb.dma(POOL, kx[kv][64 * hf:64 * hf + 64, 0:128], st_kT.ap()[li, sq_i][64 * kv:64 * kv + 64, :], writes=[kx[kv]])
                    kb.op(ACT, lambda kv=kv: nc.scalar.copy(kx[kv][:, 128:136], KKs[kv][:, 8 * sq_i:8 * sq_i + 8]), reads=[KKs[kv]], writes=[kx[kv]])
                    kb.dma(POOL, vz[0:120, 64 + 128 * kv:128 + 128 * kv], st_v.ap()[li, sq_i][8:128, 64 * kv:64 * kv + 64], writes=[vz])
                    kb.dma(POOL, vb[0:8, 64 + 128 * kv:128 + 128 * kv], st_v.ap()[li, sq_i][0:8, 64 * kv:64 * kv + 64], writes=[vb])
                    kb.dma(POOL, vz[120:128, 64 + 128 * kv:128 + 128 * kv], vbf[8 * sq_i:8 * sq_i + 8, 64 * kv:64 * kv + 64], reads=[vbf], writes=[vz])
                ba, bb = sbanks[2 * (sq_i % 2)], sbanks[2 * (sq_i % 2) + 1]
                for c in range(4):
                    kv = c // 2
                    for e in range(2):
                        col = c * 16 + e * 8
                        kb.op(PE, lambda: nc.tensor.matmul(ba[:, col:col + 8], kx[kv][:, 8:136],
                                                           QT[e][c][:, 8 * sq_i:8 * sq_i + 8], start=True, stop=True),
                              reads=[kx[kv], QT[e][c]], writes=[ba], inc=False)
                        kb.op(PE, lambda: nc.tensor.matmul(bb[0:8, col:col + 8], kx[kv][:, 0:8],
                                                           QT[e][c][:, 8 * sq_i:8 * sq_i + 8], start=True, stop=True),
                              reads=[kx[kv], QT[e][c]], writes=[bb], inc=(c == 3 and e == 1))
                kb.op(DVE, lambda: nc.vector.tensor_copy(EXA[:, :], ba[:, 0:64]), reads=[ba], writes=[EXA])
                kb.op(ACT, lambda: nc.scalar.activation(EXA[:, :], EXA[:, :], AF.Exp, scale=0.125), reads=[EXA], writes=[EXA])
                kb.op(DVE, lambda: nc.vector.tensor_copy(EXB[:, :], bb[0:8, 0:64]), reads=[bb], writes=[EXB])
                kb.op(ACT, lambda: nc.scalar.activation(EXB[:, :], EXB[:, :], AF.Exp, scale=0.125), reads=[EXB], writes=[EXB])
                pa, pb = PAs[sq_i % 2], PBs[sq_i % 2]
                kb.op(DVE, lambda: nc.vector.tensor_tensor(pa[:, :], EXA[:, :], EAt[:, :], ALU.mult), reads=[EXA, EAt], writes=[pa])
                kb.op(DVE, lambda: nc.vector.tensor_tensor(pb[:, :], EXB[:, :], EBt[:, :], ALU.mult), reads=[EXB, EBt], writes=[pb])
                for c in range(4):
                    kv = c // 2
                    npairs, dpairs = [], []
                    for e in range(2):
                        vs = (64 + 128 * kv, 192 + 128 * kv) if e == 0 else (128 * kv, 128 + 128 * kv)
                        osl = (64, 192) if e == 0 else (0, 128)
                        col = c * 16 + e * 8
                        npairs.append((vz[:, vs[0]:vs[1]], pa[:, col:col + 8], [vz, pa]))
                        npairs.append((vb[0:8, vs[0]:vs[1]], pb[0:8, col:col + 8], [vb, pb]))
                        dpairs.append((Oz[:, osl[0]:osl[1]], pa[:, col:col + 8], [Oz, pa]))
                        dpairs.append((Oz[0:8, osl[0]:osl[1]], pb[0:8, col:col + 8], [Oz, pb]))
                    oc = sq_i * 32 + c * 8
                    mm_group(bn[:, oc:oc + 8], npairs, bn, [])
                    mm_group(bd[:, oc:oc + 8], dpairs, bd, [])
            for c in range(4):
                dv = V(dn_t, 0, 128, c * 8, [(32, NSEQ), (1, 8)])
                kb.op(DVE, lambda: nc.vector.tensor_scalar(dv, V(bd, 0, 128, c * 8, [(32, NSEQ), (1, 8)]), EsT[li][:, c:c + 1], None, ALU.add),
                      reads=[bd, EsT[li]], writes=[dn_t])
            kb.op(DVE, lambda: nc.vector.reciprocal(dn_t[:, 0:512], dn_t[:, 0:512]), reads=[dn_t], writes=[dn_t])
            for c in range(4):
                kb.op(DVE, lambda: nc.vector.tensor_tensor(V(MIX[4 + c], 0, 128, 0, [(8, NSEQ), (1, 8)]), V(bn, 0, 128, c * 8, [(32, NSEQ), (1, 8)]),
                                                           V(dn_t, 0, 128, c * 8, [(32, NSEQ), (1, 8)]), ALU.mult),
                      reads=[bn, dn_t], writes=[MIX[4 + c]])

        if do_sample:
            KKs = [kb.sb("KKs%d" % k, [128, 128], BF16) for k in range(2)]
            vbf = kb.sb("vbf", [128, 128], BF16)
            KX = [[kb.sb("KX%d%d" % (k, i), [128, 136], BF16) for i in range(2)] for k in range(2)]
            VZS = [kb.sb("VZS%d" % i, [128, 320], BF16) for i in range(2)]
            VBS = [kb.sb("VBS%d" % i, [8, 320], BF16) for i in range(2)]
            for t_ in VZS + VBS:
                kb.op(POOL, lambda t_=t_: nc.gpsimd.memset(t_[:], 0.0), writes=[t_])
            EXA = kb.sb("EXA", [128, 64], F32)
            EXB = kb.sb("EXB", [8, 64], F32)
            PAs = [kb.sb("PAs%d" % i, [128, 64], BF16) for i in range(2)]
            PBs = [kb.sb("PBs%d" % i, [8, 64], BF16) for i in range(2)]
            W = 128
            for c in range(8):
                kb.dma(SP, X[c][:, 0:W], xT_s.ap()[c * 128:(c + 1) * 128, :], writes=[X[c]])
            for l in range(4):
                rmsnorm(X, l, H, W)
                if l % 2 == 0:
                    li = l // 2
                    li_cur[0] = li
                    in_proj(li, W, None)
                    s5_core(S5[li], W, NSEQ, 8, sample_h0=li)
                    attn_sample(li)
                    if li == 0:
                        dump16(MIX, W)
                        dump([P32[2]] * 8, 128)
                    out_proj(WB["w_out"].ap()[li], W, 8, MIX)
                else:
                    lo = l // 2
                    conformer(lo, W, NSEQ, 8, None)
                rmsnorm(X, 4 + l, H, W)
                ffn_up(l, W, NSEQ, 8, None)
                ffn_down(l, W)
            YO = [P32[c] for c in range(8)]
            rmsnorm(X, 8, None, W, out_f32=YO)
            for c in range(8):
                kb.dma(SP, yT_s.ap()[c * 128:(c + 1) * 128, :], YO[c][:, 0:W], reads=[YO[c]], is_out=True)

        kb.finish()
    return kb.nc


_CACHE = {}


def _prep_common(inp):
    f = np.float32
    d = {}
    gv = np.concatenate([inp["g_mix"], inp["g_ffn"], inp["g_final"][None]], 0)
    d["gvec"] = np.ascontiguousarray(gv.reshape(9, 8, 128).transpose(2, 0, 1)).astype(f)
    wi = inp["w_in_mix"]
    u, q = wi[:, :, 0:512], wi[:, :, 512:1024]
    k0, k1, v = wi[:, :, 1024:1088], wi[:, :, 1088:1152], wi[:, :, 1152:1280]
    def tile_w(w, ncols):
        L, K, N = w.shape
        t = w.reshape(L, K // 128, 128, N // ncols, ncols).transpose(0, 3, 2, 1, 4)
        return np.ascontiguousarray(t.reshape(L, N // ncols, 128, (K // 128) * ncols)).astype(f)
    d["w_in"] = tile_w(np.concatenate([u, q, k0, k0, k1, k1, v], -1), 128)

    def st(a):
        return a.reshape(2, 16, 2, 64).transpose(0, 2, 3, 1).reshape(2, 128, 16)
    ls = np.broadcast_to(inp["ssm_log_step"][:, :, None], (2, 32, 64))
    d["lam"] = np.ascontiguousarray(np.stack([st(inp["ssm_lambda_re"]), st(inp["ssm_lambda_im"]), st(ls)], 1)).astype(f)
    d["ssm_b"] = np.ascontiguousarray(np.stack([inp["ssm_b_re"], inp["ssm_b_im"]], 1)).astype(f)
    d["ssm_c"] = np.ascontiguousarray(np.stack([inp["ssm_c_re"], inp["ssm_c_im"]], 1)).astype(f)
    d["ssm_d"] = np.ascontiguousarray(inp["ssm_d"].reshape(2, 4, 128).transpose(0, 2, 1)).astype(f)
    d["w_glu"] = tile_w(inp["ssm_w_glu"], 128)
    d["b_glu"] = np.ascontiguousarray(inp["ssm_b_glu"].reshape(2, 4, 128).transpose(0, 2, 1)).astype(f)
    d["relb"] = np.ascontiguousarray(inp["rel_bias"]).astype(f)
    sk = inp["attn_sinks"]
    d["sinks"] = np.ascontiguousarray(np.repeat(sk.reshape(2, 4, 2), 64, axis=2).transpose(0, 2, 1)).astype(f)
    d["w_out"] = tile_w(inp["w_out_mix"], 128)
    p1 = inp["conv_w_pw1"]
    a, g = p1[:, :, :1024].reshape(2, 1024, 8, 128), p1[:, :, 1024:].reshape(2, 1024, 8, 128)
    d["w_pw1"] = tile_w(np.concatenate([a, g], -1).reshape(2, 1024, 2048), 256)
    d["w_dw"] = np.ascontiguousarray(inp["conv_w_dw"].reshape(2, 31, 8, 128).transpose(0, 3, 2, 1)).astype(f)
    cv = np.stack([inp["conv_b_dw"], inp["conv_ln_g"], inp["conv_ln_b"]], 1)
    d["cvec"] = np.ascontiguousarray(cv.reshape(2, 3, 8, 128).transpose(0, 3, 1, 2)).astype(f)
    d["w_pw2"] = tile_w(inp["conv_w_pw2"], 128)
    wu = inp["ffn_w_up"]
    gg, uu = wu[:, :, :DFF].reshape(4, 1024, NJ, 128), wu[:, :, DFF:].reshape(4, 1024, NJ, 128)
    d["w_up"] = tile_w(np.concatenate([gg, uu], -1).reshape(4, 1024, NJ * 256), 256)
    d["f_cw"] = np.ascontiguousarray(inp["ffn_w_conv"].reshape(4, 3, NJ, 128).transpose(0, 3, 2, 1)).astype(f)
    d["f_cb"] = np.ascontiguousarray(inp["ffn_b_conv"].reshape(4, NJ, 128).transpose(0, 2, 1)).astype(f)
    wd = inp["ffn_w_down"].reshape(4, 2, 11, 128, 8, 128).transpose(0, 4, 1, 3, 2, 5)
    d["w_dn"] = np.ascontiguousarray(wd.reshape(4, 8, 2, 128, 11 * 128)).astype(f)
    m = np.arange(384)
    dist = m - 127
    inside = (dist >= 0) & (dist < 128)
    oh = np.zeros((32, 384), f)
    bk = t5_bucket_np(np.clip(dist, 0, 127))
    oh[bk[inside], m[inside]] = 1.0
    d["oh_bucket"] = oh
    d["msk_ext"] = np.ascontiguousarray(np.broadcast_to(np.where(inside, 0.0, NEG).astype(f)[None], (8, 384)))
    d["antiI"] = np.ascontiguousarray(np.eye(128, dtype=f)[::-1])
    return d


def kernel(**inp):
    inp = {k: np.asarray(v) for k, v in inp.items()}
    f = np.float32
    if "nc" not in _CACHE:
        _CACHE["nc"] = build()
    nc = _CACHE["nc"]
    com = _prep_common(inp)
    in_maps = []
    for c in range(8):
        d = dict(com)
        s = c % 2
        xp = np.concatenate([np.zeros((NPAD, D), f), inp["meta_tokens"], inp["x_prompt"][s]], 0)
        d["xT_p"] = np.ascontiguousarray(xp.T)
        sl = slice(c * NSEQ, (c + 1) * NSEQ)
        d["xT_s"] = np.ascontiguousarray(inp["x_sample"][sl].reshape(NSEQ * 8, D).T)

        def st(a):
            return a.reshape(2, NSEQ, 16, 2, 64).transpose(0, 3, 4, 2, 1).reshape(2, 128, 16, NSEQ)
        d["st_ssm"] = np.ascontiguousarray(np.stack([st(inp["state_ssm_re"][:, sl]), st(inp["state_ssm_im"][:, sl])], 1)).astype(f)
        d["st_kT"] = np.ascontiguousarray(inp["cache_swa_k"][:, sl].reshape(2, NSEQ, 128, 128).transpose(0, 1, 3, 2)).astype(f)
        d["st_v"] = np.ascontiguousarray(inp["cache_swa_v"][:, sl].reshape(2, NSEQ, 128, 128)).astype(f)
        d["st_conv"] = np.ascontiguousarray(inp["state_conv"][:, sl].transpose(0, 3, 1, 2)).astype(f)
        d["st_ffn"] = np.ascontiguousarray(inp["state_ffn"][:, sl].transpose(0, 3, 1, 2)).astype(f)
        in_maps.append(d)
    res = run_bass_kernel_spmd(nc, in_maps, core_ids=list(range(8)))
    R = res.results
    _CACHE["last"] = R
    y_p = np.stack([R[s]["yT_p"][:, 128:].T for s in range(2)], 0)
    y_s = np.concatenate([R[c]["yT_s"].T.reshape(NSEQ, 8, D) for c in range(8)], 0)

    def ust(a):
        return a.reshape(2, 2, 64, 16).transpose(0, 3, 1, 2).reshape(2, 32, 64)
    sr_p = np.stack([ust(R[s]["o_ssm_p"][:, 0]) for s in range(2)], 1)
    si_p = np.stack([ust(R[s]["o_ssm_p"][:, 1]) for s in range(2)], 1)
    k_p = np.stack([R[s]["o_kT_p"].transpose(0, 2, 1).reshape(2, 128, 2, 64) for s in range(2)], 1)
    v_p = np.stack([R[s]["o_v_p"].reshape(2, 128, 2, 64) for s in range(2)], 1)
    c_p = np.stack([R[s]["o_conv_p"].transpose(0, 2, 1) for s in range(2)], 1)
    f_p = np.stack([R[s]["o_ffn_p"].transpose(0, 2, 1) for s in range(2)], 1)

    def usts(a):
        return a.reshape(2, 2, 64, 16, NSEQ).transpose(0, 4, 3, 1, 2).reshape(2, NSEQ, 32, 64)
    sr_s = np.concatenate([usts(R[c]["o_ssm_s"][:, 0]) for c in range(8)], 1)
    si_s = np.concatenate([usts(R[c]["o_ssm_s"][:, 1]) for c in range(8)], 1)
    k_s = np.concatenate([R[c]["o_kT_s"].transpose(0, 1, 3, 2).reshape(2, NSEQ, 128, 2, 64) for c in range(8)], 1)
    v_s = np.concatenate([R[c]["o_v_s"].reshape(2, NSEQ, 128, 2, 64) for c in range(8)], 1)
    c_s = np.concatenate([R[c]["o_conv_s"].transpose(0, 2, 3, 1) for c in range(8)], 1)
    f_s = np.concatenate([R[c]["o_ffn_s"].transpose(0, 2, 3, 1) for c in range(8)], 1)
    outs = (y_p, y_s, sr_p, si_p, k_p, v_p, c_p, f_p, sr_s, si_s, k_s, v_s, c_s, f_s)
    return tuple(np.ascontiguousarray(o).astype(np.float32) for o in outs)
```

```python
import contextlib
import math
import numpy as np
import concourse.bass as bass
import concourse.mybir as mybir
from concourse.bass_utils import run_bass_kernel_spmd

F32 = mybir.dt.float32
BF16 = mybir.dt.bfloat16
AF = mybir.ActivationFunctionType
ALU = mybir.AluOpType

D = 1024
NC8 = 8
DFF = 2816
NJ = 22
NPAD = 112
TPAD = 8320
NBLK = 65
NSEQ = 16
EPS = 1e-6
NEG = -30000.0


def t5_bucket_np(dist):
    n = np.maximum(dist, 0)
    max_exact = 16
    nf = np.maximum(n, max_exact).astype(np.float32)
    large = max_exact + (np.log(nf / max_exact) / math.log(128 / max_exact) * (32 - max_exact)).astype(np.int32)
    large = np.minimum(large, 31)
    return np.where(n < max_exact, n, large)


class Buf:
    __slots__ = ("t", "name", "last_w", "readers", "dsem", "dcnt", "wn")

    def __init__(self, t, name):
        self.t = t
        self.name = name
        self.last_w = None
        self.readers = {}
        self.dsem = None
        self.dcnt = 0
        self.wn = None

    def __getitem__(self, idx):
        if self.wn is not None and isinstance(idx, tuple) and len(idx) == 3 and isinstance(idx[1], int):
            cs = idx[2]
            return V(self, 0, 128, idx[1] * self.wn + cs.start, [(1, cs.stop - cs.start)])
        return self.t[idx]


class _Stop(Exception):
    pass


_LAST = {}


class KB:
    def __init__(self):
        self.nc = bass.Bass("TRN2", target_bir_lowering=False)
        self.es = contextlib.ExitStack()
        nc = self.nc
        self.eng = {"pe": nc.tensor, "act": nc.scalar, "dve": nc.vector, "pool": nc.gpsimd, "sp": nc.sync}
        self.sem = {}
        self.cnt = {}
        self.seen = {e: {} for e in self.eng}
        self.uid = 0
        self.psum_rr = 0
        self.out_events = []
        self.dead = False

    def start(self):
        import os
        for i in range(int(os.environ.get("KDUMMYSEM", "0"))):
            self.es.enter_context(self.nc.semaphore("dummy%d" % i))
        for e in self.eng:
            self.sem[e] = self.es.enter_context(self.nc.semaphore("prog_" + e))
            self.cnt[e] = 0
        self.banks = []
        for i in range(8):
            t = self.es.enter_context(self.nc.psum_tensor("bank%d" % i, [128, 512], F32))
            self.banks.append(Buf(t, "bank%d" % i))

    def sb(self, name, shape, dtype):
        self.uid += 1
        t = self.es.enter_context(self.nc.sbuf_tensor("%s_%d" % (name, self.uid), list(shape), dtype))
        return Buf(t, name)

    def dram(self, name, shape, dtype, kind):
        return self.nc.dram_tensor(name, list(shape), dtype, kind=kind)

    def bank(self):
        b = self.banks[self.psum_rr % 8]
        self.psum_rr += 1
        return b

    def _waits(self, e, reads, writes):
        deps = []
        for b in reads:
            if b.last_w is not None:
                deps.append((b.last_w, True))
        for b in writes:
            if b.last_w is not None:
                deps.append((b.last_w, False))
            for ev in b.readers.values():
                deps.append((ev, False))
        own = self.sem.get(e)
        for (sem, val), raw in deps:
            if sem is own:
                if e in ("pe", "sp"):
                    continue
            key = id(sem)
            if self.seen[e].get(key, 0) < val:
                self.eng[e].wait_ge(sem, val)
                self.seen[e][key] = val

    def chk(self, tag):
        import os
        if os.environ.get("KSTOP") == tag:
            for i in range(int(os.environ.get("KEXTRA", "0"))):
                tgt = self._xtra if os.environ.get("KXT") else self._misc
                w = 128 if os.environ.get("KXT") else 1
                if os.environ.get("KXT") == "2":
                    self.op("dve", lambda: self.nc.vector.tensor_copy(self._xtra[:, 0:128], self._xtra2[:, 0:128]), reads=[self._xtra2], writes=[self._xtra])
                else:
                    self.op("dve", lambda: self.nc.vector.memset(tgt[:, 0:w], 0.0), writes=[tgt])
            self.dead = True
            if os.environ.get("KRAISE"):
                self.dead = False
                self.finish()
                raise _Stop()

    def op(self, e, fn, reads=(), writes=(), inc=True):
        if self.dead:
            return None
        self._waits(e, reads, writes)
        inst = fn()
        if inc:
            self.cnt[e] += 1
            inst.then_inc(self.sem[e], 1)
            ev = (self.sem[e], self.cnt[e])
        else:
            ev = (self.sem[e], self.cnt[e] + 1)
        for b in writes:
            b.last_w = ev
            b.readers = {}
        for b in reads:
            b.readers[e] = ev
        return inst

    def dma(self, q, out_ap, in_ap, reads=(), writes=(), slow=False, is_out=False):
        if self.dead:
            return None
        self._waits(q, reads, writes)
        kw = {}
        if slow:
            kw["allow_slow_non_contiguous"] = True
        inst = self.eng[q].dma_start(out=out_ap, in_=in_ap, **kw)
        tgt = writes[0] if writes else (reads[0] if reads else None)
        if tgt is None:
            tgt = self._misc
        if tgt.dsem is None:
            self.uid += 1
            tgt.dsem = self.es.enter_context(self.nc.semaphore("d_%s_%d" % (tgt.name, self.uid)))
        tgt.dcnt += 16
        inst.then_inc(tgt.dsem, 16)
        ev = (tgt.dsem, tgt.dcnt)
        for b in writes:
            b.last_w = ev
            b.readers = {}
        for b in reads:
            b.readers[("dma", id(tgt))] = ev
        self.out_events.append(ev)
        return inst

    def finish(self):
        last = {}
        for sem, val in self.out_events:
            k = id(sem)
            if k not in last or last[k][1] < val:
                last[k] = (sem, val)
        for sem, val in last.values():
            self.eng["sp"].wait_ge(sem, val)
        for e in self.eng:
            for e2 in ("pe", "act", "dve", "pool"):
                if e2 != e and self.cnt[e2] > 0:
                    self.eng[e].wait_ge(self.sem[e2], self.cnt[e2])


def V(buf, p0, np_, off, dims):
    t = buf.t
    shape = t.shape
    fsz = 1
    for s in shape[1:]:
        fsz *= s
    return bass.AP(t, p0 * fsz + off, [[fsz, np_]] + [[s, c] for (s, c) in dims])


def build(nchunk_prompt=17, do_sample=True, dbg=False, dbg_ci=1):
    try:
        return _build(nchunk_prompt, do_sample, dbg, dbg_ci)
    except _Stop:
        return _LAST["kb"].nc


def _build(nchunk_prompt=17, do_sample=True, dbg=False, dbg_ci=1):
    kb = KB()
    _LAST["kb"] = kb
    nc = kb.nc
    es = kb.es
    with es:
        kb.start()
        import os as _os
        kb._misc = kb.sb("misc", [128, 1], F32)
        PE, ACT, DVE, POOL, SP = "pe", "act", "dve", "pool", "sp"
        din = {}

        def DI(name, shape):
            din[name] = kb.dram(name, shape, F32, "ExternalInput")
            return din[name]

        dout = {}

        def DO(name, shape):
            dout[name] = kb.dram(name, shape, F32, "ExternalOutput")
            return dout[name]

        xT_p = DI("xT_p", [D, TPAD])
        xT_s = DI("xT_s", [D, 128])
        st_ssm = DI("st_ssm", [2, 2, 128, 16, NSEQ])
        st_kT = DI("st_kT", [2, NSEQ, 128, 128])
        st_v = DI("st_v", [2, NSEQ, 128, 128])
        st_conv = DI("st_conv", [2, D, NSEQ, 30])
        st_ffn = DI("st_ffn", [4, DFF, NSEQ, 2])
        gvec = DI("gvec", [128, 9, 8])
        w_in = DI("w_in", [2, 11, 128, 1024])
        lam = DI("lam", [2, 3, 128, 16])
        ssm_b = DI("ssm_b", [2, 2, 32, 64, 16])
        ssm_c = DI("ssm_c", [2, 2, 32, 16, 64])
        ssm_d = DI("ssm_d", [2, 128, 4])
        w_glu = DI("w_glu", [2, 4, 128, 512])
        b_glu = DI("b_glu", [2, 128, 4])
        relb = DI("relb", [32, 8])
        sinks = DI("sinks", [2, 128, 4])
        w_out = DI("w_out", [2, 8, 128, 1024])
        w_pw1 = DI("w_pw1", [2, 8, 128, 2048])
        w_dw = DI("w_dw", [2, 128, 8, 31])
        cvec = DI("cvec", [2, 128, 3, 8])
        w_pw2 = DI("w_pw2", [2, 8, 128, 1024])
        w_up = DI("w_up", [4, NJ, 128, 2048])
        f_cw = DI("f_cw", [4, 128, NJ, 3])
        f_cb = DI("f_cb", [4, 128, NJ])
        w_dn = DI("w_dn", [4, 8, 2, 128, 1408])
        oh_bucket = DI("oh_bucket", [32, 384])
        msk_ext = DI("msk_ext", [8, 384])
        antiI = DI("antiI", [128, 128])

        yT_p = DO("yT_p", [D, TPAD])
        yT_s = DO("yT_s", [D, 128])
        o_ssm_p = DO("o_ssm_p", [2, 2, 128, 16])
        o_ssm_s = DO("o_ssm_s", [2, 2, 128, 16, NSEQ])
        o_kT_p = DO("o_kT_p", [2, 128, 128])
        o_v_p = DO("o_v_p", [2, 128, 128])
        o_kT_s = DO("o_kT_s", [2, NSEQ, 128, 128])
        o_v_s = DO("o_v_s", [2, NSEQ, 128, 128])
        o_conv_p = DO("o_conv_p", [2, D, 30])
        o_conv_s = DO("o_conv_s", [2, D, NSEQ, 30])
        o_ffn_p = DO("o_ffn_p", [4, DFF, 2])
        o_ffn_s = DO("o_ffn_s", [4, DFF, NSEQ, 2])
        scr = kb.dram("scr_bias", [8, 384], F32, "Internal")
        if dbg:
            dbg_o = DO("dbg", [16, D, 512])
        dbgc = [0]

        def dump16(Xl, W):
            if not dbg:
                return
            for c in range(8):
                kb.dma(POOL, dbg_o.ap()[dbgc[0], c * 128:(c + 1) * 128, 0:W], Xl[c][:, 0:W], reads=[Xl[c]], is_out=True)
            dbgc[0] += 1

        def dump(Xl, W):
            if not dbg:
                return
            for c in range(8):
                kb.dma(SP, dbg_o.ap()[dbgc[0], c * 128:(c + 1) * 128, 0:W], Xl[c][:, 0:W], reads=[Xl[c]], is_out=True)
            dbgc[0] += 1

        ident = kb.sb("ident", [128, 128], F32)
        kb.op(POOL, lambda: nc.gpsimd.memset(ident[:], 1.0), writes=[ident])
        kb.op(POOL, lambda: nc.gpsimd.affine_select(ident[:], ident[:], pattern=[[-1, 128]], compare_op=ALU.is_equal,
                                                     fill=0.0, base=0, channel_multiplier=1), reads=[ident], writes=[ident])
        ones_bf = kb.sb("ones_bf", [128, 128], BF16)
        kb.op(DVE, lambda: nc.vector.memset(ones_bf[:], 1.0), writes=[ones_bf])
        ones_f = kb.sb("ones_f", [128, 128], F32)
        kb.op(DVE, lambda: nc.vector.memset(ones_f[:], 1.0), writes=[ones_f])
        Oz = kb.sb("Oz", [128, 192], BF16)
        Oz0 = kb.sb("Oz0", [128, 192], BF16)
        for o in (Oz, Oz0):
            kb.op(DVE, lambda o=o: nc.vector.memset(o[:], 0.0), writes=[o])
            kb.op(DVE, lambda o=o: nc.vector.memset(o[:, 64:128], 1.0), writes=[o])
        kb.op(DVE, lambda: nc.vector.memset(Oz0[0:NPAD, :], 0.0), writes=[Oz0])
        gv = kb.sb("gv", [128, 9, 8], F32)
        kb.dma(SP, gv[:], gvec.ap(), writes=[gv])
        kb.chk("c0")

        WSL = [kb.sb("wslab%d" % i, [128, 8, 256], BF16) for i in range(2)]
        WDN = [kb.sb("wdn%d" % i, [128, 11, 128], BF16) for i in range(2)]
        wctr = {"a": 0, "b": 0}
        UT = [kb.sb("UT%d" % m, [128, 512], BF16) for m in range(4)]
        U32 = [kb.sb("U32_%d" % m, [128, 512], F32) for m in range(4)]
        QT = [[kb.sb("QT%d_%d" % (e, c), [128, 512], BF16) for c in range(4)] for e in range(2)]
        for e in range(2):
            for c in range(4):
                kb.op(POOL, lambda e=e, c=c: nc.gpsimd.memset(QT[e][c][:], 0.0), writes=[QT[e][c]])

        wconv = kb.sb("wconv", [128, 1], F32)
        WB = {}
        for nm, src in (("w_in", w_in), ("w_glu", w_glu), ("w_out", w_out), ("w_pw1", w_pw1), ("w_pw2", w_pw2), ("w_up", w_up), ("w_dn", w_dn)):
            shp = list(src.shape)
            dst = kb.dram(nm + "_bf", shp, BF16, "Internal")
            WB[nm] = dst
            lead = 1
            for d_ in shp[:-2]:
                lead *= d_
            sflat = src.ap().flatten_outer_dims() if len(shp) > 2 else src.ap()
            dflat = dst.ap().flatten_outer_dims() if len(shp) > 2 else dst.ap()
            for i_ in range(lead):
                kb.dma(POOL, dflat[i_ * 128:(i_ + 1) * 128, :], sflat[i_ * 128:(i_ + 1) * 128, :], writes=[wconv])

        def load_w(dram_ap, kc, ncols):
            b = WSL[wctr["a"] % len(WSL)]
            wctr["a"] += 1
            b.wn = ncols
            kb.dma(SP, V(b, 0, 128, 0, [(1, kc * ncols)]), dram_ap, reads=[wconv], writes=[b])
            return b

        def load_wdn(dram_ap):
            b = WDN[wctr["b"] % len(WDN)]
            wctr["b"] += 1
            kb.dma(SP, V(b, 0, 128, 0, [(1, 11 * 128)]), dram_ap, reads=[wconv], writes=[b])
            return b

        def mm_group(out_ap, pairs, bankbuf, rbufs):
            n = len(pairs)
            if _os.environ.get("KHOIST"):
                allr = []
                for (_l, _r, bs_) in pairs:
                    allr += list(bs_)
                kb._waits(PE, allr + list(rbufs), [bankbuf])
            for i, (l, r, bs) in enumerate(pairs):
                kb.op(PE, lambda l=l, r=r, i=i: nc.tensor.matmul(out_ap, l, r, start=(i == 0), stop=(i == n - 1)),
                      reads=list(bs) + list(rbufs), writes=[bankbuf], inc=(i == n - 1))

        P32 = [kb.sb("P32_%d" % i, [128, 608], F32) for i in range(14)]
        P16 = [kb.sb("P16_%d" % i, [128, 512], BF16) for i in range(22)]
        kb._xtra = P32[5]
        kb._xtra2 = P32[6]
        sq = P16[12:20]
        rs = P32[12]
        rinv = P32[13]
        eps_t = kb.sb("eps_t", [128, 1], F32)
        kb.op(DVE, lambda: nc.vector.memset(eps_t[:], EPS), writes=[eps_t])

        def rmsnorm(X, gi, Hout, W, out_f32=None):
            lvl = int(_os.environ.get("KRMS", "9"))
            if lvl < 1:
                return
            sqf = P32[0:8]
            for c in range(8):
                kb.op(ACT, lambda c=c: nc.scalar.activation(sqf[c][:, 0:W], X[c][:, 0:W], AF.Square), reads=[X[c]], writes=[sqf[c]])
            if lvl < 2:
                return
            bk = kb.bank()
            mm_group(bk[:, 0:W], [(ones_f[:], sqf[c][:, 0:W], [sqf[c]]) for c in range(8)], bk, [ones_f])
            if lvl < 3:
                return
            kb.op(DVE, lambda: nc.vector.tensor_scalar(rs[:, 0:W], bk[:, 0:W], 1.0 / D, EPS, ALU.mult, ALU.add), reads=[bk], writes=[rs])
            kb.op(ACT, lambda: nc.scalar.activation(rs[:, 0:W], rs[:, 0:W], AF.Sqrt), reads=[rs], writes=[rs])
            if lvl < 4:
                return
            kb.op(DVE, lambda: nc.vector.reciprocal(rinv[:, 0:W], rs[:, 0:W]), reads=[rs], writes=[rinv])
            if lvl < 5:
                return
            for c in range(8):
                o = Hout[c] if out_f32 is None else out_f32[c]
                kb.op(DVE, lambda c=c, o=o: nc.vector.scalar_tensor_tensor(o[:, 0:W], X[c][:, 0:W], gv[:, gi, c:c + 1], rinv[:, 0:W],
                                                                          ALU.mult, ALU.mult),
                      reads=[X[c], gv, rinv], writes=[o])

        X = [kb.sb("X%d" % c, [128, 512], F32) for c in range(8)]
        H = [kb.sb("H%d" % c, [128, 512], BF16) for c in range(8)]

        S5 = []
        import os as _os
        for li in range(0 if _os.environ.get("KSKIP_S5") else 2):
            T = {}
            lm = kb.sb("lam", [128, 3, 16], F32)
            kb.dma(SP, lm[:], lam.ap()[li].rearrange("k p j -> p k j"), writes=[lm])
            dt = kb.sb("dt", [128, 16], F32)
            kb.op(ACT, lambda: nc.scalar.activation(dt[:], lm[:, 2, :], AF.Exp), reads=[lm], writes=[dt])
            lrdt = kb.sb("lrdt", [128, 16], F32)
            th = kb.sb("th", [128, 16], F32)
            kb.op(DVE, lambda: nc.vector.tensor_tensor(lrdt[:], lm[:, 0, :], dt[:], ALU.mult), reads=[lm, dt], writes=[lrdt])
            kb.op(DVE, lambda: nc.vector.tensor_tensor(th[:], lm[:, 1, :], dt[:], ALU.mult), reads=[lm, dt], writes=[th])
            rho = kb.sb("rho", [128, 16], F32)
            kb.op(ACT, lambda: nc.scalar.activation(rho[:], lrdt[:], AF.Exp), reads=[lrdt], writes=[rho])
            T["rho"] = rho
            kk = P32[0]
            kb.op(POOL, lambda: nc.gpsimd.iota(kk[:, 0:64], pattern=[[1, 64]], base=1, channel_multiplier=0,
                                               allow_small_or_imprecise_dtypes=True), writes=[kk])
            ctab = kb.sb("ctab", [128, 16, 64], F32)
            stab = kb.sb("stab", [128, 16, 64], F32)
            negpi = kb.sb("negpi", [128, 1], F32)
            ki32 = kb.sb("ki32", [128, 64], mybir.dt.int32)
            kb.op(DVE, lambda: nc.vector.memset(negpi[:], -math.pi), writes=[negpi])
            for j in range(16):
                ang = P32[1 + (j % 2)]
                kb.op(DVE, lambda j=j, ang=ang: nc.vector.tensor_scalar(ang[:, 0:64], kk[:, 0:64], th[:, j:j + 1], None, ALU.mult),
                      reads=[kk, th], writes=[ang])
                for (dst, sh, ti) in ((stab, 0.5, 3), (ctab, 0.75, 5)):
                    tmp = P32[ti + (j % 2)]
                    kb.op(DVE, lambda sh=sh, tmp=tmp, ang=ang: nc.vector.tensor_scalar(tmp[:, 0:64], ang[:, 0:64], 1.0 / (2 * math.pi), sh, ALU.mult, ALU.add),
                          reads=[ang], writes=[tmp])
                    kb.op(DVE, lambda tmp=tmp: nc.vector.tensor_copy(ki32[:, 0:64], tmp[:, 0:64]), reads=[tmp], writes=[ki32])
                    kb.op(DVE, lambda tmp=tmp: nc.vector.tensor_copy(tmp[:, 64:128], ki32[:, 0:64]), reads=[ki32], writes=[tmp])
                    kb.op(DVE, lambda tmp=tmp: nc.vector.tensor_tensor(tmp[:, 0:64], tmp[:, 0:64], tmp[:, 64:128], ALU.subtract), reads=[tmp], writes=[tmp])
                    kb.op(DVE, lambda tmp=tmp: nc.vector.tensor_scalar(tmp[:, 64:128], tmp[:, 0:64], 0.0, None, ALU.is_lt), reads=[tmp], writes=[tmp])
                    kb.op(DVE, lambda tmp=tmp: nc.vector.tensor_tensor(tmp[:, 0:64], tmp[:, 0:64], tmp[:, 64:128], ALU.add), reads=[tmp], writes=[tmp])
                    kb.op(ACT, lambda dst=dst, tmp=tmp, j=j: nc.scalar.activation(dst[:, j, :], tmp[:, 0:64], AF.Sin, bias=negpi[:], scale=2 * math.pi),
                          reads=[tmp, negpi], writes=[dst])
            T["ctab"], T["stab"] = ctab, stab
            kb.chk("s5ang")
            are = kb.sb("are", [128, 16], F32)
            aim = kb.sb("aim", [128, 16], F32)
            kb.op(DVE, lambda: nc.vector.tensor_tensor(are[:], rho[:], ctab[:, :, 0], ALU.mult), reads=[rho, ctab], writes=[are])
            kb.op(DVE, lambda: nc.vector.tensor_tensor(aim[:], rho[:], stab[:, :, 0], ALU.mult), reads=[rho, stab], writes=[aim])
            den = kb.sb("den", [128, 16], F32)
            t1 = kb.sb("t1", [128, 16], F32)
            t2 = kb.sb("t2", [128, 16], F32)
            kb.op(DVE, lambda: nc.vector.tensor_tensor(den[:], lm[:, 0, :], lm[:, 0, :], ALU.mult), reads=[lm], writes=[den])
            kb.op(DVE, lambda: nc.vector.tensor_tensor(t1[:], lm[:, 1, :], lm[:, 1, :], ALU.mult), reads=[lm], writes=[t1])
            kb.op(DVE, lambda: nc.vector.tensor_tensor(den[:], den[:], t1[:], ALU.add), reads=[den, t1], writes=[den])
            rden = kb.sb("rden", [128, 16], F32)
            kb.op(DVE, lambda: nc.vector.reciprocal(rden[:], den[:]), reads=[den], writes=[rden])
            nre = kb.sb("nre", [128, 16], F32)
            kb.op(DVE, lambda: nc.vector.tensor_scalar_add(nre[:], are[:], -1.0), reads=[are], writes=[nre])
            cre = kb.sb("cre", [128, 16], F32)
            cim = kb.sb("cim", [128, 16], F32)
            ncim = kb.sb("ncim", [128, 16], F32)
            kb.op(DVE, lambda: nc.vector.tensor_tensor(t1[:], nre[:], lm[:, 0, :], ALU.mult), reads=[nre, lm], writes=[t1])
            kb.op(DVE, lambda: nc.vector.tensor_tensor(t2[:], aim[:], lm[:, 1, :], ALU.mult), reads=[aim, lm], writes=[t2])
            kb.op(DVE, lambda: nc.vector.tensor_tensor(t1[:], t1[:], t2[:], ALU.add), reads=[t1, t2], writes=[t1])
            kb.op(DVE, lambda: nc.vector.tensor_tensor(cre[:], t1[:], rden[:], ALU.mult), reads=[t1, rden], writes=[cre])
            kb.op(DVE, lambda: nc.vector.tensor_tensor(t1[:], aim[:], lm[:, 0, :], ALU.mult), reads=[aim, lm], writes=[t1])
            kb.op(DVE, lambda: nc.vector.tensor_tensor(t2[:], nre[:], lm[:, 1, :], ALU.mult), reads=[nre, lm], writes=[t2])
            kb.op(DVE, lambda: nc.vector.tensor_tensor(t1[:], t1[:], t2[:], ALU.subtract), reads=[t1, t2], writes=[t1])
            kb.op(DVE, lambda: nc.vector.tensor_tensor(cim[:], t1[:], rden[:], ALU.mult), reads=[t1, rden], writes=[cim])
            kb.op(DVE, lambda: nc.vector.tensor_scalar_mul(ncim[:], cim[:], -1.0), reads=[cim], writes=[ncim])
            kb.chk("s5coef")
            Bl = [kb.sb("Bl_re", [128, 16, 128], BF16), kb.sb("Bl_im", [128, 16, 128], BF16)]
            Cl = [kb.sb("Cl_re", [128, 16, 128], BF16), kb.sb("Cl_im", [128, 16, 128], BF16)]
            for j in range(16):
                o = 128 * (j % 2)
                zb = [P32[7], P32[8]]
                zc = [P32[9], P32[10]]
                zbb = [P32[11], P32[12]]
                for z in zb + zc:
                    kb.op(POOL, lambda z=z, o=o: nc.gpsimd.memset(z[:, o:o + 128], 0.0), writes=[z])
                for ri in range(2):
                    for e in range(2):
                        g = 2 * j + e
                        c0 = 32 * (j % 4) + 16 * e
                        kb.dma(SP, zb[ri][64 * e:64 * e + 64, o + c0:o + c0 + 16], ssm_b.ap()[li, ri, g], writes=[zb[ri]])
                        kb.dma(SP, zc[ri][c0:c0 + 16, o + 64 * e:o + 64 * e + 64], ssm_c.ap()[li, ri, g], writes=[zc[ri]])
                kb.op(DVE, lambda j=j, o=o: nc.vector.tensor_scalar(zbb[0][:, o:o + 128], zb[0][:, o:o + 128], cre[:, j:j + 1], None, ALU.mult),
                      reads=[zb[0], cre], writes=[zbb[0]])
                kb.op(DVE, lambda j=j, o=o: nc.vector.scalar_tensor_tensor(zbb[0][:, o:o + 128], zb[1][:, o:o + 128], ncim[:, j:j + 1], zbb[0][:, o:o + 128],
                                                                      ALU.mult, ALU.add), reads=[zb[1], ncim, zbb[0]], writes=[zbb[0]])
                kb.op(DVE, lambda j=j, o=o: nc.vector.tensor_scalar(zbb[1][:, o:o + 128], zb[1][:, o:o + 128], cre[:, j:j + 1], None, ALU.mult),
                      reads=[zb[1], cre], writes=[zbb[1]])
                kb.op(DVE, lambda j=j, o=o: nc.vector.scalar_tensor_tensor(zbb[1][:, o:o + 128], zb[0][:, o:o + 128], cim[:, j:j + 1], zbb[1][:, o:o + 128],
                                                                      ALU.mult, ALU.add), reads=[zb[0], cim, zbb[1]], writes=[zbb[1]])
                for ri in range(2):
                    bk = kb.bank()
                    kb.op(PE, lambda bk=bk, ri=ri, o=o: nc.tensor.transpose(bk[:, 0:128], zbb[ri][:, o:o + 128], ident[:]),
                          reads=[zbb[ri], ident], writes=[bk])
                    kb.op(PE, lambda bk=bk, ri=ri, o=o: nc.tensor.transpose(bk[:, 128:256], zc[ri][:, o:o + 128], ident[:]),
                          reads=[zc[ri], ident], writes=[bk])
                    kb.op(DVE, lambda bk=bk, ri=ri, j=j: nc.vector.tensor_copy(Bl[ri][:, j, :], bk[:, 0:128]), reads=[bk], writes=[Bl[ri]])
                    if ri == 0:
                        kb.op(DVE, lambda bk=bk, ri=ri, j=j: nc.vector.tensor_copy(Cl[ri][:, j, :], bk[:, 128:256]), reads=[bk], writes=[Cl[ri]])
                    else:
                        kb.op(DVE, lambda bk=bk, ri=ri, j=j: nc.vector.tensor_scalar(Cl[ri][:, j, :], bk[:, 128:256], -1.0, None, ALU.mult), reads=[bk], writes=[Cl[ri]])
            T["Bl"], T["Cl"] = Bl, Cl
            kb.chk("s5bc")
            dsk = kb.sb("dsk", [128, 4], F32)
            bgl = kb.sb("bgl", [128, 4], F32)
            kb.dma(SP, dsk[:], ssm_d.ap()[li], writes=[dsk])
            kb.dma(SP, bgl[:], b_glu.ap()[li], writes=[bgl])
            T["dsk"], T["bgl"] = dsk, bgl
            rho9 = kb.sb("rho9", [128, 16, 9], F32)
            kb.op(DVE, lambda: nc.vector.memset(rho9[:], 0.0), writes=[rho9])
            kb.op(DVE, lambda: nc.vector.tensor_scalar(rho9[:, :, 1:9], V(rho, 0, 128, 0, [(1, 16), (0, 8)]), 1.0, None, ALU.mult),
                  reads=[rho], writes=[rho9])
            T["rho9"] = rho9
            T["car"] = [kb.sb("car_re", [128, 16], F32), kb.sb("car_im", [128, 16], F32)]
            for cbuf in T["car"]:
                kb.op(DVE, lambda cbuf=cbuf: nc.vector.memset(cbuf[:], 0.0), writes=[cbuf])
            S5.append(T)
        kb.chk("s5")

        kb.dead = bool(_os.environ.get("KSKIP_BIAS"))
        rb = kb.sb("rb", [32, 8], F32)
        oh = P32[3]
        kb.dma(SP, rb[:], relb.ap(), writes=[rb])
        kb.dma(SP, oh[0:32, 0:384], oh_bucket.ap(), writes=[oh])
        mk8 = P32[4]
        kb.dma(SP, mk8[0:8, 0:384], msk_ext.ap(), writes=[mk8])
        bk = kb.bank()
        kb.op(PE, lambda: nc.tensor.matmul(bk[0:8, 0:384], rb[:], oh[0:32, 0:384], start=True, stop=True), reads=[rb, oh], writes=[bk])
        bv = P32[5]
        kb.op(DVE, lambda: nc.vector.tensor_tensor(bv[0:8, 0:384], bk[0:8, 0:384], mk8[0:8, 0:384], ALU.add), reads=[bk, mk8], writes=[bv])
        kb.dma(SP, scr.ap(), bv[0:8, 0:384], reads=[bv], writes=[kb._misc])
        aI = kb.sb("aI", [128, 128], F32)
        kb.dma(SP, aI[:], antiI.ap(), writes=[aI])
        Etab = []
        for c in range(4):
            E = kb.sb("Etab%d" % c, [128, 512], F32)
            hank = P32[c % 2]
            kb.dma(SP, hank[:, 0:512], bass.AP(scr, 2 * c * 384, [[1, 128], [384, 2], [1, 256]]), reads=[kb._misc], writes=[hank])
            bk = kb.bank()
            for e in range(2):
                kb.op(PE, lambda e=e, bk=bk, hank=hank: nc.tensor.matmul(bk[:, e * 128:(e + 1) * 128], aI[:], hank[:, e * 256:e * 256 + 128], start=True, stop=True),
                      reads=[aI, hank], writes=[bk])
                kb.op(PE, lambda e=e, bk=bk, hank=hank: nc.tensor.matmul(bk[:, 256 + e * 128:256 + (e + 1) * 128], aI[:], hank[:, e * 256 + 128:e * 256 + 256],
                                                                   start=True, stop=True), reads=[aI, hank], writes=[bk])
            kb.op(DVE, lambda E=E, bk=bk: nc.vector.tensor_copy(E[:], bk[:]), reads=[bk], writes=[E])
            kb.op(ACT, lambda E=E: nc.scalar.activation(E[:], E[:], AF.Exp), reads=[E], writes=[E])
            Etab.append(E)
        EAt = kb.sb("EAt", [128, 64], F32)
        EBt = kb.sb("EBt", [8, 64], F32)
        hk = P32[2]
        kb.dma(SP, hk[:, 0:64], bass.AP(scr, 120, [[1, 128], [384, 8], [1, 8]]), reads=[kb._misc], writes=[hk], slow=True)
        kb.dma(SP, hk[0:8, 64:128], bass.AP(scr, 248, [[1, 8], [384, 8], [1, 8]]), reads=[kb._misc], writes=[hk], slow=True)
        bk = kb.bank()
        kb.op(PE, lambda: nc.tensor.matmul(bk[:, 0:64], aI[:], hk[:, 0:64], start=True, stop=True), reads=[aI, hk], writes=[bk])
        kb.op(PE, lambda: nc.tensor.matmul(bk[0:8, 64:128], aI[0:8, 120:128], hk[0:8, 64:128], start=True, stop=True), reads=[aI, hk], writes=[bk])
        kb.op(DVE, lambda: nc.vector.tensor_copy(EAt[:], bk[:, 0:64]), reads=[bk], writes=[EAt])
        kb.op(ACT, lambda: nc.scalar.activation(EAt[:], EAt[:], AF.Exp), reads=[EAt], writes=[EAt])
        kb.op(DVE, lambda: nc.vector.tensor_copy(EBt[:], bk[0:8, 64:128]), reads=[bk], writes=[EBt])
        kb.op(ACT, lambda: nc.scalar.activation(EBt[:], EBt[:], AF.Exp), reads=[EBt], writes=[EBt])
        EsT = []
        for li in range(2):
            sk = kb.sb("sk", [128, 4], F32)
            kb.dma(SP, sk[:], sinks.ap()[li], writes=[sk])
            kb.op(ACT, lambda sk=sk: nc.scalar.activation(sk[:], sk[:], AF.Exp), reads=[sk], writes=[sk])
            est = sk
            EsT.append(est)

        kb.dead = False
        kb.chk("bias")
        kb.dead = bool(_os.environ.get("KSKIP_STATE"))
        NVB = 5
        ATT = []
        for li in range(2):
            A = {}
            A["KK"] = [kb.sb("KK%d" % k, [128, 128 + 512], BF16) for k in range(2)]
            A["Vz"] = [kb.sb("Vz%d" % i, [128, 320], BF16) for i in range(NVB)]
            for vz in A["Vz"]:
                kb.op(POOL, lambda vz=vz: nc.gpsimd.memset(vz[:], 0.0), writes=[vz])
            A["kT32"] = kb.sb("kT32", [128, 128], F32)
            A["v32"] = kb.sb("v32", [128, 128], F32)
            ATT.append(A)
        CONV = []
        for li in range(2):
            C = {}
            C["halo"] = kb.sb("chalo", [128, 8, 30], F32)
            kb.op(POOL, lambda b=C["halo"]: nc.gpsimd.memset(b[:], 0.0), writes=[C["halo"]])
            C["wdw"] = kb.sb("wdw", [128, 8, 31], F32)
            kb.dma(SP, C["wdw"][:], w_dw.ap()[li], writes=[C["wdw"]])
            C["cv"] = kb.sb("cv", [128, 3, 8], F32)
            kb.dma(SP, C["cv"][:], cvec.ap()[li], writes=[C["cv"]])
            CONV.append(C)
        FFN = []
        for l in range(4):
            Fd = {}
            Fd["halo"] = kb.sb("fhalo", [128, NJ, 2], F32)
            kb.op(POOL, lambda b=Fd["halo"]: nc.gpsimd.memset(b[:], 0.0), writes=[Fd["halo"]])
            Fd["cw"] = kb.sb("fcw", [128, NJ, 3], F32)
            Fd["cb"] = kb.sb("fcb", [128, NJ], F32)
            kb.dma(SP, Fd["cw"][:], f_cw.ap()[l], writes=[Fd["cw"]])
            kb.dma(SP, Fd["cb"][:], f_cb.ap()[l], writes=[Fd["cb"]])
            FFN.append(Fd)

        kb.dead = False
        MIX = P16[16:22] + [kb.sb("MIX%d" % c, [128, 512], BF16) for c in range(2)]
        XR = [[P32[0], P32[1]], [P32[2], P32[3]]]
        GG = [[P32[4], P32[5]], [P32[6], P32[7]]]
        tA = [P32[8], P32[9]]
        tB = [P32[10], P32[11]]
        YS = [P32[12], P32[13]]
        HB = [[P16[0], P16[1], P16[2], P16[3]], [P16[4], P16[5], P16[6], P16[7]]]
        GEL = P16[8:12]
        cfx = [kb.sb("cfx%d" % i, [128, 2], F32) for i in range(4)]
        sso = [kb.sb("sso%d" % i, [128, NSEQ], F32) for i in range(2)]
        m9 = kb.sb("m9", [128, NSEQ * 9], F32)
        PT = P16[12:16]
        EX = [P32[0], P32[1]]
        dn_t = P32[2]
        YF = P16
        GR = [P32[0], P32[1], P32[2]]
        ACC = [P32[3], P32[4], P32[5]]
        SG = [P32[12], P32[13]]
        LNY = P32[0:8]
        GLW = [P32[8], P32[9]]
        SQF = [P32[10], P32[11]]
        mu_t = P32[12]
        LNS = P16[0:8]
        rr = {"t": 0, "pt": 0, "ex": 0, "gr": 0, "acc": 0, "sg": 0, "ys": 0}

        def nxt(lst, key):
            b = lst[rr[key] % len(lst)]
            rr[key] += 1
            return b

        def s5_core(T, W, nrep, L, sample_h0=None, ssm_out=None):
            def v3(b):
                return V(b, 0, 128, 0, [(L, nrep), (1, L)])
            for m in range(4):
                bky = kb.bank()
                cpairs = []
                for half in range(2):
                    _set = (2 * m + half) % 2
                    XR = [[P32[4 * _set + 0], P32[4 * _set + 1]], [P32[4 * _set + 2], P32[4 * _set + 3]]]
                    GG = XR
                    for q in range(2):
                        jj = 2 * half + q
                        j = 4 * m + jj
                        bre, bim = kb.bank(), kb.bank()
                        for ri, bkk in ((0, bre), (1, bim)):
                            mm_group(bkk[:, 0:W], [(T["Bl"][ri][:, j, :], UT[m][:, 0:W], [T["Bl"][ri], UT[m]])], bkk, [])
                        cv = V(T["ctab"], 0, 128, j * 64, [(0, nrep), (1, L)])
                        sv = V(T["stab"], 0, 128, j * 64, [(0, nrep), (1, L)])
                        a1, a2 = tA[0], tB[0]
                        kb.op(DVE, lambda: nc.vector.tensor_tensor(v3(a1), v3(bre), cv, ALU.mult), reads=[bre, T["ctab"]], writes=[a1])
                        kb.op(DVE, lambda: nc.vector.tensor_tensor(v3(a2), v3(bim), sv, ALU.mult), reads=[bim, T["stab"]], writes=[a2])
                        kb.op(DVE, lambda: nc.vector.tensor_tensor(XR[0][q][:, 0:W], a1[:, 0:W], a2[:, 0:W], ALU.add),
                              reads=[a1, a2], writes=[XR[0][q]])
                        a1, a2 = tA[1], tB[1]
                        kb.op(DVE, lambda: nc.vector.tensor_tensor(v3(a1), v3(bim), cv, ALU.mult), reads=[bim, T["ctab"]], writes=[a1])
                        kb.op(DVE, lambda: nc.vector.tensor_tensor(v3(a2), v3(bre), sv, ALU.mult), reads=[bre, T["stab"]], writes=[a2])
                        kb.op(DVE, lambda: nc.vector.tensor_tensor(XR[1][q][:, 0:W], a1[:, 0:W], a2[:, 0:W], ALU.subtract),
                              reads=[a1, a2], writes=[XR[1][q]])
                    j0 = 4 * m + 2 * half
                    if sample_h0 is None:
                        nseg = W // 64
                        for sgi in range(nseg):
                            for q in range(2):
                                j = j0 + q
                                for ri in range(2):
                                    kb.op(DVE, lambda q=q, j=j, ri=ri, sgi=sgi: nc.vector.tensor_tensor_scan(
                                        GG[ri][q][:, sgi * 64:(sgi + 1) * 64], V(T["rho"], 0, 128, j, [(0, 64)]),
                                        XR[ri][q][:, sgi * 64:(sgi + 1) * 64], T["car"][ri][:, j:j + 1], ALU.mult, ALU.add),
                                        reads=[T["rho"], XR[ri][q], T["car"][ri]], writes=[GG[ri][q]])
                            col = sgi * 64 + 63
                            for ri in range(2):
                                for q in range(2):
                                    kb.op(DVE, lambda ri=ri, q=q: nc.vector.tensor_copy(cfx[ri][:, q:q + 1], GG[ri][q][:, col:col + 1]),
                                          reads=[GG[ri][q]], writes=[cfx[ri]])
                            cc = V(T["ctab"], 0, 128, j0 * 64 + 63, [(64, 2)])
                            ss = V(T["stab"], 0, 128, j0 * 64 + 63, [(64, 2)])
                            kb.op(DVE, lambda: nc.vector.tensor_tensor(cfx[2][:], cfx[0][:], cc, ALU.mult), reads=[cfx[0], T["ctab"]], writes=[cfx[2]])
                            kb.op(DVE, lambda: nc.vector.tensor_tensor(cfx[3][:], cfx[1][:], ss, ALU.mult), reads=[cfx[1], T["stab"]], writes=[cfx[3]])
                            kb.op(DVE, lambda: nc.vector.tensor_tensor(T["car"][0][:, j0:j0 + 2], cfx[2][:], cfx[3][:], ALU.subtract),
                                  reads=[cfx[2], cfx[3]], writes=[T["car"][0]])
                            kb.op(DVE, lambda: nc.vector.tensor_tensor(cfx[2][:], cfx[0][:], ss, ALU.mult), reads=[cfx[0], T["stab"]], writes=[cfx[2]])
                            kb.op(DVE, lambda: nc.vector.tensor_tensor(cfx[3][:], cfx[1][:], cc, ALU.mult), reads=[cfx[1], T["ctab"]], writes=[cfx[3]])
                            kb.op(DVE, lambda: nc.vector.tensor_tensor(T["car"][1][:, j0:j0 + 2], cfx[2][:], cfx[3][:], ALU.add),
                                  reads=[cfx[2], cfx[3]], writes=[T["car"][1]])
                    else:
                        for q in range(2):
                            j = j0 + q
                            for ri in range(2):
                                x9, g9 = tA[ri], tB[ri]
                                kb.dma(SP, V(x9, 0, 128, 0, [(9, NSEQ)]), st_ssm.ap()[sample_h0, ri][:, j, :], writes=[x9], slow=True)
                                kb.op(DVE, lambda: nc.vector.tensor_copy(V(x9, 0, 128, 1, [(9, NSEQ), (1, 8)]),
                                                                         V(XR[ri][q], 0, 128, 0, [(8, NSEQ), (1, 8)])),
                                      reads=[XR[ri][q]], writes=[x9])
                                if ri == 0:
                                    kb.op(DVE, lambda: nc.vector.tensor_copy(V(m9, 0, 128, 0, [(9, NSEQ), (1, 9)]), V(T["rho9"], 0, 128, j * 9, [(0, NSEQ), (1, 9)])),
                                          reads=[T["rho9"]], writes=[m9])
                                kb.op(DVE, lambda: nc.vector.tensor_tensor_scan(g9[:, 0:NSEQ * 9], m9[:, 0:NSEQ * 9],
                                                                                x9[:, 0:NSEQ * 9], 0.0, ALU.mult, ALU.add),
                                      reads=[m9, x9], writes=[g9])
                                kb.op(DVE, lambda: nc.vector.tensor_copy(V(GG[ri][q], 0, 128, 0, [(8, NSEQ), (1, 8)]),
                                                                         V(g9, 0, 128, 1, [(9, NSEQ), (1, 8)])),
                                      reads=[g9], writes=[GG[ri][q]])
                    for q in range(2):
                        jj = 2 * half + q
                        j = j0 + q
                        cv = V(T["ctab"], 0, 128, j * 64, [(0, nrep), (1, L)])
                        sv = V(T["stab"], 0, 128, j * 64, [(0, nrep), (1, L)])
                        a1, a2 = P32[12], P32[13]
                        kb.op(POOL, lambda: nc.gpsimd.tensor_tensor(v3(a1), v3(GG[0][q]), cv, ALU.mult), reads=[GG[0][q], T["ctab"]], writes=[a1])
                        kb.op(POOL, lambda: nc.gpsimd.tensor_tensor(v3(a2), v3(GG[1][q]), sv, ALU.mult), reads=[GG[1][q], T["stab"]], writes=[a2])
                        kb.op(POOL, lambda: nc.gpsimd.tensor_tensor(HB[0][jj][:, 0:W], a1[:, 0:W], a2[:, 0:W], ALU.subtract),
                              reads=[a1, a2], writes=[HB[0][jj]])
                        if sample_h0 is not None:
                            kb.op(POOL, lambda: nc.gpsimd.tensor_tensor(sso[0][:, :], V(a1, 0, 128, 7, [(8, NSEQ)]), V(a2, 0, 128, 7, [(8, NSEQ)]), ALU.subtract),
                                  reads=[a1, a2], writes=[sso[0]])
                            kb.dma(SP, o_ssm_s.ap()[sample_h0, 0][:, j, :], sso[0][:, :], reads=[sso[0]], is_out=True)
                        a1, a2 = P32[12], P32[13]
                        kb.op(POOL, lambda: nc.gpsimd.tensor_tensor(v3(a1), v3(GG[0][q]), sv, ALU.mult), reads=[GG[0][q], T["stab"]], writes=[a1])
                        kb.op(POOL, lambda: nc.gpsimd.tensor_tensor(v3(a2), v3(GG[1][q]), cv, ALU.mult), reads=[GG[1][q], T["ctab"]], writes=[a2])
                        kb.op(POOL, lambda: nc.gpsimd.tensor_tensor(HB[1][jj][:, 0:W], a1[:, 0:W], a2[:, 0:W], ALU.add),
                              reads=[a1, a2], writes=[HB[1][jj]])
                        if sample_h0 is not None:
                            kb.op(POOL, lambda: nc.gpsimd.tensor_tensor(sso[1][:, :], V(a1, 0, 128, 7, [(8, NSEQ)]), V(a2, 0, 128, 7, [(8, NSEQ)]), ALU.add),
                                  reads=[a1, a2], writes=[sso[1]])
                            kb.dma(SP, o_ssm_s.ap()[sample_h0, 1][:, j, :], sso[1][:, :], reads=[sso[1]], is_out=True)
                        for ri in range(2):
                            cpairs.append((T["Cl"][ri][:, j, :], HB[ri][jj][:, 0:W], [T["Cl"][ri], HB[ri][jj]]))
                mm_group(bky[:, 0:W], cpairs, bky, [])
                ys = U32[m]
                kb.op(DVE, lambda: nc.vector.scalar_tensor_tensor(ys[:, 0:W], U32[m][:, 0:W], T["dsk"][:, m:m + 1], bky[:, 0:W], ALU.mult, ALU.add),
                      reads=[U32[m], T["dsk"], bky], writes=[ys])
                kb.op(ACT, lambda: nc.scalar.activation(U32[m][:, 0:W], ys[:, 0:W], AF.Gelu_apprx_tanh), reads=[ys], writes=[U32[m]])
                kb.op(ACT, lambda: nc.scalar.copy(GEL[m][:, 0:W], U32[m][:, 0:W]), reads=[U32[m]], writes=[GEL[m]])
            for n in range(4):
                wb = load_w(WB["w_glu"].ap()[li_cur[0], n], 4, 128)
                bk = kb.bank()
                mm_group(bk[:, 0:W], [(wb[:, k, 0:128], GEL[k][:, 0:W], [wb, GEL[k]]) for k in range(4)], bk, [])
                sg = SG[n % 2]
                kb.op(DVE, lambda: nc.vector.tensor_scalar(sg[:, 0:W], bk[:, 0:W], T["bgl"][:, n:n + 1], None, ALU.add), reads=[bk, T["bgl"]], writes=[sg])
                kb.op(ACT, lambda: nc.scalar.activation(sg[:, 0:W], sg[:, 0:W], AF.Sigmoid), reads=[sg], writes=[sg])
                kb.op(DVE, lambda: nc.vector.tensor_tensor(MIX[n][:, 0:W], U32[n][:, 0:W], sg[:, 0:W], ALU.mult),
                      reads=[U32[n], sg], writes=[MIX[n]])

        li_cur = [0]

        def in_proj(li, W, want_v_tok_blocks):
            for n in range(4):
                wb = load_w(WB["w_in"].ap()[li, n], 8, 128)
                if n == 0:
                    kb.chk("ip_w")
                bk = kb.bank()
                _mv = _os.environ.get("KMM", "")
                if _mv == "k":
                    mm_group(bk[:, 0:W], [(ones_bf[:], H[k][:, 0:W], [ones_bf, H[k]]) for k in range(8)], bk, [])
                elif _mv == "m":
                    for k in range(8):
                        kb.op(DVE, lambda k=k: nc.vector.tensor_copy(H[k][:, 0:W], X[k][:, 0:W]), reads=[X[k]], writes=[H[k]])
                    mm_group(bk[:, 0:W], [(wb[:, k, 0:128], H[k][:, 0:W], [wb, H[k]]) for k in range(8)], bk, [])
                elif _mv == "q":
                    for k in range(8):
                        kb.op(DVE, lambda k=k: nc.vector.tensor_copy(P16[k][:, 0:W], X[k][:, 0:W]), reads=[X[k]], writes=[P16[k]])
                    mm_group(bk[:, 0:W], [(wb[:, k, 0:128], P16[k][:, 0:W], [wb, P16[k]]) for k in range(8)], bk, [])
                elif _mv == "n":
                    for k in range(8):
                        kb.op(ACT, lambda k=k: nc.scalar.copy(H[k][:, 0:W], X[k][:, 0:W]), reads=[X[k]], writes=[H[k]])
                    mm_group(bk[:, 0:W], [(wb[:, k, 0:128], H[k][:, 0:W], [wb, H[k]]) for k in range(8)], bk, [])
                elif _mv == "l":
                    mm_group(bk[:, 0:W], [(wb[:, k, 0:128], sq[k][:, 0:W], [wb, sq[k]]) for k in range(8)], bk, [])
                else:
                    mm_group(bk[:, 0:W], [(wb[:, k, 0:128], H[k][:, 0:W], [wb, H[k]]) for k in range(8)], bk, [])
                if n == 0:
                    kb.chk("ip_mm")
                kb.op(DVE, lambda: nc.vector.tensor_copy(UT[n][:, 0:W], bk[:, 0:W]), reads=[bk], writes=[UT[n]])
                if n == 0:
                    kb.chk("ip_act")
                import os
                tv = os.environ.get("KVAR", "")
                if tv == "a":
                    kb.op(DVE, lambda: nc.vector.tensor_copy(P32[5][:, 0:W], bk[:, 0:W]), reads=[bk], writes=[P32[5]])
                elif tv == "c":
                    kb.op(DVE, lambda: nc.vector.tensor_scalar(U32[n][:, 0:W], bk[:, 0:W], 1.0, None, ALU.mult), reads=[bk], writes=[U32[n]])
                elif tv == "d":
                    kb.op(DVE, lambda: nc.vector.tensor_copy(U32[n][:, 0:W], bk[:, 0:W]), reads=[bk], writes=[U32[n]])
                elif tv == "g":
                    ob = kb.banks[(kb.psum_rr - 2) % 8]
                    kb.op(DVE, lambda: nc.vector.tensor_copy(P32[5][:, 0:W], ob[:, 0:W]), reads=[ob], writes=[P32[5]])
                elif tv == "g2":
                    ob = kb.banks[(kb.psum_rr - 2) % 8]
                    kb.op(DVE, lambda: nc.vector.tensor_copy(P32[5][:, 0:W], ob[:, 0:W]), reads=[ob, bk], writes=[P32[5]])
                elif tv == "j":
                    ob = kb.banks[(kb.psum_rr - 2) % 8]
                    kb.op(PE, lambda: nc.tensor.matmul(ob[:, 0:W], ones_bf[:], H[0][:, 0:W], start=True, stop=True), reads=[ones_bf, H[0]], writes=[ob])
                    kb.op(DVE, lambda: nc.vector.tensor_copy(U32[n][:, 0:W], bk[:, 0:W]), reads=[bk], writes=[U32[n]])
                elif tv == "h":
                    kb.op(DVE, lambda: nc.vector.tensor_copy(P32[5][0:64, 0:W], bk[0:64, 0:W]), reads=[bk], writes=[P32[5]])
                elif tv == "i":
                    kb.op(DVE, lambda: nc.vector.tensor_copy(P32[5][:, 0:64], bk[:, 0:64]), reads=[bk], writes=[P32[5]])
                elif tv == "e":
                    kb.op(DVE, lambda: nc.vector.tensor_copy(U32[n][:, 0:W], P32[6][:, 0:W]), reads=[P32[6]], writes=[U32[n]])
                elif tv == "f":
                    kb.op(DVE, lambda: nc.vector.tensor_copy(P32[5][:, 0:W], X[0][:, 0:W]), reads=[X[0]], writes=[P32[5]])
                elif tv == "b":
                    kb.op(DVE, lambda: nc.vector.tensor_copy(U32[n][:, 0:W], X[0][:, 0:W]), reads=[X[0]], writes=[U32[n]])
                else:
                    kb.op(DVE, lambda: nc.vector.tensor_copy(U32[n][:, 0:W], bk[:, 0:W]), reads=[bk], writes=[U32[n]])
                if n == 0:
                    kb.chk("ip_u0")
            kb.chk("ip_u")
            for n in range(4):
                wb = load_w(WB["w_in"].ap()[li, 4 + n], 8, 128)
                bk = kb.bank()
                mm_group(bk[:, 0:W], [(wb[:, k, 0:128], H[k][:, 0:W], [wb, H[k]]) for k in range(8)], bk, [])
                kb.op(DVE, lambda: nc.vector.tensor_copy(QT[0][n][0:64, 0:W], bk[0:64, 0:W]), reads=[bk], writes=[QT[0][n]])
                kb.op(DVE, lambda: nc.vector.tensor_copy(QT[1][n][64:128, 0:W], bk[64:128, 0:W]), reads=[bk], writes=[QT[1][n]])

        def attn_prompt(li, W, blk0):
            A = ATT[li]
            nb = W // 128
            for kv in range(2):
                wb = load_w(WB["w_in"].ap()[li, 8 + kv], 8, 128)
                bk = kb.bank()
                mm_group(bk[:, 0:W], [(wb[:, k, 0:128], H[k][:, 0:W], [wb, H[k]]) for k in range(8)], bk, [])
                kb.op(DVE, lambda: nc.vector.tensor_copy(A["KK"][kv][:, 128:128 + W], bk[:, 0:W]), reads=[bk], writes=[A["KK"][kv]])
                if kv == 0:
                    kb.op(DVE, lambda: nc.vector.tensor_copy(A["kT32"][0:64, :], bk[0:64, W - 128:W]), reads=[bk], writes=[A["kT32"]])
                else:
                    kb.op(DVE, lambda: nc.vector.tensor_copy(A["kT32"][64:128, :], bk[64:128, W - 128:W]), reads=[bk], writes=[A["kT32"]])
            kb.chk("at_k")
            wv = load_w(WB["w_in"].ap()[li, 10], 8, 128)
            for b in range(nb):
                vz = A["Vz"][(blk0 + b) % NVB]
                bk = kb.bank()
                mm_group(bk[:, 0:128], [(H[k][:, b * 128:(b + 1) * 128], wv[:, k, 0:128], [H[k], wv]) for k in range(8)], bk, [])
                kb.op(DVE, lambda: nc.vector.tensor_copy(vz[:, 64:128], bk[:, 0:64]), reads=[bk], writes=[vz])
                kb.op(DVE, lambda: nc.vector.tensor_copy(vz[:, 192:256], bk[:, 64:128]), reads=[bk], writes=[vz])
                if b == nb - 1:
                    kb.op(DVE, lambda: nc.vector.tensor_copy(A["v32"][:, :], bk[:, 0:128]), reads=[bk], writes=[A["v32"]])
            kb.chk("at_v")
            for b in range(nb):
                gb = blk0 + b
                has_prev = gb >= 1
                q0 = b * 128
                pts = []
                for c in range(4):
                    kv = c // 2
                    bk = kb.bank()
                    for e in range(2):
                        kb.op(PE, lambda e=e, bk=bk: nc.tensor.matmul(bk[:, e * 128:(e + 1) * 128],
                                                                      A["KK"][kv][:, 128 + q0:256 + q0],
                                                                      QT[e][c][:, q0:q0 + 128], start=True, stop=True),
                              reads=[A["KK"][kv], QT[e][c]], writes=[bk], inc=(e == 1 and not has_prev))
                    if has_prev:
                        for e in range(2):
                            kb.op(PE, lambda e=e, bk=bk: nc.tensor.matmul(bk[:, 256 + e * 128:256 + (e + 1) * 128],
                                                                          A["KK"][kv][:, q0:q0 + 128],
                                                                          QT[e][c][:, q0:q0 + 128], start=True, stop=True),
                                  reads=[A["KK"][kv], QT[e][c]], writes=[bk], inc=(e == 1))
                    ncol = 512 if has_prev else 256
                    ex = EX[c % 2]
                    kb.chk("at_mm")
                    kb.op(DVE, lambda bk=bk, ex=ex: nc.vector.tensor_copy(ex[:, 0:ncol], bk[:, 0:ncol]), reads=[bk], writes=[ex])
                    kb.chk("at_cp")
                    kb.op(ACT, lambda ex=ex: nc.scalar.activation(ex[:, 0:ncol], ex[:, 0:ncol], AF.Exp, scale=0.125), reads=[ex], writes=[ex])
                    kb.chk("at_ex")
                    pt = PT[c]
                    kb.op(DVE, lambda ex=ex, pt=pt, c=c: nc.vector.tensor_tensor(pt[:, 0:ncol], ex[:, 0:ncol], Etab[c][:, 0:ncol], ALU.mult),
                          reads=[ex, Etab[c]], writes=[pt])
                    pts.append(pt)
                kb.chk("at_s")
                bn, bd = kb.bank(), kb.bank()
                vcur = A["Vz"][gb % NVB]
                vprev = A["Vz"][(gb - 1) % NVB]
                ocur = Oz0 if gb == 0 else Oz
                oprev = Oz0 if gb == 1 else Oz
                for c in range(4):
                    kv = c // 2
                    npairs, dpairs = [], []
                    for e in range(2):
                        vs = (64 + 128 * kv, 192 + 128 * kv) if e == 0 else (128 * kv, 128 + 128 * kv)
                        osl = (64, 192) if e == 0 else (0, 128)
                        npairs.append((vcur[:, vs[0]:vs[1]], pts[c][:, e * 128:(e + 1) * 128], [vcur, pts[c]]))
                        dpairs.append((ocur[:, osl[0]:osl[1]], pts[c][:, e * 128:(e + 1) * 128], [ocur, pts[c]]))
                        if has_prev:
                            npairs.append((vprev[:, vs[0]:vs[1]], pts[c][:, 256 + e * 128:256 + (e + 1) * 128], [vprev, pts[c]]))
                            dpairs.append((oprev[:, osl[0]:osl[1]], pts[c][:, 256 + e * 128:256 + (e + 1) * 128], [oprev, pts[c]]))
                    mm_group(bn[:, c * 128:(c + 1) * 128], npairs, bn, [])
                    mm_group(bd[:, c * 128:(c + 1) * 128], dpairs, bd, [])
                kb.chk("at_pv")
                for c in range(4):
                    kb.op(DVE, lambda bd=bd, c=c: nc.vector.tensor_scalar(dn_t[:, c * 128:(c + 1) * 128], bd[:, c * 128:(c + 1) * 128], EsT[li][:, c:c + 1], None, ALU.add),
                          reads=[bd, EsT[li]], writes=[dn_t])
                kb.op(DVE, lambda: nc.vector.reciprocal(dn_t[:, 0:512], dn_t[:, 0:512]), reads=[dn_t], writes=[dn_t])
                for c in range(4):
                    kb.op(DVE, lambda c=c, bn=bn: nc.vector.tensor_tensor(MIX[4 + c][:, q0:q0 + 128], bn[:, c * 128:(c + 1) * 128],
                                                                          dn_t[:, c * 128:(c + 1) * 128], ALU.mult),
                          reads=[bn, dn_t], writes=[MIX[4 + c]])
            for kv in range(2):
                kb.op(ACT, lambda kv=kv: nc.scalar.copy(A["KK"][kv][:, 0:128], A["KK"][kv][:, W:W + 128]),
                      reads=[A["KK"][kv]], writes=[A["KK"][kv]])

        def out_proj(dram_w, W, nk, src, after=None):
            wb_next = load_w(dram_w[0], nk, 128)
            for n in range(8):
                wb = wb_next
                if n + 1 < 8:
                    wb_next = load_w(dram_w[n + 1], nk, 128)
                bk = kb.bank()
                mm_group(bk[:, 0:W], [(wb[:, k, 0:128], src[k][:, 0:W], [wb, src[k]]) for k in range(nk)], bk, [])
                kb.op(DVE, lambda n=n, bk=bk: nc.vector.tensor_tensor(X[n][:, 0:W], X[n][:, 0:W], bk[:, 0:W], ALU.add),
                      reads=[X[n], bk], writes=[X[n]])

        def conformer(lo, W, nseq, Tt, halo, zero_pad_cols=0):
            C = CONV[lo]
            ext = 30 + Tt
            wb_next = load_w(WB["w_pw1"].ap()[lo, 0], 8, 256)
            for c in range(8):
                wb = wb_next
                if c + 1 < 8:
                    wb_next = load_w(WB["w_pw1"].ap()[lo, c + 1], 8, 256)
                ba, bg = kb.bank(), kb.bank()
                mm_group(ba[:, 0:W], [(wb[:, k, 0:128], H[k][:, 0:W], [wb, H[k]]) for k in range(8)], ba, [])
                mm_group(bg[:, 0:W], [(wb[:, k, 128:256], H[k][:, 0:W], [wb, H[k]]) for k in range(8)], bg, [])
                sg = SG[c % 2]
                gl = GLW[c % 2]
                kb.op(DVE, lambda: nc.vector.tensor_copy(sg[:, 0:W], bg[:, 0:W]), reads=[bg], writes=[sg])
                kb.op(ACT, lambda: nc.scalar.activation(sg[:, 0:W], sg[:, 0:W], AF.Sigmoid), reads=[sg], writes=[sg])
                if halo is not None:
                    kb.op(ACT, lambda: nc.scalar.copy(V(gl, 0, 128, 0, [(ext, nseq), (1, 30)]), V(halo, 0, 128, c * nseq * 30, [(30, nseq), (1, 30)])),
                          reads=[halo], writes=[gl])
                else:
                    kb.dma(SP, V(gl, 0, 128, 0, [(ext, nseq), (1, 30)]), st_conv.ap()[lo, c * 128:(c + 1) * 128, :, :], writes=[gl])
                kb.op(DVE, lambda: nc.vector.tensor_tensor(V(gl, 0, 128, 30, [(ext, nseq), (1, Tt)]),
                                                           V(ba, 0, 128, 0, [(Tt, nseq), (1, Tt)]),
                                                           V(sg, 0, 128, 0, [(Tt, nseq), (1, Tt)]), ALU.mult),
                      reads=[ba, sg], writes=[gl])
                if halo is not None:
                    kb.op(ACT, lambda: nc.scalar.copy(V(halo, 0, 128, c * nseq * 30, [(30, nseq), (1, 30)]), V(gl, 0, 128, Tt, [(ext, nseq), (1, 30)])),
                          reads=[gl], writes=[halo])
                else:
                    kb.dma(SP, o_conv_s.ap()[lo, c * 128:(c + 1) * 128, :, :], V(gl, 0, 128, Tt, [(ext, nseq), (1, 30)]), reads=[gl], is_out=True)
                eng_e, eng = (DVE, nc.vector)
                y = LNY[c]
                yv = V(y, 0, 128, 0, [(Tt, nseq), (1, Tt)])
                kb.op(eng_e, lambda: eng.tensor_scalar(yv, V(gl, 0, 128, 0, [(ext, nseq), (1, Tt)]), C["wdw"][:, c, 0:1], C["cv"][:, 0, c:c + 1],
                                                       ALU.mult, ALU.add), reads=[gl, C["wdw"], C["cv"]], writes=[y])
                for k in range(1, 31):
                    kb.op(eng_e, lambda k=k: eng.scalar_tensor_tensor(yv, V(gl, 0, 128, k, [(ext, nseq), (1, Tt)]), C["wdw"][:, c, k:k + 1], yv,
                                                                      ALU.mult, ALU.add), reads=[gl, C["wdw"], y], writes=[y])
            bm, b2 = kb.bank(), kb.bank()
            mm_group(bm[:, 0:W], [(ones_f[:], LNY[c][:, 0:W], [LNY[c]]) for c in range(8)], bm, [ones_f])
            kb.op(DVE, lambda: nc.vector.tensor_scalar(mu_t[:, 0:W], bm[:, 0:W], 1.0 / D, None, ALU.mult), reads=[bm], writes=[mu_t])
            for c in range(8):
                kb.op(DVE, lambda c=c: nc.vector.tensor_tensor(LNY[c][:, 0:W], LNY[c][:, 0:W], mu_t[:, 0:W], ALU.subtract),
                      reads=[LNY[c], mu_t], writes=[LNY[c]])
                sqf = SQF[c % 2]
                kb.op(ACT, lambda c=c, sqf=sqf: nc.scalar.activation(sqf[:, 0:W], LNY[c][:, 0:W], AF.Square), reads=[LNY[c]], writes=[sqf])
                kb.op(PE, lambda c=c, sqf=sqf: nc.tensor.matmul(b2[:, 0:W], ones_f[:], sqf[:, 0:W], start=(c == 0), stop=(c == 7)),
                      reads=[ones_f, sqf], writes=[b2])
            kb.op(DVE, lambda: nc.vector.tensor_scalar(rs[:, 0:W], b2[:, 0:W], 1.0 / D, EPS, ALU.mult, ALU.add), reads=[b2], writes=[rs])
            kb.op(ACT, lambda: nc.scalar.activation(rs[:, 0:W], rs[:, 0:W], AF.Sqrt), reads=[rs], writes=[rs])
            kb.op(DVE, lambda: nc.vector.reciprocal(rinv[:, 0:W], rs[:, 0:W]), reads=[rs], writes=[rinv])
            for c in range(8):
                kb.op(DVE, lambda c=c: nc.vector.scalar_tensor_tensor(LNY[c][:, 0:W], LNY[c][:, 0:W], C["cv"][:, 1, c:c + 1], rinv[:, 0:W],
                                                                      ALU.mult, ALU.mult), reads=[LNY[c], C["cv"], rinv], writes=[LNY[c]])
                kb.op(ACT, lambda c=c: nc.scalar.activation(LNY[c][:, 0:W], LNY[c][:, 0:W], AF.Silu, bias=C["cv"][:, 2, c:c + 1], scale=1.0),
                      reads=[LNY[c], C["cv"]], writes=[LNY[c]])
                kb.op(DVE, lambda c=c: nc.vector.tensor_copy(LNS[c][:, 0:W], LNY[c][:, 0:W]), reads=[LNY[c]], writes=[LNS[c]])
            out_proj(WB["w_pw2"].ap()[lo], W, 8, LNS)
            if zero_pad_cols:
                for c in range(8):
                    kb.op(DVE, lambda c=c: nc.vector.memset(X[c][:, 0:zero_pad_cols], 0.0), writes=[X[c]])


        def ffn_down(l, W):
            for n in range(8):
                wd0 = load_wdn(WB["w_dn"].ap()[l, n, 0])
                wd1 = load_wdn(WB["w_dn"].ap()[l, n, 1])
                bk = kb.bank()
                mm_group(bk[:, 0:W], [((wd0 if j < 11 else wd1)[:, j % 11, :], YF[j][:, 0:W], [wd0 if j < 11 else wd1, YF[j]]) for j in range(NJ)], bk, [])
                kb.op(DVE, lambda n=n, bk=bk: nc.vector.tensor_tensor(X[n][:, 0:W], X[n][:, 0:W], bk[:, 0:W], ALU.add),
                      reads=[X[n], bk], writes=[X[n]])

        def ffn_up(l, W, nseq, Tt, halo):
            Fd = FFN[l]
            ext = 2 + Tt
            wb_next = load_w(WB["w_up"].ap()[l, 0], 8, 256)
            for j in range(NJ):
                wb = wb_next
                if j + 1 < NJ:
                    wb_next = load_w(WB["w_up"].ap()[l, j + 1], 8, 256)
                bg, bu = kb.bank(), kb.bank()
                mm_group(bg[:, 0:W], [(wb[:, k, 0:128], H[k][:, 0:W], [wb, H[k]]) for k in range(8)], bg, [])
                mm_group(bu[:, 0:W], [(wb[:, k, 128:256], H[k][:, 0:W], [wb, H[k]]) for k in range(8)], bu, [])
                gr = GR[j % 3]
                kb.op(DVE, lambda: nc.vector.tensor_copy(V(gr, 0, 128, 2, [(ext, nseq), (1, Tt)]), V(bg, 0, 128, 0, [(Tt, nseq), (1, Tt)])),
                      reads=[bg], writes=[gr])
                if halo is not None:
                    kb.op(ACT, lambda: nc.scalar.copy(V(gr, 0, 128, 0, [(ext, nseq), (1, 2)]), V(halo, 0, 128, j * nseq * 2, [(2, nseq), (1, 2)])),
                          reads=[halo], writes=[gr])
                else:
                    kb.dma(SP, V(gr, 0, 128, 0, [(ext, nseq), (1, 2)]), st_ffn.ap()[l, j * 128:(j + 1) * 128, :, :], writes=[gr], slow=True)
                acc = ACC[j % 3]
                av = V(acc, 0, 128, 0, [(Tt, nseq), (1, Tt)])
                kb.op(DVE, lambda: nc.vector.tensor_scalar(av, V(gr, 0, 128, 0, [(ext, nseq), (1, Tt)]), Fd["cw"][:, j, 0:1], Fd["cb"][:, j:j + 1],
                                                           ALU.mult, ALU.add), reads=[gr, Fd["cw"], Fd["cb"]], writes=[acc])
                for k in (1, 2):
                    kb.op(DVE, lambda k=k: nc.vector.scalar_tensor_tensor(av, V(gr, 0, 128, k, [(ext, nseq), (1, Tt)]), Fd["cw"][:, j, k:k + 1], av,
                                                                          ALU.mult, ALU.add), reads=[gr, Fd["cw"], acc], writes=[acc])
                if halo is not None:
                    kb.op(ACT, lambda: nc.scalar.copy(V(halo, 0, 128, j * nseq * 2, [(2, nseq), (1, 2)]), V(gr, 0, 128, Tt, [(ext, nseq), (1, 2)])),
                          reads=[gr], writes=[halo])
                else:
                    kb.dma(SP, o_ffn_s.ap()[l, j * 128:(j + 1) * 128, :, :], V(gr, 0, 128, Tt, [(ext, nseq), (1, 2)]), reads=[gr], is_out=True, slow=True)
                kb.op(ACT, lambda: nc.scalar.activation(acc[:, 0:W], acc[:, 0:W], AF.Gelu_apprx_tanh), reads=[acc], writes=[acc])
                kb.op(DVE, lambda: nc.vector.tensor_tensor(YF[j][:, 0:W], acc[:, 0:W], bu[:, 0:W], ALU.mult), reads=[acc, bu], writes=[YF[j]])

        chunks = [(0, 128)] + [(128 + 512 * i, 512) for i in range(16)]
        chunks = chunks[:nchunk_prompt]
        for ci, (t0, W) in enumerate(chunks):
            blk0 = t0 // 128
            for c in range(8):
                kb.dma(SP, X[c][:, 0:W], xT_p.ap()[c * 128:(c + 1) * 128, t0:t0 + W], writes=[X[c]])
            for l in range(4):
                rmsnorm(X, l, H, W)
                kb.chk("n0")
                if l % 2 == 0:
                    li = l // 2
                    li_cur[0] = li
                    in_proj(li, W, None)
                    kb.chk("inproj")
                    s5_core(S5[li], W, W // 64, 64)
                    kb.chk("s5c")
                    attn_prompt(li, W, blk0)
                    kb.chk("att")
                    if ci == dbg_ci and li == 0:
                        dump16(MIX, W)
                        dump16(UT + QT[0], W)
                        dump(Etab + Etab, 512)
                    out_proj(WB["w_out"].ap()[li], W, 8, MIX)
                    if ci == dbg_ci:
                        dump(X, W)
                else:
                    lo = l // 2
                    conformer(lo, W, 1, W, CONV[lo]["halo"], zero_pad_cols=(NPAD if ci == 0 else 0))
                    if ci == dbg_ci:
                        dump(X, W)
                rmsnorm(X, 4 + l, H, W)
                ffn_up(l, W, 1, W, FFN[l]["halo"])
                ffn_down(l, W)
                if ci == dbg_ci:
                    dump(X, W)
                if ci == 0 and l % 2 == 1:
                    pass
            for c in range(8):
                pass
            YO = [P32[c] for c in range(8)]
            rmsnorm(X, 8, None, W, out_f32=YO)
            for c in range(8):
                kb.dma(SP, yT_p.ap()[c * 128:(c + 1) * 128, t0:t0 + W], YO[c][:, 0:W], reads=[YO[c]], is_out=True)
        if nchunk_prompt == 17:
            for li in range(2):
                for ri in range(2):
                    kb.dma(SP, o_ssm_p.ap()[li, ri], S5[li]["car"][ri][:], reads=[S5[li]["car"][ri]], is_out=True)
                kb.dma(SP, o_kT_p.ap()[li], ATT[li]["kT32"][:], reads=[ATT[li]["kT32"]], is_out=True)
                kb.dma(SP, o_v_p.ap()[li], ATT[li]["v32"][:], reads=[ATT[li]["v32"]], is_out=True)
                for c in range(8):
                    kb.dma(SP, o_conv_p.ap()[li, c * 128:(c + 1) * 128, :], CONV[li]["halo"][:, c, :], reads=[CONV[li]["halo"]], is_out=True)
            for l in range(4):
                kb.dma(SP, o_ffn_p.ap()[l].rearrange("(j p) t -> p j t", p=128), FFN[l]["halo"][:], reads=[FFN[l]["halo"]], is_out=True)

        def attn_sample(li):
            A = ATT[li]
            W = 128
            for kv in range(2):
                wb = load_w(WB["w_in"].ap()[li, 8 + kv], 8, 128)
                bk = kb.bank()
                mm_group(bk[:, 0:W], [(wb[:, k, 0:128], H[k][:, 0:W], [wb, H[k]]) for k in range(8)], bk, [])
                kb.op(DVE, lambda: nc.vector.tensor_copy(KKs[kv][:, :], bk[:, 0:W]), reads=[bk], writes=[KKs[kv]])
                kb.op(DVE, lambda: nc.vector.tensor_copy(A["kT32"][64 * kv:64 * kv + 64, :], bk[64 * kv:64 * kv + 64, 0:W]), reads=[bk], writes=[A["kT32"]])
            wv = load_w(WB["w_in"].ap()[li, 10], 8, 128)
            bk = kb.bank()
            mm_group(bk[:, 0:128], [(H[k][:, 0:128], wv[:, k, 0:128], [H[k], wv]) for k in range(8)], bk, [])
            kb.op(DVE, lambda: nc.vector.tensor_copy(vbf[:, :], bk[:, 0:128]), reads=[bk], writes=[vbf])
            kb.op(DVE, lambda: nc.vector.tensor_copy(A["v32"][:, :], bk[:, 0:128]), reads=[bk], writes=[A["v32"]])
            kb.dma(SP, o_kT_s.ap()[li][:, :, 0:120], st_kT.ap()[li][:, :, 8:128], is_out=True, slow=True)
            kb.dma(SP, o_v_s.ap()[li][:, 0:120, :], st_v.ap()[li][:, 8:128, :], is_out=True)
            kb.dma(SP, o_kT_s.ap()[li][:, :, 120:128].rearrange("s p t -> p s t"), V(A["kT32"], 0, 128, 0, [(8, NSEQ), (1, 8)]),
                   reads=[A["kT32"]], is_out=True, slow=True)
            for s_ in range(NSEQ):
                kb.dma(SP, o_v_s.ap()[li, s_][120:128, :], A["v32"][8 * s_:8 * s_ + 8, :], reads=[A["v32"]], is_out=True)
            bn, bd = kb.bank(), kb.bank()
            sbanks = [kb.bank() for _ in range(4)]
            for sq_i in range(NSEQ):
                kx = [KX[kv][sq_i % 2] for kv in range(2)]
                vz, vb = VZS[sq_i % 2], VBS[sq_i % 2]
                for kv in range(2):
                    for hf in range(2):
                        kb.dma(POOL, kx[kv][64 * hf:64 * hf + 64, 0:128], st_kT.ap()[li, sq_i][64 * kv:64 * kv + 64, :], writes=[kx[kv]])
                    kb.op(ACT, lambda kv=kv: nc.scalar.copy(kx[kv][:, 128:136], KKs[kv][:, 8 * sq_i:8 * sq_i + 8]), reads=[KKs[kv]], writes=[kx[kv]])
                    kb.dma(POOL, vz[0:120, 64 + 128 * kv:128 + 128 * kv], st_v.ap()[li, sq_i][8:128, 64 * kv:64 * kv + 64], writes=[vz])
                    kb.dma(POOL, vb[0:8, 64 + 128 * kv:128 + 128 * kv], st_v.ap()[li, sq_i][0:8, 64 * kv:64 * kv + 64], writes=[vb])
                    kb.dma(POOL, vz[120:128, 64 + 128 * kv:128 + 128 * kv], vbf[8 * sq_i:8 * sq_i + 8, 64 * kv:64 * kv + 64], reads=[vbf], writes=[vz])
                ba, bb = sbanks[2 * (sq_i % 2)], sbanks[2 * (sq_i % 2) + 1]
                for c in range(4):
                    kv = c // 2
                    for e in range(2):
                        col = c * 16 + e * 8
                        kb.op(PE, lambda: nc.tensor.matmul(ba[:, col:col + 8], kx[kv][:, 8:136],
                                                           QT[e][c][:, 8 * sq_i:8 * sq_i + 8], start=True, stop=True),
                              reads=[kx[kv], QT[e][c]], writes=[ba], inc=False)
                        kb.op(PE, lambda: nc.tensor.matmul(bb[0:8, col:col + 8], kx[kv][:, 0:8],
                                                           QT[e][c][:, 8 * sq_i:8 * sq_i + 8], start=True, stop=True),
                              reads=[kx[kv], QT[e][c]], writes=[bb], inc=(c == 3 and e == 1))
                kb.op(DVE, lambda: nc.vector.tensor_copy(EXA[:, :], ba[:, 0:64]), reads=[ba], writes=[EXA])
                kb.op(ACT, lambda: nc.scalar.activation(EXA[:, :], EXA[:, :], AF.Exp, scale=0.125), reads=[EXA], writes=[EXA])
                kb.op(DVE, lambda: nc.vector.tensor_copy(EXB[:, :], bb[0:8, 0:64]), reads=[bb], writes=[EXB])
                kb.op(ACT, lambda: nc.scalar.activation(EXB[:, :], EXB[:, :], AF.Exp, scale=0.125), reads=[EXB], writes=[EXB])
                pa, pb = PAs[sq_i % 2], PBs[sq_i % 2]
                kb.op(DVE, lambda: nc.vector.tensor_tensor(pa[:, :], EXA[:, :], EAt[:, :], ALU.mult), reads=[EXA, EAt], writes=[pa])
                kb.op(DVE, lambda: nc.vector.tensor_tensor(pb[:, :], EXB[:, :], EBt[:, :], ALU.mult), reads=[EXB, EBt], writes=[pb])
                for c in range(4):
                    kv = c // 2
                    npairs, dpairs = [], []
                    for e in range(2):
                        vs = (64 + 128 * kv, 192 + 128 * kv) if e == 0 else (128 * kv, 128 + 128 * kv)
                        osl = (64, 192) if e == 0 else (0, 128)
                        col = c * 16 + e * 8
                        npairs.append((vz[:, vs[0]:vs[1]], pa[:, col:col + 8], [vz, pa]))
                        npairs.append((vb[0:8, vs[0]:vs[1]], pb[0:8, col:col + 8], [vb, pb]))
                        dpairs.append((Oz[:, osl[0]:osl[1]], pa[:, col:col + 8], [Oz, pa]))
                        dpairs.append((Oz[0:8, osl[0]:osl[1]], pb[0:8, col:col + 8], [Oz, pb]))
                    oc = sq_i * 32 + c * 8
                    mm_group(bn[:, oc:oc + 8], npairs, bn, [])
                    mm_group(bd[:, oc:oc + 8], dpairs, bd, [])
            for c in range(4):
                dv = V(dn_t, 0, 128, c * 8, [(32, NSEQ), (1, 8)])
                kb.op(DVE, lambda: nc.vector.tensor_scalar(dv, V(bd, 0, 128, c * 8, [(32, NSEQ), (1, 8)]), EsT[li][:, c:c + 1], None, ALU.add),
                      reads=[bd, EsT[li]], writes=[dn_t])
            kb.op(DVE, lambda: nc.vector.reciprocal(dn_t[:, 0:512], dn_t[:, 0:512]), reads=[dn_t], writes=[dn_t])
            for c in range(4):
                kb.op(DVE, lambda: nc.vector.tensor_tensor(V(MIX[4 + c], 0, 128, 0, [(8, NSEQ), (1, 8)]), V(bn, 0, 128, c * 8, [(32, NSEQ), (1, 8)]),
                                                           V(dn_t, 0, 128, c * 8, [(32, NSEQ), (1, 8)]), ALU.mult),
                      reads=[bn, dn_t], writes=[MIX[4 + c]])

        if do_sample:
            KKs = [kb.sb("KKs%d" % k, [128, 128], BF16) for k in range(2)]
            vbf = kb.sb("vbf", [128, 128], BF16)
            KX = [[kb.sb("KX%d%d" % (k, i), [128, 136], BF16) for i in range(2)] for k in range(2)]
            VZS = [kb.sb("VZS%d" % i, [128, 320], BF16) for i in range(2)]
            VBS = [kb.sb("VBS%d" % i, [8, 320], BF16) for i in range(2)]
            for t_ in VZS + VBS:
                kb.op(POOL, lambda t_=t_: nc.gpsimd.memset(t_[:], 0.0), writes=[t_])
            EXA = kb.sb("EXA", [128, 64], F32)
            EXB = kb.sb("EXB", [8, 64], F32)
            PAs = [kb.sb("PAs%d" % i, [128, 64], BF16) for i in range(2)]
            PBs = [kb.sb("PBs%d" % i, [8, 64], BF16) for i in range(2)]
            W = 128
            for c in range(8):
                kb.dma(SP, X[c][:, 0:W], xT_s.ap()[c * 128:(c + 1) * 128, :], writes=[X[c]])
            for l in range(4):
                rmsnorm(X, l, H, W)
                if l % 2 == 0:
                    li = l // 2
                    li_cur[0] = li
                    in_proj(li, W, None)
                    s5_core(S5[li], W, NSEQ, 8, sample_h0=li)
                    attn_sample(li)
                    if li == 0:
                        dump16(MIX, W)
                        dump([P32[2]] * 8, 128)
                    out_proj(WB["w_out"].ap()[li], W, 8, MIX)
                else:
                    lo = l // 2
                    conformer(lo, W, NSEQ, 8, None)
                rmsnorm(X, 4 + l, H, W)
                ffn_up(l, W, NSEQ, 8, None)
                ffn_down(l, W)
            YO = [P32[c] for c in range(8)]
            rmsnorm(X, 8, None, W, out_f32=YO)
            for c in range(8):
                kb.dma(SP, yT_s.ap()[c * 128:(c + 1) * 128, :], YO[c][:, 0:W], reads=[YO[c]], is_out=True)

        kb.finish()
    return kb.nc


_CACHE = {}


def _prep_common(inp):
    f = np.float32
    d = {}
    gv = np.concatenate([inp["g_mix"], inp["g_ffn"], inp["g_final"][None]], 0)
    d["gvec"] = np.ascontiguousarray(gv.reshape(9, 8, 128).transpose(2, 0, 1)).astype(f)
    wi = inp["w_in_mix"]
    u, q = wi[:, :, 0:512], wi[:, :, 512:1024]
    k0, k1, v = wi[:, :, 1024:1088], wi[:, :, 1088:1152], wi[:, :, 1152:1280]
    def tile_w(w, ncols):
        L, K, N = w.shape
        t = w.reshape(L, K // 128, 128, N // ncols, ncols).transpose(0, 3, 2, 1, 4)
        return np.ascontiguousarray(t.reshape(L, N // ncols, 128, (K // 128) * ncols)).astype(f)
    d["w_in"] = tile_w(np.concatenate([u, q, k0, k0, k1, k1, v], -1), 128)

    def st(a):
        return a.reshape(2, 16, 2, 64).transpose(0, 2, 3, 1).reshape(2, 128, 16)
    ls = np.broadcast_to(inp["ssm_log_step"][:, :, None], (2, 32, 64))
    d["lam"] = np.ascontiguousarray(np.stack([st(inp["ssm_lambda_re"]), st(inp["ssm_lambda_im"]), st(ls)], 1)).astype(f)
    d["ssm_b"] = np.ascontiguousarray(np.stack([inp["ssm_b_re"], inp["ssm_b_im"]], 1)).astype(f)
    d["ssm_c"] = np.ascontiguousarray(np.stack([inp["ssm_c_re"], inp["ssm_c_im"]], 1)).astype(f)
    d["ssm_d"] = np.ascontiguousarray(inp["ssm_d"].reshape(2, 4, 128).transpose(0, 2, 1)).astype(f)
    d["w_glu"] = tile_w(inp["ssm_w_glu"], 128)
    d["b_glu"] = np.ascontiguousarray(inp["ssm_b_glu"].reshape(2, 4, 128).transpose(0, 2, 1)).astype(f)
    d["relb"] = np.ascontiguousarray(inp["rel_bias"]).astype(f)
    sk = inp["attn_sinks"]
    d["sinks"] = np.ascontiguousarray(np.repeat(sk.reshape(2, 4, 2), 64, axis=2).transpose(0, 2, 1)).astype(f)
    d["w_out"] = tile_w(inp["w_out_mix"], 128)
    p1 = inp["conv_w_pw1"]
    a, g = p1[:, :, :1024].reshape(2, 1024, 8, 128), p1[:, :, 1024:].reshape(2, 1024, 8, 128)
    d["w_pw1"] = tile_w(np.concatenate([a, g], -1).reshape(2, 1024, 2048), 256)
    d["w_dw"] = np.ascontiguousarray(inp["conv_w_dw"].reshape(2, 31, 8, 128).transpose(0, 3, 2, 1)).astype(f)
    cv = np.stack([inp["conv_b_dw"], inp["conv_ln_g"], inp["conv_ln_b"]], 1)
    d["cvec"] = np.ascontiguousarray(cv.reshape(2, 3, 8, 128).transpose(0, 3, 1, 2)).astype(f)
    d["w_pw2"] = tile_w(inp["conv_w_pw2"], 128)
    wu = inp["ffn_w_up"]
    gg, uu = wu[:, :, :DFF].reshape(4, 1024, NJ, 128), wu[:, :, DFF:].reshape(4, 1024, NJ, 128)
    d["w_up"] = tile_w(np.concatenate([gg, uu], -1).reshape(4, 1024, NJ * 256), 256)
    d["f_cw"] = np.ascontiguousarray(inp["ffn_w_conv"].reshape(4, 3, NJ, 128).transpose(0, 3, 2, 1)).astype(f)
    d["f_cb"] = np.ascontiguousarray(inp["ffn_b_conv"].reshape(4, NJ, 128).transpose(0, 2, 1)).astype(f)
    wd = inp["ffn_w_down"].reshape(4, 2, 11, 128, 8, 128).transpose(0, 4, 1, 3, 2, 5)
    d["w_dn"] = np.ascontiguousarray(wd.reshape(4, 8, 2, 128, 11 * 128)).astype(f)
    m = np.arange(384)
    dist = m - 127
    inside = (dist >= 0) & (dist < 128)
    oh = np.zeros((32, 384), f)
    bk = t5_bucket_np(np.clip(dist, 0, 127))
    oh[bk[inside], m[inside]] = 1.0
    d["oh_bucket"] = oh
    d["msk_ext"] = np.ascontiguousarray(np.broadcast_to(np.where(inside, 0.0, NEG).astype(f)[None], (8, 384)))
    d["antiI"] = np.ascontiguousarray(np.eye(128, dtype=f)[::-1])
    return d


def kernel(**inp):
    inp = {k: np.asarray(v) for k, v in inp.items()}
    f = np.float32
    if "nc" not in _CACHE:
        _CACHE["nc"] = build()
    nc = _CACHE["nc"]
    com = _prep_common(inp)
    in_maps = []
    for c in range(8):
        d = dict(com)
        s = c % 2
        xp = np.concatenate([np.zeros((NPAD, D), f), inp["meta_tokens"], inp["x_prompt"][s]], 0)
        d["xT_p"] = np.ascontiguousarray(xp.T)
        sl = slice(c * NSEQ, (c + 1) * NSEQ)
        d["xT_s"] = np.ascontiguousarray(inp["x_sample"][sl].reshape(NSEQ * 8, D).T)

        def st(a):
            return a.reshape(2, NSEQ, 16, 2, 64).transpose(0, 3, 4, 2, 1).reshape(2, 128, 16, NSEQ)
        d["st_ssm"] = np.ascontiguousarray(np.stack([st(inp["state_ssm_re"][:, sl]), st(inp["state_ssm_im"][:, sl])], 1)).astype(f)
        d["st_kT"] = np.ascontiguousarray(inp["cache_swa_k"][:, sl].reshape(2, NSEQ, 128, 128).transpose(0, 1, 3, 2)).astype(f)
        d["st_v"] = np.ascontiguousarray(inp["cache_swa_v"][:, sl].reshape(2, NSEQ, 128, 128)).astype(f)
        d["st_conv"] = np.ascontiguousarray(inp["state_conv"][:, sl].transpose(0, 3, 1, 2)).astype(f)
        d["st_ffn"] = np.ascontiguousarray(inp["state_ffn"][:, sl].transpose(0, 3, 1, 2)).astype(f)
        in_maps.append(d)
    res = run_bass_kernel_spmd(nc, in_maps, core_ids=list(range(8)))
    R = res.results
    _CACHE["last"] = R
    y_p = np.stack([R[s]["yT_p"][:, 128:].T for s in range(2)], 0)
    y_s = np.concatenate([R[c]["yT_s"].T.reshape(NSEQ, 8, D) for c in range(8)], 0)

    def ust(a):
        return a.reshape(2, 2, 64, 16).transpose(0, 3, 1, 2).reshape(2, 32, 64)
    sr_p = np.stack([ust(R[s]["o_ssm_p"][:, 0]) for s in range(2)], 1)
    si_p = np.stack([ust(R[s]["o_ssm_p"][:, 1]) for s in range(2)], 1)
    k_p = np.stack([R[s]["o_kT_p"].transpose(0, 2, 1).reshape(2, 128, 2, 64) for s in range(2)], 1)
    v_p = np.stack([R[s]["o_v_p"].reshape(2, 128, 2, 64) for s in range(2)], 1)
    c_p = np.stack([R[s]["o_conv_p"].transpose(0, 2, 1) for s in range(2)], 1)
    f_p = np.stack([R[s]["o_ffn_p"].transpose(0, 2, 1) for s in range(2)], 1)

    def usts(a):
        return a.reshape(2, 2, 64, 16, NSEQ).transpose(0, 4, 3, 1, 2).reshape(2, NSEQ, 32, 64)
    sr_s = np.concatenate([usts(R[c]["o_ssm_s"][:, 0]) for c in range(8)], 1)
    si_s = np.concatenate([usts(R[c]["o_ssm_s"][:, 1]) for c in range(8)], 1)
    k_s = np.concatenate([R[c]["o_kT_s"].transpose(0, 1, 3, 2).reshape(2, NSEQ, 128, 2, 64) for c in range(8)], 1)
    v_s = np.concatenate([R[c]["o_v_s"].reshape(2, NSEQ, 128, 2, 64) for c in range(8)], 1)
    c_s = np.concatenate([R[c]["o_conv_s"].transpose(0, 2, 3, 1) for c in range(8)], 1)
    f_s = np.concatenate([R[c]["o_ffn_s"].transpose(0, 2, 3, 1) for c in range(8)], 1)
    outs = (y_p, y_s, sr_p, si_p, k_p, v_p, c_p, f_p, sr_s, si_s, k_s, v_s, c_s, f_s)
    return tuple(np.ascontiguousarray(o).astype(np.float32) for o in outs)
```
